# Optimizing a Trainium2 kernel written in Bass

```python
import math
import jax, jax.numpy as jnp
from jax import lax
import numpy as np

D_MODEL = 1024
BATCH = 8
SEQ = 4096
DEPTH = 4
DEC_BATCH = 8
DEC_SEQ = 8192
PAST_LEN = 128

GRID_W = 64
D_MIX = D_MODEL
ATT_WIDTH = D_MIX // 2
HEAD_DIM = 64
N_HEADS = ATT_WIDTH // HEAD_DIM
N_KV_HEADS = 2
KV_GROUP = N_HEADS // N_KV_HEADS
ROPE_THETA = 10000.0
ROPE_FREQS = HEAD_DIM // 4
Q_BLOCK = 128
QK_EPS = 1e-6
LRU_WIDTH = D_MIX // 4
LRU_HEADS = 4
LRU_HEAD_DIM = LRU_WIDTH // LRU_HEADS
LRU_CONV = 4
LRU_PAD_L = 2
LRU_PAD_R = 1
LRU_C = 8.0
HY_WIDTH = D_MIX - ATT_WIDTH - LRU_WIDTH
HY_ORDER = 2
HY_SHORT = 3
HY_EMB = 33
HY_BANDS = (HY_EMB - 1) // 2
HY_FFN = 64
HY_MIN_DECAY = abs(math.log(1e-2)) / 1.5
HY_MAX_DECAY = abs(math.log(1e-2)) / 0.3
D_FF = 2816
FFN_CONV = 3
ALPHA = (2 * DEPTH) ** 0.25
BETA = (8 * DEPTH) ** -0.25
LN_EPS = 1e-5
IN_Q = N_HEADS * HEAD_DIM
IN_KV = N_KV_HEADS * HEAD_DIM
IN_TOTAL = IN_Q + 2 * IN_KV + 2 * LRU_WIDTH + (HY_ORDER + 1) * HY_WIDTH
IN_SPLITS = (IN_Q, IN_Q + IN_KV, IN_Q + 2 * IN_KV, IN_Q + 2 * IN_KV + LRU_WIDTH, IN_Q + 2 * IN_KV + 2 * LRU_WIDTH)

kernel_name = 'hybrid_attn_rglru_hyena_encoder'


def layer_norm(x, g, b):
    xf = x.astype(jnp.float32)
    mu = jnp.mean(xf, axis=-1, keepdims=True)
    var = jnp.mean(jnp.square(xf - mu), axis=-1, keepdims=True)
    return ((xf - mu) * lax.rsqrt(var + LN_EPS) * g + b).astype(x.dtype)


def rms_norm(x, g):
    xf = x.astype(jnp.float32)
    return (xf * lax.rsqrt(jnp.mean(jnp.square(xf), axis=-1, keepdims=True) + QK_EPS) * g).astype(x.dtype)


def dw_conv(x, w, b, left, right):
    L = x.shape[1]
    xp = jnp.pad(x, ((0, 0), (left, right), (0, 0)))
    y = b
    for k in range(w.shape[0]):
        y = y + xp[:, k:k + L] * w[k]
    return y.astype(x.dtype)


def axial_rope(L):
    rows = L // GRID_W
    row = jnp.repeat(jnp.arange(rows, dtype=jnp.float32), GRID_W)
    col = jnp.tile(jnp.arange(GRID_W, dtype=jnp.float32), rows)
    inv = ROPE_THETA ** (-jnp.arange(ROPE_FREQS, dtype=jnp.float32) / ROPE_FREQS)
    ang = jnp.stack([row[:, None] * inv, col[:, None] * inv], axis=1)
    return jnp.cos(ang), jnp.sin(ang)


def apply_rope(x, cos, sin):
    B, L, H, _ = x.shape
    xs = x.astype(jnp.float32).reshape(B, L, H, 2, 2 * ROPE_FREQS)
    x1, x2 = xs[..., :ROPE_FREQS], xs[..., ROPE_FREQS:]
    c = cos[None, :, None]
    s = sin[None, :, None]
    out = jnp.concatenate([x1 * c - x2 * s, x2 * c + x1 * s], axis=-1)
    return out.reshape(B, L, H, HEAD_DIM).astype(x.dtype)


def block_attention(q, k, v):
    B, L = q.shape[:2]
    nblk = L // Q_BLOCK
    qb = q.reshape(B, nblk, Q_BLOCK, N_KV_HEADS, KV_GROUP, HEAD_DIM).transpose(1, 0, 2, 3, 4, 5)
    scale = HEAD_DIM ** -0.5

    def one_block(qi):
        s = jnp.einsum('bqkgd,bskd->bkgqs', qi, k).astype(jnp.float32) * scale
        p = jax.nn.softmax(s, axis=-1).astype(v.dtype)
        return jnp.einsum('bkgqs,bskd->bqkgd', p, v)

    o = lax.map(one_block, qb)
    return o.transpose(1, 0, 2, 3, 4, 5).reshape(B, L, N_HEADS * HEAD_DIM)


def rglru_scan(xc, wa, ba, wx, bx, lam, reverse):
    B, L, C = xc.shape
    xh = xc.reshape(B, L, LRU_HEADS, LRU_HEAD_DIM)
    r = jax.nn.sigmoid(jnp.einsum('blhi,hij->blhj', xh, wa).reshape(B, L, C) + ba)
    i = jax.nn.sigmoid(jnp.einsum('blhi,hij->blhj', xh, wx).reshape(B, L, C) + bx)
    log_a = (-LRU_C * r * jax.nn.softplus(-lam)).astype(jnp.float32)
    a = jnp.exp(log_a)
    b = jnp.sqrt(-jnp.expm1(2.0 * log_a)) * (i * xc).astype(jnp.float32)

    def combine(e1, e2):
        a1, b1 = e1
        a2, b2 = e2
        return a1 * a2, a2 * b1 + b2

    _, h = lax.associative_scan(combine, (a, b), reverse=reverse, axis=1)
    return h


def hyena_filters(L, w1, b1, w2, b2, w3, b3, freq, wout):
    t = jnp.linspace(0.0, 1.0, L, dtype=jnp.float32)[:, None]
    w = 2.0 * math.pi * jnp.arange(L, dtype=jnp.float32)[:, None] / L
    f = jnp.linspace(1e-4, HY_BANDS - 1, HY_BANDS, dtype=jnp.float32)[None, :]
    z = jnp.concatenate([t, jnp.cos(f * w), -jnp.sin(f * w)], axis=-1)
    h = jnp.sin(freq * (z @ w1 + b1))
    h = jnp.sin(freq * (h @ w2 + b2))
    h = jnp.sin(freq * (h @ w3 + b3))
    k = (h @ wout).astype(jnp.float32).reshape(L, HY_ORDER, 2, HY_WIDTH)
    deltas = jnp.linspace(HY_MIN_DECAY, HY_MAX_DECAY, HY_WIDTH, dtype=jnp.float32)
    k = k * jnp.exp(-t * deltas)[:, None, None, :]
    kf, kb = k[:, :, 0], k[:, :, 1]
    full = jnp.concatenate([kf.at[0].add(kb[0]), jnp.zeros((1, HY_ORDER, HY_WIDTH), jnp.float32), kb[:0:-1]], axis=0)
    full = full / jnp.sum(jnp.abs(full), axis=0, keepdims=True)
    return jnp.fft.rfft(full, axis=0)


def long_conv(z, kf, d):
    L = z.shape[1]
    zf = z.astype(jnp.float32)
    y = jnp.fft.irfft(jnp.fft.rfft(zf, n=2 * L, axis=1) * kf, n=2 * L, axis=1)[:, :L]
    return (y + zf * d).astype(z.dtype)


def mixer(u, w_in, q_gain, k_gain, lru_conv_w, lru_conv_b, lru_wa, lru_ba, lru_wx, lru_bx, lru_lambda,
          hy_conv_w, hy_conv_b, hy_w1, hy_b1, hy_w2, hy_b2, hy_w3, hy_b3, hy_freq, hy_wout, hy_bias, w_out):
    B, L, _ = u.shape
    proj = u @ w_in
    q, k, v, xr, gr, hy = jnp.split(proj, IN_SPLITS, axis=-1)
    q = rms_norm(q.reshape(B, L, N_HEADS, HEAD_DIM), q_gain)
    k = rms_norm(k.reshape(B, L, N_KV_HEADS, HEAD_DIM), k_gain)
    v = v.reshape(B, L, N_KV_HEADS, HEAD_DIM)
    cos, sin = axial_rope(L)
    attn = block_attention(apply_rope(q, cos, sin), apply_rope(k, cos, sin), v)
    xc = dw_conv(xr, lru_conv_w, lru_conv_b, LRU_PAD_L, LRU_PAD_R)
    h = (rglru_scan(xc, lru_wa[0], lru_ba[0], lru_wx[0], lru_bx[0], lru_lambda[0], False)
         + rglru_scan(xc, lru_wa[1], lru_ba[1], lru_wx[1], lru_bx[1], lru_lambda[1], True))
    lru = h.astype(u.dtype) * jax.nn.gelu(gr)
    hc = dw_conv(hy, hy_conv_w, hy_conv_b, 1, 1)
    z, g1, g2 = jnp.split(hc, 3, axis=-1)
    filt = hyena_filters(L, hy_w1, hy_b1, hy_w2, hy_b2, hy_w3, hy_b3, hy_freq, hy_wout)
    z = g1 * long_conv(z, filt[:, 0], hy_bias[0])
    z = g2 * long_conv(z, filt[:, 1], hy_bias[1])
    return jnp.concatenate([attn, lru, z], axis=-1) @ w_out


def conv_ffn(u, w_up, conv_w, conv_b, w_down):
    hcat = dw_conv(u @ w_up, conv_w, conv_b, 1, 1)
    gate, val = jnp.split(hcat, 2, axis=-1)
    return (jax.nn.gelu(gate) * val) @ w_down


def layer(x, c, ada_w, ada_b, w_in, q_gain, k_gain, lru_conv_w, lru_conv_b, lru_wa, lru_ba, lru_wx, lru_bx,
          lru_lambda, hy_conv_w, hy_conv_b, hy_w1, hy_b1, hy_w2, hy_b2, hy_w3, hy_b3, hy_freq, hy_wout, hy_bias,
          w_out, ln1_g, ln1_b, ffn_w_up, ffn_conv_w, ffn_conv_b, ffn_w_down, ln2_g, ln2_b):
    mod = (jax.nn.silu(c) @ ada_w + ada_b)[:, None, :]
    sh1, sc1, gt1, sh2, sc2, gt2 = jnp.split(mod, 6, axis=-1)
    u = x * (1.0 + sc1) + sh1
    m = mixer(u, w_in, q_gain, k_gain, lru_conv_w, lru_conv_b, lru_wa, lru_ba, lru_wx, lru_bx, lru_lambda,
              hy_conv_w, hy_conv_b, hy_w1, hy_b1, hy_w2, hy_b2, hy_w3, hy_b3, hy_freq, hy_wout, hy_bias, w_out)
    x = layer_norm(ALPHA * x + gt1 * m, ln1_g, ln1_b)
    u = x * (1.0 + sc2) + sh2
    f = conv_ffn(u, ffn_w_up, ffn_conv_w, ffn_conv_b, ffn_w_down)
    return layer_norm(ALPHA * x + gt2 * f, ln2_g, ln2_b)


def trunk(x, c, params):
    for l in range(DEPTH):
        x = layer(x, c, *[p[l] for p in params])
    return x


def setup_inputs(seed: int = 0) -> dict:
    key = jax.random.key(seed)
    ks = jax.random.split(key, 40)
    f32 = jnp.float32

    def nrm(k, shape, scale):
        return jax.random.normal(k, shape, f32) * scale

    u = jax.random.uniform(ks[15], (DEPTH, 2, LRU_WIDTH), f32, minval=0.9, maxval=0.999)
    a0 = u ** (1.0 / LRU_C)
    lru_lambda = jnp.log(a0) - jnp.log1p(-a0)
    return {
        'x_prompt': nrm(ks[0], (BATCH, SEQ, D_MODEL), 1.0),
        'x_sample': nrm(ks[1], (DEC_BATCH, DEC_SEQ, D_MODEL), 1.0),
        'c_prompt': nrm(ks[2], (BATCH, D_MODEL), 1.0),
        'c_sample': nrm(ks[3], (DEC_BATCH, D_MODEL), 1.0),
        'ada_w': nrm(ks[4], (DEPTH, D_MODEL, 6 * D_MODEL), D_MODEL ** -0.5),
        'ada_b': nrm(ks[5], (DEPTH, 6 * D_MODEL), 0.02),
        'w_in': nrm(ks[6], (DEPTH, D_MODEL, IN_TOTAL), D_MODEL ** -0.5),
        'q_gain': 1.0 + nrm(ks[7], (DEPTH, HEAD_DIM), 0.02),
        'k_gain': 1.0 + nrm(ks[8], (DEPTH, HEAD_DIM), 0.02),
        'lru_conv_w': nrm(ks[9], (DEPTH, LRU_CONV, LRU_WIDTH), LRU_CONV ** -0.5),
        'lru_conv_b': nrm(ks[10], (DEPTH, LRU_WIDTH), 0.02),
        'lru_wa': nrm(ks[11], (DEPTH, 2, LRU_HEADS, LRU_HEAD_DIM, LRU_HEAD_DIM), LRU_HEAD_DIM ** -0.5),
        'lru_ba': nrm(ks[12], (DEPTH, 2, LRU_WIDTH), 0.02),
        'lru_wx': nrm(ks[13], (DEPTH, 2, LRU_HEADS, LRU_HEAD_DIM, LRU_HEAD_DIM), LRU_HEAD_DIM ** -0.5),
        'lru_bx': nrm(ks[14], (DEPTH, 2, LRU_WIDTH), 0.02),
        'lru_lambda': lru_lambda,
        'hy_conv_w': nrm(ks[16], (DEPTH, HY_SHORT, (HY_ORDER + 1) * HY_WIDTH), HY_SHORT ** -0.5),
        'hy_conv_b': nrm(ks[17], (DEPTH, (HY_ORDER + 1) * HY_WIDTH), 0.02),
        'hy_w1': nrm(ks[18], (DEPTH, HY_EMB, HY_FFN), HY_EMB ** -0.5),
        'hy_b1': nrm(ks[19], (DEPTH, HY_FFN), 0.02),
        'hy_w2': nrm(ks[20], (DEPTH, HY_FFN, HY_FFN), HY_FFN ** -0.5),
        'hy_b2': nrm(ks[21], (DEPTH, HY_FFN), 0.02),
        'hy_w3': nrm(ks[22], (DEPTH, HY_FFN, HY_FFN), HY_FFN ** -0.5),
        'hy_b3': nrm(ks[23], (DEPTH, HY_FFN), 0.02),
        'hy_freq': 1.0 + nrm(ks[24], (DEPTH, HY_FFN), 0.02),
        'hy_wout': nrm(ks[25], (DEPTH, HY_FFN, HY_ORDER * 2 * HY_WIDTH), HY_FFN ** -0.5),
        'hy_bias': nrm(ks[26], (DEPTH, HY_ORDER, HY_WIDTH), 0.1),
        'w_out': nrm(ks[27], (DEPTH, D_MIX, D_MODEL), D_MIX ** -0.5 * BETA),
        'ln1_g': 1.0 + nrm(ks[28], (DEPTH, D_MODEL), 0.02),
        'ln1_b': nrm(ks[29], (DEPTH, D_MODEL), 0.02),
        'ffn_w_up': nrm(ks[30], (DEPTH, D_MODEL, 2 * D_FF), D_MODEL ** -0.5),
        'ffn_conv_w': nrm(ks[31], (DEPTH, FFN_CONV, 2 * D_FF), FFN_CONV ** -0.5),
        'ffn_conv_b': nrm(ks[32], (DEPTH, 2 * D_FF), 0.02),
        'ffn_w_down': nrm(ks[33], (DEPTH, D_FF, D_MODEL), D_FF ** -0.5 * BETA),
        'ln2_g': 1.0 + nrm(ks[34], (DEPTH, D_MODEL), 0.02),
        'ln2_b': nrm(ks[35], (DEPTH, D_MODEL), 0.02),
    }


def reference(x_prompt, x_sample, c_prompt, c_sample, ada_w, ada_b, w_in, q_gain, k_gain, lru_conv_w, lru_conv_b,
              lru_wa, lru_ba, lru_wx, lru_bx, lru_lambda, hy_conv_w, hy_conv_b, hy_w1, hy_b1, hy_w2, hy_b2, hy_w3,
              hy_b3, hy_freq, hy_wout, hy_bias, w_out, ln1_g, ln1_b, ffn_w_up, ffn_conv_w, ffn_conv_b, ffn_w_down,
              ln2_g, ln2_b):
    params = (ada_w, ada_b, w_in, q_gain, k_gain, lru_conv_w, lru_conv_b, lru_wa, lru_ba, lru_wx, lru_bx,
              lru_lambda, hy_conv_w, hy_conv_b, hy_w1, hy_b1, hy_w2, hy_b2, hy_w3, hy_b3, hy_freq, hy_wout,
              hy_bias, w_out, ln1_g, ln1_b, ffn_w_up, ffn_conv_w, ffn_conv_b, ffn_w_down, ln2_g, ln2_b)
    y_prompt = trunk(x_prompt, c_prompt, params)
    y_sample = trunk(x_sample, c_sample, params)
    return (y_prompt, y_sample)
```

```python
import math
import numpy as np
from contextlib import ExitStack
import concourse.bass as bass
import concourse.mybir as mybir
from concourse.bass_types import AP as APc
from concourse.bass_utils import run_bass_kernel_spmd

F32 = mybir.dt.float32
BF16 = mybir.dt.bfloat16
AF = mybir.ActivationFunctionType
ALU = mybir.AluOpType
AX = mybir.AxisListType

D = 1024
DEPTH = 4
LP = 4096
LS = 8192
DFF = 2816
ALPHA = (2 * DEPTH) ** 0.25
LN_EPS = 1e-5
QK_EPS = 1e-6
HY_MIN_DECAY = abs(math.log(1e-2)) / 1.5
HY_MAX_DECAY = abs(math.log(1e-2)) / 0.3
MAGIC = 12582912.0

ENGS = ("pe", "act", "dve", "pool", "sp")
EMBED_WAITS = True


class Buf:
    __slots__ = ("name", "w", "r")

    def __init__(self, name):
        self.name = name
        self.w = None
        self.r = []


class Sched:
    NDMA = 8

    def __init__(self, nc, ctx):
        self.nc = nc
        self.streams = {e: [] for e in ENGS}
        self.sems = {}
        self.cnt = {}
        for e in ENGS:
            self.sems[e] = ctx.enter_context(nc.semaphore("s_" + e))
            self.cnt[e] = 0
        self.dnext = {}
        for q in ("sp", "act", "pool"):
            for i in range(self.NDMA):
                k = "d_%s%d" % (q, i)
                self.sems[k] = ctx.enter_context(nc.semaphore(k))
                self.cnt[k] = 0
            self.dnext[q] = 0
        self.waited = {e: {} for e in ENGS}
        self.nbuf = 0

    def buf(self, name=None):
        self.nbuf += 1
        return Buf(name or ("b%d" % self.nbuf))

    def _need(self, eng, ev, same_ok):
        if ev is None:
            return
        k, v, src = ev
        if src == eng and same_ok:
            return
        if self.waited[eng].get(k, 0) >= v:
            return
        self.waited[eng][k] = v
        self.streams[eng].append(("w", k, v))

    def _deps(self, eng, rd, wr):
        pe = (eng == "pe")
        for b in rd:
            self._need(eng, b.w, pe)
        for b in wr:
            self._need(eng, b.w, pe)
            for ev in b.r:
                self._need(eng, ev, True)

    def _mark(self, ev, rd, wr):
        for b in rd:
            b.r.append(ev)
            if len(b.r) > 64:
                last = {}
                for e2 in b.r:
                    if e2[0] not in last or last[e2[0]][1] < e2[1]:
                        last[e2[0]] = e2
                b.r = list(last.values())
        for b in wr:
            b.w = ev
            b.r = []

    def op(self, eng, fn, rd=(), wr=()):
        self._deps(eng, rd, wr)
        self.cnt[eng] += 1
        ev = (eng, self.cnt[eng], eng)
        self.streams[eng].append(("o", fn, eng, 1))
        self._mark(ev, rd, wr)
        return ev

    def dma(self, q, out, in_, rd=(), wr=(), slow=False):
        self._deps(q, rd, wr)
        i = self.dnext[q]
        self.dnext[q] = (i + 1) % self.NDMA
        k = "d_%s%d" % (q, i)
        if self.cnt[k] > 0:
            self._need(q, (k, self.cnt[k], None), False)
        self.cnt[k] += 16
        ev = (k, self.cnt[k], None)
        if slow:
            self.streams[q].append(("o", (lambda e: e.dma_start(out=out, in_=in_, allow_slow_non_contiguous=True)), k, 16))
        else:
            self.streams[q].append(("o", (lambda e: e.dma_start(out=out, in_=in_)), k, 16))
        self._mark(ev, rd, wr)
        return ev

    def barrier(self):
        for e in ENGS:
            for k, v in self.cnt.items():
                if v > 0 and k != e:
                    self._need(e, (k, v, None), False)

    def emit(self):
        nc = self.nc
        if not any(self.streams[e] for e in ENGS):
            return
        engobj = {"pe": "tensor", "act": "scalar", "dve": "vector", "pool": "gpsimd", "sp": "sync"}
        with nc.Block() as block:
            for e in ENGS:
                items = self.streams[e]
                sems = self.sems

                def body(eng, items=items, sems=sems):
                    n = len(items)
                    i = 0
                    while i < n:
                        it = items[i]
                        if it[0] == "w":
                            if EMBED_WAITS and i + 1 < n and items[i + 1][0] == "o":
                                nx = items[i + 1]
                                ins = nx[1](eng)
                                ins._wait_ge(sems[it[1]], it[2])
                                ins.then_inc(sems[nx[2]], nx[3])
                                i += 2
                                continue
                            eng.wait_ge(sems[it[1]], it[2])
                        else:
                            it[1](eng).then_inc(sems[it[2]], it[3])
                        i += 1
                getattr(block, engobj[e])(body)
        self.streams = {e: [] for e in ENGS}


class TB:
    __slots__ = ("t", "b")

    def __init__(self, t, b):
        self.t = t
        self.b = b

    def __getitem__(self, k):
        return self.t[k]


def rev(ap2d):
    (ps, pn), (fs, fn) = ap2d.ap
    return APc(ap2d.tensor, ap2d.offset + (fn - 1) * fs, [[ps, pn], [-fs, fn]])


def dap(t, off, dims):
    return APc(t.tensor, t.offset + off, [[a, b] for a, b in dims])


class K:
    def __init__(self, nc, s, ctx, nlayers=DEPTH, dbg=None):
        self.nc, self.s, self.gctx = nc, s, ctx
        self.nlayers = nlayers
        self.dbg = dbg or {}
        self.pctx = None
        self.uid = 0

    def _nm(self, n):
        self.uid += 1
        return "%s_%d" % (n, self.uid)

    def sb(self, name, shape, dt, glob=False):
        c = self.gctx if glob else self.pctx
        t = c.enter_context(self.nc.sbuf_tensor(self._nm(name), list(shape), dt))
        return TB(t, self.s.buf(name))

    def ps(self, name, shape, dt=F32):
        t = self.pctx.enter_context(self.nc.psum_tensor(self._nm(name), list(shape), dt))
        return TB(t, self.s.buf(name))

    def dram(self, name, shape, dt, kind="Internal"):
        return self.nc.dram_tensor(name, list(shape), dt, kind=kind).ap()

    def mm(self, out, lhsT, rhs, start, stop, rd, wr, tp=None):
        if tp is None:
            f = lambda e: e.matmul(out=out, lhsT=lhsT, rhs=rhs, start=start, stop=stop)
        else:
            f = lambda e: e.matmul(out=out, lhsT=lhsT, rhs=rhs, start=start, stop=stop, tile_position=tp)
        return self.s.op("pe", f, rd=rd, wr=wr)

    def tr(self, out, in_, ident, rd, wr):
        return self.s.op("pe", lambda e: e.transpose(out=out, in_=in_, identity=ident), rd=rd, wr=wr)

    def act(self, out, in_, func, rd, wr, bias=None, scale=None, accum=None):
        kw = {}
        if bias is not None:
            kw["bias"] = bias
        if scale is not None:
            kw["scale"] = scale
        if accum is not None:
            kw["accum_out"] = accum
        return self.s.op("act", lambda e: e.activation(out=out, in_=in_, func=func, **kw), rd=rd, wr=wr)

    def tt(self, eng, out, in0, in1, op, rd, wr):
        return self.s.op(eng, lambda e: e.tensor_tensor(out=out, in0=in0, in1=in1, op=op), rd=rd, wr=wr)

    def ts(self, eng, out, in0, s1, s2, op0, op1, rd, wr):
        if op1 is None:
            f = lambda e: e.tensor_scalar(out=out, in0=in0, scalar1=s1, scalar2=None, op0=op0)
        else:
            f = lambda e: e.tensor_scalar(out=out, in0=in0, scalar1=s1, scalar2=s2, op0=op0, op1=op1)
        return self.s.op(eng, f, rd=rd, wr=wr)

    def stt(self, eng, out, in0, scalar, in1, op0, op1, rd, wr):
        return self.s.op(eng, lambda e: e.scalar_tensor_tensor(out=out, in0=in0, scalar=scalar, in1=in1, op0=op0, op1=op1), rd=rd, wr=wr)

    def cp(self, eng, out, in_, rd, wr):
        if eng == "act":
            return self.s.op("act", lambda e: e.copy(out=out, in_=in_), rd=rd, wr=wr)
        return self.s.op(eng, lambda e: e.tensor_copy(out=out, in_=in_), rd=rd, wr=wr)

    def memset(self, eng, ap, val, wr):
        return self.s.op(eng, lambda e: e.memset(ap, val), rd=(), wr=wr)

    def recip(self, out, in_, rd, wr):
        return self.s.op("dve", lambda e: e.reciprocal(out=out, in_=in_), rd=rd, wr=wr)

    def dma(self, q, out, in_, rd=(), wr=(), slow=False):
        return self.s.dma(q, out, in_, rd=rd, wr=wr, slow=slow)

    def phase_begin(self):
        self.pctx = ExitStack()

    def phase_end(self):
        self.s.barrier()
        self.pctx.close()
        self.pctx = None


def _init_arena(self):
    self.pctx = None


def _sb(self, name, cols, dt=F32, glob=False, parts=128):
    c = self.gctx if glob else self.pctx
    t = c.enter_context(self.nc.sbuf_tensor(self._nm(name), [128, cols], dt))
    return TB(t[0:parts, :], self.s.buf(name))


def _ps(self, name, cols=512, dt=F32, parts=128):
    full = 512 if dt == F32 else 1024
    t = self.pctx.enter_context(self.nc.psum_tensor(self._nm(name), [128, full], dt))
    return TB(t[0:parts, 0:cols], self.s.buf(name))


def _ps2(self, name):
    t = self.pctx.enter_context(self.nc.psum_tensor(self._nm(name), [128, 1024], F32))
    return TB(t[:, :], self.s.buf(name))


def _phase_begin(self):
    self.pctx = ExitStack()


def _phase_end(self):
    self.s.barrier()
    self.s.emit()
    self.pctx.close()
    self.pctx = None


K.init_arena = _init_arena
K.sb = _sb
K.ps = _ps
K.ps2 = _ps2
K.phase_begin = _phase_begin
K.phase_end = _phase_end


def _consts():
    c = {}
    c["ident"] = np.eye(128, dtype=np.float32)
    t = np.arange(LS)
    row = (t // 64).astype(np.float32)
    col = (t % 64).astype(np.float32)
    inv = (10000.0 ** (-np.arange(16, dtype=np.float32) / 16)).astype(np.float32)
    ang = np.stack([row[:, None] * inv, col[:, None] * inv], axis=1).astype(np.float32)
    cs = np.cos(ang).reshape(LS // 128, 128, 32).transpose(1, 0, 2)
    sn = np.sin(ang).reshape(LS // 128, 128, 32).transpose(1, 0, 2)
    for s, L in enumerate((LP, LS)):
        f32 = np.float32
        t = np.linspace(0.0, 1.0, L, dtype=f32)[:, None]
        w = (f32(2.0 * math.pi) * np.arange(L, dtype=f32)[:, None] / f32(L)).astype(f32)
        f = np.linspace(1e-4, 15, 16, dtype=f32)[None, :]
        z = np.concatenate([t, np.cos(f * w), -np.sin(f * w)], axis=-1).astype(f32)
        c["hz%d" % s] = np.ascontiguousarray(z.T)
        c["htl%d" % s] = np.ascontiguousarray(t.T)
        NB = L // 128
        N1 = 2 * NB
        N = 2 * L
        a = np.arange(N1, dtype=np.float64)[:, None]
        kl = np.arange(N1, dtype=np.float64)[None, :]
        th = 2 * np.pi * a * kl / N1
        c["hFN1_%d" % s] = np.concatenate([np.cos(th), -np.sin(th)], axis=1).astype(f32)
        i = np.arange(128, dtype=np.float64)[:, None]
        th = 2 * np.pi * i * kl / N
        c["hT_%d" % s] = np.stack([np.cos(th), -np.sin(th)], axis=1).astype(f32)
        c["hT2_%d" % s] = np.stack([np.cos(th.T), np.sin(th.T)], axis=1).astype(f32)
        aa = np.arange(NB, dtype=np.float64)[None, :]
        th = 2 * np.pi * np.arange(N1, dtype=np.float64)[:, None] * aa / N1
        c["hGN1_%d" % s] = np.stack([np.cos(th) / N, -np.sin(th) / N], axis=1).astype(f32)
    i = np.arange(128, dtype=np.float64)[:, None]
    kh = np.arange(128, dtype=np.float64)[None, :]
    th = 2 * np.pi * i * kh / 128
    c["hF128"] = np.stack([np.cos(th), -np.sin(th), np.sin(th)], axis=1).astype(np.float32)
    c["hG128"] = np.stack([np.concatenate([np.cos(th), np.sin(th)], axis=1),
                           np.concatenate([-np.sin(th), np.cos(th)], axis=1)], axis=1).astype(np.float32)
    dl = np.linspace(HY_MIN_DECAY, HY_MAX_DECAY, 256, dtype=np.float32)
    c["hnegd"] = np.ascontiguousarray((-dl).reshape(2, 128).T)
    c["rope_cos"] = np.ascontiguousarray(cs, dtype=np.float32)
    c["rope_sin"] = np.ascontiguousarray(sn, dtype=np.float32)
    return c


def host_prep(inp):
    f32 = np.float32
    A = lambda a: np.ascontiguousarray(a, dtype=f32)
    sh = {}
    sh["ada_w"] = A(inp["ada_w"])
    sh["ada_b"] = A(inp["ada_b"])
    sh["ada_bT"] = A(inp["ada_b"].reshape(DEPTH, 48, 128).transpose(0, 2, 1))
    sh["w_in"] = A(inp["w_in"])
    sh["w_out"] = A(inp["w_out"])
    sh["w_up"] = A(inp["ffn_w_up"])
    sh["w_down"] = A(inp["ffn_w_down"])
    sh["qkg"] = A(np.concatenate([np.tile(inp["q_gain"], (1, 8)), np.tile(inp["k_gain"], (1, 2))], axis=1))
    lw = np.concatenate([inp["lru_conv_w"], inp["lru_conv_b"][:, None, :]], axis=1)
    sh["lruw"] = A(lw.reshape(DEPTH, 5, 2, 128).transpose(0, 3, 2, 1))
    W = np.zeros((DEPTH, 2, 2, 2, 128, 128), f32)
    for gi, nm in enumerate(("lru_wa", "lru_wx")):
        w = np.asarray(inp[nm])
        for cc in range(2):
            for h2 in range(2):
                W[:, :, gi, cc, h2 * 64:(h2 + 1) * 64, h2 * 64:(h2 + 1) * 64] = w[:, :, 2 * cc + h2]
    sh["lruW"] = W
    lb = np.stack([inp["lru_ba"], inp["lru_bx"], inp["lru_lambda"]], axis=-1)
    sh["lrub"] = A(lb.reshape(DEPTH, 2, 2, 128, 3).transpose(0, 3, 2, 1, 4))
    fw = np.concatenate([inp["ffn_conv_w"], inp["ffn_conv_b"][:, None, :]], axis=1)
    sh["ffw"] = A(fw.reshape(DEPTH, 4, 44, 128).transpose(0, 3, 2, 1))
    sh["ln1_g"] = A(inp["ln1_g"]); sh["ln1_b"] = A(inp["ln1_b"])
    sh["ln2_g"] = A(inp["ln2_g"]); sh["ln2_b"] = A(inp["ln2_b"])
    hw = np.concatenate([inp["hy_conv_w"], inp["hy_conv_b"][:, None, :]], axis=1)
    sh["hyw"] = A(hw.reshape(DEPTH, 4, 6, 128).transpose(0, 3, 2, 1))
    sh["hy_w1"] = A(inp["hy_w1"]); sh["hy_w2"] = A(inp["hy_w2"]); sh["hy_w3"] = A(inp["hy_w3"])
    sh["hy_wout"] = A(inp["hy_wout"])
    sh["hyb"] = A(np.stack([inp["hy_b1"], inp["hy_b2"], inp["hy_b3"], inp["hy_freq"]], axis=-1))
    sh["hy_bias"] = A(inp["hy_bias"])
    sh.update(_consts())
    return sh


SHARED_SHAPES = {
    "ada_w": [DEPTH, 1024, 6144], "ada_b": [DEPTH, 6144], "ada_bT": [DEPTH, 128, 48],
    "w_in": [DEPTH, 1024, 2048], "w_out": [DEPTH, 1024, 1024], "w_up": [DEPTH, 1024, 2 * DFF],
    "w_down": [DEPTH, DFF, 1024], "qkg": [DEPTH, 640],
    "lruw": [DEPTH, 128, 2, 5], "lruW": [DEPTH, 2, 2, 2, 128, 128], "lrub": [DEPTH, 128, 2, 2, 3],
    "ffw": [DEPTH, 128, 44, 4], "ln1_g": [DEPTH, 1024], "ln1_b": [DEPTH, 1024], "ln2_g": [DEPTH, 1024], "ln2_b": [DEPTH, 1024],
    "hyw": [DEPTH, 128, 6, 4], "hy_w1": [DEPTH, 33, 64], "hy_w2": [DEPTH, 64, 64], "hy_w3": [DEPTH, 64, 64],
    "hy_wout": [DEPTH, 64, 1024], "hyb": [DEPTH, 64, 4], "hy_bias": [DEPTH, 2, 256],
    "hz0": [33, LP], "hz1": [33, LS], "htl0": [1, LP], "htl1": [1, LS],
    "hFN1_0": [64, 128], "hFN1_1": [128, 256], "hT_0": [128, 2, 64], "hT_1": [128, 2, 128],
    "hT2_0": [64, 2, 128], "hT2_1": [128, 2, 128], "hGN1_0": [64, 2, 32], "hGN1_1": [128, 2, 64],
    "hF128": [128, 3, 128], "hG128": [128, 2, 256], "hnegd": [128, 2],
    "ident": [128, 128], "rope_cos": [128, 64, 32], "rope_sin": [128, 64, 32],
}


def core_prep(inp, i):
    c = np.stack([inp["c_prompt"][i], inp["c_sample"][i]], axis=0)
    cT = np.ascontiguousarray(c.reshape(2, 8, 128).transpose(2, 1, 0), dtype=np.float32)
    return {"xp": np.ascontiguousarray(inp["x_prompt"][i], dtype=np.float32),
            "xs": np.ascontiguousarray(inp["x_sample"][i], dtype=np.float32),
            "cT": cT}


def _setup(self):
    nc = self.nc
    dk = "ExternalOutput" if self.dbg.get("dump") else "Internal"
    self.din = {}
    for k, shp in SHARED_SHAPES.items():
        self.din[k] = self.dram(k, shp, F32, kind="ExternalInput")
    self.xin = [self.dram("xp", [LP, D], F32, kind="ExternalInput"),
                self.dram("xs", [LS, D], F32, kind="ExternalInput")]
    self.cT = self.dram("cT", [128, 8, 2], F32, kind="ExternalInput")
    self.yout = [self.dram("yp", [LP, D], F32, kind="ExternalOutput"),
                 self.dram("ys", [LS, D], F32, kind="ExternalOutput")]
    self.Ls = [LP, LS]
    self.xa = [self.dram("xa%d" % s, [L, D], F32, kind=dk) for s, L in enumerate(self.Ls)]
    self.xres = [self.dram("xres%d" % s, [L, D], F32, kind=dk) for s, L in enumerate(self.Ls)]
    self.qT = [self.dram("qT%d" % s, [512, L], BF16, kind=dk) for s, L in enumerate(self.Ls)]
    self.kT = [self.dram("kT%d" % s, [128, L], BF16, kind=dk) for s, L in enumerate(self.Ls)]
    self.vaug = [self.dram("vaug%d" % s, [2, 128, L // 128, 192], BF16, kind=dk) for s, L in enumerate(self.Ls)]
    self.lh = [self.dram("lh%d" % s, [1280, L], F32, kind=dk) for s, L in enumerate(self.Ls)]
    self.catT = [self.dram("catT%d" % s, [1024, L], BF16, kind=dk) for s, L in enumerate(self.Ls)]
    self.hc = [self.dram("hc%d" % s, [768, L], F32, kind=dk) for s, L in enumerate(self.Ls)]
    self.kfull = [self.dram("kfull%d" % s, [2, 256, 2 * L], F32, kind=dk) for s, L in enumerate(self.Ls)]
    self.kfs = [self.dram("kfs%d" % s, [2, 128, 256, 2, L // 64], BF16, kind=dk) for s, L in enumerate(self.Ls)]
    self.z1 = [self.dram("z1_%d" % s, [256, L], F32, kind=dk) for s, L in enumerate(self.Ls)]
    self.rnd = self.dram("rnd", [2, 256], F32, kind=dk)
    self.init_arena()
    self.ident = self.sb("ident", 128, BF16, glob=True)
    self.csT = self.sb("csT", 16, F32, glob=True)
    self.modT = self.sb("modT", 96, F32, glob=True)
    self.grow = [self.sb("grow%d" % i, 1024, F32, glob=True) for i in range(4)]
    self.epsq = self.sb("epsq", 1, F32, glob=True)
    self.epsl = self.sb("epsl", 1, F32, glob=True)
    self.phase_begin()
    self.dma("pool", self.ident.t, self.din["ident"], wr=[self.ident.b])
    ct = self.sb("ct", 16, F32)
    self.dma("sp", ct.t, self.cT.rearrange("p k s -> p (k s)"), wr=[ct.b])
    self.act(self.csT.t, ct.t, AF.Silu, rd=[ct.b], wr=[self.csT.b])
    self.memset("dve", self.epsq.t, QK_EPS, wr=[self.epsq.b])
    self.memset("dve", self.epsl.t, LN_EPS, wr=[self.epsl.b])
    self.phase_end()


def _phase0(self, l):
    self.phase_begin()
    adab = self.sb("adab", 48, F32)
    self.dma("sp", adab.t, self.din["ada_bT"][l], wr=[adab.b])
    self.csrep = self.sb("csrep", 16 * 128, F32)
    self.cp("dve", self.csrep.t.rearrange("p (k n) -> p k n", n=128),
            self.csT.t.unsqueeze(2).to_broadcast([128, 16, 128]), rd=[self.csT.b], wr=[self.csrep.b])
    was = [self.sb("wa%d" % i, 8 * 512, F32) for i in range(2)]
    brow = [self.sb("brow%d" % i, 512, F32) for i in range(2)]
    pm = self.ps("pm", 96)
    pgs = [self.ps("pg%d" % i) for i in range(2)]
    npg = 0
    aw = self.din["ada_w"]
    for gi in range(12):
        wa = was[gi % 2]
        src = dap(aw, l * 1024 * 6144 + gi * 512, [(6144, 128), (128 * 6144, 8), (1, 512)])
        self.dma("sp", wa.t.rearrange("p (k c) -> p k c", k=8), src, wr=[wa.b])
        wa3 = wa.t.rearrange("p (k c) -> p k c", k=8)
        for m in range(4):
            ch = gi * 4 + m
            for kc in range(8):
                self.mm(pm.t[:, ch * 2:ch * 2 + 2], wa3[:, kc, m * 128:(m + 1) * 128],
                        self.csT.t[:, kc * 2:kc * 2 + 2], kc == 0, kc == 7, rd=[wa.b, self.csT.b], wr=[pm.b])
        if gi in (4, 5, 10, 11):
            g = 0 if gi < 6 else 1
            half = gi % 2 if gi < 6 else (gi - 10)
            br = brow[half]
            self.dma("sp", br.t, dap(self.din["ada_b"], l * 6144 + gi * 512, [(0, 128), (1, 512)]), wr=[br.b])
            cr4 = self.csrep.t.rearrange("p (k s n) -> p k s n", k=8, s=2)
            for sq in range(2):
                pg = pgs[npg % 2]
                npg += 1
                for kc in range(8):
                    self.mm(pg.t, cr4[:, kc, sq, :], wa3[:, kc, :], kc == 0, kc == 7,
                            rd=[self.csrep.b, wa.b], wr=[pg.b])
                gr = self.grow[g * 2 + sq]
                self.tt("dve", gr.t[:, half * 512:(half + 1) * 512], pg.t, br.t, ALU.add,
                        rd=[pg.b, br.b], wr=[gr.b])
    m3 = self.modT.t.rearrange("p (c s) -> p c s", s=2)
    self.tt("dve", m3, pm.t.rearrange("p (c s) -> p c s", s=2),
            adab.t.unsqueeze(2).to_broadcast([128, 48, 2]), ALU.add, rd=[pm.b, adab.b], wr=[self.modT.b])
    for c0 in (8, 32):
        self.ts("dve", m3[:, c0:c0 + 8, :], m3[:, c0:c0 + 8, :], 1.0, None, ALU.add, None,
                rd=[self.modT.b], wr=[self.modT.b])
    self.phase_end()


def _phase1(self, l, xsrc):
    self.phase_begin()
    wi = self.sb("wi", 8 * 2048, BF16)
    wi3 = wi.t.rearrange("p (k n) -> p k n", k=8)
    wib = [self.s.buf("wib%d" % k) for k in range(8)]
    for kc in range(8):
        self.dma("pool", wi3[:, kc, :], self.din["w_in"][l, kc * 128:(kc + 1) * 128, :], wr=[wib[kc]])
    gain = self.sb("gain", 640, F32)
    self.dma("sp", gain.t, dap(self.din["qkg"], l * 640, [(0, 128), (1, 640)]), wr=[gain.b])
    self.rcos = self.sb("rcos", 64 * 32, F32)
    self.rsin = self.sb("rsin", 64 * 32, F32)
    self.dma("sp", self.rcos.t, self.din["rope_cos"].rearrange("p t c -> p (t c)"), wr=[self.rcos.b])
    self.dma("sp", self.rsin.t, self.din["rope_sin"].rearrange("p t c -> p (t c)"), wr=[self.rsin.b])
    xbs = [self.sb("xb%d" % i, 4 * 1024, BF16) for i in range(2)]
    uTs = [self.sb("uT%d" % i, 8 * 512, BF16) for i in range(2)]
    lhs_ = [self.sb("lhs%d" % i, 10 * 512, F32) for i in range(2)]
    lhb = [[self.s.buf() for m in range(10)] for i in range(2)]
    sq = self.sb("sq", 640, F32)
    ss = self.sb("ss", 10, F32)
    rstd = self.sb("rstd", 10, F32)
    qn = self.sb("qn", 640, F32)
    tmp = [self.sb("tmp%d" % i, 320, F32) for i in range(4)]
    qkb = self.sb("qkb", 640, BF16)
    qkTs = [self.sb("qkT%d" % i, 5 * 512, BF16) for i in range(2)]
    vgs = [self.sb("vg%d" % i, 2 * 4 * 192, BF16) for i in range(2)]
    for vg in vgs:
        self.memset("dve", vg.t, 1.0, wr=[vg.b])
    pts = [self.ps("pt%d" % i, 512, BF16) for i in range(2)]
    pfs = [self.ps("pf%d" % i) for i in range(2)]
    psq = self.ps("psq")
    pskv = self.ps("pskv", 256)
    pT = self.ps("pT", 640, BF16)
    m3 = self.modT.t.rearrange("p (c s) -> p c s", s=2)
    rc3 = self.rcos.t.rearrange("p (t c) -> p t c", c=32)
    rs3 = self.rsin.t.rearrange("p (t c) -> p t c", c=32)
    it = 0
    for s in range(2):
        L = self.Ls[s]
        NT = L // 128
        for w in range(L // 512):
            xb, uT, lh, qkT, vg = xbs[it % 2], uTs[it % 2], lhs_[it % 2], qkTs[it % 2], vgs[it % 2]
            lb = lhb[it % 2]
            it += 1
            xb3 = xb.t.rearrange("p (j c) -> p j c", j=4)
            uT3 = uT.t.rearrange("p (k t) -> p k t", k=8)
            lh3 = lh.t.rearrange("p (m t) -> p m t", m=10)
            qkT3 = qkT.t.rearrange("p (c t) -> p c t", c=5)
            vg4 = vg.t.rearrange("p (g j c) -> p g j c", g=2, j=4)
            self.dma("pool", xb3, dap(xsrc[s], w * 512 * D, [(D, 128), (128 * D, 4), (1, D)]), wr=[xb.b])
            for kc in range(8):
                pt = pts[kc % 2]
                for j in range(4):
                    self.tr(pt.t[:, j * 128:(j + 1) * 128], xb3[:, j, kc * 128:(kc + 1) * 128], self.ident.t,
                            rd=[xb.b, self.ident.b], wr=[pt.b])
                self.act(uT3[:, kc, :], pt.t, AF.Identity, rd=[pt.b, self.modT.b], wr=[uT.b],
                         scale=m3[:, 8 + kc, s:s + 1], bias=m3[:, kc, s:s + 1])
            for m in range(10):
                pf = pfs[m % 2]
                for kc in range(8):
                    self.mm(pf.t, wi3[:, kc, 768 + m * 128:768 + (m + 1) * 128], uT3[:, kc, :], kc == 0, kc == 7,
                            rd=[wib[kc], uT.b], wr=[pf.b])
                self.cp("act" if m % 2 else "dve", lh3[:, m, :], pf.t, rd=[pf.b], wr=[lb[m]])
            self.dma("sp", dap(self.lh[s], w * 512, [(L, 128), (128 * L, 10), (1, 512)]), lh3, rd=lb)
            for j in range(4):
                T = w * 4 + j
                for kc in range(8):
                    self.mm(psq.t, uT3[:, kc, j * 128:(j + 1) * 128], wi3[:, kc, 0:512], kc == 0, kc == 7,
                            rd=[wib[kc], uT.b], wr=[psq.b])
                for kc in range(8):
                    self.mm(pskv.t, uT3[:, kc, j * 128:(j + 1) * 128], wi3[:, kc, 512:768], kc == 0, kc == 7,
                            rd=[wib[kc], uT.b], wr=[pskv.b])
                self.act(sq.t[:, 0:512], psq.t, AF.Square, rd=[psq.b], wr=[sq.b])
                self.act(sq.t[:, 512:640], pskv.t[:, 0:128], AF.Square, rd=[pskv.b], wr=[sq.b])
                self.s.op("dve", (lambda e, o=ss.t, i=sq.t.rearrange("p (h d) -> p h d", d=64):
                                  e.tensor_reduce(out=o, in_=i, axis=AX.X, op=ALU.add)), rd=[sq.b], wr=[ss.b])
                self.act(rstd.t, ss.t, AF.Sqrt, rd=[ss.b, self.epsq.b], wr=[rstd.b], scale=1.0 / 64, bias=self.epsq.t)
                self.recip(rstd.t, rstd.t, rd=[rstd.b], wr=[rstd.b])
                qn3 = qn.t.rearrange("p (h d) -> p h d", d=64)
                self.tt("dve", qn3[:, 0:8, :], psq.t.rearrange("p (h d) -> p h d", d=64),
                        rstd.t[:, 0:8].unsqueeze(2).to_broadcast([128, 8, 64]), ALU.mult,
                        rd=[psq.b, rstd.b], wr=[qn.b])
                self.tt("dve", qn3[:, 8:10, :], pskv.t[:, 0:128].rearrange("p (h d) -> p h d", d=64),
                        rstd.t[:, 8:10].unsqueeze(2).to_broadcast([128, 2, 64]), ALU.mult,
                        rd=[pskv.b, rstd.b], wr=[qn.b])
                self.tt("dve", qn.t, qn.t, gain.t, ALU.mult, rd=[qn.b, gain.b], wr=[qn.b])
                for c0 in (0, 128):
                    self.cp("act", vg4[:, :, j, c0:c0 + 64], pskv.t[:, 128:256].rearrange("p (g d) -> p g d", g=2),
                            rd=[pskv.b], wr=[vg.b])
                qn5 = qn.t.rearrange("p (h a two f) -> p h a two f", h=10, a=2, two=2)
                qb5 = qkb.t.rearrange("p (h a two f) -> p h a two f", h=10, a=2, two=2)
                x1, x2 = qn5[:, :, :, 0, :], qn5[:, :, :, 1, :]
                cb = rc3[:, T, :].rearrange("p (a f) -> p a f", a=2).unsqueeze(1).to_broadcast([128, 10, 2, 16])
                sb_ = rs3[:, T, :].rearrange("p (a f) -> p a f", a=2).unsqueeze(1).to_broadcast([128, 10, 2, 16])
                t4 = [t.t.rearrange("p (h a f) -> p h a f", h=10, a=2) for t in tmp]
                self.tt("dve", t4[0], x1, cb, ALU.mult, rd=[qn.b, self.rcos.b], wr=[tmp[0].b])
                self.tt("dve", t4[1], x2, sb_, ALU.mult, rd=[qn.b, self.rsin.b], wr=[tmp[1].b])
                self.tt("dve", qb5[:, :, :, 0, :], t4[0], t4[1], ALU.subtract, rd=[tmp[0].b, tmp[1].b], wr=[qkb.b])
                self.tt("dve", t4[2], x2, cb, ALU.mult, rd=[qn.b, self.rcos.b], wr=[tmp[2].b])
                self.tt("dve", t4[3], x1, sb_, ALU.mult, rd=[qn.b, self.rsin.b], wr=[tmp[3].b])
                self.tt("dve", qb5[:, :, :, 1, :], t4[2], t4[3], ALU.add, rd=[tmp[2].b, tmp[3].b], wr=[qkb.b])
                for c in range(5):
                    self.tr(pT.t[:, c * 128:(c + 1) * 128], qkb.t[:, c * 128:(c + 1) * 128], self.ident.t,
                            rd=[qkb.b, self.ident.b], wr=[pT.b])
                self.cp("act", qkT3[:, :, j * 128:(j + 1) * 128], pT.t.rearrange("p (c t) -> p c t", c=5),
                        rd=[pT.b], wr=[qkT.b])
            self.dma("sp", dap(self.qT[s], w * 512, [(L, 128), (128 * L, 4), (1, 512)]), qkT3[:, 0:4, :], rd=[qkT.b])
            self.dma("sp", self.kT[s][:, w * 512:(w + 1) * 512], qkT3[:, 4, :], rd=[qkT.b])
            self.dma("sp", dap(self.vaug[s], w * 4 * 192, [(NT * 192, 128), (128 * NT * 192, 2), (1, 768)]),
                     vg.t.rearrange("p (g c) -> p g c", g=2), rd=[vg.b])
    self.phase_end()


K.setup = _setup
K.phase0 = _phase0
K.phase1 = _phase1


def _phase2(self, l):
    self.phase_begin()
    Kd = self.sb("Kd", LS, BF16)
    Va = self.sb("Va", 64 * 192, BF16)
    Q2 = self.sb("Q2", 2 * LS, BF16)
    PAB = [self.sb("PAB%d" % i, 1024, BF16) for i in range(3)]
    rr = self.sb("rr", 512, F32)
    rAb, rBb = self.s.buf("rA"), self.s.buf("rB")
    atts = [self.sb("att%d" % i, 512, BF16) for i in range(2)]
    psAB = [self.ps2("psAB%d" % i) for i in range(2)]
    oA = [self.ps("oA%d" % i) for i in range(2)]
    oB = [self.ps("oB%d" % i) for i in range(2)]
    nblk = 0
    for s in range(2):
        L = self.Ls[s]
        NT = L // 128
        for g in range(2):
            self.dma("sp", Kd.t[0:64, 0:L], self.kT[s][g * 64:(g + 1) * 64, :], wr=[Kd.b])
            self.dma("sp", Kd.t[64:128, 0:L], self.kT[s][g * 64:(g + 1) * 64, :], wr=[Kd.b])
            self.dma("sp", Va.t[:, 0:NT * 192], self.vaug[s][g].rearrange("p t c -> p (t c)"), wr=[Va.b])
            Q3 = Q2.t.rearrange("p (h t) -> p h t", h=2)
            self.dma("sp", Q3[:, :, 0:L], dap(self.qT[s], 2 * g * 128 * L, [(L, 128), (128 * L, 2), (1, L)]), wr=[Q2.b])
            Va3 = Va.t.rearrange("p (t c) -> p t c", c=192)
            steps = [(qc, hp, st) for qc in range(L // 512) for hp in range(2) for st in range(NT)]

            def qk(i):
                qc, hp, st = steps[i]
                ib = i % 2
                self.mm(psAB[ib].t[:, 0:512], Kd.t[0:64, st * 128:(st + 1) * 128], Q3[0:64, hp, qc * 512:(qc + 1) * 512],
                        True, True, rd=[Kd.b, Q2.b], wr=[psAB[ib].b], tp=(0, 0))
                self.mm(psAB[ib].t[:, 512:1024], Kd.t[64:128, st * 128:(st + 1) * 128], Q3[64:128, hp, qc * 512:(qc + 1) * 512],
                        True, True, rd=[Kd.b, Q2.b], wr=[psAB[ib].b], tp=(64, 0))
            qk(0)
            for i, (qc, hp, st) in enumerate(steps):
                if i + 1 < len(steps):
                    qk(i + 1)
                ib, ip = i % 2, i % 3
                if st == 0:
                    nblk += 1
                io = nblk % 2
                self.act(PAB[ip].t, psAB[ib].t, AF.Exp, rd=[psAB[ib].b], wr=[PAB[ip].b], scale=0.125)
                self.mm(oA[io].t, Va3[:, st, 0:128], PAB[ip].t[:, 0:512], st == 0, st == NT - 1, rd=[Va.b, PAB[ip].b], wr=[oA[io].b])
                self.mm(oB[io].t, Va3[:, st, 64:192], PAB[ip].t[:, 512:1024], st == 0, st == NT - 1, rd=[Va.b, PAB[ip].b], wr=[oB[io].b])
                if st == NT - 1:
                    att = atts[io]
                    self.recip(rr.t[64:128, :], oA[io].t[64:128, :], rd=[oA[io].b], wr=[rAb])
                    self.tt("dve", att.t[0:64, :], oA[io].t[0:64, :], rr.t[64:128, :], ALU.mult,
                            rd=[oA[io].b, rAb], wr=[att.b])
                    self.recip(rr.t[0:64, :], oB[io].t[0:64, :], rd=[oB[io].b], wr=[rBb])
                    self.tt("dve", att.t[64:128, :], oB[io].t[64:128, :], rr.t[0:64, :], ALU.mult,
                            rd=[oB[io].b, rBb], wr=[att.b])
                    self.dma("sp", self.catT[s][(2 * g + hp) * 128:(2 * g + hp + 1) * 128, qc * 512:(qc + 1) * 512],
                             att.t, rd=[att.b])
    self.phase_end()


K.phase2 = _phase2


def _phase3(self, l):
    self.phase_begin()
    TC = 1024
    XC = self.sb("XC", LS, F32)
    XCB = self.sb("XCB", LS, BF16)
    HF = self.sb("HF", LS, F32)
    WA = self.sb("WA", 8 * 128, BF16)
    WA5 = WA.t.rearrange("p (d g c n) -> p d g c n", d=2, g=2, c=2)
    self.dma("pool", WA5, self.din["lruW"][l].rearrange("d g c k n -> k d g c n"), wr=[WA.b])
    cw = self.sb("cw", 10, F32)
    self.dma("sp", cw.t, self.din["lruw"][l].rearrange("p c k -> p (c k)"), wr=[cw.b])
    lb = self.sb("lb", 12, F32)
    self.dma("sp", lb.t, self.din["lrub"][l].rearrange("p c d k -> p (c d k)"), wr=[lb.b])
    lb4 = lb.t.rearrange("p (c d k) -> p c d k", c=2, d=2)
    cw3 = cw.t.rearrange("p (c k) -> p c k", c=2)
    sp_ = self.sb("sp", 4, F32)
    c12 = self.sb("c12", 8, F32)
    sp3 = sp_.t.rearrange("p (c d) -> p c d", c=2)
    self.act(sp3, lb4[:, :, :, 2], AF.Exp, rd=[lb.b], wr=[sp_.b], scale=-1.0)
    self.act(sp_.t, sp_.t, AF.Ln, rd=[sp_.b], wr=[sp_.b], bias=1.0)
    self.ts("dve", c12.t[:, 0:4], sp_.t, -8.0, None, ALU.mult, None, rd=[sp_.b], wr=[c12.b])
    self.ts("dve", c12.t[:, 4:8], sp_.t, -16.0, None, ALU.mult, None, rd=[sp_.b], wr=[c12.b])
    c4 = c12.t.rearrange("p (k c d) -> p k c d", k=2, c=2)
    xhs = [self.sb("xh%d" % i, TC + 3, F32) for i in range(2)]
    tR = [self.sb("tR%d" % i, TC, F32) for i in range(2)]
    tI = [self.sb("tI%d" % i, TC, F32) for i in range(2)]
    tA = [self.sb("tA%d" % i, TC, F32) for i in range(2)]
    tT = [self.sb("tT%d" % i, TC, F32) for i in range(2)]
    tB = [self.sb("tB%d" % i, TC, F32) for i in range(2)]
    tH = [self.sb("tH%d" % i, TC, F32) for i in range(2)]
    gs = [self.sb("g%d" % i, TC, F32) for i in range(2)]
    ggs = [self.sb("gg%d" % i, TC, F32) for i in range(2)]
    obs = [self.sb("ob%d" % i, TC, BF16) for i in range(2)]
    carry = self.sb("carry", 1, F32)
    prs = [self.ps("pr%d" % i) for i in range(4)]
    pis = [self.ps("pi%d" % i) for i in range(4)]
    it = 0
    nps = 0
    for s in range(2):
        L = self.Ls[s]
        NCH = L // TC
        for cc in range(2):
            for c in range(NCH):
                xh = xhs[it % 2]
                it += 1
                t0 = c * TC
                lo = max(t0 - 2, 0)
                hi = min(t0 + TC + 1, L)
                if c == 0:
                    self.memset("dve", xh.t[:, 0:2], 0.0, wr=[xh.b])
                if c == NCH - 1:
                    self.memset("dve", xh.t[:, TC + 2:TC + 3], 0.0, wr=[xh.b])
                self.dma("sp", xh.t[:, lo - (t0 - 2):hi - (t0 - 2)], self.lh[s][cc * 128:(cc + 1) * 128, lo:hi], wr=[xh.b])
                xc = XC.t[:, t0:t0 + TC]
                self.ts("dve", xc, xh.t[:, 0:TC], cw3[:, cc, 0:1], cw3[:, cc, 4:5], ALU.mult, ALU.add,
                        rd=[xh.b, cw.b], wr=[XC.b])
                for k in range(1, 4):
                    self.stt("dve", xc, xh.t[:, k:k + TC], cw3[:, cc, k:k + 1], xc, ALU.mult, ALU.add,
                             rd=[xh.b, cw.b, XC.b], wr=[XC.b])
                self.cp("act", XCB.t[:, t0:t0 + TC], xc, rd=[XC.b], wr=[XCB.b])
            for d in range(2):
                order = list(range(NCH)) if d == 0 else list(range(NCH - 1, -1, -1))
                for ci, c in enumerate(order):
                    i2 = it % 2
                    it += 1
                    t0 = c * TC
                    for sub in range(2):
                        pr, pi = prs[nps % 4], pis[nps % 4]
                        nps += 1
                        cols = slice(t0 + sub * 512, t0 + (sub + 1) * 512)
                        self.mm(pr.t, WA5[:, d, 0, cc, :], XCB.t[:, cols], True, True, rd=[WA.b, XCB.b], wr=[pr.b])
                        self.mm(pi.t, WA5[:, d, 1, cc, :], XCB.t[:, cols], True, True, rd=[WA.b, XCB.b], wr=[pi.b])
                        self.act(tR[i2].t[:, sub * 512:(sub + 1) * 512], pr.t, AF.Sigmoid, rd=[pr.b, lb.b], wr=[tR[i2].b],
                                 bias=lb4[:, cc, d, 0:1])
                        self.act(tI[i2].t[:, sub * 512:(sub + 1) * 512], pi.t, AF.Sigmoid, rd=[pi.b, lb.b], wr=[tI[i2].b],
                                 bias=lb4[:, cc, d, 1:2])
                    self.act(tA[i2].t, tR[i2].t, AF.Exp, rd=[tR[i2].b, c12.b], wr=[tA[i2].b], scale=c4[:, 0, cc, d:d + 1])
                    self.act(tT[i2].t, tR[i2].t, AF.Exp, rd=[tR[i2].b, c12.b], wr=[tT[i2].b], scale=c4[:, 1, cc, d:d + 1])
                    self.act(tT[i2].t, tT[i2].t, AF.Sqrt, rd=[tT[i2].b], wr=[tT[i2].b], scale=-1.0, bias=1.0)
                    self.tt("dve", tB[i2].t, tI[i2].t, XC.t[:, t0:t0 + TC], ALU.mult, rd=[tI[i2].b, XC.b], wr=[tB[i2].b])
                    self.tt("dve", tB[i2].t, tB[i2].t, tT[i2].t, ALU.mult, rd=[tB[i2].b, tT[i2].b], wr=[tB[i2].b])
                    if d == 0:
                        init = 0.0 if ci == 0 else HF.t[:, t0 - 1:t0]
                        self.s.op("dve", (lambda e, o=HF.t[:, t0:t0 + TC], a=tA[i2].t, b=tB[i2].t, i0=init:
                                          e.tensor_tensor_scan(out=o, data0=a, data1=b, initial=i0, op0=ALU.mult, op1=ALU.add)),
                                  rd=[tA[i2].b, tB[i2].b, HF.b], wr=[HF.b])
                    else:
                        init = 0.0 if ci == 0 else carry.t
                        self.s.op("dve", (lambda e, o=rev(tH[i2].t), a=rev(tA[i2].t), b=rev(tB[i2].t), i0=init:
                                          e.tensor_tensor_scan(out=o, data0=a, data1=b, initial=i0, op0=ALU.mult, op1=ALU.add)),
                                  rd=[tA[i2].b, tB[i2].b, carry.b], wr=[tH[i2].b])
                        self.cp("dve", carry.t, tH[i2].t[:, 0:1], rd=[tH[i2].b], wr=[carry.b])
                        self.tt("dve", HF.t[:, t0:t0 + TC], HF.t[:, t0:t0 + TC], tH[i2].t, ALU.add,
                                rd=[HF.b, tH[i2].b], wr=[HF.b])
            for c in range(NCH):
                i2 = it % 2
                it += 1
                t0 = c * TC
                self.dma("sp", gs[i2].t, self.lh[s][256 + cc * 128:256 + (cc + 1) * 128, t0:t0 + TC], wr=[gs[i2].b])
                self.act(ggs[i2].t, gs[i2].t, AF.Gelu, rd=[gs[i2].b], wr=[ggs[i2].b])
                self.tt("dve", obs[i2].t, HF.t[:, t0:t0 + TC], ggs[i2].t, ALU.mult, rd=[HF.b, ggs[i2].b], wr=[obs[i2].b])
                self.dma("sp", self.catT[s][512 + cc * 128:512 + (cc + 1) * 128, t0:t0 + TC], obs[i2].t, rd=[obs[i2].b])
    self.phase_end()


K.phase3 = _phase3


def _ln_tail(self, po, n, xsrc_rows, gate, lg, lbias, dst_rows, T):
    i2 = T["i"] % 2
    T["i"] += 1
    xr, ysb, st, mv, rs, nm = T["xr"][i2], T["y"][i2], T["st"], T["mv"], T["rs"], T["nm"]
    self.dma("sp", xr.t[0:n, :], xsrc_rows, wr=[xr.b])
    for h in range(2):
        self.tt("dve", ysb.t[0:n, h * 512:(h + 1) * 512], po[h].t[0:n, :], gate.t[0:n, h * 512:(h + 1) * 512], ALU.mult,
                rd=[po[h].b, gate.b], wr=[ysb.b])
    self.stt("dve", ysb.t[0:n, :], xr.t[0:n, :], ALPHA, ysb.t[0:n, :], ALU.mult, ALU.add, rd=[xr.b, ysb.b], wr=[ysb.b])
    st3 = st.t.rearrange("p (c k) -> p c k", c=2)
    for h in range(2):
        self.s.op("dve", (lambda e, o=st3[0:n, h, :], i=ysb.t[0:n, h * 512:(h + 1) * 512]: e.bn_stats(out=o, in_=i)),
                  rd=[ysb.b], wr=[st.b])
    self.s.op("dve", (lambda e, o=mv.t[0:n, :], i=st3[0:n, :, :]: e.bn_aggr(out=o, in_=i)), rd=[st.b], wr=[mv.b])
    self.act(rs.t[0:n, :], mv.t[0:n, 1:2], AF.Sqrt, rd=[mv.b, self.epsl.b], wr=[rs.b], bias=self.epsl.t[0:n, :])
    self.recip(rs.t[0:n, :], rs.t[0:n, :], rd=[rs.b], wr=[rs.b])
    self.ts("dve", nm.t[0:n, :], mv.t[0:n, 0:1], -1.0, rs.t[0:n, :], ALU.mult, ALU.mult, rd=[mv.b, rs.b], wr=[nm.b])
    self.act(ysb.t[0:n, :], ysb.t[0:n, :], AF.Identity, rd=[ysb.b, rs.b, nm.b], wr=[ysb.b],
             scale=rs.t[0:n, :], bias=nm.t[0:n, :])
    self.tt("pool", ysb.t[0:n, :], ysb.t[0:n, :], lg.t[0:n, :], ALU.mult, rd=[ysb.b, lg.b], wr=[ysb.b])
    self.tt("pool", xr.t[0:n, :], ysb.t[0:n, :], lbias.t[0:n, :], ALU.add, rd=[ysb.b, lbias.b], wr=[xr.b])
    self.dma("sp", dst_rows, xr.t[0:n, :], rd=[xr.b])


def _ln_bufs(self):
    return {"i": 0, "xr": [self.sb("lnx%d" % i, 1024, F32) for i in range(2)],
            "y": [self.sb("lny%d" % i, 1024, F32) for i in range(2)],
            "st": self.sb("lnst", 12, F32), "mv": self.sb("lnmv", 2, F32),
            "rs": self.sb("lnrs", 1, F32), "nm": self.sb("lnnm", 1, F32)}


def _phase5(self, l, xsrc):
    self.phase_begin()
    wo = self.sb("wo", 8 * 1024, BF16)
    wo3 = wo.t.rearrange("p (k n) -> p k n", k=8)
    wob = [self.s.buf() for k in range(8)]
    for kc in range(8):
        self.dma("pool", wo3[:, kc, :], self.din["w_out"][l, kc * 128:(kc + 1) * 128, :], wr=[wob[kc]])
    lg = self.sb("lg", 1024, F32)
    lb = self.sb("lb", 1024, F32)
    self.dma("sp", lg.t, dap(self.din["ln1_g"], l * 1024, [(0, 128), (1, 1024)]), wr=[lg.b])
    self.dma("sp", lb.t, dap(self.din["ln1_b"], l * 1024, [(0, 128), (1, 1024)]), wr=[lb.b])
    T = self.ln_bufs()
    cts = [self.sb("ct%d" % i, 8 * 512, BF16) for i in range(2)]
    pos = [[self.ps("po%d%d" % (i, h)) for h in range(2)] for i in range(2)]
    it = 0
    nj = 0
    for s in range(2):
        L = self.Ls[s]
        for w in range(L // 512):
            ct = cts[it % 2]
            it += 1
            ct3 = ct.t.rearrange("p (k t) -> p k t", k=8)
            self.dma("sp", ct3, dap(self.catT[s], w * 512, [(L, 128), (128 * L, 8), (1, 512)]), wr=[ct.b])
            for j in range(4):
                po = pos[nj % 2]
                nj += 1
                for h in range(2):
                    for kc in range(8):
                        self.mm(po[h].t, ct3[:, kc, j * 128:(j + 1) * 128], wo3[:, kc, h * 512:(h + 1) * 512],
                                kc == 0, kc == 7, rd=[ct.b, wob[kc]], wr=[po[h].b])
                r0 = w * 512 + j * 128
                self.ln_tail(po, 128, xsrc[s][r0:r0 + 128, :], self.grow[0 * 2 + s], lg, lb,
                             self.xres[s][r0:r0 + 128, :], T)
    self.phase_end()


def _phase6(self, l, dst):
    self.phase_begin()
    WN = 254
    wu = self.sb("wu", 8 * 2 * DFF, BF16)
    wu3 = wu.t.rearrange("p (k n) -> p k n", k=8)
    wub = [self.s.buf() for k in range(8)]
    for kc in range(8):
        self.dma("pool", wu3[:, kc, :], self.din["w_up"][l, kc * 128:(kc + 1) * 128, :], wr=[wub[kc]])
    wd = self.sb("wd", 22 * 1024, BF16)
    wd3 = wd.t.rearrange("p (k n) -> p k n", k=22)
    wdb = [self.s.buf() for k in range(22)]
    for kc in range(22):
        self.dma("pool", wd3[:, kc, :], self.din["w_down"][l, kc * 128:(kc + 1) * 128, :], wr=[wdb[kc]])
    cw = self.sb("fcw", 44 * 4, F32)
    self.dma("sp", cw.t, self.din["ffw"][l].rearrange("p c k -> p (c k)"), wr=[cw.b])
    cw3 = cw.t.rearrange("p (c k) -> p c k", k=4)
    lg = self.sb("lg", 1024, F32)
    lb = self.sb("lb", 1024, F32)
    self.dma("sp", lg.t, dap(self.din["ln2_g"], l * 1024, [(0, 128), (1, 1024)]), wr=[lg.b])
    self.dma("sp", lb.t, dap(self.din["ln2_b"], l * 1024, [(0, 128), (1, 1024)]), wr=[lb.b])
    T = self.ln_bufs()
    xbs = [self.sb("fxb%d" % i, 2 * 1024, BF16) for i in range(2)]
    uT = self.sb("fuT", 8 * 256, BF16)
    uT3 = uT.t.rearrange("p (k t) -> p k t", k=8)
    g = self.sb("fg", 22 * 256, BF16)
    g3 = g.t.rearrange("p (k t) -> p k t", k=22)
    tgs = [self.sb("tg%d" % i, 256, F32) for i in range(2)]
    tvs = [self.sb("tv%d" % i, 256, F32) for i in range(2)]
    pts = [self.ps("fpt%d" % i, 256, BF16) for i in range(2)]
    pgs = [self.ps("fpg%d" % i, 256) for i in range(2)]
    pvs = [self.ps("fpv%d" % i, 256) for i in range(2)]
    po = [self.ps("fpo%d" % h) for h in range(2)]
    m3 = self.modT.t.rearrange("p (c s) -> p c s", s=2)
    it = 0
    for s in range(2):
        L = self.Ls[s]
        xs_ = self.xres[s]
        nw = (L + WN - 1) // WN
        for w in range(nw):
            t0 = w * WN
            nv = min(WN, L - t0)
            xb = xbs[it % 2]
            it += 1
            xb3 = xb.t.rearrange("p (j c) -> p j c", j=2)
            edge = (w == 0) or (t0 + 255 > L)
            if edge:
                self.memset("dve", xb.t, 0.0, wr=[xb.b])
            for j in range(2):
                a = t0 - 1 + 128 * j
                lo, hi = max(a, 0), min(a + 128, L)
                if hi > lo:
                    self.dma("pool", xb3[lo - a:hi - a, j, :], xs_[lo:hi, :], wr=[xb.b])
            for kc in range(8):
                pt = pts[kc % 2]
                for j in range(2):
                    self.tr(pt.t[:, j * 128:(j + 1) * 128], xb3[:, j, kc * 128:(kc + 1) * 128], self.ident.t,
                            rd=[xb.b, self.ident.b], wr=[pt.b])
                self.act(uT3[:, kc, :], pt.t, AF.Identity, rd=[pt.b, self.modT.b], wr=[uT.b],
                         scale=m3[:, 32 + kc, s:s + 1], bias=m3[:, 24 + kc, s:s + 1])
            if w == 0:
                self.memset("dve", uT3[:, :, 0:1], 0.0, wr=[uT.b])
            if t0 + nv >= L:
                c0 = L - (t0 - 1)
                self.memset("dve", uT3[:, :, c0:256], 0.0, wr=[uT.b])
            for jc in range(22):
                pg, pv = pgs[jc % 2], pvs[jc % 2]
                tg, tv = tgs[jc % 2], tvs[jc % 2]
                for kc in range(8):
                    self.mm(pg.t, wu3[:, kc, jc * 128:(jc + 1) * 128], uT3[:, kc, :], kc == 0, kc == 7,
                            rd=[wub[kc], uT.b], wr=[pg.b])
                for kc in range(8):
                    self.mm(pv.t, wu3[:, kc, DFF + jc * 128:DFF + (jc + 1) * 128], uT3[:, kc, :], kc == 0, kc == 7,
                            rd=[wub[kc], uT.b], wr=[pv.b])
                for (pp, tt_, ch) in ((pg, tg, jc), (pv, tv, 22 + jc)):
                    self.act(tt_.t[:, 0:WN], pp.t[:, 0:WN], AF.Identity, rd=[pp.b, cw.b], wr=[tt_.b],
                             scale=cw3[:, ch, 0:1], bias=cw3[:, ch, 3:4])
                    for k in (1, 2):
                        self.stt("dve", tt_.t[:, 0:WN], pp.t[:, k:k + WN], cw3[:, ch, k:k + 1], tt_.t[:, 0:WN],
                                 ALU.mult, ALU.add, rd=[pp.b, cw.b, tt_.b], wr=[tt_.b])
                self.act(tg.t[:, 0:WN], tg.t[:, 0:WN], AF.Gelu, rd=[tg.b], wr=[tg.b])
                self.tt("dve", g3[:, jc, 0:WN], tg.t[:, 0:WN], tv.t[:, 0:WN], ALU.mult, rd=[tg.b, tv.b], wr=[g.b])
            for c0 in (0, 128):
                n = min(128, nv - c0)
                if n <= 0:
                    continue
                for h in range(2):
                    for kc in range(22):
                        self.mm(po[h].t[0:n, :], g3[:, kc, c0:c0 + n], wd3[:, kc, h * 512:(h + 1) * 512],
                                kc == 0, kc == 21, rd=[g.b, wdb[kc]], wr=[po[h].b])
                r0 = t0 + c0
                self.ln_tail(po, n, xs_[r0:r0 + n, :], self.grow[1 * 2 + s], lg, lb, dst[s][r0:r0 + n, :], T)
    self.phase_end()


K.ln_tail = _ln_tail
K.ln_bufs = _ln_bufs
K.phase5 = _phase5
K.phase6 = _phase6


def _sync(self):
    self.s.barrier()


def _phase4(self, l):
    self.phase_begin()
    TWO_PI = 2.0 * math.pi
    F128 = self.sb("F128", 3 * 128, BF16)
    self.dma("pool", F128.t, self.din["hF128"].rearrange("p r k -> p (r k)"), wr=[F128.b])
    F3 = F128.t.rearrange("p (r k) -> p r k", r=3)
    G128 = self.sb("G128", 2 * 256, BF16)
    self.dma("pool", G128.t, self.din["hG128"].rearrange("p r k -> p (r k)"), wr=[G128.b])
    G3 = G128.t.rearrange("p (r k) -> p r k", r=2)
    hyw = self.sb("hyw", 24, F32)
    self.dma("sp", hyw.t, self.din["hyw"][l].rearrange("p c k -> p (c k)"), wr=[hyw.b])
    hyw3 = hyw.t.rearrange("p (c k) -> p c k", k=4)
    drow = self.sb("drow", 512, F32)
    self.dma("sp", drow.t, dap(self.din["hy_bias"], l * 512, [(0, 128), (1, 512)]), wr=[drow.b])
    negd = self.sb("negd", 2, F32)
    self.dma("sp", negd.t, self.din["hnegd"], wr=[negd.b])
    w1 = self.sb("hw1", 64, F32)
    w2 = self.sb("hw2", 64, F32)
    w3 = self.sb("hw3", 64, F32)
    wout = self.sb("hwout", 1024, F32)
    hyb = self.sb("hyb", 4, F32)
    self.dma("sp", w1.t[0:33, :], self.din["hy_w1"][l], wr=[w1.b])
    self.dma("sp", w2.t[0:64, :], self.din["hy_w2"][l], wr=[w2.b])
    self.dma("sp", w3.t[0:64, :], self.din["hy_w3"][l], wr=[w3.b])
    self.dma("sp", wout.t[0:64, :], self.din["hy_wout"][l], wr=[wout.b])
    self.dma("sp", hyb.t[0:64, :], self.din["hyb"][l], wr=[hyb.b])
    bfq = self.sb("bfq", 3, F32)
    self.ts("dve", bfq.t[0:64, :], hyb.t[0:64, 0:3], hyb.t[0:64, 3:4], None, ALU.mult, None, rd=[hyb.b], wr=[bfq.b])
    zero1 = self.sb("zero1", 1, F32)
    self.memset("dve", zero1.t, 0.0, wr=[zero1.b])
    FN1 = self.sb("FN1", 256, BF16)
    Tt = self.sb("Tt", 256, F32)
    T2 = self.sb("T2", 256, F32)
    GN1 = self.sb("GN1", 128, BF16)
    rnrow = self.sb("rnrow", 512, F32)
    xhs = [self.sb("hxh%d" % i, 2050, F32) for i in range(2)]
    cos_ = [self.sb("hco%d" % i, 2048, F32) for i in range(2)]
    zcs = [self.sb("hzc%d" % i, 512, F32) for i in range(2)]
    tls = [self.sb("htl%d" % i, 512, F32) for i in range(2)]
    decs = [[self.sb("hdec%d%d" % (i, c), 512, F32) for c in range(2)] for i in range(2)]
    ya = self.sb("hya", 512, F32)
    kk = self.sb("hkk", 512, F32)
    hks = [self.sb("hk%d" % i, 512, F32) for i in range(3)]
    kts = [self.sb("hkt%d" % i, 512, F32) for i in range(3)]
    kab = self.sb("hkab", 512, F32)
    acc = self.sb("hacc", 8 * 16, F32)
    kb0 = self.sb("hkb0", 4, F32)
    nrm = self.sb("hnrm", 4, F32)
    Xbs = [self.sb("hXb%d" % i, 512, BF16) for i in range(2)]
    zfs = [self.sb("hzf%d" % i, 512, F32) for i in range(2)]
    gfs = [self.sb("hgf%d" % i, 512, F32) for i in range(2)]
    kfts = [self.sb("hkft%d" % i, 4 * 2 * 128, BF16) for i in range(2)]
    Apre = self.sb("hApre", 512, BF16)
    Apim = self.sb("hApim", 512, BF16)
    Yre = self.sb("hYre", 512, BF16)
    Yim = self.sb("hYim", 512, BF16)
    Bpre = self.sb("hBpre", 512, BF16)
    Bpim = self.sb("hBpim", 512, BF16)
    tq = [self.sb("htq%d" % i, 512, F32) for i in range(4)]
    outf = [self.sb("houtf%d" % i, 512, F32) for i in range(2)]
    outb = [self.sb("houtb%d" % i, 512, BF16) for i in range(2)]
    ps1 = self.ps2("hps1")
    pXre = self.ps("hpXre")
    pXim = self.ps("hpXim")
    pB = self.ps2("hpB")
    py = self.ps("hpy")
    it = 0

    def fft_fwd(Xb, Kp, N1):
        nb = 512 // (2 * N1)
        X3 = Xb.t.rearrange("p (c i) -> p c i", c=4)
        for ch in range(4):
            off = (ch // nb) * 512 + (ch % nb) * 2 * N1
            self.mm(ps1.t[:, off:off + 2 * N1], X3[0:Kp, ch, :], FN1.t[0:Kp, 0:2 * N1], True, True,
                    rd=[Xb.b, FN1.b], wr=[ps1.b])
        Tt3 = Tt.t[:, 0:2 * N1].rearrange("p (r k) -> p r k", r=2)
        Ar3 = Apre.t[:, 0:4 * N1].rearrange("p (c k) -> p c k", c=4)
        Ai3 = Apim.t[:, 0:4 * N1].rearrange("p (c k) -> p c k", c=4)
        if nb == 2:
            v = ps1.t.rearrange("p (c r k) -> p c r k", c=4, r=2)
        else:
            v = ps1.t[:, 0:512].rearrange("p (c r k) -> p c r k", c=4, r=2)
        Are, Aim = v[:, :, 0, :], v[:, :, 1, :]
        Tre = Tt3[:, 0, :].unsqueeze(1).to_broadcast([128, 4, N1])
        Tim = Tt3[:, 1, :].unsqueeze(1).to_broadcast([128, 4, N1])
        t = [q.t[:, 0:4 * N1].rearrange("p (c k) -> p c k", c=4) for q in tq]
        self.tt("dve", t[0], Are, Tre, ALU.mult, rd=[ps1.b, Tt.b], wr=[tq[0].b])
        self.tt("dve", t[1], Aim, Tim, ALU.mult, rd=[ps1.b, Tt.b], wr=[tq[1].b])
        self.tt("dve", Ar3, t[0], t[1], ALU.subtract, rd=[tq[0].b, tq[1].b], wr=[Apre.b])
        self.tt("dve", t[2], Are, Tim, ALU.mult, rd=[ps1.b, Tt.b], wr=[tq[2].b])
        self.tt("dve", t[3], Aim, Tre, ALU.mult, rd=[ps1.b, Tt.b], wr=[tq[3].b])
        self.tt("dve", Ai3, t[2], t[3], ALU.add, rd=[tq[2].b, tq[3].b], wr=[Apim.b])
        W = 4 * N1
        self.mm(pXre.t[:, 0:W], F3[:, 0, :], Apre.t[:, 0:W], True, False, rd=[F128.b, Apre.b], wr=[pXre.b])
        self.mm(pXre.t[:, 0:W], F3[:, 2, :], Apim.t[:, 0:W], False, True, rd=[F128.b, Apim.b], wr=[pXre.b])
        self.mm(pXim.t[:, 0:W], F3[:, 0, :], Apim.t[:, 0:W], True, False, rd=[F128.b, Apim.b], wr=[pXim.b])
        self.mm(pXim.t[:, 0:W], F3[:, 1, :], Apre.t[:, 0:W], False, True, rd=[F128.b, Apre.b], wr=[pXim.b])

    for s in range(2):
        L = self.Ls[s]
        NB = L // 128
        N1 = 2 * NB
        self.dma("pool", FN1.t[0:N1, 0:2 * N1], self.din["hFN1_%d" % s], wr=[FN1.b])
        self.dma("sp", Tt.t[:, 0:2 * N1], self.din["hT_%d" % s].rearrange("p r k -> p (r k)"), wr=[Tt.b])
        self.dma("sp", T2.t[0:N1, :], self.din["hT2_%d" % s].rearrange("p r k -> p (r k)"), wr=[T2.b])
        self.dma("pool", GN1.t[0:N1, 0:2 * NB], self.din["hGN1_%d" % s].rearrange("p r k -> p (r k)"), wr=[GN1.b])
        TCc = 2048
        for m in range(6):
            for c in range(L // TCc):
                xh, co = xhs[it % 2], cos_[it % 2]
                it += 1
                t0 = c * TCc
                lo, hi = max(t0 - 1, 0), min(t0 + TCc + 1, L)
                if c == 0:
                    self.memset("dve", xh.t[:, 0:1], 0.0, wr=[xh.b])
                if hi == L:
                    self.memset("dve", xh.t[:, TCc + 1:TCc + 2], 0.0, wr=[xh.b])
                self.dma("sp", xh.t[:, lo - (t0 - 1):hi - (t0 - 1)], self.lh[s][512 + m * 128:512 + (m + 1) * 128, lo:hi], wr=[xh.b])
                self.ts("dve", co.t, xh.t[:, 0:TCc], hyw3[:, m, 0:1], hyw3[:, m, 3:4], ALU.mult, ALU.add,
                        rd=[xh.b, hyw.b], wr=[co.b])
                for k in (1, 2):
                    self.stt("dve", co.t, xh.t[:, k:k + TCc], hyw3[:, m, k:k + 1], co.t, ALU.mult, ALU.add,
                             rd=[xh.b, hyw.b, co.b], wr=[co.b])
                self.dma("sp", self.hc[s][m * 128:(m + 1) * 128, t0:t0 + TCc], co.t, rd=[co.b])
        NCH = L // 512
        self.memset("dve", acc.t, 0.0, wr=[acc.b])
        acc3 = acc.t.rearrange("p (m c) -> p m c", m=8)
        for c in range(NCH):
            zc, tl = zcs[c % 2], tls[c % 2]
            dec = decs[c % 2]
            t0 = c * 512
            self.dma("sp", zc.t[0:33, :], self.din["hz%d" % s][:, t0:t0 + 512], wr=[zc.b])
            self.dma("sp", tl.t, dap(self.din["htl%d" % s], t0, [(0, 128), (1, 512)]), wr=[tl.b])
            for cc in range(2):
                self.act(dec[cc].t, tl.t, AF.Exp, rd=[tl.b, negd.b], wr=[dec[cc].b], scale=negd.t[:, cc:cc + 1])
            h, hK = zc, 33
            for k, wk in enumerate((w1, w2, w3)):
                ph = ps1
                pho = (k % 2) * 512
                self.mm(ph.t[0:64, pho:pho + 512], wk.t[0:hK, :], h.t[0:hK, :], True, True, rd=[wk.b, h.b], wr=[ph.b])
                self.ts("dve", ya.t[0:64, :], ph.t[0:64, pho:pho + 512], hyb.t[0:64, 3:4], bfq.t[0:64, k:k + 1], ALU.mult, ALU.add,
                        rd=[ph.b, hyb.b, bfq.b], wr=[ya.b])
                self.ts("dve", kk.t[0:64, :], ya.t[0:64, :], 1.0 / TWO_PI, MAGIC, ALU.mult, ALU.add, rd=[ya.b], wr=[kk.b])
                self.ts("dve", kk.t[0:64, :], kk.t[0:64, :], -MAGIC, -TWO_PI, ALU.add, ALU.mult, rd=[kk.b], wr=[kk.b])
                self.tt("dve", ya.t[0:64, :], kk.t[0:64, :], ya.t[0:64, :], ALU.add, rd=[kk.b, ya.b], wr=[ya.b])
                self.ts("dve", ya.t[0:64, :], ya.t[0:64, :], math.pi, -math.pi, ALU.min, ALU.max, rd=[ya.b], wr=[ya.b])
                hk = hks[k]
                self.act(hk.t[0:64, :], ya.t[0:64, :], AF.Sin, rd=[ya.b], wr=[hk.b])
                h, hK = hk, 64
            nk = 0
            for o in range(2):
                for cc in range(2):
                    for dr in (1, 0):
                        m = o * 2 + dr
                        col0 = (m * 2 + cc) * 128
                        pk = (pXre, pXim)[nk % 2]
                        kt = kts[nk % 3]
                        nk += 1
                        self.mm(pk.t, wout.t[0:64, col0:col0 + 128], h.t[0:64, :], True, True, rd=[wout.b, h.b], wr=[pk.b])
                        row0 = cc * 128
                        ai = (o * 2 + cc) * 2 + dr
                        kdst = self.kfull[s][o, row0:row0 + 128, :]
                        if dr == 1:
                            self.tt("dve", rev(kt.t), pk.t, dec[cc].t, ALU.mult, rd=[pk.b, dec[cc].b], wr=[kt.b])
                            if c == 0:
                                self.cp("dve", kb0.t[:, o * 2 + cc:o * 2 + cc + 1], kt.t[:, 511:512], rd=[kt.b], wr=[kb0.b])
                                n_ = 511
                                self.dma("sp", kdst[:, 2 * L - 511:2 * L], kt.t[:, 0:511], rd=[kt.b])
                            else:
                                n_ = 512
                                self.dma("sp", kdst[:, 2 * L - t0 - 511:2 * L - t0 + 1], kt.t, rd=[kt.b])
                        else:
                            self.tt("dve", kt.t, pk.t, dec[cc].t, ALU.mult, rd=[pk.b, dec[cc].b], wr=[kt.b])
                            if c == 0:
                                self.tt("dve", kt.t[:, 0:1], kt.t[:, 0:1], kb0.t[:, o * 2 + cc:o * 2 + cc + 1], ALU.add,
                                        rd=[kt.b, kb0.b], wr=[kt.b])
                            n_ = 512
                            self.dma("sp", kdst[:, t0:t0 + 512], kt.t, rd=[kt.b])
                        self.act(kab.t[:, 0:n_], kt.t[:, 0:n_], AF.Abs, rd=[kt.b, acc.b], wr=[kab.b, acc.b],
                                 accum=acc3[:, ai, c:c + 1])
        self.s.op("dve", (lambda e, o_=nrm.t, i_=acc.t.rearrange("p (m c) -> p m c", m=4):
                          e.tensor_reduce(out=o_, in_=i_, axis=AX.X, op=ALU.add)), rd=[acc.b], wr=[nrm.b])
        self.recip(nrm.t, nrm.t, rd=[nrm.b], wr=[nrm.b])
        for o in range(2):
            for cc in range(2):
                self.dma("sp", dap(self.rnd, o * 256 + cc * 128, [(1, 128), (1, 1)]), nrm.t[:, o * 2 + cc:o * 2 + cc + 1], rd=[nrm.b], slow=True)
                self.dma("sp", dap(self.kfull[s], (o * 256 + cc * 128) * 2 * L + L, [(2 * L, 128), (1, 1)]), zero1.t, rd=[zero1.b], slow=True)
        self.sync()
        self.dma("sp", rnrow.t, dap(self.rnd, 0, [(0, 128), (1, 512)]), wr=[rnrow.b])
        for o in range(2):
            for grp in range(64):
                c0 = grp * 4
                Xb, kft = Xbs[it % 2], kfts[it % 2]
                it += 1
                self.dma("pool", Xb.t[0:N1, :].rearrange("p (c i) -> p c i", c=4),
                         dap(self.kfull[s], (o * 256 + c0) * 2 * L, [(128, N1), (2 * L, 4), (1, 128)]), wr=[Xb.b])
                fft_fwd(Xb, N1, N1)
                k4 = kft.t[:, 0:8 * N1].rearrange("p (c r k) -> p c r k", c=4, r=2)
                rb = rnrow.t[:, o * 256 + c0:o * 256 + c0 + 4].unsqueeze(2).to_broadcast([128, 4, N1])
                self.tt("dve", k4[:, :, 0, :], pXre.t[:, 0:4 * N1].rearrange("p (c k) -> p c k", c=4), rb, ALU.mult,
                        rd=[pXre.b, rnrow.b], wr=[kft.b])
                self.tt("dve", k4[:, :, 1, :], pXim.t[:, 0:4 * N1].rearrange("p (c k) -> p c k", c=4), rb, ALU.mult,
                        rd=[pXim.b, rnrow.b], wr=[kft.b])
                self.dma("sp", dap(self.kfs[s], (o * 128 * 256 + c0) * 2 * N1, [(256 * 2 * N1, 128), (1, 8 * N1)]),
                         kft.t[:, 0:8 * N1], rd=[kft.b])
        self.sync()
        T23 = T2.t.rearrange("p (r i) -> p r i", r=2)
        GN3 = GN1.t[:, 0:2 * NB].rearrange("p (r a) -> p r a", r=2)
        for o in range(2):
            zsrc = self.hc[s] if o == 0 else self.z1[s]
            for grp in range(64):
                c0 = grp * 4
                i2 = it % 2
                it += 1
                Xb, kft, zf, gf = Xbs[i2], kfts[i2], zfs[i2], gfs[i2]
                zap = dap(zsrc, c0 * L, [(128, NB), (L, 4), (1, 128)])
                self.dma("pool", Xb.t[0:NB, :].rearrange("p (c i) -> p c i", c=4), zap, wr=[Xb.b])
                self.dma("sp", zf.t[0:NB, :].rearrange("p (c i) -> p c i", c=4), zap, wr=[zf.b])
                self.dma("sp", gf.t[0:NB, :].rearrange("p (c i) -> p c i", c=4),
                         dap(self.hc[s], (256 * (o + 1) + c0) * L, [(128, NB), (L, 4), (1, 128)]), wr=[gf.b])
                self.dma("sp", kft.t[:, 0:8 * N1],
                         dap(self.kfs[s], (o * 128 * 256 + c0) * 2 * N1, [(256 * 2 * N1, 128), (1, 8 * N1)]), wr=[kft.b])
                fft_fwd(Xb, NB, N1)
                W = 4 * N1
                k4 = kft.t[:, 0:8 * N1].rearrange("p (c r k) -> p c r k", c=4, r=2)
                Kre, Kim = k4[:, :, 0, :], k4[:, :, 1, :]
                Xr = pXre.t[:, 0:W].rearrange("p (c k) -> p c k", c=4)
                Xi = pXim.t[:, 0:W].rearrange("p (c k) -> p c k", c=4)
                t = [q.t[:, 0:W].rearrange("p (c k) -> p c k", c=4) for q in tq]
                self.tt("dve", t[0], Xr, Kre, ALU.mult, rd=[pXre.b, kft.b], wr=[tq[0].b])
                self.tt("dve", t[1], Xi, Kim, ALU.mult, rd=[pXim.b, kft.b], wr=[tq[1].b])
                self.tt("dve", Yre.t[:, 0:W].rearrange("p (c k) -> p c k", c=4), t[0], t[1], ALU.subtract,
                        rd=[tq[0].b, tq[1].b], wr=[Yre.b])
                self.tt("dve", t[2], Xr, Kim, ALU.mult, rd=[pXre.b, kft.b], wr=[tq[2].b])
                self.tt("dve", t[3], Xi, Kre, ALU.mult, rd=[pXim.b, kft.b], wr=[tq[3].b])
                self.tt("dve", Yim.t[:, 0:W].rearrange("p (c k) -> p c k", c=4), t[2], t[3], ALU.add,
                        rd=[tq[2].b, tq[3].b], wr=[Yim.b])
                Yr3 = Yre.t[:, 0:W].rearrange("p (c k) -> p c k", c=4)
                Yi3 = Yim.t[:, 0:W].rearrange("p (c k) -> p c k", c=4)
                for ch in range(4):
                    off = ch * 256
                    self.mm(pB.t[0:N1, off:off + 256], Yr3[:, ch, :], G3[:, 0, :], True, False,
                            rd=[Yre.b, G128.b], wr=[pB.b])
                    self.mm(pB.t[0:N1, off:off + 256], Yi3[:, ch, :], G3[:, 1, :], False, True,
                            rd=[Yim.b, G128.b], wr=[pB.b])
                Br4 = Bpre.t[0:N1, :].rearrange("p (c i) -> p c i", c=4)
                Bi4 = Bpim.t[0:N1, :].rearrange("p (c i) -> p c i", c=4)
                v = pB.t[0:N1, :].rearrange("p (c r i) -> p c r i", c=4, r=2)
                Bre, Bim = v[:, :, 0, :], v[:, :, 1, :]
                Tre = T23[0:N1, 0, :].unsqueeze(1).to_broadcast([N1, 4, 128])
                Tim = T23[0:N1, 1, :].unsqueeze(1).to_broadcast([N1, 4, 128])
                t = [q.t[0:N1, 0:512].rearrange("p (c i) -> p c i", c=4) for q in tq]
                self.tt("dve", t[0], Bre, Tre, ALU.mult, rd=[pB.b, T2.b], wr=[tq[0].b])
                self.tt("dve", t[1], Bim, Tim, ALU.mult, rd=[pB.b, T2.b], wr=[tq[1].b])
                self.tt("dve", Br4, t[0], t[1], ALU.subtract, rd=[tq[0].b, tq[1].b], wr=[Bpre.b])
                self.tt("dve", t[2], Bre, Tim, ALU.mult, rd=[pB.b, T2.b], wr=[tq[2].b])
                self.tt("dve", t[3], Bim, Tre, ALU.mult, rd=[pB.b, T2.b], wr=[tq[3].b])
                self.tt("dve", Bi4, t[2], t[3], ALU.add, rd=[tq[2].b, tq[3].b], wr=[Bpim.b])
                self.mm(py.t[0:NB, :], GN3[0:N1, 0, :], Bpre.t[0:N1, :], True, False, rd=[GN1.b, Bpre.b], wr=[py.b])
                self.mm(py.t[0:NB, :], GN3[0:N1, 1, :], Bpim.t[0:N1, :], False, True, rd=[GN1.b, Bpim.b], wr=[py.b])
                t0_ = tq[0].t[0:NB, :]
                db = drow.t[0:NB, o * 256 + c0:o * 256 + c0 + 4].unsqueeze(2).to_broadcast([NB, 4, 128])
                self.tt("dve", t0_.rearrange("p (c i) -> p c i", c=4), zf.t[0:NB, :].rearrange("p (c i) -> p c i", c=4), db,
                        ALU.mult, rd=[zf.b, drow.b], wr=[tq[0].b])
                self.tt("dve", t0_, t0_, py.t[0:NB, :], ALU.add, rd=[tq[0].b, py.b], wr=[tq[0].b])
                if o == 0:
                    ob = outf[i2]
                    self.tt("dve", ob.t[0:NB, :], t0_, gf.t[0:NB, :], ALU.mult, rd=[tq[0].b, gf.b], wr=[ob.b])
                    self.dma("sp", dap(self.z1[s], c0 * L, [(128, NB), (L, 4), (1, 128)]),
                             ob.t[0:NB, :].rearrange("p (c i) -> p c i", c=4), rd=[ob.b])
                else:
                    ob = outb[i2]
                    self.tt("dve", ob.t[0:NB, :], t0_, gf.t[0:NB, :], ALU.mult, rd=[tq[0].b, gf.b], wr=[ob.b])
                    self.dma("sp", dap(self.catT[s], (768 + c0) * L, [(128, NB), (L, 4), (1, 128)]),
                             ob.t[0:NB, :].rearrange("p (c i) -> p c i", c=4), rd=[ob.b])
            self.sync()
    self.phase_end()


K.sync = _sync
K.phase4 = _phase4


def build_program(nlayers=DEPTH, dump=False):
    nc = bass.Bass("TRN2", target_bir_lowering=False)
    with ExitStack() as ctx:
        s = Sched(nc, ctx)
        k = K(nc, s, ctx, nlayers=nlayers, dbg={"dump": dump})
        k.setup()
        for l in range(nlayers):
            xsrc = k.xin if l == 0 else k.xa
            dst = k.yout if l == nlayers - 1 else k.xa
            k.phase0(l)
            k.phase1(l, xsrc)
            k.phase2(l)
            k.phase3(l)
            k.phase4(l)
            k.phase5(l, xsrc)
            k.phase6(l, dst)
        s.barrier()
        s.emit()
    return nc


_NC_CACHE = {}


def kernel(**inputs):
    inp = {k: np.asarray(v) for k, v in inputs.items()}
    if "nc" not in _NC_CACHE:
        _NC_CACHE["nc"] = build_program()
    nc = _NC_CACHE["nc"]
    sh = host_prep(inp)
    in_maps = []
    for i in range(8):
        m = dict(sh)
        m.update(core_prep(inp, i))
        in_maps.append(m)
    res = run_bass_kernel_spmd(nc, in_maps, core_ids=list(range(8)))
    yp = np.stack([np.asarray(r["yp"], dtype=np.float32) for r in res.results], axis=0)
    ys = np.stack([np.asarray(r["ys"], dtype=np.float32) for r in res.results], axis=0)
    return (yp, ys)
```

```python
import math
import numpy as np
from contextlib import ExitStack
import concourse.bass as bass
import concourse.mybir as mybir
from concourse.bass_types import AP as APc
from concourse.bass_utils import run_bass_kernel_spmd

F32 = mybir.dt.float32
BF16 = mybir.dt.bfloat16
AF = mybir.ActivationFunctionType
ALU = mybir.AluOpType
AX = mybir.AxisListType

D = 1024
DEPTH = 4
LP = 4096
LS = 8192
DFF = 2816
ALPHA = (2 * DEPTH) ** 0.25
LN_EPS = 1e-5
QK_EPS = 1e-6
HY_MIN_DECAY = abs(math.log(1e-2)) / 1.5
HY_MAX_DECAY = abs(math.log(1e-2)) / 0.3
MAGIC = 12582912.0

ENGS = ("pe", "act", "dve", "pool", "sp")
EMBED_WAITS = True


class Buf:
    __slots__ = ("name", "w", "r")

    def __init__(self, name):
        self.name = name
        self.w = None
        self.r = []


class Sched:
    NDMA = 8

    def __init__(self, nc, ctx):
        self.nc = nc
        self.streams = {e: [] for e in ENGS}
        self.sems = {}
        self.cnt = {}
        for e in ENGS:
            self.sems[e] = ctx.enter_context(nc.semaphore("s_" + e))
            self.cnt[e] = 0
        self.dnext = {}
        for q in ("sp", "act", "pool"):
            for i in range(self.NDMA):
                k = "d_%s%d" % (q, i)
                self.sems[k] = ctx.enter_context(nc.semaphore(k))
                self.cnt[k] = 0
            self.dnext[q] = 0
        self.waited = {e: {} for e in ENGS}
        self.nbuf = 0

    def buf(self, name=None):
        self.nbuf += 1
        return Buf(name or ("b%d" % self.nbuf))

    def _need(self, eng, ev, same_ok):
        if ev is None:
            return
        k, v, src = ev
        if src == eng and same_ok:
            return
        if self.waited[eng].get(k, 0) >= v:
            return
        self.waited[eng][k] = v
        self.streams[eng].append(("w", k, v))

    def _deps(self, eng, rd, wr):
        pe = (eng == "pe")
        for b in rd:
            self._need(eng, b.w, pe)
        for b in wr:
            self._need(eng, b.w, pe)
            for ev in b.r:
                self._need(eng, ev, True)

    def _mark(self, ev, rd, wr):
        for b in rd:
            b.r.append(ev)
            if len(b.r) > 64:
                last = {}
                for e2 in b.r:
                    if e2[0] not in last or last[e2[0]][1] < e2[1]:
                        last[e2[0]] = e2
                b.r = list(last.values())
        for b in wr:
            b.w = ev
            b.r = []

    def op(self, eng, fn, rd=(), wr=()):
        self._deps(eng, rd, wr)
        self.cnt[eng] += 1
        ev = (eng, self.cnt[eng], eng)
        self.streams[eng].append(("o", fn, eng, 1))
        self._mark(ev, rd, wr)
        return ev

    def dma(self, q, out, in_, rd=(), wr=(), slow=False):
        self._deps(q, rd, wr)
        i = self.dnext[q]
        self.dnext[q] = (i + 1) % self.NDMA
        k = "d_%s%d" % (q, i)
        if self.cnt[k] > 0:
            self._need(q, (k, self.cnt[k], None), False)
        self.cnt[k] += 16
        ev = (k, self.cnt[k], None)
        if slow:
            self.streams[q].append(("o", (lambda e: e.dma_start(out=out, in_=in_, allow_slow_non_contiguous=True)), k, 16))
        else:
            self.streams[q].append(("o", (lambda e: e.dma_start(out=out, in_=in_)), k, 16))
        self._mark(ev, rd, wr)
        return ev

    def barrier(self):
        for e in ENGS:
            for k, v in self.cnt.items():
                if v > 0 and k != e:
                    self._need(e, (k, v, None), False)

    def emit(self):
        nc = self.nc
        if not any(self.streams[e] for e in ENGS):
            return
        engobj = {"pe": "tensor", "act": "scalar", "dve": "vector", "pool": "gpsimd", "sp": "sync"}
        with nc.Block() as block:
            for e in ENGS:
                items = self.streams[e]
                sems = self.sems

                def body(eng, items=items, sems=sems):
                    n = len(items)
                    i = 0
                    while i < n:
                        it = items[i]
                        if it[0] == "w":
                            if EMBED_WAITS and i + 1 < n and items[i + 1][0] == "o":
                                nx = items[i + 1]
                                ins = nx[1](eng)
                                ins._wait_ge(sems[it[1]], it[2])
                                ins.then_inc(sems[nx[2]], nx[3])
                                i += 2
                                continue
                            eng.wait_ge(sems[it[1]], it[2])
                        else:
                            it[1](eng).then_inc(sems[it[2]], it[3])
                        i += 1
                getattr(block, engobj[e])(body)
        self.streams = {e: [] for e in ENGS}


class TB:
    __slots__ = ("t", "b")

    def __init__(self, t, b):
        self.t = t
        self.b = b

    def __getitem__(self, k):
        return self.t[k]


def rev(ap2d):
    (ps, pn), (fs, fn) = ap2d.ap
    return APc(ap2d.tensor, ap2d.offset + (fn - 1) * fs, [[ps, pn], [-fs, fn]])


def dap(t, off, dims):
    return APc(t.tensor, t.offset + off, [[a, b] for a, b in dims])


class K:
    def __init__(self, nc, s, ctx, nlayers=DEPTH, dbg=None):
        self.nc, self.s, self.gctx = nc, s, ctx
        self.nlayers = nlayers
        self.dbg = dbg or {}
        self.pctx = None
        self.uid = 0

    def _nm(self, n):
        self.uid += 1
        return "%s_%d" % (n, self.uid)

    def sb(self, name, shape, dt, glob=False):
        c = self.gctx if glob else self.pctx
        t = c.enter_context(self.nc.sbuf_tensor(self._nm(name), list(shape), dt))
        return TB(t, self.s.buf(name))

    def ps(self, name, shape, dt=F32):
        t = self.pctx.enter_context(self.nc.psum_tensor(self._nm(name), list(shape), dt))
        return TB(t, self.s.buf(name))

    def dram(self, name, shape, dt, kind="Internal"):
        return self.nc.dram_tensor(name, list(shape), dt, kind=kind).ap()

    def mm(self, out, lhsT, rhs, start, stop, rd, wr, tp=None):
        if tp is None:
            f = lambda e: e.matmul(out=out, lhsT=lhsT, rhs=rhs, start=start, stop=stop)
        else:
            f = lambda e: e.matmul(out=out, lhsT=lhsT, rhs=rhs, start=start, stop=stop, tile_position=tp)
        return self.s.op("pe", f, rd=rd, wr=wr)

    def tr(self, out, in_, ident, rd, wr):
        return self.s.op("pe", lambda e: e.transpose(out=out, in_=in_, identity=ident), rd=rd, wr=wr)

    def act(self, out, in_, func, rd, wr, bias=None, scale=None, accum=None):
        kw = {}
        if bias is not None:
            kw["bias"] = bias
        if scale is not None:
            kw["scale"] = scale
        if accum is not None:
            kw["accum_out"] = accum
        return self.s.op("act", lambda e: e.activation(out=out, in_=in_, func=func, **kw), rd=rd, wr=wr)

    def tt(self, eng, out, in0, in1, op, rd, wr):
        return self.s.op(eng, lambda e: e.tensor_tensor(out=out, in0=in0, in1=in1, op=op), rd=rd, wr=wr)

    def ts(self, eng, out, in0, s1, s2, op0, op1, rd, wr):
        if op1 is None:
            f = lambda e: e.tensor_scalar(out=out, in0=in0, scalar1=s1, scalar2=None, op0=op0)
        else:
            f = lambda e: e.tensor_scalar(out=out, in0=in0, scalar1=s1, scalar2=s2, op0=op0, op1=op1)
        return self.s.op(eng, f, rd=rd, wr=wr)

    def stt(self, eng, out, in0, scalar, in1, op0, op1, rd, wr):
        return self.s.op(eng, lambda e: e.scalar_tensor_tensor(out=out, in0=in0, scalar=scalar, in1=in1, op0=op0, op1=op1), rd=rd, wr=wr)

    def cp(self, eng, out, in_, rd, wr):
        if eng == "act":
            return self.s.op("act", lambda e: e.copy(out=out, in_=in_), rd=rd, wr=wr)
        return self.s.op(eng, lambda e: e.tensor_copy(out=out, in_=in_), rd=rd, wr=wr)

    def memset(self, eng, ap, val, wr):
        return self.s.op(eng, lambda e: e.memset(ap, val), rd=(), wr=wr)

    def recip(self, out, in_, rd, wr):
        return self.s.op("dve", lambda e: e.reciprocal(out=out, in_=in_), rd=rd, wr=wr)

    def dma(self, q, out, in_, rd=(), wr=(), slow=False):
        return self.s.dma(q, out, in_, rd=rd, wr=wr, slow=slow)

    def phase_begin(self):
        self.pctx = ExitStack()

    def phase_end(self):
        self.s.barrier()
        self.pctx.close()
        self.pctx = None


def _init_arena(self):
    self.pctx = None


def _sb(self, name, cols, dt=F32, glob=False, parts=128, ctx=None):
    c = ctx if ctx is not None else (self.gctx if glob else self.pctx)
    t = c.enter_context(self.nc.sbuf_tensor(self._nm(name), [128, cols], dt))
    return TB(t[0:parts, :], self.s.buf(name))


def _ps(self, name, cols=512, dt=F32, parts=128):
    full = 512 if dt == F32 else 1024
    t = self.pctx.enter_context(self.nc.psum_tensor(self._nm(name), [128, full], dt))
    return TB(t[0:parts, 0:cols], self.s.buf(name))


def _ps2(self, name):
    t = self.pctx.enter_context(self.nc.psum_tensor(self._nm(name), [128, 1024], F32))
    return TB(t[:, :], self.s.buf(name))


def _phase_begin(self):
    self.pctx = ExitStack()


def _phase_end(self):
    self.s.barrier()
    self.s.emit()
    self.pctx.close()
    self.pctx = None


K.init_arena = _init_arena
K.sb = _sb
K.ps = _ps
K.ps2 = _ps2
K.phase_begin = _phase_begin
K.phase_end = _phase_end


def _consts():
    c = {}
    c["ident"] = np.eye(128, dtype=np.float32)
    t = np.arange(LS)
    row = (t // 64).astype(np.float32)
    col = (t % 64).astype(np.float32)
    inv = (10000.0 ** (-np.arange(16, dtype=np.float32) / 16)).astype(np.float32)
    ang = np.stack([row[:, None] * inv, col[:, None] * inv], axis=1).astype(np.float32)
    cs = np.cos(ang).reshape(LS // 128, 128, 32).transpose(1, 0, 2)
    sn = np.sin(ang).reshape(LS // 128, 128, 32).transpose(1, 0, 2)
    for s, L in enumerate((LP, LS)):
        f32 = np.float32
        t = np.linspace(0.0, 1.0, L, dtype=f32)[:, None]
        w = (f32(2.0 * math.pi) * np.arange(L, dtype=f32)[:, None] / f32(L)).astype(f32)
        f = np.linspace(1e-4, 15, 16, dtype=f32)[None, :]
        z = np.concatenate([t, np.cos(f * w), -np.sin(f * w)], axis=-1).astype(f32)
        c["hz%d" % s] = np.ascontiguousarray(z.T)
        c["htl%d" % s] = np.ascontiguousarray(t.T)
        NB = L // 128
        N1 = 2 * NB
        N = 2 * L
        NH = N1 // 2 + 1
        a = np.arange(N1, dtype=np.float64)[:, None]
        kl = np.arange(NH, dtype=np.float64)[None, :]
        th = 2 * np.pi * a * kl / N1
        c["hFN1_%d" % s] = np.concatenate([np.cos(th), -np.sin(th)], axis=1).astype(f32)
        i = np.arange(128, dtype=np.float64)[:, None]
        th = 2 * np.pi * i * kl / N
        c["hT_%d" % s] = np.stack([np.cos(th), -np.sin(th)], axis=1).astype(f32)
        c["hT2_%d" % s] = np.stack([np.cos(th.T), np.sin(th.T)], axis=1).astype(f32)
        aa = np.arange(NB, dtype=np.float64)[None, :]
        klc = np.arange(NH, dtype=np.float64)[:, None]
        th = 2 * np.pi * klc * aa / N1
        wgt = np.full((NH, 1), 2.0)
        wgt[0, 0] = 1.0
        wgt[NH - 1, 0] = 1.0
        c["hGN1_%d" % s] = np.stack([wgt * np.cos(th) / N, -wgt * np.sin(th) / N], axis=1).astype(f32)
    i = np.arange(128, dtype=np.float64)[:, None]
    kh = np.arange(128, dtype=np.float64)[None, :]
    th = 2 * np.pi * i * kh / 128
    c["hF128"] = np.stack([np.cos(th), -np.sin(th), np.sin(th)], axis=1).astype(np.float32)
    c["hG128"] = np.stack([np.concatenate([np.cos(th), np.sin(th)], axis=1),
                           np.concatenate([-np.sin(th), np.cos(th)], axis=1)], axis=1).astype(np.float32)
    dl = np.linspace(HY_MIN_DECAY, HY_MAX_DECAY, 256, dtype=np.float32)
    c["hnegd"] = np.ascontiguousarray((-dl).reshape(2, 128).T)
    c["rope_cos"] = np.ascontiguousarray(cs, dtype=np.float32)
    c["rope_sin"] = np.ascontiguousarray(sn, dtype=np.float32)
    return c


def host_prep(inp):
    f32 = np.float32
    A = lambda a: np.ascontiguousarray(a, dtype=f32)
    sh = {}
    sh["ada_w"] = A(inp["ada_w"])
    sh["ada_b"] = A(inp["ada_b"])
    sh["ada_bT"] = A(inp["ada_b"].reshape(DEPTH, 48, 128).transpose(0, 2, 1))
    sh["w_in"] = A(inp["w_in"])
    sh["w_out"] = A(inp["w_out"])
    sh["w_up"] = A(inp["ffn_w_up"])
    sh["w_down"] = A(inp["ffn_w_down"])
    sh["qkg"] = A(np.concatenate([np.tile(inp["q_gain"], (1, 8)), np.tile(inp["k_gain"], (1, 2))], axis=1))
    lw = np.concatenate([inp["lru_conv_w"], inp["lru_conv_b"][:, None, :]], axis=1)
    sh["lruw"] = A(lw.reshape(DEPTH, 5, 2, 128).transpose(0, 3, 2, 1))
    W = np.zeros((DEPTH, 2, 2, 2, 128, 128), f32)
    for gi, nm in enumerate(("lru_wa", "lru_wx")):
        w = np.asarray(inp[nm])
        for cc in range(2):
            for h2 in range(2):
                W[:, :, gi, cc, h2 * 64:(h2 + 1) * 64, h2 * 64:(h2 + 1) * 64] = w[:, :, 2 * cc + h2]
    sh["lruW"] = W
    lb = np.stack([inp["lru_ba"], inp["lru_bx"], inp["lru_lambda"]], axis=-1)
    sh["lrub"] = A(lb.reshape(DEPTH, 2, 2, 128, 3).transpose(0, 3, 2, 1, 4))
    fw = np.concatenate([inp["ffn_conv_w"], inp["ffn_conv_b"][:, None, :]], axis=1)
    sh["ffw"] = A(fw.reshape(DEPTH, 4, 44, 128).transpose(0, 3, 2, 1))
    sh["ln1_g"] = A(inp["ln1_g"]); sh["ln1_b"] = A(inp["ln1_b"])
    sh["ln2_g"] = A(inp["ln2_g"]); sh["ln2_b"] = A(inp["ln2_b"])
    hw = np.concatenate([inp["hy_conv_w"], inp["hy_conv_b"][:, None, :]], axis=1)
    sh["hyw"] = A(hw.reshape(DEPTH, 4, 6, 128).transpose(0, 3, 2, 1))
    sh["hy_w1"] = A(inp["hy_w1"]); sh["hy_w2"] = A(inp["hy_w2"]); sh["hy_w3"] = A(inp["hy_w3"])
    sh["hy_wout"] = A(inp["hy_wout"])
    sh["hyb"] = A(np.stack([inp["hy_b1"], inp["hy_b2"], inp["hy_b3"], inp["hy_freq"]], axis=-1))
    sh["hy_bias"] = A(inp["hy_bias"])
    sh.update(_consts())
    return sh


SHARED_SHAPES = {
    "ada_w": [DEPTH, 1024, 6144], "ada_b": [DEPTH, 6144], "ada_bT": [DEPTH, 128, 48],
    "w_in": [DEPTH, 1024, 2048], "w_out": [DEPTH, 1024, 1024], "w_up": [DEPTH, 1024, 2 * DFF],
    "w_down": [DEPTH, DFF, 1024], "qkg": [DEPTH, 640],
    "lruw": [DEPTH, 128, 2, 5], "lruW": [DEPTH, 2, 2, 2, 128, 128], "lrub": [DEPTH, 128, 2, 2, 3],
    "ffw": [DEPTH, 128, 44, 4], "ln1_g": [DEPTH, 1024], "ln1_b": [DEPTH, 1024], "ln2_g": [DEPTH, 1024], "ln2_b": [DEPTH, 1024],
    "hyw": [DEPTH, 128, 6, 4], "hy_w1": [DEPTH, 33, 64], "hy_w2": [DEPTH, 64, 64], "hy_w3": [DEPTH, 64, 64],
    "hy_wout": [DEPTH, 64, 1024], "hyb": [DEPTH, 64, 4], "hy_bias": [DEPTH, 2, 256],
    "hz0": [33, LP], "hz1": [33, LS], "htl0": [1, LP], "htl1": [1, LS],
    "hFN1_0": [64, 66], "hFN1_1": [128, 130], "hT_0": [128, 2, 33], "hT_1": [128, 2, 65],
    "hT2_0": [33, 2, 128], "hT2_1": [65, 2, 128], "hGN1_0": [33, 2, 32], "hGN1_1": [65, 2, 64],
    "hF128": [128, 3, 128], "hG128": [128, 2, 256], "hnegd": [128, 2],
    "ident": [128, 128], "rope_cos": [128, 64, 32], "rope_sin": [128, 64, 32],
}


def core_prep(inp, i):
    c = np.stack([inp["c_prompt"][i], inp["c_sample"][i]], axis=0)
    cT = np.ascontiguousarray(c.reshape(2, 8, 128).transpose(2, 1, 0), dtype=np.float32)
    return {"xp": np.ascontiguousarray(inp["x_prompt"][i], dtype=np.float32),
            "xs": np.ascontiguousarray(inp["x_sample"][i], dtype=np.float32),
            "cT": cT}


def _setup(self):
    nc = self.nc
    dk = "ExternalOutput" if self.dbg.get("dump") else "Internal"
    self.din = {}
    for k, shp in SHARED_SHAPES.items():
        self.din[k] = self.dram(k, shp, F32, kind="ExternalInput")
    self.xin = [self.dram("xp", [LP, D], F32, kind="ExternalInput"),
                self.dram("xs", [LS, D], F32, kind="ExternalInput")]
    self.cT = self.dram("cT", [128, 8, 2], F32, kind="ExternalInput")
    self.yout = [self.dram("yp", [LP, D], F32, kind="ExternalOutput"),
                 self.dram("ys", [LS, D], F32, kind="ExternalOutput")]
    self.Ls = [LP, LS]
    self.xa = [self.dram("xa%d" % s, [L, D], F32, kind=dk) for s, L in enumerate(self.Ls)]
    self.xres = [self.dram("xres%d" % s, [L, D], F32, kind=dk) for s, L in enumerate(self.Ls)]
    self.qT = [self.dram("qT%d" % s, [512, L], BF16, kind=dk) for s, L in enumerate(self.Ls)]
    self.kT = [self.dram("kT%d" % s, [128, L], BF16, kind=dk) for s, L in enumerate(self.Ls)]
    self.vaug = [self.dram("vaug%d" % s, [2, 128, L // 128, 192], BF16, kind=dk) for s, L in enumerate(self.Ls)]
    self.lh = [self.dram("lh%d" % s, [1280, L], F32, kind=dk) for s, L in enumerate(self.Ls)]
    self.catT = [self.dram("catT%d" % s, [1024, L], BF16, kind=dk) for s, L in enumerate(self.Ls)]
    self.hc = [self.dram("hc%d" % s, [768, L], F32, kind=dk) for s, L in enumerate(self.Ls)]
    self.kfull = [self.dram("kfull%d" % s, [2, 256, 2 * L], F32, kind=dk) for s, L in enumerate(self.Ls)]
    self.kfs = [self.dram("kfs%d" % s, [2, 128, 256, 2, L // 128 + 1], BF16, kind=dk) for s, L in enumerate(self.Ls)]
    self.z1 = [self.dram("z1_%d" % s, [256, L], F32, kind=dk) for s, L in enumerate(self.Ls)]
    self.rnd = self.dram("rnd", [2, 256], F32, kind=dk)
    self.init_arena()
    self.ident = self.sb("ident", 128, BF16, glob=True)
    self.csT = self.sb("csT", 16, F32, glob=True)
    self.modT = self.sb("modT", 96, F32, glob=True)
    self.grow = [self.sb("grow%d" % i, 1024, F32, glob=True) for i in range(4)]
    self.epsq = self.sb("epsq", 1, F32, glob=True)
    self.epsl = self.sb("epsl", 1, F32, glob=True)
    self.phase_begin()
    self.dma("pool", self.ident.t, self.din["ident"], wr=[self.ident.b])
    ct = self.sb("ct", 16, F32)
    self.dma("sp", ct.t, self.cT.rearrange("p k s -> p (k s)"), wr=[ct.b])
    self.act(self.csT.t, ct.t, AF.Silu, rd=[ct.b], wr=[self.csT.b])
    self.memset("dve", self.epsq.t, QK_EPS, wr=[self.epsq.b])
    self.memset("dve", self.epsl.t, LN_EPS, wr=[self.epsl.b])
    self.phase_end()


def _phase0(self, l):
    self.wctx = ExitStack()
    wi = self.sb("wi", 8 * 2048, BF16, ctx=self.wctx)
    wi3 = wi.t.rearrange("p (k n) -> p k n", k=8)
    wib = [self.s.buf("wib%d" % k) for k in range(8)]
    self.phase_begin()
    for kc in range(8):
        self.dma("pool", wi3[:, kc, :], self.din["w_in"][l, kc * 128:(kc + 1) * 128, :], wr=[wib[kc]])
    self.pre_wi = (wi, wib)
    adab = self.sb("adab", 48, F32)
    self.dma("sp", adab.t, self.din["ada_bT"][l], wr=[adab.b])
    self.csrep = self.sb("csrep", 16 * 128, F32)
    self.cp("dve", self.csrep.t.rearrange("p (k n) -> p k n", n=128),
            self.csT.t.unsqueeze(2).to_broadcast([128, 16, 128]), rd=[self.csT.b], wr=[self.csrep.b])
    was = [self.sb("wa%d" % i, 8 * 512, F32) for i in range(2)]
    brow = [self.sb("brow%d" % i, 512, F32) for i in range(2)]
    pm = self.ps("pm", 96)
    pgs = [self.ps("pg%d" % i) for i in range(2)]
    npg = 0
    aw = self.din["ada_w"]
    for gi in range(12):
        wa = was[gi % 2]
        src = dap(aw, l * 1024 * 6144 + gi * 512, [(6144, 128), (128 * 6144, 8), (1, 512)])
        self.dma("sp", wa.t.rearrange("p (k c) -> p k c", k=8), src, wr=[wa.b])
        wa3 = wa.t.rearrange("p (k c) -> p k c", k=8)
        for m in range(4):
            ch = gi * 4 + m
            for kc in range(8):
                self.mm(pm.t[:, ch * 2:ch * 2 + 2], wa3[:, kc, m * 128:(m + 1) * 128],
                        self.csT.t[:, kc * 2:kc * 2 + 2], kc == 0, kc == 7, rd=[wa.b, self.csT.b], wr=[pm.b])
        if gi in (4, 5, 10, 11):
            g = 0 if gi < 6 else 1
            half = gi % 2 if gi < 6 else (gi - 10)
            br = brow[half]
            self.dma("sp", br.t, dap(self.din["ada_b"], l * 6144 + gi * 512, [(0, 128), (1, 512)]), wr=[br.b])
            cr4 = self.csrep.t.rearrange("p (k s n) -> p k s n", k=8, s=2)
            for sq in range(2):
                pg = pgs[npg % 2]
                npg += 1
                for kc in range(8):
                    self.mm(pg.t, cr4[:, kc, sq, :], wa3[:, kc, :], kc == 0, kc == 7,
                            rd=[self.csrep.b, wa.b], wr=[pg.b])
                gr = self.grow[g * 2 + sq]
                self.tt("dve", gr.t[:, half * 512:(half + 1) * 512], pg.t, br.t, ALU.add,
                        rd=[pg.b, br.b], wr=[gr.b])
    m3 = self.modT.t.rearrange("p (c s) -> p c s", s=2)
    self.tt("dve", m3, pm.t.rearrange("p (c s) -> p c s", s=2),
            adab.t.unsqueeze(2).to_broadcast([128, 48, 2]), ALU.add, rd=[pm.b, adab.b], wr=[self.modT.b])
    for c0 in (8, 32):
        self.ts("dve", m3[:, c0:c0 + 8, :], m3[:, c0:c0 + 8, :], 1.0, None, ALU.add, None,
                rd=[self.modT.b], wr=[self.modT.b])
    self.phase_end()


def _phase1(self, l, xsrc):
    self.phase_begin()
    wi, wib = self.pre_wi
    wi3 = wi.t.rearrange("p (k n) -> p k n", k=8)
    gain = self.sb("gain", 640, F32)
    self.dma("sp", gain.t, dap(self.din["qkg"], l * 640, [(0, 128), (1, 640)]), wr=[gain.b])
    self.rcos = self.sb("rcos", 64 * 32, F32)
    self.rsin = self.sb("rsin", 64 * 32, F32)
    self.dma("sp", self.rcos.t, self.din["rope_cos"].rearrange("p t c -> p (t c)"), wr=[self.rcos.b])
    self.dma("sp", self.rsin.t, self.din["rope_sin"].rearrange("p t c -> p (t c)"), wr=[self.rsin.b])
    xbs = [self.sb("xb%d" % i, 4 * 1024, BF16) for i in range(2)]
    uTs = [self.sb("uT%d" % i, 8 * 512, BF16) for i in range(2)]
    lhs_ = [self.sb("lhs%d" % i, 10 * 512, F32) for i in range(2)]
    lhb = [[self.s.buf() for m in range(10)] for i in range(2)]
    sq = self.sb("sq", 640, F32)
    ss = self.sb("ss", 10, F32)
    rstd = self.sb("rstd", 10, F32)
    qn = self.sb("qn", 640, F32)
    tmp = [self.sb("tmp%d" % i, 320, F32) for i in range(4)]
    qkb = self.sb("qkb", 640, BF16)
    qkTs = [self.sb("qkT%d" % i, 5 * 512, BF16) for i in range(2)]
    vgs = [self.sb("vg%d" % i, 2 * 4 * 192, BF16) for i in range(2)]
    for vg in vgs:
        self.memset("dve", vg.t, 1.0, wr=[vg.b])
    pts = [self.ps("pt%d" % i, 512, BF16) for i in range(2)]
    pfs = [self.ps("pf%d" % i) for i in range(2)]
    psq = self.ps("psq")
    pskv = self.ps("pskv", 256)
    pT = self.ps("pT", 640, BF16)
    m3 = self.modT.t.rearrange("p (c s) -> p c s", s=2)
    rc3 = self.rcos.t.rearrange("p (t c) -> p t c", c=32)
    rs3 = self.rsin.t.rearrange("p (t c) -> p t c", c=32)
    it = 0
    for s in range(2):
        L = self.Ls[s]
        NT = L // 128
        for w in range(L // 512):
            xb, uT, lh, qkT, vg = xbs[it % 2], uTs[it % 2], lhs_[it % 2], qkTs[it % 2], vgs[it % 2]
            lb = lhb[it % 2]
            it += 1
            xb3 = xb.t.rearrange("p (j c) -> p j c", j=4)
            uT3 = uT.t.rearrange("p (k t) -> p k t", k=8)
            lh3 = lh.t.rearrange("p (m t) -> p m t", m=10)
            qkT3 = qkT.t.rearrange("p (c t) -> p c t", c=5)
            vg4 = vg.t.rearrange("p (g j c) -> p g j c", g=2, j=4)
            self.dma("pool", xb3, dap(xsrc[s], w * 512 * D, [(D, 128), (128 * D, 4), (1, D)]), wr=[xb.b])
            for kc in range(8):
                pt = pts[kc % 2]
                for j in range(4):
                    self.tr(pt.t[:, j * 128:(j + 1) * 128], xb3[:, j, kc * 128:(kc + 1) * 128], self.ident.t,
                            rd=[xb.b, self.ident.b], wr=[pt.b])
                self.act(uT3[:, kc, :], pt.t, AF.Identity, rd=[pt.b, self.modT.b], wr=[uT.b],
                         scale=m3[:, 8 + kc, s:s + 1], bias=m3[:, kc, s:s + 1])
            for m in range(10):
                pf = pfs[m % 2]
                for kc in range(8):
                    self.mm(pf.t, wi3[:, kc, 768 + m * 128:768 + (m + 1) * 128], uT3[:, kc, :], kc == 0, kc == 7,
                            rd=[wib[kc], uT.b], wr=[pf.b])
                self.cp("act" if m % 2 else "dve", lh3[:, m, :], pf.t, rd=[pf.b], wr=[lb[m]])
            self.dma("sp", dap(self.lh[s], w * 512, [(L, 128), (128 * L, 10), (1, 512)]), lh3, rd=lb)
            for j in range(4):
                T = w * 4 + j
                for kc in range(8):
                    self.mm(psq.t, uT3[:, kc, j * 128:(j + 1) * 128], wi3[:, kc, 0:512], kc == 0, kc == 7,
                            rd=[wib[kc], uT.b], wr=[psq.b])
                for kc in range(8):
                    self.mm(pskv.t, uT3[:, kc, j * 128:(j + 1) * 128], wi3[:, kc, 512:768], kc == 0, kc == 7,
                            rd=[wib[kc], uT.b], wr=[pskv.b])
                self.act(sq.t[:, 0:512], psq.t, AF.Square, rd=[psq.b], wr=[sq.b])
                self.act(sq.t[:, 512:640], pskv.t[:, 0:128], AF.Square, rd=[pskv.b], wr=[sq.b])
                self.s.op("dve", (lambda e, o=ss.t, i=sq.t.rearrange("p (h d) -> p h d", d=64):
                                  e.tensor_reduce(out=o, in_=i, axis=AX.X, op=ALU.add)), rd=[sq.b], wr=[ss.b])
                self.act(rstd.t, ss.t, AF.Sqrt, rd=[ss.b, self.epsq.b], wr=[rstd.b], scale=1.0 / 64, bias=self.epsq.t)
                self.recip(rstd.t, rstd.t, rd=[rstd.b], wr=[rstd.b])
                qn3 = qn.t.rearrange("p (h d) -> p h d", d=64)
                self.tt("dve", qn3[:, 0:8, :], psq.t.rearrange("p (h d) -> p h d", d=64),
                        rstd.t[:, 0:8].unsqueeze(2).to_broadcast([128, 8, 64]), ALU.mult,
                        rd=[psq.b, rstd.b], wr=[qn.b])
                self.tt("dve", qn3[:, 8:10, :], pskv.t[:, 0:128].rearrange("p (h d) -> p h d", d=64),
                        rstd.t[:, 8:10].unsqueeze(2).to_broadcast([128, 2, 64]), ALU.mult,
                        rd=[pskv.b, rstd.b], wr=[qn.b])
                self.tt("dve", qn.t, qn.t, gain.t, ALU.mult, rd=[qn.b, gain.b], wr=[qn.b])
                for c0 in (0, 128):
                    self.cp("act", vg4[:, :, j, c0:c0 + 64], pskv.t[:, 128:256].rearrange("p (g d) -> p g d", g=2),
                            rd=[pskv.b], wr=[vg.b])
                qn5 = qn.t.rearrange("p (h a two f) -> p h a two f", h=10, a=2, two=2)
                qb5 = qkb.t.rearrange("p (h a two f) -> p h a two f", h=10, a=2, two=2)
                x1, x2 = qn5[:, :, :, 0, :], qn5[:, :, :, 1, :]
                cb = rc3[:, T, :].rearrange("p (a f) -> p a f", a=2).unsqueeze(1).to_broadcast([128, 10, 2, 16])
                sb_ = rs3[:, T, :].rearrange("p (a f) -> p a f", a=2).unsqueeze(1).to_broadcast([128, 10, 2, 16])
                t4 = [t.t.rearrange("p (h a f) -> p h a f", h=10, a=2) for t in tmp]
                self.tt("dve", t4[0], x1, cb, ALU.mult, rd=[qn.b, self.rcos.b], wr=[tmp[0].b])
                self.tt("dve", t4[1], x2, sb_, ALU.mult, rd=[qn.b, self.rsin.b], wr=[tmp[1].b])
                self.tt("dve", qb5[:, :, :, 0, :], t4[0], t4[1], ALU.subtract, rd=[tmp[0].b, tmp[1].b], wr=[qkb.b])
                self.tt("dve", t4[2], x2, cb, ALU.mult, rd=[qn.b, self.rcos.b], wr=[tmp[2].b])
                self.tt("dve", t4[3], x1, sb_, ALU.mult, rd=[qn.b, self.rsin.b], wr=[tmp[3].b])
                self.tt("dve", qb5[:, :, :, 1, :], t4[2], t4[3], ALU.add, rd=[tmp[2].b, tmp[3].b], wr=[qkb.b])
                for c in range(5):
                    self.tr(pT.t[:, c * 128:(c + 1) * 128], qkb.t[:, c * 128:(c + 1) * 128], self.ident.t,
                            rd=[qkb.b, self.ident.b], wr=[pT.b])
                self.cp("act", qkT3[:, :, j * 128:(j + 1) * 128], pT.t.rearrange("p (c t) -> p c t", c=5),
                        rd=[pT.b], wr=[qkT.b])
            self.dma("sp", dap(self.qT[s], w * 512, [(L, 128), (128 * L, 4), (1, 512)]), qkT3[:, 0:4, :], rd=[qkT.b])
            self.dma("sp", self.kT[s][:, w * 512:(w + 1) * 512], qkT3[:, 4, :], rd=[qkT.b])
            self.dma("sp", dap(self.vaug[s], w * 4 * 192, [(NT * 192, 128), (128 * NT * 192, 2), (1, 768)]),
                     vg.t.rearrange("p (g c) -> p g c", g=2), rd=[vg.b])
    self.phase_end()
    self.wctx.close()


K.setup = _setup
K.phase0 = _phase0
K.phase1 = _phase1


def _phase2(self, l):
    self.phase_begin()
    Kd = self.sb("Kd", LS, BF16)
    Va = self.sb("Va", 64 * 192, BF16)
    Q2 = self.sb("Q2", 2 * LS, BF16)
    PAB = [self.sb("PAB%d" % i, 1024, BF16) for i in range(3)]
    rr = self.sb("rr", 512, F32)
    rAb, rBb = self.s.buf("rA"), self.s.buf("rB")
    atts = [self.sb("att%d" % i, 512, BF16) for i in range(2)]
    psAB = [self.ps2("psAB%d" % i) for i in range(2)]
    oA = [self.ps("oA%d" % i) for i in range(2)]
    oB = [self.ps("oB%d" % i) for i in range(2)]
    nblk = 0
    for s in range(2):
        L = self.Ls[s]
        NT = L // 128
        for g in range(2):
            self.dma("sp", Kd.t[0:64, 0:L], self.kT[s][g * 64:(g + 1) * 64, :], wr=[Kd.b])
            self.dma("sp", Kd.t[64:128, 0:L], self.kT[s][g * 64:(g + 1) * 64, :], wr=[Kd.b])
            self.dma("sp", Va.t[:, 0:NT * 192], self.vaug[s][g].rearrange("p t c -> p (t c)"), wr=[Va.b])
            Q3 = Q2.t.rearrange("p (h t) -> p h t", h=2)
            self.dma("sp", Q3[:, :, 0:L], dap(self.qT[s], 2 * g * 128 * L, [(L, 128), (128 * L, 2), (1, L)]), wr=[Q2.b])
            Va3 = Va.t.rearrange("p (t c) -> p t c", c=192)
            steps = [(qc, hp, st) for qc in range(L // 512) for hp in range(2) for st in range(NT)]

            def qk(i):
                qc, hp, st = steps[i]
                ib = i % 2
                self.mm(psAB[ib].t[:, 0:512], Kd.t[0:64, st * 128:(st + 1) * 128], Q3[0:64, hp, qc * 512:(qc + 1) * 512],
                        True, True, rd=[Kd.b, Q2.b], wr=[psAB[ib].b], tp=(0, 0))
                self.mm(psAB[ib].t[:, 512:1024], Kd.t[64:128, st * 128:(st + 1) * 128], Q3[64:128, hp, qc * 512:(qc + 1) * 512],
                        True, True, rd=[Kd.b, Q2.b], wr=[psAB[ib].b], tp=(64, 0))
            qk(0)
            for i, (qc, hp, st) in enumerate(steps):
                if i + 1 < len(steps):
                    qk(i + 1)
                ib, ip = i % 2, i % 3
                if st == 0:
                    nblk += 1
                io = nblk % 2
                self.act(PAB[ip].t, psAB[ib].t, AF.Exp, rd=[psAB[ib].b], wr=[PAB[ip].b], scale=0.125)
                self.mm(oA[io].t, Va3[:, st, 0:128], PAB[ip].t[:, 0:512], st == 0, st == NT - 1, rd=[Va.b, PAB[ip].b], wr=[oA[io].b])
                self.mm(oB[io].t, Va3[:, st, 64:192], PAB[ip].t[:, 512:1024], st == 0, st == NT - 1, rd=[Va.b, PAB[ip].b], wr=[oB[io].b])
                if st == NT - 1:
                    att = atts[io]
                    self.recip(rr.t[64:128, :], oA[io].t[64:128, :], rd=[oA[io].b], wr=[rAb])
                    self.tt("dve", att.t[0:64, :], oA[io].t[0:64, :], rr.t[64:128, :], ALU.mult,
                            rd=[oA[io].b, rAb], wr=[att.b])
                    self.recip(rr.t[0:64, :], oB[io].t[0:64, :], rd=[oB[io].b], wr=[rBb])
                    self.tt("dve", att.t[64:128, :], oB[io].t[64:128, :], rr.t[0:64, :], ALU.mult,
                            rd=[oB[io].b, rBb], wr=[att.b])
                    self.dma("sp", self.catT[s][(2 * g + hp) * 128:(2 * g + hp + 1) * 128, qc * 512:(qc + 1) * 512],
                             att.t, rd=[att.b])
    self.phase_end()


K.phase2 = _phase2


def _phase3(self, l):
    self.phase_begin()
    TC = 1024
    XC = self.sb("XC", LS, F32)
    XCB = self.sb("XCB", LS, BF16)
    HF = self.sb("HF", LS, F32)
    WA = self.sb("WA", 8 * 128, BF16)
    WA5 = WA.t.rearrange("p (d g c n) -> p d g c n", d=2, g=2, c=2)
    self.dma("pool", WA5, self.din["lruW"][l].rearrange("d g c k n -> k d g c n"), wr=[WA.b])
    cw = self.sb("cw", 10, F32)
    self.dma("sp", cw.t, self.din["lruw"][l].rearrange("p c k -> p (c k)"), wr=[cw.b])
    lb = self.sb("lb", 12, F32)
    self.dma("sp", lb.t, self.din["lrub"][l].rearrange("p c d k -> p (c d k)"), wr=[lb.b])
    lb4 = lb.t.rearrange("p (c d k) -> p c d k", c=2, d=2)
    cw3 = cw.t.rearrange("p (c k) -> p c k", c=2)
    sp_ = self.sb("sp", 4, F32)
    c12 = self.sb("c12", 8, F32)
    sp3 = sp_.t.rearrange("p (c d) -> p c d", c=2)
    self.act(sp3, lb4[:, :, :, 2], AF.Exp, rd=[lb.b], wr=[sp_.b], scale=-1.0)
    self.act(sp_.t, sp_.t, AF.Ln, rd=[sp_.b], wr=[sp_.b], bias=1.0)
    self.ts("dve", c12.t[:, 0:4], sp_.t, -8.0, None, ALU.mult, None, rd=[sp_.b], wr=[c12.b])
    self.ts("dve", c12.t[:, 4:8], sp_.t, -16.0, None, ALU.mult, None, rd=[sp_.b], wr=[c12.b])
    c4 = c12.t.rearrange("p (k c d) -> p k c d", k=2, c=2)
    xhs = [self.sb("xh%d" % i, TC + 3, F32) for i in range(2)]
    tR = [self.sb("tR%d" % i, TC, F32) for i in range(2)]
    tI = [self.sb("tI%d" % i, TC, F32) for i in range(2)]
    tA = [self.sb("tA%d" % i, TC, F32) for i in range(2)]
    tT = [self.sb("tT%d" % i, TC, F32) for i in range(2)]
    tB = [self.sb("tB%d" % i, TC, F32) for i in range(2)]
    tH = [self.sb("tH%d" % i, TC, F32) for i in range(2)]
    gs = [self.sb("g%d" % i, TC, F32) for i in range(2)]
    ggs = [self.sb("gg%d" % i, TC, F32) for i in range(2)]
    obs = [self.sb("ob%d" % i, TC, BF16) for i in range(2)]
    carry = self.sb("carry", 1, F32)
    prs = [self.ps("pr%d" % i) for i in range(4)]
    pis = [self.ps("pi%d" % i) for i in range(4)]
    it = 0
    nps = 0
    for s in range(2):
        L = self.Ls[s]
        NCH = L // TC
        for cc in range(2):
            for c in range(NCH):
                xh = xhs[it % 2]
                it += 1
                t0 = c * TC
                lo = max(t0 - 2, 0)
                hi = min(t0 + TC + 1, L)
                if c == 0:
                    self.memset("dve", xh.t[:, 0:2], 0.0, wr=[xh.b])
                if c == NCH - 1:
                    self.memset("dve", xh.t[:, TC + 2:TC + 3], 0.0, wr=[xh.b])
                self.dma("sp", xh.t[:, lo - (t0 - 2):hi - (t0 - 2)], self.lh[s][cc * 128:(cc + 1) * 128, lo:hi], wr=[xh.b])
                xc = XC.t[:, t0:t0 + TC]
                self.ts("dve", xc, xh.t[:, 0:TC], cw3[:, cc, 0:1], cw3[:, cc, 4:5], ALU.mult, ALU.add,
                        rd=[xh.b, cw.b], wr=[XC.b])
                for k in range(1, 4):
                    self.stt("dve", xc, xh.t[:, k:k + TC], cw3[:, cc, k:k + 1], xc, ALU.mult, ALU.add,
                             rd=[xh.b, cw.b, XC.b], wr=[XC.b])
                self.cp("act", XCB.t[:, t0:t0 + TC], xc, rd=[XC.b], wr=[XCB.b])
            for d in range(2):
                order = list(range(NCH)) if d == 0 else list(range(NCH - 1, -1, -1))
                for ci, c in enumerate(order):
                    i2 = it % 2
                    it += 1
                    t0 = c * TC
                    for sub in range(2):
                        pr, pi = prs[nps % 4], pis[nps % 4]
                        nps += 1
                        cols = slice(t0 + sub * 512, t0 + (sub + 1) * 512)
                        self.mm(pr.t, WA5[:, d, 0, cc, :], XCB.t[:, cols], True, True, rd=[WA.b, XCB.b], wr=[pr.b])
                        self.mm(pi.t, WA5[:, d, 1, cc, :], XCB.t[:, cols], True, True, rd=[WA.b, XCB.b], wr=[pi.b])
                        self.act(tR[i2].t[:, sub * 512:(sub + 1) * 512], pr.t, AF.Sigmoid, rd=[pr.b, lb.b], wr=[tR[i2].b],
                                 bias=lb4[:, cc, d, 0:1])
                        self.act(tI[i2].t[:, sub * 512:(sub + 1) * 512], pi.t, AF.Sigmoid, rd=[pi.b, lb.b], wr=[tI[i2].b],
                                 bias=lb4[:, cc, d, 1:2])
                    self.act(tA[i2].t, tR[i2].t, AF.Exp, rd=[tR[i2].b, c12.b], wr=[tA[i2].b], scale=c4[:, 0, cc, d:d + 1])
                    self.act(tT[i2].t, tR[i2].t, AF.Exp, rd=[tR[i2].b, c12.b], wr=[tT[i2].b], scale=c4[:, 1, cc, d:d + 1])
                    self.act(tT[i2].t, tT[i2].t, AF.Sqrt, rd=[tT[i2].b], wr=[tT[i2].b], scale=-1.0, bias=1.0)
                    self.tt("dve", tB[i2].t, tI[i2].t, XC.t[:, t0:t0 + TC], ALU.mult, rd=[tI[i2].b, XC.b], wr=[tB[i2].b])
                    self.tt("dve", tB[i2].t, tB[i2].t, tT[i2].t, ALU.mult, rd=[tB[i2].b, tT[i2].b], wr=[tB[i2].b])
                    if d == 0:
                        init = 0.0 if ci == 0 else HF.t[:, t0 - 1:t0]
                        self.s.op("dve", (lambda e, o=HF.t[:, t0:t0 + TC], a=tA[i2].t, b=tB[i2].t, i0=init:
                                          e.tensor_tensor_scan(out=o, data0=a, data1=b, initial=i0, op0=ALU.mult, op1=ALU.add)),
                                  rd=[tA[i2].b, tB[i2].b, HF.b], wr=[HF.b])
                    else:
                        init = 0.0 if ci == 0 else carry.t
                        self.s.op("dve", (lambda e, o=rev(tH[i2].t), a=rev(tA[i2].t), b=rev(tB[i2].t), i0=init:
                                          e.tensor_tensor_scan(out=o, data0=a, data1=b, initial=i0, op0=ALU.mult, op1=ALU.add)),
                                  rd=[tA[i2].b, tB[i2].b, carry.b], wr=[tH[i2].b])
                        self.cp("dve", carry.t, tH[i2].t[:, 0:1], rd=[tH[i2].b], wr=[carry.b])
                        self.tt("dve", HF.t[:, t0:t0 + TC], HF.t[:, t0:t0 + TC], tH[i2].t, ALU.add,
                                rd=[HF.b, tH[i2].b], wr=[HF.b])
            for c in range(NCH):
                i2 = it % 2
                it += 1
                t0 = c * TC
                self.dma("sp", gs[i2].t, self.lh[s][256 + cc * 128:256 + (cc + 1) * 128, t0:t0 + TC], wr=[gs[i2].b])
                self.act(ggs[i2].t, gs[i2].t, AF.Gelu, rd=[gs[i2].b], wr=[ggs[i2].b])
                self.tt("dve", obs[i2].t, HF.t[:, t0:t0 + TC], ggs[i2].t, ALU.mult, rd=[HF.b, ggs[i2].b], wr=[obs[i2].b])
                self.dma("sp", self.catT[s][512 + cc * 128:512 + (cc + 1) * 128, t0:t0 + TC], obs[i2].t, rd=[obs[i2].b])
    self.phase_end()


K.phase3 = _phase3


def _ln_tail(self, po, n, xsrc_rows, gate, lg, lbias, dst_rows, T):
    i2 = T["i"] % 2
    T["i"] += 1
    xr, ysb, st, mv, rs, nm = T["xr"][i2], T["y"][i2], T["st"], T["mv"], T["rs"], T["nm"]
    self.dma("sp", xr.t[0:n, :], xsrc_rows, wr=[xr.b])
    for h in range(2):
        self.tt("dve", ysb.t[0:n, h * 512:(h + 1) * 512], po[h].t[0:n, :], gate.t[0:n, h * 512:(h + 1) * 512], ALU.mult,
                rd=[po[h].b, gate.b], wr=[ysb.b])
    self.stt("dve", ysb.t[0:n, :], xr.t[0:n, :], ALPHA, ysb.t[0:n, :], ALU.mult, ALU.add, rd=[xr.b, ysb.b], wr=[ysb.b])
    st3 = st.t.rearrange("p (c k) -> p c k", c=2)
    for h in range(2):
        self.s.op("dve", (lambda e, o=st3[0:n, h, :], i=ysb.t[0:n, h * 512:(h + 1) * 512]: e.bn_stats(out=o, in_=i)),
                  rd=[ysb.b], wr=[st.b])
    self.s.op("dve", (lambda e, o=mv.t[0:n, :], i=st3[0:n, :, :]: e.bn_aggr(out=o, in_=i)), rd=[st.b], wr=[mv.b])
    self.act(rs.t[0:n, :], mv.t[0:n, 1:2], AF.Sqrt, rd=[mv.b, self.epsl.b], wr=[rs.b], bias=self.epsl.t[0:n, :])
    self.recip(rs.t[0:n, :], rs.t[0:n, :], rd=[rs.b], wr=[rs.b])
    self.ts("dve", nm.t[0:n, :], mv.t[0:n, 0:1], -1.0, rs.t[0:n, :], ALU.mult, ALU.mult, rd=[mv.b, rs.b], wr=[nm.b])
    self.act(ysb.t[0:n, :], ysb.t[0:n, :], AF.Identity, rd=[ysb.b, rs.b, nm.b], wr=[ysb.b],
             scale=rs.t[0:n, :], bias=nm.t[0:n, :])
    self.tt("pool", ysb.t[0:n, :], ysb.t[0:n, :], lg.t[0:n, :], ALU.mult, rd=[ysb.b, lg.b], wr=[ysb.b])
    self.tt("pool", xr.t[0:n, :], ysb.t[0:n, :], lbias.t[0:n, :], ALU.add, rd=[ysb.b, lbias.b], wr=[xr.b])
    self.dma("sp", dst_rows, xr.t[0:n, :], rd=[xr.b])


def _ln_bufs(self):
    return {"i": 0, "xr": [self.sb("lnx%d" % i, 1024, F32) for i in range(2)],
            "y": [self.sb("lny%d" % i, 1024, F32) for i in range(2)],
            "st": self.sb("lnst", 12, F32), "mv": self.sb("lnmv", 2, F32),
            "rs": self.sb("lnrs", 1, F32), "nm": self.sb("lnnm", 1, F32)}


def _phase5(self, l, xsrc):
    self.wctx = ExitStack()
    wu = self.sb("wu", 8 * 2 * DFF, BF16, ctx=self.wctx)
    wu3 = wu.t.rearrange("p (k n) -> p k n", k=8)
    wub = [self.s.buf() for k in range(8)]
    self.pre_wu = (wu, wub)
    self.phase_begin()
    wo = self.sb("wo", 8 * 1024, BF16)
    wo3 = wo.t.rearrange("p (k n) -> p k n", k=8)
    wob = [self.s.buf() for k in range(8)]
    for kc in range(8):
        self.dma("pool", wo3[:, kc, :], self.din["w_out"][l, kc * 128:(kc + 1) * 128, :], wr=[wob[kc]])
    lg = self.sb("lg", 1024, F32)
    lb = self.sb("lb", 1024, F32)
    self.dma("sp", lg.t, dap(self.din["ln1_g"], l * 1024, [(0, 128), (1, 1024)]), wr=[lg.b])
    self.dma("sp", lb.t, dap(self.din["ln1_b"], l * 1024, [(0, 128), (1, 1024)]), wr=[lb.b])
    for kc in range(8):
        self.dma("pool", wu3[:, kc, :], self.din["w_up"][l, kc * 128:(kc + 1) * 128, :], wr=[wub[kc]])
    T = self.ln_bufs()
    cts = [self.sb("ct%d" % i, 8 * 512, BF16) for i in range(2)]
    pos = [[self.ps("po%d%d" % (i, h)) for h in range(2)] for i in range(2)]
    it = 0
    nj = 0
    for s in range(2):
        L = self.Ls[s]
        for w in range(L // 512):
            ct = cts[it % 2]
            it += 1
            ct3 = ct.t.rearrange("p (k t) -> p k t", k=8)
            self.dma("sp", ct3, dap(self.catT[s], w * 512, [(L, 128), (128 * L, 8), (1, 512)]), wr=[ct.b])
            for j in range(4):
                po = pos[nj % 2]
                nj += 1
                for h in range(2):
                    for kc in range(8):
                        self.mm(po[h].t, ct3[:, kc, j * 128:(j + 1) * 128], wo3[:, kc, h * 512:(h + 1) * 512],
                                kc == 0, kc == 7, rd=[ct.b, wob[kc]], wr=[po[h].b])
                r0 = w * 512 + j * 128
                self.ln_tail(po, 128, xsrc[s][r0:r0 + 128, :], self.grow[0 * 2 + s], lg, lb,
                             self.xres[s][r0:r0 + 128, :], T)
    self.phase_end()


def _phase6(self, l, dst):
    self.phase_begin()
    WN = 254
    wu, wub = self.pre_wu
    wu3 = wu.t.rearrange("p (k n) -> p k n", k=8)
    wd = self.sb("wd", 22 * 1024, BF16)
    wd3 = wd.t.rearrange("p (k n) -> p k n", k=22)
    wdb = [self.s.buf() for k in range(22)]
    for kc in range(22):
        self.dma("pool", wd3[:, kc, :], self.din["w_down"][l, kc * 128:(kc + 1) * 128, :], wr=[wdb[kc]])
    cw = self.sb("fcw", 44 * 4, F32)
    self.dma("sp", cw.t, self.din["ffw"][l].rearrange("p c k -> p (c k)"), wr=[cw.b])
    cw3 = cw.t.rearrange("p (c k) -> p c k", k=4)
    lg = self.sb("lg", 1024, F32)
    lb = self.sb("lb", 1024, F32)
    self.dma("sp", lg.t, dap(self.din["ln2_g"], l * 1024, [(0, 128), (1, 1024)]), wr=[lg.b])
    self.dma("sp", lb.t, dap(self.din["ln2_b"], l * 1024, [(0, 128), (1, 1024)]), wr=[lb.b])
    T = self.ln_bufs()
    xbs = [self.sb("fxb%d" % i, 2 * 1024, BF16) for i in range(2)]
    uT = self.sb("fuT", 8 * 256, BF16)
    uT3 = uT.t.rearrange("p (k t) -> p k t", k=8)
    g = self.sb("fg", 22 * 256, BF16)
    g3 = g.t.rearrange("p (k t) -> p k t", k=22)
    tgs = [self.sb("tg%d" % i, 256, F32) for i in range(2)]
    tvs = [self.sb("tv%d" % i, 256, F32) for i in range(2)]
    pts = [self.ps("fpt%d" % i, 256, BF16) for i in range(2)]
    pgs = [self.ps("fpg%d" % i, 256) for i in range(2)]
    pvs = [self.ps("fpv%d" % i, 256) for i in range(2)]
    po = [self.ps("fpo%d" % h) for h in range(2)]
    m3 = self.modT.t.rearrange("p (c s) -> p c s", s=2)
    it = 0
    for s in range(2):
        L = self.Ls[s]
        xs_ = self.xres[s]
        nw = (L + WN - 1) // WN
        for w in range(nw):
            t0 = w * WN
            nv = min(WN, L - t0)
            xb = xbs[it % 2]
            it += 1
            xb3 = xb.t.rearrange("p (j c) -> p j c", j=2)
            edge = (w == 0) or (t0 + 255 > L)
            if edge:
                self.memset("dve", xb.t, 0.0, wr=[xb.b])
            for j in range(2):
                a = t0 - 1 + 128 * j
                lo, hi = max(a, 0), min(a + 128, L)
                if hi > lo:
                    self.dma("pool", xb3[lo - a:hi - a, j, :], xs_[lo:hi, :], wr=[xb.b])
            for kc in range(8):
                pt = pts[kc % 2]
                for j in range(2):
                    self.tr(pt.t[:, j * 128:(j + 1) * 128], xb3[:, j, kc * 128:(kc + 1) * 128], self.ident.t,
                            rd=[xb.b, self.ident.b], wr=[pt.b])
                self.act(uT3[:, kc, :], pt.t, AF.Identity, rd=[pt.b, self.modT.b], wr=[uT.b],
                         scale=m3[:, 32 + kc, s:s + 1], bias=m3[:, 24 + kc, s:s + 1])
            if w == 0:
                self.memset("dve", uT3[:, :, 0:1], 0.0, wr=[uT.b])
            if t0 + nv >= L:
                c0 = L - (t0 - 1)
                self.memset("dve", uT3[:, :, c0:256], 0.0, wr=[uT.b])
            for jc in range(22):
                pg, pv = pgs[jc % 2], pvs[jc % 2]
                tg, tv = tgs[jc % 2], tvs[jc % 2]
                for kc in range(8):
                    self.mm(pg.t, wu3[:, kc, jc * 128:(jc + 1) * 128], uT3[:, kc, :], kc == 0, kc == 7,
                            rd=[wub[kc], uT.b], wr=[pg.b])
                for kc in range(8):
                    self.mm(pv.t, wu3[:, kc, DFF + jc * 128:DFF + (jc + 1) * 128], uT3[:, kc, :], kc == 0, kc == 7,
                            rd=[wub[kc], uT.b], wr=[pv.b])
                for (pp, tt_, ch) in ((pg, tg, jc), (pv, tv, 22 + jc)):
                    self.act(tt_.t[:, 0:WN], pp.t[:, 0:WN], AF.Identity, rd=[pp.b, cw.b], wr=[tt_.b],
                             scale=cw3[:, ch, 0:1], bias=cw3[:, ch, 3:4])
                    for k in (1, 2):
                        self.stt("dve", tt_.t[:, 0:WN], pp.t[:, k:k + WN], cw3[:, ch, k:k + 1], tt_.t[:, 0:WN],
                                 ALU.mult, ALU.add, rd=[pp.b, cw.b, tt_.b], wr=[tt_.b])
                self.act(tg.t[:, 0:WN], tg.t[:, 0:WN], AF.Gelu, rd=[tg.b], wr=[tg.b])
                self.tt("dve", g3[:, jc, 0:WN], tg.t[:, 0:WN], tv.t[:, 0:WN], ALU.mult, rd=[tg.b, tv.b], wr=[g.b])
            for c0 in (0, 128):
                n = min(128, nv - c0)
                if n <= 0:
                    continue
                for h in range(2):
                    for kc in range(22):
                        self.mm(po[h].t[0:n, :], g3[:, kc, c0:c0 + n], wd3[:, kc, h * 512:(h + 1) * 512],
                                kc == 0, kc == 21, rd=[g.b, wdb[kc]], wr=[po[h].b])
                r0 = t0 + c0
                self.ln_tail(po, n, xs_[r0:r0 + n, :], self.grow[1 * 2 + s], lg, lb, dst[s][r0:r0 + n, :], T)
    self.phase_end()
    self.wctx.close()


K.ln_tail = _ln_tail
K.ln_bufs = _ln_bufs
K.phase5 = _phase5
K.phase6 = _phase6


def _sync(self):
    self.s.barrier()


def _phase4(self, l):
    self.phase_begin()
    TWO_PI = 2.0 * math.pi
    F128 = self.sb("F128", 3 * 128, BF16)
    self.dma("pool", F128.t, self.din["hF128"].rearrange("p r k -> p (r k)"), wr=[F128.b])
    F3 = F128.t.rearrange("p (r k) -> p r k", r=3)
    G128 = self.sb("G128", 2 * 256, BF16)
    self.dma("pool", G128.t, self.din["hG128"].rearrange("p r k -> p (r k)"), wr=[G128.b])
    G3 = G128.t.rearrange("p (r k) -> p r k", r=2)
    hyw = self.sb("hyw", 24, F32)
    self.dma("sp", hyw.t, self.din["hyw"][l].rearrange("p c k -> p (c k)"), wr=[hyw.b])
    hyw3 = hyw.t.rearrange("p (c k) -> p c k", k=4)
    drow = self.sb("drow", 512, F32)
    self.dma("sp", drow.t, dap(self.din["hy_bias"], l * 512, [(0, 128), (1, 512)]), wr=[drow.b])
    negd = self.sb("negd", 2, F32)
    self.dma("sp", negd.t, self.din["hnegd"], wr=[negd.b])
    w1 = self.sb("hw1", 64, F32)
    w2 = self.sb("hw2", 64, F32)
    w3 = self.sb("hw3", 64, F32)
    wout = self.sb("hwout", 1024, F32)
    hyb = self.sb("hyb", 4, F32)
    self.dma("sp", w1.t[0:33, :], self.din["hy_w1"][l], wr=[w1.b])
    self.dma("sp", w2.t[0:64, :], self.din["hy_w2"][l], wr=[w2.b])
    self.dma("sp", w3.t[0:64, :], self.din["hy_w3"][l], wr=[w3.b])
    self.dma("sp", wout.t[0:64, :], self.din["hy_wout"][l], wr=[wout.b])
    self.dma("sp", hyb.t[0:64, :], self.din["hyb"][l], wr=[hyb.b])
    bfq = self.sb("bfq", 3, F32)
    self.ts("dve", bfq.t[0:64, :], hyb.t[0:64, 0:3], hyb.t[0:64, 3:4], None, ALU.mult, None, rd=[hyb.b], wr=[bfq.b])
    zero1 = self.sb("zero1", 1, F32)
    self.memset("dve", zero1.t, 0.0, wr=[zero1.b])
    FN1 = self.sb("FN1", 256, BF16)
    Tt = self.sb("Tt", 256, F32)
    T2 = self.sb("T2", 256, F32)
    GN1 = self.sb("GN1", 128, BF16)
    rnrow = self.sb("rnrow", 512, F32)
    xhs = [self.sb("hxh%d" % i, 2050, F32) for i in range(2)]
    cos_ = [self.sb("hco%d" % i, 2048, F32) for i in range(2)]
    zcs = [self.sb("hzc%d" % i, 512, F32) for i in range(2)]
    tls = [self.sb("htl%d" % i, 512, F32) for i in range(2)]
    decs = [[self.sb("hdec%d%d" % (i, c), 512, F32) for c in range(2)] for i in range(2)]
    ya = self.sb("hya", 512, F32)
    kk = self.sb("hkk", 512, F32)
    hks = [self.sb("hk%d" % i, 512, F32) for i in range(3)]
    kts = [self.sb("hkt%d" % i, 512, F32) for i in range(3)]
    kab = self.sb("hkab", 512, F32)
    acc = self.sb("hacc", 8 * 16, F32)
    kb0 = self.sb("hkb0", 4, F32)
    nrm = self.sb("hnrm", 4, F32)
    Xbs = [self.sb("hXb%d" % i, 512, BF16) for i in range(2)]
    zfs = [self.sb("hzf%d" % i, 512, F32) for i in range(2)]
    gfs = [self.sb("hgf%d" % i, 512, F32) for i in range(2)]
    kfts = [self.sb("hkft%d" % i, 4 * 2 * 128, BF16) for i in range(2)]
    Apre = self.sb("hApre", 512, BF16)
    Apim = self.sb("hApim", 512, BF16)
    Yre = self.sb("hYre", 512, BF16)
    Yim = self.sb("hYim", 512, BF16)
    Bpre = self.sb("hBpre", 512, BF16)
    Bpim = self.sb("hBpim", 512, BF16)
    tq = [self.sb("htq%d" % i, 512, F32) for i in range(4)]
    outf = [self.sb("houtf%d" % i, 512, F32) for i in range(2)]
    outb = [self.sb("houtb%d" % i, 512, BF16) for i in range(2)]
    ps1 = self.ps2("hps1")
    pXre = self.ps("hpXre")
    pXim = self.ps("hpXim")
    pB = self.ps2("hpB")
    py = self.ps("hpy")
    it = 0

    def fft_fwd(Xb, Kp, N1):
        NH = N1 // 2 + 1
        cw = 2 * NH
        CS = 256 if N1 == 128 else 128
        X3 = Xb.t.rearrange("p (c i) -> p c i", c=4)
        for ch in range(4):
            off = ch * CS
            self.mm(ps1.t[:, off:off + cw], X3[0:Kp, ch, :], FN1.t[0:Kp, 0:cw], True, True,
                    rd=[Xb.b, FN1.b], wr=[ps1.b])
        Tt3 = Tt.t[:, 0:cw].rearrange("p (r k) -> p r k", r=2)
        Ar3 = Apre.t[:, 0:4 * NH].rearrange("p (c k) -> p c k", c=4)
        Ai3 = Apim.t[:, 0:4 * NH].rearrange("p (c k) -> p c k", c=4)
        v = ps1.t[:, 0:4 * CS].rearrange("p (c w) -> p c w", c=4)
        Are, Aim = v[:, :, 0:NH], v[:, :, NH:cw]
        Tre = Tt3[:, 0, :].unsqueeze(1).to_broadcast([128, 4, NH])
        Tim = Tt3[:, 1, :].unsqueeze(1).to_broadcast([128, 4, NH])
        t = [q.t[:, 0:4 * NH].rearrange("p (c k) -> p c k", c=4) for q in tq]
        self.tt("dve", t[0], Are, Tre, ALU.mult, rd=[ps1.b, Tt.b], wr=[tq[0].b])
        self.tt("dve", t[1], Aim, Tim, ALU.mult, rd=[ps1.b, Tt.b], wr=[tq[1].b])
        self.tt("dve", Ar3, t[0], t[1], ALU.subtract, rd=[tq[0].b, tq[1].b], wr=[Apre.b])
        self.tt("dve", t[2], Are, Tim, ALU.mult, rd=[ps1.b, Tt.b], wr=[tq[2].b])
        self.tt("dve", t[3], Aim, Tre, ALU.mult, rd=[ps1.b, Tt.b], wr=[tq[3].b])
        self.tt("dve", Ai3, t[2], t[3], ALU.add, rd=[tq[2].b, tq[3].b], wr=[Apim.b])
        W = 4 * NH
        self.mm(pXre.t[:, 0:W], F3[:, 0, :], Apre.t[:, 0:W], True, False, rd=[F128.b, Apre.b], wr=[pXre.b])
        self.mm(pXre.t[:, 0:W], F3[:, 2, :], Apim.t[:, 0:W], False, True, rd=[F128.b, Apim.b], wr=[pXre.b])
        self.mm(pXim.t[:, 0:W], F3[:, 0, :], Apim.t[:, 0:W], True, False, rd=[F128.b, Apim.b], wr=[pXim.b])
        self.mm(pXim.t[:, 0:W], F3[:, 1, :], Apre.t[:, 0:W], False, True, rd=[F128.b, Apre.b], wr=[pXim.b])

    for s in range(2):
        L = self.Ls[s]
        NB = L // 128
        N1 = 2 * NB
        NH = N1 // 2 + 1
        self.dma("pool", FN1.t[0:N1, 0:2 * NH], self.din["hFN1_%d" % s], wr=[FN1.b])
        self.dma("sp", Tt.t[:, 0:2 * NH], self.din["hT_%d" % s].rearrange("p r k -> p (r k)"), wr=[Tt.b])
        self.dma("sp", T2.t[0:NH, :], self.din["hT2_%d" % s].rearrange("p r k -> p (r k)"), wr=[T2.b])
        self.dma("pool", GN1.t[0:NH, 0:2 * NB], self.din["hGN1_%d" % s].rearrange("p r k -> p (r k)"), wr=[GN1.b])
        TCc = 2048
        for m in range(6):
            for c in range(L // TCc):
                xh, co = xhs[it % 2], cos_[it % 2]
                it += 1
                t0 = c * TCc
                lo, hi = max(t0 - 1, 0), min(t0 + TCc + 1, L)
                if c == 0:
                    self.memset("dve", xh.t[:, 0:1], 0.0, wr=[xh.b])
                if hi == L:
                    self.memset("dve", xh.t[:, TCc + 1:TCc + 2], 0.0, wr=[xh.b])
                self.dma("sp", xh.t[:, lo - (t0 - 1):hi - (t0 - 1)], self.lh[s][512 + m * 128:512 + (m + 1) * 128, lo:hi], wr=[xh.b])
                self.ts("dve", co.t, xh.t[:, 0:TCc], hyw3[:, m, 0:1], hyw3[:, m, 3:4], ALU.mult, ALU.add,
                        rd=[xh.b, hyw.b], wr=[co.b])
                for k in (1, 2):
                    self.stt("dve", co.t, xh.t[:, k:k + TCc], hyw3[:, m, k:k + 1], co.t, ALU.mult, ALU.add,
                             rd=[xh.b, hyw.b, co.b], wr=[co.b])
                self.dma("sp", self.hc[s][m * 128:(m + 1) * 128, t0:t0 + TCc], co.t, rd=[co.b])
        NCH = L // 512
        self.memset("dve", acc.t, 0.0, wr=[acc.b])
        acc3 = acc.t.rearrange("p (m c) -> p m c", m=8)
        for c in range(NCH):
            zc, tl = zcs[c % 2], tls[c % 2]
            dec = decs[c % 2]
            t0 = c * 512
            self.dma("sp", zc.t[0:33, :], self.din["hz%d" % s][:, t0:t0 + 512], wr=[zc.b])
            self.dma("sp", tl.t, dap(self.din["htl%d" % s], t0, [(0, 128), (1, 512)]), wr=[tl.b])
            for cc in range(2):
                self.act(dec[cc].t, tl.t, AF.Exp, rd=[tl.b, negd.b], wr=[dec[cc].b], scale=negd.t[:, cc:cc + 1])
            h, hK = zc, 33
            for k, wk in enumerate((w1, w2, w3)):
                ph = ps1
                pho = (k % 2) * 512
                self.mm(ph.t[0:64, pho:pho + 512], wk.t[0:hK, :], h.t[0:hK, :], True, True, rd=[wk.b, h.b], wr=[ph.b])
                self.ts("dve", ya.t[0:64, :], ph.t[0:64, pho:pho + 512], hyb.t[0:64, 3:4], bfq.t[0:64, k:k + 1], ALU.mult, ALU.add,
                        rd=[ph.b, hyb.b, bfq.b], wr=[ya.b])
                self.ts("dve", kk.t[0:64, :], ya.t[0:64, :], 1.0 / TWO_PI, MAGIC, ALU.mult, ALU.add, rd=[ya.b], wr=[kk.b])
                self.ts("dve", kk.t[0:64, :], kk.t[0:64, :], -MAGIC, -TWO_PI, ALU.add, ALU.mult, rd=[kk.b], wr=[kk.b])
                self.tt("dve", ya.t[0:64, :], kk.t[0:64, :], ya.t[0:64, :], ALU.add, rd=[kk.b, ya.b], wr=[ya.b])
                self.ts("dve", ya.t[0:64, :], ya.t[0:64, :], math.pi, -math.pi, ALU.min, ALU.max, rd=[ya.b], wr=[ya.b])
                hk = hks[k]
                self.act(hk.t[0:64, :], ya.t[0:64, :], AF.Sin, rd=[ya.b], wr=[hk.b])
                h, hK = hk, 64
            nk = 0
            for o in range(2):
                for cc in range(2):
                    for dr in (1, 0):
                        m = o * 2 + dr
                        col0 = (m * 2 + cc) * 128
                        pk = (pXre, pXim)[nk % 2]
                        kt = kts[nk % 3]
                        nk += 1
                        self.mm(pk.t, wout.t[0:64, col0:col0 + 128], h.t[0:64, :], True, True, rd=[wout.b, h.b], wr=[pk.b])
                        row0 = cc * 128
                        ai = (o * 2 + cc) * 2 + dr
                        kdst = self.kfull[s][o, row0:row0 + 128, :]
                        if dr == 1:
                            self.tt("dve", rev(kt.t), pk.t, dec[cc].t, ALU.mult, rd=[pk.b, dec[cc].b], wr=[kt.b])
                            if c == 0:
                                self.cp("dve", kb0.t[:, o * 2 + cc:o * 2 + cc + 1], kt.t[:, 511:512], rd=[kt.b], wr=[kb0.b])
                                n_ = 511
                                self.dma("sp", kdst[:, 2 * L - 511:2 * L], kt.t[:, 0:511], rd=[kt.b])
                            else:
                                n_ = 512
                                self.dma("sp", kdst[:, 2 * L - t0 - 511:2 * L - t0 + 1], kt.t, rd=[kt.b])
                        else:
                            self.tt("dve", kt.t, pk.t, dec[cc].t, ALU.mult, rd=[pk.b, dec[cc].b], wr=[kt.b])
                            if c == 0:
                                self.tt("dve", kt.t[:, 0:1], kt.t[:, 0:1], kb0.t[:, o * 2 + cc:o * 2 + cc + 1], ALU.add,
                                        rd=[kt.b, kb0.b], wr=[kt.b])
                            n_ = 512
                            self.dma("sp", kdst[:, t0:t0 + 512], kt.t, rd=[kt.b])
                        self.act(kab.t[:, 0:n_], kt.t[:, 0:n_], AF.Abs, rd=[kt.b, acc.b], wr=[kab.b, acc.b],
                                 accum=acc3[:, ai, c:c + 1])
        self.s.op("dve", (lambda e, o_=nrm.t, i_=acc.t.rearrange("p (m c) -> p m c", m=4):
                          e.tensor_reduce(out=o_, in_=i_, axis=AX.X, op=ALU.add)), rd=[acc.b], wr=[nrm.b])
        self.recip(nrm.t, nrm.t, rd=[nrm.b], wr=[nrm.b])
        for o in range(2):
            for cc in range(2):
                self.dma("sp", dap(self.rnd, o * 256 + cc * 128, [(1, 128), (1, 1)]), nrm.t[:, o * 2 + cc:o * 2 + cc + 1], rd=[nrm.b], slow=True)
                self.dma("sp", dap(self.kfull[s], (o * 256 + cc * 128) * 2 * L + L, [(2 * L, 128), (1, 1)]), zero1.t, rd=[zero1.b], slow=True)
        self.sync()
        self.dma("sp", rnrow.t, dap(self.rnd, 0, [(0, 128), (1, 512)]), wr=[rnrow.b])
        for o in range(2):
            for grp in range(64):
                c0 = grp * 4
                Xb, kft = Xbs[it % 2], kfts[it % 2]
                it += 1
                self.dma("pool", Xb.t[0:N1, :].rearrange("p (c i) -> p c i", c=4),
                         dap(self.kfull[s], (o * 256 + c0) * 2 * L, [(128, N1), (2 * L, 4), (1, 128)]), wr=[Xb.b])
                fft_fwd(Xb, N1, N1)
                k4 = kft.t[:, 0:8 * NH].rearrange("p (c r k) -> p c r k", c=4, r=2)
                rb = rnrow.t[:, o * 256 + c0:o * 256 + c0 + 4].unsqueeze(2).to_broadcast([128, 4, NH])
                self.tt("dve", k4[:, :, 0, :], pXre.t[:, 0:4 * NH].rearrange("p (c k) -> p c k", c=4), rb, ALU.mult,
                        rd=[pXre.b, rnrow.b], wr=[kft.b])
                self.tt("dve", k4[:, :, 1, :], pXim.t[:, 0:4 * NH].rearrange("p (c k) -> p c k", c=4), rb, ALU.mult,
                        rd=[pXim.b, rnrow.b], wr=[kft.b])
                self.dma("sp", dap(self.kfs[s], (o * 128 * 256 + c0) * 2 * NH, [(256 * 2 * NH, 128), (1, 8 * NH)]),
                         kft.t[:, 0:8 * NH], rd=[kft.b])
        self.sync()
        T23 = T2.t.rearrange("p (r i) -> p r i", r=2)
        GN3 = GN1.t[:, 0:2 * NB].rearrange("p (r a) -> p r a", r=2)
        W = 4 * NH
        for o in range(2):
            zsrc = self.hc[s] if o == 0 else self.z1[s]
            for grp in range(64):
                c0 = grp * 4
                i2 = it % 2
                it += 1
                Xb, kft, zf, gf = Xbs[i2], kfts[i2], zfs[i2], gfs[i2]
                zap = dap(zsrc, c0 * L, [(128, NB), (L, 4), (1, 128)])
                self.dma("pool", Xb.t[0:NB, :].rearrange("p (c i) -> p c i", c=4), zap, wr=[Xb.b])
                self.dma("sp", zf.t[0:NB, :].rearrange("p (c i) -> p c i", c=4), zap, wr=[zf.b])
                self.dma("sp", gf.t[0:NB, :].rearrange("p (c i) -> p c i", c=4),
                         dap(self.hc[s], (256 * (o + 1) + c0) * L, [(128, NB), (L, 4), (1, 128)]), wr=[gf.b])
                self.dma("sp", kft.t[:, 0:8 * NH],
                         dap(self.kfs[s], (o * 128 * 256 + c0) * 2 * NH, [(256 * 2 * NH, 128), (1, 8 * NH)]), wr=[kft.b])
                fft_fwd(Xb, NB, N1)
                k4 = kft.t[:, 0:8 * NH].rearrange("p (c r k) -> p c r k", c=4, r=2)
                Kre, Kim = k4[:, :, 0, :], k4[:, :, 1, :]
                Xr = pXre.t[:, 0:W].rearrange("p (c k) -> p c k", c=4)
                Xi = pXim.t[:, 0:W].rearrange("p (c k) -> p c k", c=4)
                t = [q.t[:, 0:W].rearrange("p (c k) -> p c k", c=4) for q in tq]
                self.tt("dve", t[0], Xr, Kre, ALU.mult, rd=[pXre.b, kft.b], wr=[tq[0].b])
                self.tt("dve", t[1], Xi, Kim, ALU.mult, rd=[pXim.b, kft.b], wr=[tq[1].b])
                self.tt("dve", Yre.t[:, 0:W].rearrange("p (c k) -> p c k", c=4), t[0], t[1], ALU.subtract,
                        rd=[tq[0].b, tq[1].b], wr=[Yre.b])
                self.tt("dve", t[2], Xr, Kim, ALU.mult, rd=[pXre.b, kft.b], wr=[tq[2].b])
                self.tt("dve", t[3], Xi, Kre, ALU.mult, rd=[pXim.b, kft.b], wr=[tq[3].b])
                self.tt("dve", Yim.t[:, 0:W].rearrange("p (c k) -> p c k", c=4), t[2], t[3], ALU.add,
                        rd=[tq[2].b, tq[3].b], wr=[Yim.b])
                Yr3 = Yre.t[:, 0:W].rearrange("p (c k) -> p c k", c=4)
                Yi3 = Yim.t[:, 0:W].rearrange("p (c k) -> p c k", c=4)
                for ch in range(4):
                    off = ch * 256
                    self.mm(pB.t[0:NH, off:off + 256], Yr3[:, ch, :], G3[:, 0, :], True, False,
                            rd=[Yre.b, G128.b], wr=[pB.b])
                    self.mm(pB.t[0:NH, off:off + 256], Yi3[:, ch, :], G3[:, 1, :], False, True,
                            rd=[Yim.b, G128.b], wr=[pB.b])
                Br4 = Bpre.t[0:NH, :].rearrange("p (c i) -> p c i", c=4)
                Bi4 = Bpim.t[0:NH, :].rearrange("p (c i) -> p c i", c=4)
                v = pB.t[0:NH, :].rearrange("p (c r i) -> p c r i", c=4, r=2)
                Bre, Bim = v[:, :, 0, :], v[:, :, 1, :]
                Tre = T23[0:NH, 0, :].unsqueeze(1).to_broadcast([NH, 4, 128])
                Tim = T23[0:NH, 1, :].unsqueeze(1).to_broadcast([NH, 4, 128])
                t = [q.t[0:NH, 0:512].rearrange("p (c i) -> p c i", c=4) for q in tq]
                self.tt("dve", t[0], Bre, Tre, ALU.mult, rd=[pB.b, T2.b], wr=[tq[0].b])
                self.tt("dve", t[1], Bim, Tim, ALU.mult, rd=[pB.b, T2.b], wr=[tq[1].b])
                self.tt("dve", Br4, t[0], t[1], ALU.subtract, rd=[tq[0].b, tq[1].b], wr=[Bpre.b])
                self.tt("dve", t[2], Bre, Tim, ALU.mult, rd=[pB.b, T2.b], wr=[tq[2].b])
                self.tt("dve", t[3], Bim, Tre, ALU.mult, rd=[pB.b, T2.b], wr=[tq[3].b])
                self.tt("dve", Bi4, t[2], t[3], ALU.add, rd=[tq[2].b, tq[3].b], wr=[Bpim.b])
                self.mm(py.t[0:NB, :], GN3[0:NH, 0, :], Bpre.t[0:NH, :], True, False, rd=[GN1.b, Bpre.b], wr=[py.b])
                self.mm(py.t[0:NB, :], GN3[0:NH, 1, :], Bpim.t[0:NH, :], False, True, rd=[GN1.b, Bpim.b], wr=[py.b])
                t0_ = tq[0].t[0:NB, :]
                db = drow.t[0:NB, o * 256 + c0:o * 256 + c0 + 4].unsqueeze(2).to_broadcast([NB, 4, 128])
                self.tt("dve", t0_.rearrange("p (c i) -> p c i", c=4), zf.t[0:NB, :].rearrange("p (c i) -> p c i", c=4), db,
                        ALU.mult, rd=[zf.b, drow.b], wr=[tq[0].b])
                self.tt("dve", t0_, t0_, py.t[0:NB, :], ALU.add, rd=[tq[0].b, py.b], wr=[tq[0].b])
                if o == 0:
                    ob = outf[i2]
                    self.tt("dve", ob.t[0:NB, :], t0_, gf.t[0:NB, :], ALU.mult, rd=[tq[0].b, gf.b], wr=[ob.b])
                    self.dma("sp", dap(self.z1[s], c0 * L, [(128, NB), (L, 4), (1, 128)]),
                             ob.t[0:NB, :].rearrange("p (c i) -> p c i", c=4), rd=[ob.b])
                else:
                    ob = outb[i2]
                    self.tt("dve", ob.t[0:NB, :], t0_, gf.t[0:NB, :], ALU.mult, rd=[tq[0].b, gf.b], wr=[ob.b])
                    self.dma("sp", dap(self.catT[s], (768 + c0) * L, [(128, NB), (L, 4), (1, 128)]),
                             ob.t[0:NB, :].rearrange("p (c i) -> p c i", c=4), rd=[ob.b])
            self.sync()
    self.phase_end()


K.sync = _sync
K.phase4 = _phase4


def build_program(nlayers=DEPTH, dump=False):
    nc = bass.Bass("TRN2", target_bir_lowering=False)
    with ExitStack() as ctx:
        s = Sched(nc, ctx)
        k = K(nc, s, ctx, nlayers=nlayers, dbg={"dump": dump})
        k.setup()
        for l in range(nlayers):
            xsrc = k.xin if l == 0 else k.xa
            dst = k.yout if l == nlayers - 1 else k.xa
            k.phase0(l)
            k.phase1(l, xsrc)
            k.phase2(l)
            k.phase3(l)
            k.phase4(l)
            k.phase5(l, xsrc)
            k.phase6(l, dst)
        s.barrier()
        s.emit()
    return nc


_NC_CACHE = {}


def kernel(**inputs):
    inp = {k: np.asarray(v) for k, v in inputs.items()}
    if "nc" not in _NC_CACHE:
        _NC_CACHE["nc"] = build_program()
    nc = _NC_CACHE["nc"]
    sh = host_prep(inp)
    in_maps = []
    for i in range(8):
        m = dict(sh)
        m.update(core_prep(inp, i))
        in_maps.append(m)
    res = run_bass_kernel_spmd(nc, in_maps, core_ids=list(range(8)))
    yp = np.stack([np.asarray(r["yp"], dtype=np.float32) for r in res.results], axis=0)
    ys = np.stack([np.asarray(r["ys"], dtype=np.float32) for r in res.results], axis=0)
    return (yp, ys)
```

```python
import math
import numpy as np
from contextlib import ExitStack
import concourse.bass as bass
import concourse.mybir as mybir
from concourse.bass_types import AP as APc
from concourse.bass_utils import run_bass_kernel_spmd

F32 = mybir.dt.float32
BF16 = mybir.dt.bfloat16
AF = mybir.ActivationFunctionType
ALU = mybir.AluOpType
AX = mybir.AxisListType

D = 1024
DEPTH = 4
LP = 4096
LS = 8192
DFF = 2816
ALPHA = (2 * DEPTH) ** 0.25
LN_EPS = 1e-5
QK_EPS = 1e-6
HY_MIN_DECAY = abs(math.log(1e-2)) / 1.5
HY_MAX_DECAY = abs(math.log(1e-2)) / 0.3
MAGIC = 12582912.0

ENGS = ("pe", "act", "dve", "pool", "sp")
EMBED_WAITS = True


class Buf:
    __slots__ = ("name", "w", "r")

    def __init__(self, name):
        self.name = name
        self.w = None
        self.r = []


class Sched:
    NDMA = 8

    def __init__(self, nc, ctx):
        self.nc = nc
        self.streams = {e: [] for e in ENGS}
        self.sems = {}
        self.cnt = {}
        for e in ENGS:
            self.sems[e] = ctx.enter_context(nc.semaphore("s_" + e))
            self.cnt[e] = 0
        self.dnext = {}
        for q in ("sp", "act", "pool"):
            for i in range(self.NDMA):
                k = "d_%s%d" % (q, i)
                self.sems[k] = ctx.enter_context(nc.semaphore(k))
                self.cnt[k] = 0
            self.dnext[q] = 0
        self.waited = {e: {} for e in ENGS}
        self.nbuf = 0

    def buf(self, name=None):
        self.nbuf += 1
        return Buf(name or ("b%d" % self.nbuf))

    def _need(self, eng, ev, same_ok):
        if ev is None:
            return
        k, v, src = ev
        if src == eng and same_ok:
            return
        if self.waited[eng].get(k, 0) >= v:
            return
        self.waited[eng][k] = v
        self.streams[eng].append(("w", k, v))

    def _deps(self, eng, rd, wr):
        pe = (eng == "pe")
        for b in rd:
            self._need(eng, b.w, pe)
        for b in wr:
            self._need(eng, b.w, pe)
            for ev in b.r:
                self._need(eng, ev, True)

    def _mark(self, ev, rd, wr):
        for b in rd:
            b.r.append(ev)
            if len(b.r) > 64:
                last = {}
                for e2 in b.r:
                    if e2[0] not in last or last[e2[0]][1] < e2[1]:
                        last[e2[0]] = e2
                b.r = list(last.values())
        for b in wr:
            b.w = ev
            b.r = []

    def op(self, eng, fn, rd=(), wr=()):
        self._deps(eng, rd, wr)
        self.cnt[eng] += 1
        ev = (eng, self.cnt[eng], eng)
        self.streams[eng].append(("o", fn, eng, 1))
        self._mark(ev, rd, wr)
        return ev

    def dma(self, q, out, in_, rd=(), wr=(), slow=False):
        self._deps(q, rd, wr)
        i = self.dnext[q]
        self.dnext[q] = (i + 1) % self.NDMA
        k = "d_%s%d" % (q, i)
        if self.cnt[k] > 0:
            self._need(q, (k, self.cnt[k], None), False)
        self.cnt[k] += 16
        ev = (k, self.cnt[k], None)
        if slow:
            self.streams[q].append(("o", (lambda e: e.dma_start(out=out, in_=in_, allow_slow_non_contiguous=True)), k, 16))
        else:
            self.streams[q].append(("o", (lambda e: e.dma_start(out=out, in_=in_)), k, 16))
        self._mark(ev, rd, wr)
        return ev

    def barrier(self):
        for e in ENGS:
            for k, v in self.cnt.items():
                if v > 0 and k != e:
                    self._need(e, (k, v, None), False)

    def emit(self):
        nc = self.nc
        if not any(self.streams[e] for e in ENGS):
            return
        engobj = {"pe": "tensor", "act": "scalar", "dve": "vector", "pool": "gpsimd", "sp": "sync"}
        with nc.Block() as block:
            for e in ENGS:
                items = self.streams[e]
                sems = self.sems

                def body(eng, items=items, sems=sems):
                    n = len(items)
                    i = 0
                    while i < n:
                        it = items[i]
                        if it[0] == "w":
                            if EMBED_WAITS and i + 1 < n and items[i + 1][0] == "o":
                                nx = items[i + 1]
                                ins = nx[1](eng)
                                ins._wait_ge(sems[it[1]], it[2])
                                ins.then_inc(sems[nx[2]], nx[3])
                                i += 2
                                continue
                            eng.wait_ge(sems[it[1]], it[2])
                        else:
                            it[1](eng).then_inc(sems[it[2]], it[3])
                        i += 1
                getattr(block, engobj[e])(body)
        self.streams = {e: [] for e in ENGS}


class TB:
    __slots__ = ("t", "b")

    def __init__(self, t, b):
        self.t = t
        self.b = b

    def __getitem__(self, k):
        return self.t[k]


def rev(ap2d):
    (ps, pn), (fs, fn) = ap2d.ap
    return APc(ap2d.tensor, ap2d.offset + (fn - 1) * fs, [[ps, pn], [-fs, fn]])


def dap(t, off, dims):
    return APc(t.tensor, t.offset + off, [[a, b] for a, b in dims])


class K:
    def __init__(self, nc, s, ctx, nlayers=DEPTH, dbg=None):
        self.nc, self.s, self.gctx = nc, s, ctx
        self.nlayers = nlayers
        self.dbg = dbg or {}
        self.pctx = None
        self.uid = 0

    def _nm(self, n):
        self.uid += 1
        return "%s_%d" % (n, self.uid)

    def sb(self, name, shape, dt, glob=False):
        c = self.gctx if glob else self.pctx
        t = c.enter_context(self.nc.sbuf_tensor(self._nm(name), list(shape), dt))
        return TB(t, self.s.buf(name))

    def ps(self, name, shape, dt=F32):
        t = self.pctx.enter_context(self.nc.psum_tensor(self._nm(name), list(shape), dt))
        return TB(t, self.s.buf(name))

    def dram(self, name, shape, dt, kind="Internal"):
        return self.nc.dram_tensor(name, list(shape), dt, kind=kind).ap()

    def mm(self, out, lhsT, rhs, start, stop, rd, wr, tp=None):
        if tp is None:
            f = lambda e: e.matmul(out=out, lhsT=lhsT, rhs=rhs, start=start, stop=stop)
        else:
            f = lambda e: e.matmul(out=out, lhsT=lhsT, rhs=rhs, start=start, stop=stop, tile_position=tp)
        return self.s.op("pe", f, rd=rd, wr=wr)

    def tr(self, out, in_, ident, rd, wr):
        return self.s.op("pe", lambda e: e.transpose(out=out, in_=in_, identity=ident), rd=rd, wr=wr)

    def act(self, out, in_, func, rd, wr, bias=None, scale=None, accum=None):
        kw = {}
        if bias is not None:
            kw["bias"] = bias
        if scale is not None:
            kw["scale"] = scale
        if accum is not None:
            kw["accum_out"] = accum
        return self.s.op("act", lambda e: e.activation(out=out, in_=in_, func=func, **kw), rd=rd, wr=wr)

    def tt(self, eng, out, in0, in1, op, rd, wr):
        return self.s.op(eng, lambda e: e.tensor_tensor(out=out, in0=in0, in1=in1, op=op), rd=rd, wr=wr)

    def ts(self, eng, out, in0, s1, s2, op0, op1, rd, wr):
        if op1 is None:
            f = lambda e: e.tensor_scalar(out=out, in0=in0, scalar1=s1, scalar2=None, op0=op0)
        else:
            f = lambda e: e.tensor_scalar(out=out, in0=in0, scalar1=s1, scalar2=s2, op0=op0, op1=op1)
        return self.s.op(eng, f, rd=rd, wr=wr)

    def stt(self, eng, out, in0, scalar, in1, op0, op1, rd, wr):
        return self.s.op(eng, lambda e: e.scalar_tensor_tensor(out=out, in0=in0, scalar=scalar, in1=in1, op0=op0, op1=op1), rd=rd, wr=wr)

    def cp(self, eng, out, in_, rd, wr):
        if eng == "act":
            return self.s.op("act", lambda e: e.copy(out=out, in_=in_), rd=rd, wr=wr)
        return self.s.op(eng, lambda e: e.tensor_copy(out=out, in_=in_), rd=rd, wr=wr)

    def memset(self, eng, ap, val, wr):
        return self.s.op(eng, lambda e: e.memset(ap, val), rd=(), wr=wr)

    def recip(self, out, in_, rd, wr):
        return self.s.op("dve", lambda e: e.reciprocal(out=out, in_=in_), rd=rd, wr=wr)

    def dma(self, q, out, in_, rd=(), wr=(), slow=False):
        return self.s.dma(q, out, in_, rd=rd, wr=wr, slow=slow)

    def phase_begin(self):
        self.pctx = ExitStack()

    def phase_end(self):
        self.s.barrier()
        self.pctx.close()
        self.pctx = None


def _init_arena(self):
    self.pctx = None


def _sb(self, name, cols, dt=F32, glob=False, parts=128, ctx=None):
    c = ctx if ctx is not None else (self.gctx if glob else self.pctx)
    t = c.enter_context(self.nc.sbuf_tensor(self._nm(name), [128, cols], dt))
    return TB(t[0:parts, :], self.s.buf(name))


def _ps(self, name, cols=512, dt=F32, parts=128):
    full = 512 if dt == F32 else 1024
    t = self.pctx.enter_context(self.nc.psum_tensor(self._nm(name), [128, full], dt))
    return TB(t[0:parts, 0:cols], self.s.buf(name))


def _ps2(self, name):
    t = self.pctx.enter_context(self.nc.psum_tensor(self._nm(name), [128, 1024], F32))
    return TB(t[:, :], self.s.buf(name))


def _phase_begin(self):
    self.pctx = ExitStack()


def _phase_end(self):
    self.s.barrier()
    self.s.emit()
    self.pctx.close()
    self.pctx = None


K.init_arena = _init_arena
K.sb = _sb
K.ps = _ps
K.ps2 = _ps2
K.phase_begin = _phase_begin
K.phase_end = _phase_end


def _consts():
    c = {}
    c["ident"] = np.eye(128, dtype=np.float32)
    t = np.arange(LS)
    row = (t // 64).astype(np.float32)
    col = (t % 64).astype(np.float32)
    inv = (10000.0 ** (-np.arange(16, dtype=np.float32) / 16)).astype(np.float32)
    ang = np.stack([row[:, None] * inv, col[:, None] * inv], axis=1).astype(np.float32)
    cs = np.cos(ang).reshape(LS // 128, 128, 32).transpose(1, 0, 2)
    sn = np.sin(ang).reshape(LS // 128, 128, 32).transpose(1, 0, 2)
    for s, L in enumerate((LP, LS)):
        f32 = np.float32
        t = np.linspace(0.0, 1.0, L, dtype=f32)[:, None]
        w = (f32(2.0 * math.pi) * np.arange(L, dtype=f32)[:, None] / f32(L)).astype(f32)
        f = np.linspace(1e-4, 15, 16, dtype=f32)[None, :]
        z = np.concatenate([t, np.cos(f * w), -np.sin(f * w)], axis=-1).astype(f32)
        c["hz%d" % s] = np.ascontiguousarray(z.T)
        c["htl%d" % s] = np.ascontiguousarray(t.T)
        NB = L // 128
        N1 = 2 * NB
        N = 2 * L
        NH = N1 // 2 + 1
        a = np.arange(N1, dtype=np.float64)[:, None]
        kl = np.arange(NH, dtype=np.float64)[None, :]
        th = 2 * np.pi * a * kl / N1
        c["hFN1_%d" % s] = np.concatenate([np.cos(th), -np.sin(th)], axis=1).astype(f32)
        i = np.arange(128, dtype=np.float64)[:, None]
        th = 2 * np.pi * i * kl / N
        c["hT_%d" % s] = np.stack([np.cos(th), -np.sin(th)], axis=1).astype(f32)
        c["hT2_%d" % s] = np.stack([np.cos(th.T), np.sin(th.T)], axis=1).astype(f32)
        aa = np.arange(NB, dtype=np.float64)[None, :]
        klc = np.arange(NH, dtype=np.float64)[:, None]
        th = 2 * np.pi * klc * aa / N1
        wgt = np.full((NH, 1), 2.0)
        wgt[0, 0] = 1.0
        wgt[NH - 1, 0] = 1.0
        c["hGN1_%d" % s] = np.stack([wgt * np.cos(th) / N, -wgt * np.sin(th) / N], axis=1).astype(f32)
    i = np.arange(128, dtype=np.float64)[:, None]
    kh = np.arange(128, dtype=np.float64)[None, :]
    th = 2 * np.pi * i * kh / 128
    c["hF128"] = np.stack([np.cos(th), -np.sin(th), np.sin(th)], axis=1).astype(np.float32)
    c["hG128"] = np.stack([np.concatenate([np.cos(th), np.sin(th)], axis=1),
                           np.concatenate([-np.sin(th), np.cos(th)], axis=1)], axis=1).astype(np.float32)
    dl = np.linspace(HY_MIN_DECAY, HY_MAX_DECAY, 256, dtype=np.float32)
    c["hnegd"] = np.ascontiguousarray((-dl).reshape(2, 128).T)
    c["rope_cos"] = np.ascontiguousarray(cs, dtype=np.float32)
    c["rope_sin"] = np.ascontiguousarray(sn, dtype=np.float32)
    return c


def host_prep(inp):
    f32 = np.float32
    A = lambda a: np.ascontiguousarray(a, dtype=f32)
    sh = {}
    sh["ada_w"] = A(inp["ada_w"])
    sh["ada_b"] = A(inp["ada_b"])
    sh["ada_bT"] = A(inp["ada_b"].reshape(DEPTH, 48, 128).transpose(0, 2, 1))
    sh["w_in"] = A(inp["w_in"])
    sh["w_out"] = A(inp["w_out"])
    sh["w_up"] = A(inp["ffn_w_up"])
    sh["w_down"] = A(inp["ffn_w_down"])
    sh["qkg"] = A(np.concatenate([np.tile(inp["q_gain"], (1, 8)), np.tile(inp["k_gain"], (1, 2))], axis=1))
    lw = np.concatenate([inp["lru_conv_w"], inp["lru_conv_b"][:, None, :]], axis=1)
    sh["lruw"] = A(lw.reshape(DEPTH, 5, 2, 128).transpose(0, 3, 2, 1))
    W = np.zeros((DEPTH, 2, 2, 2, 128, 128), f32)
    for gi, nm in enumerate(("lru_wa", "lru_wx")):
        w = np.asarray(inp[nm])
        for cc in range(2):
            for h2 in range(2):
                W[:, :, gi, cc, h2 * 64:(h2 + 1) * 64, h2 * 64:(h2 + 1) * 64] = w[:, :, 2 * cc + h2]
    sh["lruW"] = W
    lb = np.stack([inp["lru_ba"], inp["lru_bx"], inp["lru_lambda"]], axis=-1)
    sh["lrub"] = A(lb.reshape(DEPTH, 2, 2, 128, 3).transpose(0, 3, 2, 1, 4))
    fw = np.concatenate([inp["ffn_conv_w"], inp["ffn_conv_b"][:, None, :]], axis=1)
    sh["ffw"] = A(fw.reshape(DEPTH, 4, 44, 128).transpose(0, 3, 2, 1))
    sh["ln1_g"] = A(inp["ln1_g"]); sh["ln1_b"] = A(inp["ln1_b"])
    sh["ln2_g"] = A(inp["ln2_g"]); sh["ln2_b"] = A(inp["ln2_b"])
    hw = np.concatenate([inp["hy_conv_w"], inp["hy_conv_b"][:, None, :]], axis=1)
    sh["hyw"] = A(hw.reshape(DEPTH, 4, 6, 128).transpose(0, 3, 2, 1))
    sh["hy_w1"] = A(inp["hy_w1"]); sh["hy_w2"] = A(inp["hy_w2"]); sh["hy_w3"] = A(inp["hy_w3"])
    sh["hy_wout"] = A(inp["hy_wout"])
    sh["hyb"] = A(np.stack([inp["hy_b1"], inp["hy_b2"], inp["hy_b3"], inp["hy_freq"]], axis=-1))
    sh["hy_bias"] = A(inp["hy_bias"])
    sh.update(_consts())
    return sh


SHARED_SHAPES = {
    "ada_w": [DEPTH, 1024, 6144], "ada_b": [DEPTH, 6144], "ada_bT": [DEPTH, 128, 48],
    "w_in": [DEPTH, 1024, 2048], "w_out": [DEPTH, 1024, 1024], "w_up": [DEPTH, 1024, 2 * DFF],
    "w_down": [DEPTH, DFF, 1024], "qkg": [DEPTH, 640],
    "lruw": [DEPTH, 128, 2, 5], "lruW": [DEPTH, 2, 2, 2, 128, 128], "lrub": [DEPTH, 128, 2, 2, 3],
    "ffw": [DEPTH, 128, 44, 4], "ln1_g": [DEPTH, 1024], "ln1_b": [DEPTH, 1024], "ln2_g": [DEPTH, 1024], "ln2_b": [DEPTH, 1024],
    "hyw": [DEPTH, 128, 6, 4], "hy_w1": [DEPTH, 33, 64], "hy_w2": [DEPTH, 64, 64], "hy_w3": [DEPTH, 64, 64],
    "hy_wout": [DEPTH, 64, 1024], "hyb": [DEPTH, 64, 4], "hy_bias": [DEPTH, 2, 256],
    "hz0": [33, LP], "hz1": [33, LS], "htl0": [1, LP], "htl1": [1, LS],
    "hFN1_0": [64, 66], "hFN1_1": [128, 130], "hT_0": [128, 2, 33], "hT_1": [128, 2, 65],
    "hT2_0": [33, 2, 128], "hT2_1": [65, 2, 128], "hGN1_0": [33, 2, 32], "hGN1_1": [65, 2, 64],
    "hF128": [128, 3, 128], "hG128": [128, 2, 256], "hnegd": [128, 2],
    "ident": [128, 128], "rope_cos": [128, 64, 32], "rope_sin": [128, 64, 32],
}


def core_prep(inp, i):
    c = np.stack([inp["c_prompt"][i], inp["c_sample"][i]], axis=0)
    cT = np.ascontiguousarray(c.reshape(2, 8, 128).transpose(2, 1, 0), dtype=np.float32)
    return {"xp": np.ascontiguousarray(inp["x_prompt"][i], dtype=np.float32),
            "xs": np.ascontiguousarray(inp["x_sample"][i], dtype=np.float32),
            "cT": cT}


def _setup(self):
    nc = self.nc
    dk = "ExternalOutput" if self.dbg.get("dump") else "Internal"
    self.din = {}
    for k, shp in SHARED_SHAPES.items():
        self.din[k] = self.dram(k, shp, F32, kind="ExternalInput")
    self.xin = [self.dram("xp", [LP, D], F32, kind="ExternalInput"),
                self.dram("xs", [LS, D], F32, kind="ExternalInput")]
    self.cT = self.dram("cT", [128, 8, 2], F32, kind="ExternalInput")
    self.yout = [self.dram("yp", [LP, D], F32, kind="ExternalOutput"),
                 self.dram("ys", [LS, D], F32, kind="ExternalOutput")]
    self.Ls = [LP, LS]
    self.xa = [self.dram("xa%d" % s, [L, D], F32, kind=dk) for s, L in enumerate(self.Ls)]
    self.xres = [self.dram("xres%d" % s, [L, D], F32, kind=dk) for s, L in enumerate(self.Ls)]
    self.qT = [self.dram("qT%d" % s, [512, L], BF16, kind=dk) for s, L in enumerate(self.Ls)]
    self.kT = [self.dram("kT%d" % s, [128, L], BF16, kind=dk) for s, L in enumerate(self.Ls)]
    self.vaug = [self.dram("vaug%d" % s, [2, 128, L // 128, 192], BF16, kind=dk) for s, L in enumerate(self.Ls)]
    self.lh = [self.dram("lh%d" % s, [1280, L], F32, kind=dk) for s, L in enumerate(self.Ls)]
    self.catT = [self.dram("catT%d" % s, [1024, L], BF16, kind=dk) for s, L in enumerate(self.Ls)]
    self.hc = [self.dram("hc%d" % s, [768, L], F32, kind=dk) for s, L in enumerate(self.Ls)]
    self.kfull = [self.dram("kfull%d" % s, [2, 256, 2 * L], F32, kind=dk) for s, L in enumerate(self.Ls)]
    self.kfs = [self.dram("kfs%d" % s, [2, 128, 256, 2, L // 128 + 1], BF16, kind=dk) for s, L in enumerate(self.Ls)]
    self.z1 = [self.dram("z1_%d" % s, [256, L], F32, kind=dk) for s, L in enumerate(self.Ls)]
    self.rnd = self.dram("rnd", [2, 256], F32, kind=dk)
    self.init_arena()
    self.ident = self.sb("ident", 128, BF16, glob=True)
    self.csT = self.sb("csT", 16, F32, glob=True)
    self.modT = self.sb("modT", 96, F32, glob=True)
    self.grow = [self.sb("grow%d" % i, 1024, F32, glob=True) for i in range(4)]
    self.epsq = self.sb("epsq", 1, F32, glob=True)
    self.epsl = self.sb("epsl", 1, F32, glob=True)
    self.phase_begin()
    self.dma("pool", self.ident.t, self.din["ident"], wr=[self.ident.b])
    ct = self.sb("ct", 16, F32)
    self.dma("sp", ct.t, self.cT.rearrange("p k s -> p (k s)"), wr=[ct.b])
    self.act(self.csT.t, ct.t, AF.Silu, rd=[ct.b], wr=[self.csT.b])
    self.memset("dve", self.epsq.t, QK_EPS, wr=[self.epsq.b])
    self.memset("dve", self.epsl.t, LN_EPS, wr=[self.epsl.b])
    self.phase_end()


def _phase0(self, l):
    self.wctx = ExitStack()
    wi = self.sb("wi", 8 * 2048, BF16, ctx=self.wctx)
    wi3 = wi.t.rearrange("p (k n) -> p k n", k=8)
    wib = [self.s.buf("wib%d" % k) for k in range(8)]
    self.phase_begin()
    for kc in range(8):
        self.dma("pool", wi3[:, kc, :], self.din["w_in"][l, kc * 128:(kc + 1) * 128, :], wr=[wib[kc]])
    self.pre_wi = (wi, wib)
    adab = self.sb("adab", 48, F32)
    self.dma("sp", adab.t, self.din["ada_bT"][l], wr=[adab.b])
    self.csrep = self.sb("csrep", 16 * 128, F32)
    self.cp("dve", self.csrep.t.rearrange("p (k n) -> p k n", n=128),
            self.csT.t.unsqueeze(2).to_broadcast([128, 16, 128]), rd=[self.csT.b], wr=[self.csrep.b])
    was = [self.sb("wa%d" % i, 8 * 512, F32) for i in range(2)]
    brow = [self.sb("brow%d" % i, 512, F32) for i in range(2)]
    pm = self.ps("pm", 96)
    pgs = [self.ps("pg%d" % i) for i in range(2)]
    npg = 0
    aw = self.din["ada_w"]
    for gi in range(12):
        wa = was[gi % 2]
        src = dap(aw, l * 1024 * 6144 + gi * 512, [(6144, 128), (128 * 6144, 8), (1, 512)])
        self.dma("sp", wa.t.rearrange("p (k c) -> p k c", k=8), src, wr=[wa.b])
        wa3 = wa.t.rearrange("p (k c) -> p k c", k=8)
        for m in range(4):
            ch = gi * 4 + m
            for kc in range(8):
                self.mm(pm.t[:, ch * 2:ch * 2 + 2], wa3[:, kc, m * 128:(m + 1) * 128],
                        self.csT.t[:, kc * 2:kc * 2 + 2], kc == 0, kc == 7, rd=[wa.b, self.csT.b], wr=[pm.b])
        if gi in (4, 5, 10, 11):
            g = 0 if gi < 6 else 1
            half = gi % 2 if gi < 6 else (gi - 10)
            br = brow[half]
            self.dma("sp", br.t, dap(self.din["ada_b"], l * 6144 + gi * 512, [(0, 128), (1, 512)]), wr=[br.b])
            cr4 = self.csrep.t.rearrange("p (k s n) -> p k s n", k=8, s=2)
            for sq in range(2):
                pg = pgs[npg % 2]
                npg += 1
                for kc in range(8):
                    self.mm(pg.t, cr4[:, kc, sq, :], wa3[:, kc, :], kc == 0, kc == 7,
                            rd=[self.csrep.b, wa.b], wr=[pg.b])
                gr = self.grow[g * 2 + sq]
                self.tt("dve", gr.t[:, half * 512:(half + 1) * 512], pg.t, br.t, ALU.add,
                        rd=[pg.b, br.b], wr=[gr.b])
    m3 = self.modT.t.rearrange("p (c s) -> p c s", s=2)
    self.tt("dve", m3, pm.t.rearrange("p (c s) -> p c s", s=2),
            adab.t.unsqueeze(2).to_broadcast([128, 48, 2]), ALU.add, rd=[pm.b, adab.b], wr=[self.modT.b])
    for c0 in (8, 32):
        self.ts("dve", m3[:, c0:c0 + 8, :], m3[:, c0:c0 + 8, :], 1.0, None, ALU.add, None,
                rd=[self.modT.b], wr=[self.modT.b])
    self.phase_end()


def _phase1(self, l, xsrc):
    self.phase_begin()
    wi, wib = self.pre_wi
    wi3 = wi.t.rearrange("p (k n) -> p k n", k=8)
    gain = self.sb("gain", 640, F32)
    self.dma("sp", gain.t, dap(self.din["qkg"], l * 640, [(0, 128), (1, 640)]), wr=[gain.b])
    self.rcos = self.sb("rcos", 64 * 32, F32)
    self.rsin = self.sb("rsin", 64 * 32, F32)
    self.dma("sp", self.rcos.t, self.din["rope_cos"].rearrange("p t c -> p (t c)"), wr=[self.rcos.b])
    self.dma("sp", self.rsin.t, self.din["rope_sin"].rearrange("p t c -> p (t c)"), wr=[self.rsin.b])
    xbs = [self.sb("xb%d" % i, 4 * 1024, BF16) for i in range(2)]
    uTs = [self.sb("uT%d" % i, 8 * 512, BF16) for i in range(2)]
    lhs_ = [self.sb("lhs%d" % i, 10 * 512, F32) for i in range(2)]
    lhb = [[self.s.buf() for m in range(10)] for i in range(2)]
    sq_ = [self.sb("sq%d" % i, 640, F32) for i in range(2)]
    ss_ = [self.sb("ss%d" % i, 10, F32) for i in range(2)]
    rstd_ = [self.sb("rstd%d" % i, 10, F32) for i in range(2)]
    qn_ = [self.sb("qn%d" % i, 640, F32) for i in range(2)]
    tmp_ = [[self.sb("tmp%d%d" % (b, i), 320, F32) for i in range(4)] for b in range(2)]
    qkb_ = [self.sb("qkb%d" % i, 640, BF16) for i in range(2)]
    jn = 0
    qkTs = [self.sb("qkT%d" % i, 5 * 512, BF16) for i in range(2)]
    vgs = [self.sb("vg%d" % i, 2 * 4 * 192, BF16) for i in range(2)]
    for vg in vgs:
        self.memset("dve", vg.t, 1.0, wr=[vg.b])
    pts = [self.ps("pt%d" % i, 512, BF16) for i in range(2)]
    pfs = [self.ps("pf%d" % i) for i in range(2)]
    psq_ = [self.ps("psq%d" % i) for i in range(2)]
    pskv = self.ps("pskv", 256)
    pT = self.ps("pT", 640, BF16)
    m3 = self.modT.t.rearrange("p (c s) -> p c s", s=2)
    rc3 = self.rcos.t.rearrange("p (t c) -> p t c", c=32)
    rs3 = self.rsin.t.rearrange("p (t c) -> p t c", c=32)
    it = 0
    for s in range(2):
        L = self.Ls[s]
        NT = L // 128
        for w in range(L // 512):
            xb, uT, lh, qkT, vg = xbs[it % 2], uTs[it % 2], lhs_[it % 2], qkTs[it % 2], vgs[it % 2]
            lb = lhb[it % 2]
            it += 1
            xb3 = xb.t.rearrange("p (j c) -> p j c", j=4)
            uT3 = uT.t.rearrange("p (k t) -> p k t", k=8)
            lh3 = lh.t.rearrange("p (m t) -> p m t", m=10)
            qkT3 = qkT.t.rearrange("p (c t) -> p c t", c=5)
            vg4 = vg.t.rearrange("p (g j c) -> p g j c", g=2, j=4)
            self.dma("pool", xb3, dap(xsrc[s], w * 512 * D, [(D, 128), (128 * D, 4), (1, D)]), wr=[xb.b])
            for kc in range(8):
                pt = pts[kc % 2]
                for j in range(4):
                    self.tr(pt.t[:, j * 128:(j + 1) * 128], xb3[:, j, kc * 128:(kc + 1) * 128], self.ident.t,
                            rd=[xb.b, self.ident.b], wr=[pt.b])
                self.act(uT3[:, kc, :], pt.t, AF.Identity, rd=[pt.b, self.modT.b], wr=[uT.b],
                         scale=m3[:, 8 + kc, s:s + 1], bias=m3[:, kc, s:s + 1])
            for m in range(10):
                pf = pfs[m % 2]
                for kc in range(8):
                    self.mm(pf.t, wi3[:, kc, 768 + m * 128:768 + (m + 1) * 128], uT3[:, kc, :], kc == 0, kc == 7,
                            rd=[wib[kc], uT.b], wr=[pf.b])
                self.cp("act" if m % 2 else "dve", lh3[:, m, :], pf.t, rd=[pf.b], wr=[lb[m]])
            self.dma("sp", dap(self.lh[s], w * 512, [(L, 128), (128 * L, 10), (1, 512)]), lh3, rd=lb)
            for j in range(4):
                T = w * 4 + j
                jb = jn % 2
                jn += 1
                sq, ss, rstd, qn, tmp, qkb, psq = sq_[jb], ss_[jb], rstd_[jb], qn_[jb], tmp_[jb], qkb_[jb], psq_[jb]
                for kc in range(8):
                    self.mm(psq.t, uT3[:, kc, j * 128:(j + 1) * 128], wi3[:, kc, 0:512], kc == 0, kc == 7,
                            rd=[wib[kc], uT.b], wr=[psq.b])
                for kc in range(8):
                    self.mm(pskv.t, uT3[:, kc, j * 128:(j + 1) * 128], wi3[:, kc, 512:768], kc == 0, kc == 7,
                            rd=[wib[kc], uT.b], wr=[pskv.b])
                self.act(sq.t[:, 0:512], psq.t, AF.Square, rd=[psq.b], wr=[sq.b])
                self.act(sq.t[:, 512:640], pskv.t[:, 0:128], AF.Square, rd=[pskv.b], wr=[sq.b])
                self.s.op("dve", (lambda e, o=ss.t, i=sq.t.rearrange("p (h d) -> p h d", d=64):
                                  e.tensor_reduce(out=o, in_=i, axis=AX.X, op=ALU.add)), rd=[sq.b], wr=[ss.b])
                self.act(rstd.t, ss.t, AF.Sqrt, rd=[ss.b, self.epsq.b], wr=[rstd.b], scale=1.0 / 64, bias=self.epsq.t)
                self.recip(rstd.t, rstd.t, rd=[rstd.b], wr=[rstd.b])
                qn3 = qn.t.rearrange("p (h d) -> p h d", d=64)
                self.tt("dve", qn3[:, 0:8, :], psq.t.rearrange("p (h d) -> p h d", d=64),
                        rstd.t[:, 0:8].unsqueeze(2).to_broadcast([128, 8, 64]), ALU.mult,
                        rd=[psq.b, rstd.b], wr=[qn.b])
                self.tt("dve", qn3[:, 8:10, :], pskv.t[:, 0:128].rearrange("p (h d) -> p h d", d=64),
                        rstd.t[:, 8:10].unsqueeze(2).to_broadcast([128, 2, 64]), ALU.mult,
                        rd=[pskv.b, rstd.b], wr=[qn.b])
                self.tt("dve", qn.t, qn.t, gain.t, ALU.mult, rd=[qn.b, gain.b], wr=[qn.b])
                for c0 in (0, 128):
                    self.cp("act", vg4[:, :, j, c0:c0 + 64], pskv.t[:, 128:256].rearrange("p (g d) -> p g d", g=2),
                            rd=[pskv.b], wr=[vg.b])
                qn5 = qn.t.rearrange("p (h a two f) -> p h a two f", h=10, a=2, two=2)
                qb5 = qkb.t.rearrange("p (h a two f) -> p h a two f", h=10, a=2, two=2)
                x1, x2 = qn5[:, :, :, 0, :], qn5[:, :, :, 1, :]
                cb = rc3[:, T, :].rearrange("p (a f) -> p a f", a=2).unsqueeze(1).to_broadcast([128, 10, 2, 16])
                sb_ = rs3[:, T, :].rearrange("p (a f) -> p a f", a=2).unsqueeze(1).to_broadcast([128, 10, 2, 16])
                t4 = [t.t.rearrange("p (h a f) -> p h a f", h=10, a=2) for t in tmp]
                self.tt("dve", t4[0], x1, cb, ALU.mult, rd=[qn.b, self.rcos.b], wr=[tmp[0].b])
                self.tt("dve", t4[1], x2, sb_, ALU.mult, rd=[qn.b, self.rsin.b], wr=[tmp[1].b])
                self.tt("dve", qb5[:, :, :, 0, :], t4[0], t4[1], ALU.subtract, rd=[tmp[0].b, tmp[1].b], wr=[qkb.b])
                self.tt("dve", t4[2], x2, cb, ALU.mult, rd=[qn.b, self.rcos.b], wr=[tmp[2].b])
                self.tt("dve", t4[3], x1, sb_, ALU.mult, rd=[qn.b, self.rsin.b], wr=[tmp[3].b])
                self.tt("dve", qb5[:, :, :, 1, :], t4[2], t4[3], ALU.add, rd=[tmp[2].b, tmp[3].b], wr=[qkb.b])
                for c in range(5):
                    self.tr(pT.t[:, c * 128:(c + 1) * 128], qkb.t[:, c * 128:(c + 1) * 128], self.ident.t,
                            rd=[qkb.b, self.ident.b], wr=[pT.b])
                self.cp("act", qkT3[:, :, j * 128:(j + 1) * 128], pT.t.rearrange("p (c t) -> p c t", c=5),
                        rd=[pT.b], wr=[qkT.b])
            self.dma("sp", dap(self.qT[s], w * 512, [(L, 128), (128 * L, 4), (1, 512)]), qkT3[:, 0:4, :], rd=[qkT.b])
            self.dma("sp", self.kT[s][:, w * 512:(w + 1) * 512], qkT3[:, 4, :], rd=[qkT.b])
            self.dma("sp", dap(self.vaug[s], w * 4 * 192, [(NT * 192, 128), (128 * NT * 192, 2), (1, 768)]),
                     vg.t.rearrange("p (g c) -> p g c", g=2), rd=[vg.b])
    self.phase_end()
    self.wctx.close()


K.setup = _setup
K.phase0 = _phase0
K.phase1 = _phase1


def _phase2(self, l):
    self.phase_begin()
    Kd = self.sb("Kd", LS, BF16)
    Va = self.sb("Va", 64 * 192, BF16)
    Q2 = self.sb("Q2", 2 * LS, BF16)
    PAB = [self.sb("PAB%d" % i, 1024, BF16) for i in range(3)]
    rr = self.sb("rr", 512, F32)
    rAb, rBb = self.s.buf("rA"), self.s.buf("rB")
    atts = [self.sb("att%d" % i, 512, BF16) for i in range(2)]
    psAB = [self.ps2("psAB%d" % i) for i in range(2)]
    oA = [self.ps("oA%d" % i) for i in range(2)]
    oB = [self.ps("oB%d" % i) for i in range(2)]
    nblk = 0
    for s in range(2):
        L = self.Ls[s]
        NT = L // 128
        for g in range(2):
            self.dma("sp", Kd.t[0:64, 0:L], self.kT[s][g * 64:(g + 1) * 64, :], wr=[Kd.b])
            self.dma("sp", Kd.t[64:128, 0:L], self.kT[s][g * 64:(g + 1) * 64, :], wr=[Kd.b])
            self.dma("sp", Va.t[:, 0:NT * 192], self.vaug[s][g].rearrange("p t c -> p (t c)"), wr=[Va.b])
            Q3 = Q2.t.rearrange("p (h t) -> p h t", h=2)
            self.dma("sp", Q3[:, :, 0:L], dap(self.qT[s], 2 * g * 128 * L, [(L, 128), (128 * L, 2), (1, L)]), wr=[Q2.b])
            Va3 = Va.t.rearrange("p (t c) -> p t c", c=192)
            steps = [(qc, hp, st) for qc in range(L // 512) for hp in range(2) for st in range(NT)]

            def qk(i):
                qc, hp, st = steps[i]
                ib = i % 2
                self.mm(psAB[ib].t[:, 0:512], Kd.t[0:64, st * 128:(st + 1) * 128], Q3[0:64, hp, qc * 512:(qc + 1) * 512],
                        True, True, rd=[Kd.b, Q2.b], wr=[psAB[ib].b], tp=(0, 0))
                self.mm(psAB[ib].t[:, 512:1024], Kd.t[64:128, st * 128:(st + 1) * 128], Q3[64:128, hp, qc * 512:(qc + 1) * 512],
                        True, True, rd=[Kd.b, Q2.b], wr=[psAB[ib].b], tp=(64, 0))
            qk(0)
            for i, (qc, hp, st) in enumerate(steps):
                if i + 1 < len(steps):
                    qk(i + 1)
                ib, ip = i % 2, i % 3
                if st == 0:
                    nblk += 1
                io = nblk % 2
                self.act(PAB[ip].t, psAB[ib].t, AF.Exp, rd=[psAB[ib].b], wr=[PAB[ip].b], scale=0.125)
                self.mm(oA[io].t, Va3[:, st, 0:128], PAB[ip].t[:, 0:512], st == 0, st == NT - 1, rd=[Va.b, PAB[ip].b], wr=[oA[io].b])
                self.mm(oB[io].t, Va3[:, st, 64:192], PAB[ip].t[:, 512:1024], st == 0, st == NT - 1, rd=[Va.b, PAB[ip].b], wr=[oB[io].b])
                if st == NT - 1:
                    att = atts[io]
                    self.recip(rr.t[64:128, :], oA[io].t[64:128, :], rd=[oA[io].b], wr=[rAb])
                    self.tt("dve", att.t[0:64, :], oA[io].t[0:64, :], rr.t[64:128, :], ALU.mult,
                            rd=[oA[io].b, rAb], wr=[att.b])
                    self.recip(rr.t[0:64, :], oB[io].t[0:64, :], rd=[oB[io].b], wr=[rBb])
                    self.tt("dve", att.t[64:128, :], oB[io].t[64:128, :], rr.t[0:64, :], ALU.mult,
                            rd=[oB[io].b, rBb], wr=[att.b])
                    self.dma("sp", self.catT[s][(2 * g + hp) * 128:(2 * g + hp + 1) * 128, qc * 512:(qc + 1) * 512],
                             att.t, rd=[att.b])
    self.phase_end()


K.phase2 = _phase2


def _phase3(self, l):
    self.phase_begin()
    TC = 1024
    XC = self.sb("XC", LS, F32)
    XCB = self.sb("XCB", LS, BF16)
    HF = self.sb("HF", LS, F32)
    WA = self.sb("WA", 8 * 128, BF16)
    WA5 = WA.t.rearrange("p (d g c n) -> p d g c n", d=2, g=2, c=2)
    self.dma("pool", WA5, self.din["lruW"][l].rearrange("d g c k n -> k d g c n"), wr=[WA.b])
    cw = self.sb("cw", 10, F32)
    self.dma("sp", cw.t, self.din["lruw"][l].rearrange("p c k -> p (c k)"), wr=[cw.b])
    lb = self.sb("lb", 12, F32)
    self.dma("sp", lb.t, self.din["lrub"][l].rearrange("p c d k -> p (c d k)"), wr=[lb.b])
    lb4 = lb.t.rearrange("p (c d k) -> p c d k", c=2, d=2)
    cw3 = cw.t.rearrange("p (c k) -> p c k", c=2)
    sp_ = self.sb("sp", 4, F32)
    c12 = self.sb("c12", 8, F32)
    sp3 = sp_.t.rearrange("p (c d) -> p c d", c=2)
    self.act(sp3, lb4[:, :, :, 2], AF.Exp, rd=[lb.b], wr=[sp_.b], scale=-1.0)
    self.act(sp_.t, sp_.t, AF.Ln, rd=[sp_.b], wr=[sp_.b], bias=1.0)
    self.ts("dve", c12.t[:, 0:4], sp_.t, -8.0, None, ALU.mult, None, rd=[sp_.b], wr=[c12.b])
    self.ts("dve", c12.t[:, 4:8], sp_.t, -16.0, None, ALU.mult, None, rd=[sp_.b], wr=[c12.b])
    c4 = c12.t.rearrange("p (k c d) -> p k c d", k=2, c=2)
    xhs = [self.sb("xh%d" % i, TC + 3, F32) for i in range(2)]
    tR = [self.sb("tR%d" % i, TC, F32) for i in range(2)]
    tI = [self.sb("tI%d" % i, TC, F32) for i in range(2)]
    tA = [self.sb("tA%d" % i, TC, F32) for i in range(2)]
    tT = [self.sb("tT%d" % i, TC, F32) for i in range(2)]
    tB = [self.sb("tB%d" % i, TC, F32) for i in range(2)]
    tH = [self.sb("tH%d" % i, TC, F32) for i in range(2)]
    gs = [self.sb("g%d" % i, TC, F32) for i in range(2)]
    ggs = [self.sb("gg%d" % i, TC, F32) for i in range(2)]
    obs = [self.sb("ob%d" % i, TC, BF16) for i in range(2)]
    carry = self.sb("carry", 1, F32)
    prs = [self.ps("pr%d" % i) for i in range(4)]
    pis = [self.ps("pi%d" % i) for i in range(4)]
    it = 0
    nps = 0
    for s in range(2):
        L = self.Ls[s]
        NCH = L // TC
        for cc in range(2):
            for c in range(NCH):
                xh = xhs[it % 2]
                it += 1
                t0 = c * TC
                lo = max(t0 - 2, 0)
                hi = min(t0 + TC + 1, L)
                if c == 0:
                    self.memset("dve", xh.t[:, 0:2], 0.0, wr=[xh.b])
                if c == NCH - 1:
                    self.memset("dve", xh.t[:, TC + 2:TC + 3], 0.0, wr=[xh.b])
                self.dma("sp", xh.t[:, lo - (t0 - 2):hi - (t0 - 2)], self.lh[s][cc * 128:(cc + 1) * 128, lo:hi], wr=[xh.b])
                xc = XC.t[:, t0:t0 + TC]
                self.ts("dve", xc, xh.t[:, 0:TC], cw3[:, cc, 0:1], cw3[:, cc, 4:5], ALU.mult, ALU.add,
                        rd=[xh.b, cw.b], wr=[XC.b])
                for k in range(1, 4):
                    self.stt("dve", xc, xh.t[:, k:k + TC], cw3[:, cc, k:k + 1], xc, ALU.mult, ALU.add,
                             rd=[xh.b, cw.b, XC.b], wr=[XC.b])
                self.cp("act", XCB.t[:, t0:t0 + TC], xc, rd=[XC.b], wr=[XCB.b])
            for d in range(2):
                order = list(range(NCH)) if d == 0 else list(range(NCH - 1, -1, -1))
                for ci, c in enumerate(order):
                    i2 = it % 2
                    it += 1
                    t0 = c * TC
                    for sub in range(2):
                        pr, pi = prs[nps % 4], pis[nps % 4]
                        nps += 1
                        cols = slice(t0 + sub * 512, t0 + (sub + 1) * 512)
                        self.mm(pr.t, WA5[:, d, 0, cc, :], XCB.t[:, cols], True, True, rd=[WA.b, XCB.b], wr=[pr.b])
                        self.mm(pi.t, WA5[:, d, 1, cc, :], XCB.t[:, cols], True, True, rd=[WA.b, XCB.b], wr=[pi.b])
                        self.act(tR[i2].t[:, sub * 512:(sub + 1) * 512], pr.t, AF.Sigmoid, rd=[pr.b, lb.b], wr=[tR[i2].b],
                                 bias=lb4[:, cc, d, 0:1])
                        self.act(tI[i2].t[:, sub * 512:(sub + 1) * 512], pi.t, AF.Sigmoid, rd=[pi.b, lb.b], wr=[tI[i2].b],
                                 bias=lb4[:, cc, d, 1:2])
                    self.act(tA[i2].t, tR[i2].t, AF.Exp, rd=[tR[i2].b, c12.b], wr=[tA[i2].b], scale=c4[:, 0, cc, d:d + 1])
                    self.act(tT[i2].t, tR[i2].t, AF.Exp, rd=[tR[i2].b, c12.b], wr=[tT[i2].b], scale=c4[:, 1, cc, d:d + 1])
                    self.act(tT[i2].t, tT[i2].t, AF.Sqrt, rd=[tT[i2].b], wr=[tT[i2].b], scale=-1.0, bias=1.0)
                    self.tt("dve", tB[i2].t, tI[i2].t, XC.t[:, t0:t0 + TC], ALU.mult, rd=[tI[i2].b, XC.b], wr=[tB[i2].b])
                    self.tt("dve", tB[i2].t, tB[i2].t, tT[i2].t, ALU.mult, rd=[tB[i2].b, tT[i2].b], wr=[tB[i2].b])
                    if d == 0:
                        init = 0.0 if ci == 0 else HF.t[:, t0 - 1:t0]
                        self.s.op("dve", (lambda e, o=HF.t[:, t0:t0 + TC], a=tA[i2].t, b=tB[i2].t, i0=init:
                                          e.tensor_tensor_scan(out=o, data0=a, data1=b, initial=i0, op0=ALU.mult, op1=ALU.add)),
                                  rd=[tA[i2].b, tB[i2].b, HF.b], wr=[HF.b])
                    else:
                        init = 0.0 if ci == 0 else carry.t
                        self.s.op("dve", (lambda e, o=rev(tH[i2].t), a=rev(tA[i2].t), b=rev(tB[i2].t), i0=init:
                                          e.tensor_tensor_scan(out=o, data0=a, data1=b, initial=i0, op0=ALU.mult, op1=ALU.add)),
                                  rd=[tA[i2].b, tB[i2].b, carry.b], wr=[tH[i2].b])
                        self.cp("dve", carry.t, tH[i2].t[:, 0:1], rd=[tH[i2].b], wr=[carry.b])
                        self.tt("dve", HF.t[:, t0:t0 + TC], HF.t[:, t0:t0 + TC], tH[i2].t, ALU.add,
                                rd=[HF.b, tH[i2].b], wr=[HF.b])
            for c in range(NCH):
                i2 = it % 2
                it += 1
                t0 = c * TC
                self.dma("sp", gs[i2].t, self.lh[s][256 + cc * 128:256 + (cc + 1) * 128, t0:t0 + TC], wr=[gs[i2].b])
                self.act(ggs[i2].t, gs[i2].t, AF.Gelu, rd=[gs[i2].b], wr=[ggs[i2].b])
                self.tt("dve", obs[i2].t, HF.t[:, t0:t0 + TC], ggs[i2].t, ALU.mult, rd=[HF.b, ggs[i2].b], wr=[obs[i2].b])
                self.dma("sp", self.catT[s][512 + cc * 128:512 + (cc + 1) * 128, t0:t0 + TC], obs[i2].t, rd=[obs[i2].b])
    self.phase_end()


K.phase3 = _phase3


def _ln_tail(self, po, n, xsrc_rows, gate, lg, lbias, dst_rows, T):
    i2 = T["i"] % 2
    T["i"] += 1
    xr, ysb, st, mv, rs, nm = T["xr"][i2], T["y"][i2], T["st"][i2], T["mv"][i2], T["rs"][i2], T["nm"][i2]
    self.dma("sp", xr.t[0:n, :], xsrc_rows, wr=[xr.b])
    for h in range(2):
        self.tt("dve", ysb.t[0:n, h * 512:(h + 1) * 512], po[h].t[0:n, :], gate.t[0:n, h * 512:(h + 1) * 512], ALU.mult,
                rd=[po[h].b, gate.b], wr=[ysb.b])
    self.stt("dve", ysb.t[0:n, :], xr.t[0:n, :], ALPHA, ysb.t[0:n, :], ALU.mult, ALU.add, rd=[xr.b, ysb.b], wr=[ysb.b])
    st3 = st.t.rearrange("p (c k) -> p c k", c=2)
    for h in range(2):
        self.s.op("dve", (lambda e, o=st3[0:n, h, :], i=ysb.t[0:n, h * 512:(h + 1) * 512]: e.bn_stats(out=o, in_=i)),
                  rd=[ysb.b], wr=[st.b])
    self.s.op("dve", (lambda e, o=mv.t[0:n, :], i=st3[0:n, :, :]: e.bn_aggr(out=o, in_=i)), rd=[st.b], wr=[mv.b])
    self.act(rs.t[0:n, :], mv.t[0:n, 1:2], AF.Sqrt, rd=[mv.b, self.epsl.b], wr=[rs.b], bias=self.epsl.t[0:n, :])
    self.recip(rs.t[0:n, :], rs.t[0:n, :], rd=[rs.b], wr=[rs.b])
    self.ts("dve", nm.t[0:n, :], mv.t[0:n, 0:1], -1.0, rs.t[0:n, :], ALU.mult, ALU.mult, rd=[mv.b, rs.b], wr=[nm.b])
    self.act(ysb.t[0:n, :], ysb.t[0:n, :], AF.Identity, rd=[ysb.b, rs.b, nm.b], wr=[ysb.b],
             scale=rs.t[0:n, :], bias=nm.t[0:n, :])
    self.tt("pool", ysb.t[0:n, :], ysb.t[0:n, :], lg.t[0:n, :], ALU.mult, rd=[ysb.b, lg.b], wr=[ysb.b])
    self.tt("pool", xr.t[0:n, :], ysb.t[0:n, :], lbias.t[0:n, :], ALU.add, rd=[ysb.b, lbias.b], wr=[xr.b])
    self.dma("sp", dst_rows, xr.t[0:n, :], rd=[xr.b])


def _ln_bufs(self):
    return {"i": 0, "xr": [self.sb("lnx%d" % i, 1024, F32) for i in range(2)],
            "y": [self.sb("lny%d" % i, 1024, F32) for i in range(2)],
            "st": [self.sb("lnst%d" % i, 12, F32) for i in range(2)], "mv": [self.sb("lnmv%d" % i, 2, F32) for i in range(2)],
            "rs": [self.sb("lnrs%d" % i, 1, F32) for i in range(2)], "nm": [self.sb("lnnm%d" % i, 1, F32) for i in range(2)]}


def _phase5(self, l, xsrc):
    self.wctx = ExitStack()
    wu = self.sb("wu", 8 * 2 * DFF, BF16, ctx=self.wctx)
    wu3 = wu.t.rearrange("p (k n) -> p k n", k=8)
    wub = [self.s.buf() for k in range(8)]
    self.pre_wu = (wu, wub)
    self.phase_begin()
    wo = self.sb("wo", 8 * 1024, BF16)
    wo3 = wo.t.rearrange("p (k n) -> p k n", k=8)
    wob = [self.s.buf() for k in range(8)]
    for kc in range(8):
        self.dma("pool", wo3[:, kc, :], self.din["w_out"][l, kc * 128:(kc + 1) * 128, :], wr=[wob[kc]])
    lg = self.sb("lg", 1024, F32)
    lb = self.sb("lb", 1024, F32)
    self.dma("sp", lg.t, dap(self.din["ln1_g"], l * 1024, [(0, 128), (1, 1024)]), wr=[lg.b])
    self.dma("sp", lb.t, dap(self.din["ln1_b"], l * 1024, [(0, 128), (1, 1024)]), wr=[lb.b])
    for kc in range(8):
        self.dma("pool", wu3[:, kc, :], self.din["w_up"][l, kc * 128:(kc + 1) * 128, :], wr=[wub[kc]])
    T = self.ln_bufs()
    cts = [self.sb("ct%d" % i, 8 * 512, BF16) for i in range(2)]
    pos = [[self.ps("po%d%d" % (i, h)) for h in range(2)] for i in range(2)]
    it = 0
    nj = 0
    for s in range(2):
        L = self.Ls[s]
        for w in range(L // 512):
            ct = cts[it % 2]
            it += 1
            ct3 = ct.t.rearrange("p (k t) -> p k t", k=8)
            self.dma("sp", ct3, dap(self.catT[s], w * 512, [(L, 128), (128 * L, 8), (1, 512)]), wr=[ct.b])
            for j in range(4):
                po = pos[nj % 2]
                nj += 1
                for h in range(2):
                    for kc in range(8):
                        self.mm(po[h].t, ct3[:, kc, j * 128:(j + 1) * 128], wo3[:, kc, h * 512:(h + 1) * 512],
                                kc == 0, kc == 7, rd=[ct.b, wob[kc]], wr=[po[h].b])
                r0 = w * 512 + j * 128
                self.ln_tail(po, 128, xsrc[s][r0:r0 + 128, :], self.grow[0 * 2 + s], lg, lb,
                             self.xres[s][r0:r0 + 128, :], T)
    self.phase_end()


def _phase6(self, l, dst):
    self.phase_begin()
    WN = 254
    wu, wub = self.pre_wu
    wu3 = wu.t.rearrange("p (k n) -> p k n", k=8)
    wd = self.sb("wd", 22 * 1024, BF16)
    wd3 = wd.t.rearrange("p (k n) -> p k n", k=22)
    wdb = [self.s.buf() for k in range(22)]
    for kc in range(22):
        self.dma("pool", wd3[:, kc, :], self.din["w_down"][l, kc * 128:(kc + 1) * 128, :], wr=[wdb[kc]])
    cw = self.sb("fcw", 44 * 4, F32)
    self.dma("sp", cw.t, self.din["ffw"][l].rearrange("p c k -> p (c k)"), wr=[cw.b])
    cw3 = cw.t.rearrange("p (c k) -> p c k", k=4)
    lg = self.sb("lg", 1024, F32)
    lb = self.sb("lb", 1024, F32)
    self.dma("sp", lg.t, dap(self.din["ln2_g"], l * 1024, [(0, 128), (1, 1024)]), wr=[lg.b])
    self.dma("sp", lb.t, dap(self.din["ln2_b"], l * 1024, [(0, 128), (1, 1024)]), wr=[lb.b])
    T = self.ln_bufs()
    xbs = [self.sb("fxb%d" % i, 2 * 1024, BF16) for i in range(2)]
    uT = self.sb("fuT", 8 * 256, BF16)
    uT3 = uT.t.rearrange("p (k t) -> p k t", k=8)
    g = self.sb("fg", 22 * 256, BF16)
    g3 = g.t.rearrange("p (k t) -> p k t", k=22)
    tgs = [self.sb("tg%d" % i, 256, F32) for i in range(2)]
    tvs = [self.sb("tv%d" % i, 256, F32) for i in range(2)]
    pts = [self.ps("fpt%d" % i, 256, BF16) for i in range(2)]
    pgs = [self.ps("fpg%d" % i, 256) for i in range(2)]
    pvs = [self.ps("fpv%d" % i, 256) for i in range(2)]
    po = [self.ps("fpo%d" % h) for h in range(2)]
    m3 = self.modT.t.rearrange("p (c s) -> p c s", s=2)
    it = 0
    for s in range(2):
        L = self.Ls[s]
        xs_ = self.xres[s]
        nw = (L + WN - 1) // WN
        for w in range(nw):
            t0 = w * WN
            nv = min(WN, L - t0)
            xb = xbs[it % 2]
            it += 1
            xb3 = xb.t.rearrange("p (j c) -> p j c", j=2)
            edge = (w == 0) or (t0 + 255 > L)
            if edge:
                self.memset("dve", xb.t, 0.0, wr=[xb.b])
            for j in range(2):
                a = t0 - 1 + 128 * j
                lo, hi = max(a, 0), min(a + 128, L)
                if hi > lo:
                    self.dma("pool", xb3[lo - a:hi - a, j, :], xs_[lo:hi, :], wr=[xb.b])
            for kc in range(8):
                pt = pts[kc % 2]
                for j in range(2):
                    self.tr(pt.t[:, j * 128:(j + 1) * 128], xb3[:, j, kc * 128:(kc + 1) * 128], self.ident.t,
                            rd=[xb.b, self.ident.b], wr=[pt.b])
                self.act(uT3[:, kc, :], pt.t, AF.Identity, rd=[pt.b, self.modT.b], wr=[uT.b],
                         scale=m3[:, 32 + kc, s:s + 1], bias=m3[:, 24 + kc, s:s + 1])
            if w == 0:
                self.memset("dve", uT3[:, :, 0:1], 0.0, wr=[uT.b])
            if t0 + nv >= L:
                c0 = L - (t0 - 1)
                self.memset("dve", uT3[:, :, c0:256], 0.0, wr=[uT.b])
            for jc in range(22):
                pg, pv = pgs[jc % 2], pvs[jc % 2]
                tg, tv = tgs[jc % 2], tvs[jc % 2]
                for kc in range(8):
                    self.mm(pg.t, wu3[:, kc, jc * 128:(jc + 1) * 128], uT3[:, kc, :], kc == 0, kc == 7,
                            rd=[wub[kc], uT.b], wr=[pg.b])
                for kc in range(8):
                    self.mm(pv.t, wu3[:, kc, DFF + jc * 128:DFF + (jc + 1) * 128], uT3[:, kc, :], kc == 0, kc == 7,
                            rd=[wub[kc], uT.b], wr=[pv.b])
                for (pp, tt_, ch) in ((pg, tg, jc), (pv, tv, 22 + jc)):
                    self.act(tt_.t[:, 0:WN], pp.t[:, 0:WN], AF.Identity, rd=[pp.b, cw.b], wr=[tt_.b],
                             scale=cw3[:, ch, 0:1], bias=cw3[:, ch, 3:4])
                    for k in (1, 2):
                        self.stt("dve", tt_.t[:, 0:WN], pp.t[:, k:k + WN], cw3[:, ch, k:k + 1], tt_.t[:, 0:WN],
                                 ALU.mult, ALU.add, rd=[pp.b, cw.b, tt_.b], wr=[tt_.b])
                self.act(tg.t[:, 0:WN], tg.t[:, 0:WN], AF.Gelu, rd=[tg.b], wr=[tg.b])
                self.tt("dve", g3[:, jc, 0:WN], tg.t[:, 0:WN], tv.t[:, 0:WN], ALU.mult, rd=[tg.b, tv.b], wr=[g.b])
            for c0 in (0, 128):
                n = min(128, nv - c0)
                if n <= 0:
                    continue
                for h in range(2):
                    for kc in range(22):
                        self.mm(po[h].t[0:n, :], g3[:, kc, c0:c0 + n], wd3[:, kc, h * 512:(h + 1) * 512],
                                kc == 0, kc == 21, rd=[g.b, wdb[kc]], wr=[po[h].b])
                r0 = t0 + c0
                self.ln_tail(po, n, xs_[r0:r0 + n, :], self.grow[1 * 2 + s], lg, lb, dst[s][r0:r0 + n, :], T)
    self.phase_end()
    self.wctx.close()


K.ln_tail = _ln_tail
K.ln_bufs = _ln_bufs
K.phase5 = _phase5
K.phase6 = _phase6


def _sync(self):
    self.s.barrier()


def _phase4(self, l):
    self.phase_begin()
    TWO_PI = 2.0 * math.pi
    F128 = self.sb("F128", 3 * 128, BF16)
    self.dma("pool", F128.t, self.din["hF128"].rearrange("p r k -> p (r k)"), wr=[F128.b])
    F3 = F128.t.rearrange("p (r k) -> p r k", r=3)
    G128 = self.sb("G128", 2 * 256, BF16)
    self.dma("pool", G128.t, self.din["hG128"].rearrange("p r k -> p (r k)"), wr=[G128.b])
    G3 = G128.t.rearrange("p (r k) -> p r k", r=2)
    hyw = self.sb("hyw", 24, F32)
    self.dma("sp", hyw.t, self.din["hyw"][l].rearrange("p c k -> p (c k)"), wr=[hyw.b])
    hyw3 = hyw.t.rearrange("p (c k) -> p c k", k=4)
    drow = self.sb("drow", 512, F32)
    self.dma("sp", drow.t, dap(self.din["hy_bias"], l * 512, [(0, 128), (1, 512)]), wr=[drow.b])
    negd = self.sb("negd", 2, F32)
    self.dma("sp", negd.t, self.din["hnegd"], wr=[negd.b])
    w1 = self.sb("hw1", 64, F32)
    w2 = self.sb("hw2", 64, F32)
    w3 = self.sb("hw3", 64, F32)
    wout = self.sb("hwout", 1024, F32)
    hyb = self.sb("hyb", 4, F32)
    self.dma("sp", w1.t[0:33, :], self.din["hy_w1"][l], wr=[w1.b])
    self.dma("sp", w2.t[0:64, :], self.din["hy_w2"][l], wr=[w2.b])
    self.dma("sp", w3.t[0:64, :], self.din["hy_w3"][l], wr=[w3.b])
    self.dma("sp", wout.t[0:64, :], self.din["hy_wout"][l], wr=[wout.b])
    self.dma("sp", hyb.t[0:64, :], self.din["hyb"][l], wr=[hyb.b])
    bfq = self.sb("bfq", 3, F32)
    self.ts("dve", bfq.t[0:64, :], hyb.t[0:64, 0:3], hyb.t[0:64, 3:4], None, ALU.mult, None, rd=[hyb.b], wr=[bfq.b])
    zero1 = self.sb("zero1", 1, F32)
    self.memset("dve", zero1.t, 0.0, wr=[zero1.b])
    FN1 = self.sb("FN1", 256, BF16)
    Tt = self.sb("Tt", 256, F32)
    T2 = self.sb("T2", 256, F32)
    GN1 = self.sb("GN1", 128, BF16)
    rnrow = self.sb("rnrow", 512, F32)
    xhs = [self.sb("hxh%d" % i, 2050, F32) for i in range(2)]
    cos_ = [self.sb("hco%d" % i, 2048, F32) for i in range(2)]
    zcs = [self.sb("hzc%d" % i, 512, F32) for i in range(2)]
    tls = [self.sb("htl%d" % i, 512, F32) for i in range(2)]
    decs = [[self.sb("hdec%d%d" % (i, c), 512, F32) for c in range(2)] for i in range(2)]
    ya = self.sb("hya", 512, F32)
    kk = self.sb("hkk", 512, F32)
    hks = [self.sb("hk%d" % i, 512, F32) for i in range(3)]
    kts = [self.sb("hkt%d" % i, 512, F32) for i in range(3)]
    kab = self.sb("hkab", 512, F32)
    acc = self.sb("hacc", 8 * 16, F32)
    kb0 = self.sb("hkb0", 4, F32)
    nrm = self.sb("hnrm", 4, F32)
    Xbs = [self.sb("hXb%d" % i, 512, BF16) for i in range(2)]
    zfs = [self.sb("hzf%d" % i, 512, F32) for i in range(4)]
    gfs = [self.sb("hgf%d" % i, 512, F32) for i in range(4)]
    kfts = [self.sb("hkft%d" % i, 4 * 2 * 128, BF16) for i in range(2)]
    Apre = self.sb("hApre", 512, BF16)
    Apim = self.sb("hApim", 512, BF16)
    Yre = self.sb("hYre", 512, BF16)
    Yim = self.sb("hYim", 512, BF16)
    Bpre = self.sb("hBpre", 512, BF16)
    Bpim = self.sb("hBpim", 512, BF16)
    tq = [self.sb("htq%d" % i, 512, F32) for i in range(4)]
    outf = [self.sb("houtf%d" % i, 512, F32) for i in range(2)]
    outb = [self.sb("houtb%d" % i, 512, BF16) for i in range(2)]
    ps1 = self.ps2("hps1")
    pXre = self.ps("hpXre")
    pXim = self.ps("hpXim")
    pB = self.ps2("hpB")
    py = self.ps("hpy")
    it = 0

    def fft_s1(Xb, Kp, N1):
        NH = N1 // 2 + 1
        cw = 2 * NH
        CS = 256 if N1 == 128 else 128
        X3 = Xb.t.rearrange("p (c i) -> p c i", c=4)
        for ch in range(4):
            off = ch * CS
            self.mm(ps1.t[:, off:off + cw], X3[0:Kp, ch, :], FN1.t[0:Kp, 0:cw], True, True,
                    rd=[Xb.b, FN1.b], wr=[ps1.b])

    def fft_d1s2(N1):
        NH = N1 // 2 + 1
        cw = 2 * NH
        CS = 256 if N1 == 128 else 128
        Tt3 = Tt.t[:, 0:cw].rearrange("p (r k) -> p r k", r=2)
        Ar3 = Apre.t[:, 0:4 * NH].rearrange("p (c k) -> p c k", c=4)
        Ai3 = Apim.t[:, 0:4 * NH].rearrange("p (c k) -> p c k", c=4)
        v = ps1.t[:, 0:4 * CS].rearrange("p (c w) -> p c w", c=4)
        Are, Aim = v[:, :, 0:NH], v[:, :, NH:cw]
        Tre = Tt3[:, 0, :].unsqueeze(1).to_broadcast([128, 4, NH])
        Tim = Tt3[:, 1, :].unsqueeze(1).to_broadcast([128, 4, NH])
        t = [q.t[:, 0:4 * NH].rearrange("p (c k) -> p c k", c=4) for q in tq]
        self.tt("dve", t[0], Are, Tre, ALU.mult, rd=[ps1.b, Tt.b], wr=[tq[0].b])
        self.tt("dve", t[1], Aim, Tim, ALU.mult, rd=[ps1.b, Tt.b], wr=[tq[1].b])
        self.tt("dve", Ar3, t[0], t[1], ALU.subtract, rd=[tq[0].b, tq[1].b], wr=[Apre.b])
        self.tt("dve", t[2], Are, Tim, ALU.mult, rd=[ps1.b, Tt.b], wr=[tq[2].b])
        self.tt("dve", t[3], Aim, Tre, ALU.mult, rd=[ps1.b, Tt.b], wr=[tq[3].b])
        self.tt("dve", Ai3, t[2], t[3], ALU.add, rd=[tq[2].b, tq[3].b], wr=[Apim.b])
        W = 4 * NH
        self.mm(pXre.t[:, 0:W], F3[:, 0, :], Apre.t[:, 0:W], True, False, rd=[F128.b, Apre.b], wr=[pXre.b])
        self.mm(pXre.t[:, 0:W], F3[:, 2, :], Apim.t[:, 0:W], False, True, rd=[F128.b, Apim.b], wr=[pXre.b])
        self.mm(pXim.t[:, 0:W], F3[:, 0, :], Apim.t[:, 0:W], True, False, rd=[F128.b, Apim.b], wr=[pXim.b])
        self.mm(pXim.t[:, 0:W], F3[:, 1, :], Apre.t[:, 0:W], False, True, rd=[F128.b, Apre.b], wr=[pXim.b])

    for s in range(2):
        L = self.Ls[s]
        NB = L // 128
        N1 = 2 * NB
        NH = N1 // 2 + 1
        self.dma("pool", FN1.t[0:N1, 0:2 * NH], self.din["hFN1_%d" % s], wr=[FN1.b])
        self.dma("sp", Tt.t[:, 0:2 * NH], self.din["hT_%d" % s].rearrange("p r k -> p (r k)"), wr=[Tt.b])
        self.dma("sp", T2.t[0:NH, :], self.din["hT2_%d" % s].rearrange("p r k -> p (r k)"), wr=[T2.b])
        self.dma("pool", GN1.t[0:NH, 0:2 * NB], self.din["hGN1_%d" % s].rearrange("p r k -> p (r k)"), wr=[GN1.b])
        TCc = 2048
        for m in range(6):
            for c in range(L // TCc):
                xh, co = xhs[it % 2], cos_[it % 2]
                it += 1
                t0 = c * TCc
                lo, hi = max(t0 - 1, 0), min(t0 + TCc + 1, L)
                if c == 0:
                    self.memset("dve", xh.t[:, 0:1], 0.0, wr=[xh.b])
                if hi == L:
                    self.memset("dve", xh.t[:, TCc + 1:TCc + 2], 0.0, wr=[xh.b])
                self.dma("sp", xh.t[:, lo - (t0 - 1):hi - (t0 - 1)], self.lh[s][512 + m * 128:512 + (m + 1) * 128, lo:hi], wr=[xh.b])
                self.ts("dve", co.t, xh.t[:, 0:TCc], hyw3[:, m, 0:1], hyw3[:, m, 3:4], ALU.mult, ALU.add,
                        rd=[xh.b, hyw.b], wr=[co.b])
                for k in (1, 2):
                    self.stt("dve", co.t, xh.t[:, k:k + TCc], hyw3[:, m, k:k + 1], co.t, ALU.mult, ALU.add,
                             rd=[xh.b, hyw.b, co.b], wr=[co.b])
                self.dma("sp", self.hc[s][m * 128:(m + 1) * 128, t0:t0 + TCc], co.t, rd=[co.b])
        NCH = L // 512
        self.memset("dve", acc.t, 0.0, wr=[acc.b])
        acc3 = acc.t.rearrange("p (m c) -> p m c", m=8)
        for c in range(NCH):
            zc, tl = zcs[c % 2], tls[c % 2]
            dec = decs[c % 2]
            t0 = c * 512
            self.dma("sp", zc.t[0:33, :], self.din["hz%d" % s][:, t0:t0 + 512], wr=[zc.b])
            self.dma("sp", tl.t, dap(self.din["htl%d" % s], t0, [(0, 128), (1, 512)]), wr=[tl.b])
            for cc in range(2):
                self.act(dec[cc].t, tl.t, AF.Exp, rd=[tl.b, negd.b], wr=[dec[cc].b], scale=negd.t[:, cc:cc + 1])
            h, hK = zc, 33
            for k, wk in enumerate((w1, w2, w3)):
                ph = ps1
                pho = (k % 2) * 512
                self.mm(ph.t[0:64, pho:pho + 512], wk.t[0:hK, :], h.t[0:hK, :], True, True, rd=[wk.b, h.b], wr=[ph.b])
                self.ts("dve", ya.t[0:64, :], ph.t[0:64, pho:pho + 512], hyb.t[0:64, 3:4], bfq.t[0:64, k:k + 1], ALU.mult, ALU.add,
                        rd=[ph.b, hyb.b, bfq.b], wr=[ya.b])
                self.ts("dve", kk.t[0:64, :], ya.t[0:64, :], 1.0 / TWO_PI, MAGIC, ALU.mult, ALU.add, rd=[ya.b], wr=[kk.b])
                self.ts("dve", kk.t[0:64, :], kk.t[0:64, :], -MAGIC, -TWO_PI, ALU.add, ALU.mult, rd=[kk.b], wr=[kk.b])
                self.tt("dve", ya.t[0:64, :], kk.t[0:64, :], ya.t[0:64, :], ALU.add, rd=[kk.b, ya.b], wr=[ya.b])
                self.ts("dve", ya.t[0:64, :], ya.t[0:64, :], math.pi, -math.pi, ALU.min, ALU.max, rd=[ya.b], wr=[ya.b])
                hk = hks[k]
                self.act(hk.t[0:64, :], ya.t[0:64, :], AF.Sin, rd=[ya.b], wr=[hk.b])
                h, hK = hk, 64
            nk = 0
            for o in range(2):
                for cc in range(2):
                    for dr in (1, 0):
                        m = o * 2 + dr
                        col0 = (m * 2 + cc) * 128
                        pk = (pXre, pXim)[nk % 2]
                        kt = kts[nk % 3]
                        nk += 1
                        self.mm(pk.t, wout.t[0:64, col0:col0 + 128], h.t[0:64, :], True, True, rd=[wout.b, h.b], wr=[pk.b])
                        row0 = cc * 128
                        ai = (o * 2 + cc) * 2 + dr
                        kdst = self.kfull[s][o, row0:row0 + 128, :]
                        if dr == 1:
                            self.tt("dve", rev(kt.t), pk.t, dec[cc].t, ALU.mult, rd=[pk.b, dec[cc].b], wr=[kt.b])
                            if c == 0:
                                self.cp("dve", kb0.t[:, o * 2 + cc:o * 2 + cc + 1], kt.t[:, 511:512], rd=[kt.b], wr=[kb0.b])
                                n_ = 511
                                self.dma("sp", kdst[:, 2 * L - 511:2 * L], kt.t[:, 0:511], rd=[kt.b])
                            else:
                                n_ = 512
                                self.dma("sp", kdst[:, 2 * L - t0 - 511:2 * L - t0 + 1], kt.t, rd=[kt.b])
                        else:
                            self.tt("dve", kt.t, pk.t, dec[cc].t, ALU.mult, rd=[pk.b, dec[cc].b], wr=[kt.b])
                            if c == 0:
                                self.tt("dve", kt.t[:, 0:1], kt.t[:, 0:1], kb0.t[:, o * 2 + cc:o * 2 + cc + 1], ALU.add,
                                        rd=[kt.b, kb0.b], wr=[kt.b])
                            n_ = 512
                            self.dma("sp", kdst[:, t0:t0 + 512], kt.t, rd=[kt.b])
                        self.act(kab.t[:, 0:n_], kt.t[:, 0:n_], AF.Abs, rd=[kt.b, acc.b], wr=[kab.b, acc.b],
                                 accum=acc3[:, ai, c:c + 1])
        self.s.op("dve", (lambda e, o_=nrm.t, i_=acc.t.rearrange("p (m c) -> p m c", m=4):
                          e.tensor_reduce(out=o_, in_=i_, axis=AX.X, op=ALU.add)), rd=[acc.b], wr=[nrm.b])
        self.recip(nrm.t, nrm.t, rd=[nrm.b], wr=[nrm.b])
        for o in range(2):
            for cc in range(2):
                self.dma("sp", dap(self.rnd, o * 256 + cc * 128, [(1, 128), (1, 1)]), nrm.t[:, o * 2 + cc:o * 2 + cc + 1], rd=[nrm.b], slow=True)
                self.dma("sp", dap(self.kfull[s], (o * 256 + cc * 128) * 2 * L + L, [(2 * L, 128), (1, 1)]), zero1.t, rd=[zero1.b], slow=True)
        self.sync()
        self.dma("sp", rnrow.t, dap(self.rnd, 0, [(0, 128), (1, 512)]), wr=[rnrow.b])
        def h3_a(g):
            o, c0, i2 = g
            Xb = Xbs[i2 % 2]
            self.dma("pool", Xb.t[0:N1, :].rearrange("p (c i) -> p c i", c=4),
                     dap(self.kfull[s], (o * 256 + c0) * 2 * L, [(128, N1), (2 * L, 4), (1, 128)]), wr=[Xb.b])
            fft_s1(Xb, N1, N1)

        def h3_c(g):
            o, c0, i2 = g
            kft = kfts[i2 % 2]
            k4 = kft.t[:, 0:8 * NH].rearrange("p (c r k) -> p c r k", c=4, r=2)
            rb = rnrow.t[:, o * 256 + c0:o * 256 + c0 + 4].unsqueeze(2).to_broadcast([128, 4, NH])
            self.tt("dve", k4[:, :, 0, :], pXre.t[:, 0:4 * NH].rearrange("p (c k) -> p c k", c=4), rb, ALU.mult,
                    rd=[pXre.b, rnrow.b], wr=[kft.b])
            self.tt("dve", k4[:, :, 1, :], pXim.t[:, 0:4 * NH].rearrange("p (c k) -> p c k", c=4), rb, ALU.mult,
                    rd=[pXim.b, rnrow.b], wr=[kft.b])
            self.dma("sp", dap(self.kfs[s], (o * 128 * 256 + c0) * 2 * NH, [(256 * 2 * NH, 128), (1, 8 * NH)]),
                     kft.t[:, 0:8 * NH], rd=[kft.b])

        G = [(o, grp * 4, o * 64 + grp) for o in range(2) for grp in range(64)]
        for n in range(len(G) + 1):
            if n < len(G):
                h3_a(G[n])
            if n >= 1:
                h3_c(G[n - 1])
            if n < len(G):
                fft_d1s2(N1)
        self.sync()
        T23 = T2.t.rearrange("p (r i) -> p r i", r=2)
        GN3 = GN1.t[:, 0:2 * NB].rearrange("p (r a) -> p r a", r=2)
        W = 4 * NH
        def h4_a(g):
            o, c0, i2 = g
            zsrc = self.hc[s] if o == 0 else self.z1[s]
            Xb, kft, zf, gf = Xbs[i2 % 2], kfts[i2 % 2], zfs[i2 % 4], gfs[i2 % 4]
            zap = dap(zsrc, c0 * L, [(128, NB), (L, 4), (1, 128)])
            self.dma("pool", Xb.t[0:NB, :].rearrange("p (c i) -> p c i", c=4), zap, wr=[Xb.b])
            self.dma("sp", zf.t[0:NB, :].rearrange("p (c i) -> p c i", c=4), zap, wr=[zf.b])
            self.dma("sp", gf.t[0:NB, :].rearrange("p (c i) -> p c i", c=4),
                     dap(self.hc[s], (256 * (o + 1) + c0) * L, [(128, NB), (L, 4), (1, 128)]), wr=[gf.b])
            self.dma("sp", kft.t[:, 0:8 * NH],
                     dap(self.kfs[s], (o * 128 * 256 + c0) * 2 * NH, [(256 * 2 * NH, 128), (1, 8 * NH)]), wr=[kft.b])
            fft_s1(Xb, NB, N1)

        def h4_d2s3(g):
            o, c0, i2 = g
            kft = kfts[i2 % 2]
            k4 = kft.t[:, 0:8 * NH].rearrange("p (c r k) -> p c r k", c=4, r=2)
            Kre, Kim = k4[:, :, 0, :], k4[:, :, 1, :]
            Xr = pXre.t[:, 0:W].rearrange("p (c k) -> p c k", c=4)
            Xi = pXim.t[:, 0:W].rearrange("p (c k) -> p c k", c=4)
            t = [q.t[:, 0:W].rearrange("p (c k) -> p c k", c=4) for q in tq]
            self.tt("dve", t[0], Xr, Kre, ALU.mult, rd=[pXre.b, kft.b], wr=[tq[0].b])
            self.tt("dve", t[1], Xi, Kim, ALU.mult, rd=[pXim.b, kft.b], wr=[tq[1].b])
            self.tt("dve", Yre.t[:, 0:W].rearrange("p (c k) -> p c k", c=4), t[0], t[1], ALU.subtract,
                    rd=[tq[0].b, tq[1].b], wr=[Yre.b])
            self.tt("dve", t[2], Xr, Kim, ALU.mult, rd=[pXre.b, kft.b], wr=[tq[2].b])
            self.tt("dve", t[3], Xi, Kre, ALU.mult, rd=[pXim.b, kft.b], wr=[tq[3].b])
            self.tt("dve", Yim.t[:, 0:W].rearrange("p (c k) -> p c k", c=4), t[2], t[3], ALU.add,
                    rd=[tq[2].b, tq[3].b], wr=[Yim.b])
            Yr3 = Yre.t[:, 0:W].rearrange("p (c k) -> p c k", c=4)
            Yi3 = Yim.t[:, 0:W].rearrange("p (c k) -> p c k", c=4)
            for ch in range(4):
                off = ch * 256
                self.mm(pB.t[0:NH, off:off + 256], Yr3[:, ch, :], G3[:, 0, :], True, False,
                        rd=[Yre.b, G128.b], wr=[pB.b])
                self.mm(pB.t[0:NH, off:off + 256], Yi3[:, ch, :], G3[:, 1, :], False, True,
                        rd=[Yim.b, G128.b], wr=[pB.b])

        def h4_d3s4(g):
            Br4 = Bpre.t[0:NH, :].rearrange("p (c i) -> p c i", c=4)
            Bi4 = Bpim.t[0:NH, :].rearrange("p (c i) -> p c i", c=4)
            v = pB.t[0:NH, :].rearrange("p (c r i) -> p c r i", c=4, r=2)
            Bre, Bim = v[:, :, 0, :], v[:, :, 1, :]
            Tre = T23[0:NH, 0, :].unsqueeze(1).to_broadcast([NH, 4, 128])
            Tim = T23[0:NH, 1, :].unsqueeze(1).to_broadcast([NH, 4, 128])
            t = [q.t[0:NH, 0:512].rearrange("p (c i) -> p c i", c=4) for q in tq]
            self.tt("dve", t[0], Bre, Tre, ALU.mult, rd=[pB.b, T2.b], wr=[tq[0].b])
            self.tt("dve", t[1], Bim, Tim, ALU.mult, rd=[pB.b, T2.b], wr=[tq[1].b])
            self.tt("dve", Br4, t[0], t[1], ALU.subtract, rd=[tq[0].b, tq[1].b], wr=[Bpre.b])
            self.tt("dve", t[2], Bre, Tim, ALU.mult, rd=[pB.b, T2.b], wr=[tq[2].b])
            self.tt("dve", t[3], Bim, Tre, ALU.mult, rd=[pB.b, T2.b], wr=[tq[3].b])
            self.tt("dve", Bi4, t[2], t[3], ALU.add, rd=[tq[2].b, tq[3].b], wr=[Bpim.b])
            self.mm(py.t[0:NB, :], GN3[0:NH, 0, :], Bpre.t[0:NH, :], True, False, rd=[GN1.b, Bpre.b], wr=[py.b])
            self.mm(py.t[0:NB, :], GN3[0:NH, 1, :], Bpim.t[0:NH, :], False, True, rd=[GN1.b, Bpim.b], wr=[py.b])

        def h4_d4(g):
            o, c0, i2 = g
            zf, gf = zfs[i2 % 4], gfs[i2 % 4]
            t0_ = tq[0].t[0:NB, :]
            db = drow.t[0:NB, o * 256 + c0:o * 256 + c0 + 4].unsqueeze(2).to_broadcast([NB, 4, 128])
            self.tt("dve", t0_.rearrange("p (c i) -> p c i", c=4), zf.t[0:NB, :].rearrange("p (c i) -> p c i", c=4), db,
                    ALU.mult, rd=[zf.b, drow.b], wr=[tq[0].b])
            self.tt("dve", t0_, t0_, py.t[0:NB, :], ALU.add, rd=[tq[0].b, py.b], wr=[tq[0].b])
            if o == 0:
                ob = outf[i2 % 2]
                self.tt("dve", ob.t[0:NB, :], t0_, gf.t[0:NB, :], ALU.mult, rd=[tq[0].b, gf.b], wr=[ob.b])
                self.dma("sp", dap(self.z1[s], c0 * L, [(128, NB), (L, 4), (1, 128)]),
                         ob.t[0:NB, :].rearrange("p (c i) -> p c i", c=4), rd=[ob.b])
            else:
                ob = outb[i2 % 2]
                self.tt("dve", ob.t[0:NB, :], t0_, gf.t[0:NB, :], ALU.mult, rd=[tq[0].b, gf.b], wr=[ob.b])
                self.dma("sp", dap(self.catT[s], (768 + c0) * L, [(128, NB), (L, 4), (1, 128)]),
                         ob.t[0:NB, :].rearrange("p (c i) -> p c i", c=4), rd=[ob.b])

        for o in range(2):
            G = [(o, grp * 4, grp) for grp in range(64)]
            ng = len(G)
            for n in range(ng + 3):
                if n < ng:
                    h4_a(G[n])
                if 0 <= n - 3 < ng:
                    h4_d4(G[n - 3])
                if 0 <= n - 2 < ng:
                    h4_d3s4(G[n - 2])
                if 0 <= n - 1 < ng:
                    h4_d2s3(G[n - 1])
                if n < ng:
                    fft_d1s2(N1)
            self.sync()
    self.phase_end()


K.sync = _sync
K.phase4 = _phase4


def build_program(nlayers=DEPTH, dump=False):
    nc = bass.Bass("TRN2", target_bir_lowering=False)
    with ExitStack() as ctx:
        s = Sched(nc, ctx)
        k = K(nc, s, ctx, nlayers=nlayers, dbg={"dump": dump})
        k.setup()
        for l in range(nlayers):
            xsrc = k.xin if l == 0 else k.xa
            dst = k.yout if l == nlayers - 1 else k.xa
            k.phase0(l)
            k.phase1(l, xsrc)
            k.phase2(l)
            k.phase3(l)
            k.phase4(l)
            k.phase5(l, xsrc)
            k.phase6(l, dst)
        s.barrier()
        s.emit()
    return nc


_NC_CACHE = {}


def kernel(**inputs):
    inp = {k: np.asarray(v) for k, v in inputs.items()}
    if "nc" not in _NC_CACHE:
        _NC_CACHE["nc"] = build_program()
    nc = _NC_CACHE["nc"]
    sh = host_prep(inp)
    in_maps = []
    for i in range(8):
        m = dict(sh)
        m.update(core_prep(inp, i))
        in_maps.append(m)
    res = run_bass_kernel_spmd(nc, in_maps, core_ids=list(range(8)))
    yp = np.stack([np.asarray(r["yp"], dtype=np.float32) for r in res.results], axis=0)
    ys = np.stack([np.asarray(r["ys"], dtype=np.float32) for r in res.results], axis=0)
    return (yp, ys)
```

```python
import math
import numpy as np
from contextlib import ExitStack
import concourse.bass as bass
import concourse.mybir as mybir
from concourse.bass_types import AP as APc
from concourse.bass_utils import run_bass_kernel_spmd

F32 = mybir.dt.float32
BF16 = mybir.dt.bfloat16
AF = mybir.ActivationFunctionType
ALU = mybir.AluOpType
AX = mybir.AxisListType

D = 1024
DEPTH = 4
LP = 4096
LS = 8192
DFF = 2816
ALPHA = (2 * DEPTH) ** 0.25
LN_EPS = 1e-5
QK_EPS = 1e-6
HY_MIN_DECAY = abs(math.log(1e-2)) / 1.5
HY_MAX_DECAY = abs(math.log(1e-2)) / 0.3
MAGIC = 12582912.0

ENGS = ("pe", "act", "dve", "pool", "sp")
EMBED_WAITS = True


class Buf:
    __slots__ = ("name", "w", "r")

    def __init__(self, name):
        self.name = name
        self.w = None
        self.r = []


class Sched:
    NDMA = 8

    def __init__(self, nc, ctx):
        self.nc = nc
        self.streams = {e: [] for e in ENGS}
        self.sems = {}
        self.cnt = {}
        for e in ENGS:
            self.sems[e] = ctx.enter_context(nc.semaphore("s_" + e))
            self.cnt[e] = 0
        self.dnext = {}
        for q in ("sp", "act", "pool"):
            for i in range(self.NDMA):
                k = "d_%s%d" % (q, i)
                self.sems[k] = ctx.enter_context(nc.semaphore(k))
                self.cnt[k] = 0
            self.dnext[q] = 0
        self.waited = {e: {} for e in ENGS}
        self.nbuf = 0

    def buf(self, name=None):
        self.nbuf += 1
        return Buf(name or ("b%d" % self.nbuf))

    def _need(self, eng, ev, same_ok):
        if ev is None:
            return
        k, v, src = ev
        if src == eng and same_ok:
            return
        if self.waited[eng].get(k, 0) >= v:
            return
        self.waited[eng][k] = v
        self.streams[eng].append(("w", k, v))

    def _deps(self, eng, rd, wr):
        pe = (eng == "pe")
        for b in rd:
            self._need(eng, b.w, pe)
        for b in wr:
            self._need(eng, b.w, pe)
            for ev in b.r:
                self._need(eng, ev, True)

    def _mark(self, ev, rd, wr):
        for b in rd:
            b.r.append(ev)
            if len(b.r) > 64:
                last = {}
                for e2 in b.r:
                    if e2[0] not in last or last[e2[0]][1] < e2[1]:
                        last[e2[0]] = e2
                b.r = list(last.values())
        for b in wr:
            b.w = ev
            b.r = []

    def op(self, eng, fn, rd=(), wr=()):
        self._deps(eng, rd, wr)
        self.cnt[eng] += 1
        ev = (eng, self.cnt[eng], eng)
        self.streams[eng].append(("o", fn, eng, 1))
        self._mark(ev, rd, wr)
        return ev

    def dma(self, q, out, in_, rd=(), wr=(), slow=False):
        self._deps(q, rd, wr)
        i = self.dnext[q]
        self.dnext[q] = (i + 1) % self.NDMA
        k = "d_%s%d" % (q, i)
        if self.cnt[k] > 0:
            self._need(q, (k, self.cnt[k], None), False)
        self.cnt[k] += 16
        ev = (k, self.cnt[k], None)
        if slow:
            self.streams[q].append(("o", (lambda e: e.dma_start(out=out, in_=in_, allow_slow_non_contiguous=True)), k, 16))
        else:
            self.streams[q].append(("o", (lambda e: e.dma_start(out=out, in_=in_)), k, 16))
        self._mark(ev, rd, wr)
        return ev

    def barrier(self):
        for e in ENGS:
            for k, v in self.cnt.items():
                if v > 0 and k != e:
                    self._need(e, (k, v, None), False)

    def emit(self):
        nc = self.nc
        if not any(self.streams[e] for e in ENGS):
            return
        engobj = {"pe": "tensor", "act": "scalar", "dve": "vector", "pool": "gpsimd", "sp": "sync"}
        with nc.Block() as block:
            for e in ENGS:
                items = self.streams[e]
                sems = self.sems

                def body(eng, items=items, sems=sems):
                    n = len(items)
                    i = 0
                    while i < n:
                        it = items[i]
                        if it[0] == "w":
                            if EMBED_WAITS and i + 1 < n and items[i + 1][0] == "o":
                                nx = items[i + 1]
                                ins = nx[1](eng)
                                ins._wait_ge(sems[it[1]], it[2])
                                ins.then_inc(sems[nx[2]], nx[3])
                                i += 2
                                continue
                            eng.wait_ge(sems[it[1]], it[2])
                        else:
                            it[1](eng).then_inc(sems[it[2]], it[3])
                        i += 1
                getattr(block, engobj[e])(body)
        self.streams = {e: [] for e in ENGS}


class TB:
    __slots__ = ("t", "b")

    def __init__(self, t, b):
        self.t = t
        self.b = b

    def __getitem__(self, k):
        return self.t[k]


def rev(ap2d):
    (ps, pn), (fs, fn) = ap2d.ap
    return APc(ap2d.tensor, ap2d.offset + (fn - 1) * fs, [[ps, pn], [-fs, fn]])


def dap(t, off, dims):
    return APc(t.tensor, t.offset + off, [[a, b] for a, b in dims])


class K:
    def __init__(self, nc, s, ctx, nlayers=DEPTH, dbg=None):
        self.nc, self.s, self.gctx = nc, s, ctx
        self.nlayers = nlayers
        self.dbg = dbg or {}
        self.pctx = None
        self.uid = 0

    def _nm(self, n):
        self.uid += 1
        return "%s_%d" % (n, self.uid)

    def sb(self, name, shape, dt, glob=False):
        c = self.gctx if glob else self.pctx
        t = c.enter_context(self.nc.sbuf_tensor(self._nm(name), list(shape), dt))
        return TB(t, self.s.buf(name))

    def ps(self, name, shape, dt=F32):
        t = self.pctx.enter_context(self.nc.psum_tensor(self._nm(name), list(shape), dt))
        return TB(t, self.s.buf(name))

    def dram(self, name, shape, dt, kind="Internal"):
        return self.nc.dram_tensor(name, list(shape), dt, kind=kind).ap()

    def mm(self, out, lhsT, rhs, start, stop, rd, wr, tp=None):
        if tp is None:
            f = lambda e: e.matmul(out=out, lhsT=lhsT, rhs=rhs, start=start, stop=stop)
        else:
            f = lambda e: e.matmul(out=out, lhsT=lhsT, rhs=rhs, start=start, stop=stop, tile_position=tp)
        return self.s.op("pe", f, rd=rd, wr=wr)

    def tr(self, out, in_, ident, rd, wr):
        return self.s.op("pe", lambda e: e.transpose(out=out, in_=in_, identity=ident), rd=rd, wr=wr)

    def act(self, out, in_, func, rd, wr, bias=None, scale=None, accum=None):
        kw = {}
        if bias is not None:
            kw["bias"] = bias
        if scale is not None:
            kw["scale"] = scale
        if accum is not None:
            kw["accum_out"] = accum
        return self.s.op("act", lambda e: e.activation(out=out, in_=in_, func=func, **kw), rd=rd, wr=wr)

    def tt(self, eng, out, in0, in1, op, rd, wr):
        return self.s.op(eng, lambda e: e.tensor_tensor(out=out, in0=in0, in1=in1, op=op), rd=rd, wr=wr)

    def ts(self, eng, out, in0, s1, s2, op0, op1, rd, wr):
        if op1 is None:
            f = lambda e: e.tensor_scalar(out=out, in0=in0, scalar1=s1, scalar2=None, op0=op0)
        else:
            f = lambda e: e.tensor_scalar(out=out, in0=in0, scalar1=s1, scalar2=s2, op0=op0, op1=op1)
        return self.s.op(eng, f, rd=rd, wr=wr)

    def stt(self, eng, out, in0, scalar, in1, op0, op1, rd, wr):
        return self.s.op(eng, lambda e: e.scalar_tensor_tensor(out=out, in0=in0, scalar=scalar, in1=in1, op0=op0, op1=op1), rd=rd, wr=wr)

    def cp(self, eng, out, in_, rd, wr):
        if eng == "act":
            return self.s.op("act", lambda e: e.copy(out=out, in_=in_), rd=rd, wr=wr)
        return self.s.op(eng, lambda e: e.tensor_copy(out=out, in_=in_), rd=rd, wr=wr)

    def memset(self, eng, ap, val, wr):
        return self.s.op(eng, lambda e: e.memset(ap, val), rd=(), wr=wr)

    def recip(self, out, in_, rd, wr):
        return self.s.op("dve", lambda e: e.reciprocal(out=out, in_=in_), rd=rd, wr=wr)

    def dma(self, q, out, in_, rd=(), wr=(), slow=False):
        return self.s.dma(q, out, in_, rd=rd, wr=wr, slow=slow)

    def phase_begin(self):
        self.pctx = ExitStack()

    def phase_end(self):
        self.s.barrier()
        self.pctx.close()
        self.pctx = None


def _init_arena(self):
    self.pctx = None


def _sb(self, name, cols, dt=F32, glob=False, parts=128, ctx=None):
    c = ctx if ctx is not None else (self.gctx if glob else self.pctx)
    t = c.enter_context(self.nc.sbuf_tensor(self._nm(name), [128, cols], dt))
    return TB(t[0:parts, :], self.s.buf(name))


def _ps(self, name, cols=512, dt=F32, parts=128):
    full = 512 if dt == F32 else 1024
    t = self.pctx.enter_context(self.nc.psum_tensor(self._nm(name), [128, full], dt))
    return TB(t[0:parts, 0:cols], self.s.buf(name))


def _ps2(self, name):
    t = self.pctx.enter_context(self.nc.psum_tensor(self._nm(name), [128, 1024], F32))
    return TB(t[:, :], self.s.buf(name))


def _phase_begin(self):
    self.pctx = ExitStack()


def _phase_end(self):
    self.s.barrier()
    self.s.emit()
    self.pctx.close()
    self.pctx = None


K.init_arena = _init_arena
K.sb = _sb
K.ps = _ps
K.ps2 = _ps2
K.phase_begin = _phase_begin
K.phase_end = _phase_end


def _consts():
    c = {}
    c["ident"] = np.eye(128, dtype=np.float32)
    t = np.arange(LS)
    row = (t // 64).astype(np.float32)
    col = (t % 64).astype(np.float32)
    inv = (10000.0 ** (-np.arange(16, dtype=np.float32) / 16)).astype(np.float32)
    ang = np.stack([row[:, None] * inv, col[:, None] * inv], axis=1).astype(np.float32)
    cs = np.cos(ang).reshape(LS // 128, 128, 32).transpose(1, 0, 2)
    sn = np.sin(ang).reshape(LS // 128, 128, 32).transpose(1, 0, 2)
    for s, L in enumerate((LP, LS)):
        f32 = np.float32
        t = np.linspace(0.0, 1.0, L, dtype=f32)[:, None]
        w = (f32(2.0 * math.pi) * np.arange(L, dtype=f32)[:, None] / f32(L)).astype(f32)
        f = np.linspace(1e-4, 15, 16, dtype=f32)[None, :]
        z = np.concatenate([t, np.cos(f * w), -np.sin(f * w)], axis=-1).astype(f32)
        c["hz%d" % s] = np.ascontiguousarray(z.T)
        c["htl%d" % s] = np.ascontiguousarray(t.T)
        NB = L // 128
        N1 = 2 * NB
        N = 2 * L
        NH = N1 // 2 + 1
        a = np.arange(N1, dtype=np.float64)[:, None]
        kl = np.arange(NH, dtype=np.float64)[None, :]
        th = 2 * np.pi * a * kl / N1
        c["hFN1_%d" % s] = np.concatenate([np.cos(th), -np.sin(th)], axis=1).astype(f32)
        i = np.arange(128, dtype=np.float64)[:, None]
        th = 2 * np.pi * i * kl / N
        c["hT_%d" % s] = np.stack([np.cos(th), -np.sin(th)], axis=1).astype(f32)
        c["hT2_%d" % s] = np.stack([np.cos(th.T), np.sin(th.T)], axis=1).astype(f32)
        aa = np.arange(NB, dtype=np.float64)[None, :]
        klc = np.arange(NH, dtype=np.float64)[:, None]
        th = 2 * np.pi * klc * aa / N1
        wgt = np.full((NH, 1), 2.0)
        wgt[0, 0] = 1.0
        wgt[NH - 1, 0] = 1.0
        c["hGN1_%d" % s] = np.stack([wgt * np.cos(th) / N, -wgt * np.sin(th) / N], axis=1).astype(f32)
    i = np.arange(128, dtype=np.float64)[:, None]
    kh = np.arange(128, dtype=np.float64)[None, :]
    th = 2 * np.pi * i * kh / 128
    c["hF128"] = np.stack([np.cos(th), -np.sin(th), np.sin(th)], axis=1).astype(np.float32)
    c["hG128"] = np.stack([np.concatenate([np.cos(th), np.sin(th)], axis=1),
                           np.concatenate([-np.sin(th), np.cos(th)], axis=1)], axis=1).astype(np.float32)
    dl = np.linspace(HY_MIN_DECAY, HY_MAX_DECAY, 256, dtype=np.float32)
    c["hnegd"] = np.ascontiguousarray((-dl).reshape(2, 128).T)
    c["rope_cos"] = np.ascontiguousarray(cs, dtype=np.float32)
    c["rope_sin"] = np.ascontiguousarray(sn, dtype=np.float32)
    return c


def host_prep(inp):
    f32 = np.float32
    A = lambda a: np.ascontiguousarray(a, dtype=f32)
    sh = {}
    sh["ada_w"] = A(inp["ada_w"])
    sh["ada_b"] = A(inp["ada_b"])
    sh["ada_bT"] = A(inp["ada_b"].reshape(DEPTH, 48, 128).transpose(0, 2, 1))
    sh["w_in"] = A(inp["w_in"])
    sh["w_out"] = A(inp["w_out"])
    sh["w_up"] = A(inp["ffn_w_up"])
    sh["w_down"] = A(inp["ffn_w_down"])
    sh["qkg"] = A(np.concatenate([np.tile(inp["q_gain"], (1, 8)), np.tile(inp["k_gain"], (1, 2))], axis=1))
    lw = np.concatenate([inp["lru_conv_w"], inp["lru_conv_b"][:, None, :]], axis=1)
    sh["lruw"] = A(lw.reshape(DEPTH, 5, 2, 128).transpose(0, 3, 2, 1))
    W = np.zeros((DEPTH, 2, 2, 2, 128, 128), f32)
    for gi, nm in enumerate(("lru_wa", "lru_wx")):
        w = np.asarray(inp[nm])
        for cc in range(2):
            for h2 in range(2):
                W[:, :, gi, cc, h2 * 64:(h2 + 1) * 64, h2 * 64:(h2 + 1) * 64] = w[:, :, 2 * cc + h2]
    sh["lruW"] = W
    lb = np.stack([inp["lru_ba"], inp["lru_bx"], inp["lru_lambda"]], axis=-1)
    sh["lrub"] = A(lb.reshape(DEPTH, 2, 2, 128, 3).transpose(0, 3, 2, 1, 4))
    fw = np.concatenate([inp["ffn_conv_w"], inp["ffn_conv_b"][:, None, :]], axis=1)
    sh["ffw"] = A(fw.reshape(DEPTH, 4, 44, 128).transpose(0, 3, 2, 1))
    sh["ln1_g"] = A(inp["ln1_g"]); sh["ln1_b"] = A(inp["ln1_b"])
    sh["ln2_g"] = A(inp["ln2_g"]); sh["ln2_b"] = A(inp["ln2_b"])
    hw = np.concatenate([inp["hy_conv_w"], inp["hy_conv_b"][:, None, :]], axis=1)
    sh["hyw"] = A(hw.reshape(DEPTH, 4, 6, 128).transpose(0, 3, 2, 1))
    sh["hy_w1"] = A(inp["hy_w1"]); sh["hy_w2"] = A(inp["hy_w2"]); sh["hy_w3"] = A(inp["hy_w3"])
    sh["hy_wout"] = A(inp["hy_wout"])
    sh["hyb"] = A(np.stack([inp["hy_b1"], inp["hy_b2"], inp["hy_b3"], inp["hy_freq"]], axis=-1))
    sh["hy_bias"] = A(inp["hy_bias"])
    sh.update(_consts())
    return sh


SHARED_SHAPES = {
    "ada_w": [DEPTH, 1024, 6144], "ada_b": [DEPTH, 6144], "ada_bT": [DEPTH, 128, 48],
    "w_in": [DEPTH, 1024, 2048], "w_out": [DEPTH, 1024, 1024], "w_up": [DEPTH, 1024, 2 * DFF],
    "w_down": [DEPTH, DFF, 1024], "qkg": [DEPTH, 640],
    "lruw": [DEPTH, 128, 2, 5], "lruW": [DEPTH, 2, 2, 2, 128, 128], "lrub": [DEPTH, 128, 2, 2, 3],
    "ffw": [DEPTH, 128, 44, 4], "ln1_g": [DEPTH, 1024], "ln1_b": [DEPTH, 1024], "ln2_g": [DEPTH, 1024], "ln2_b": [DEPTH, 1024],
    "hyw": [DEPTH, 128, 6, 4], "hy_w1": [DEPTH, 33, 64], "hy_w2": [DEPTH, 64, 64], "hy_w3": [DEPTH, 64, 64],
    "hy_wout": [DEPTH, 64, 1024], "hyb": [DEPTH, 64, 4], "hy_bias": [DEPTH, 2, 256],
    "hz0": [33, LP], "hz1": [33, LS], "htl0": [1, LP], "htl1": [1, LS],
    "hFN1_0": [64, 66], "hFN1_1": [128, 130], "hT_0": [128, 2, 33], "hT_1": [128, 2, 65],
    "hT2_0": [33, 2, 128], "hT2_1": [65, 2, 128], "hGN1_0": [33, 2, 32], "hGN1_1": [65, 2, 64],
    "hF128": [128, 3, 128], "hG128": [128, 2, 256], "hnegd": [128, 2],
    "ident": [128, 128], "rope_cos": [128, 64, 32], "rope_sin": [128, 64, 32],
}


def core_prep(inp, i):
    c = np.stack([inp["c_prompt"][i], inp["c_sample"][i]], axis=0)
    cT = np.ascontiguousarray(c.reshape(2, 8, 128).transpose(2, 1, 0), dtype=np.float32)
    return {"xp": np.ascontiguousarray(inp["x_prompt"][i], dtype=np.float32),
            "xs": np.ascontiguousarray(inp["x_sample"][i], dtype=np.float32),
            "cT": cT}


def _setup(self):
    nc = self.nc
    dk = "ExternalOutput" if self.dbg.get("dump") else "Internal"
    self.din = {}
    for k, shp in SHARED_SHAPES.items():
        self.din[k] = self.dram(k, shp, F32, kind="ExternalInput")
    self.xin = [self.dram("xp", [LP, D], F32, kind="ExternalInput"),
                self.dram("xs", [LS, D], F32, kind="ExternalInput")]
    self.cT = self.dram("cT", [128, 8, 2], F32, kind="ExternalInput")
    self.yout = [self.dram("yp", [LP, D], F32, kind="ExternalOutput"),
                 self.dram("ys", [LS, D], F32, kind="ExternalOutput")]
    self.Ls = [LP, LS]
    self.xa = [self.dram("xa%d" % s, [L, D], F32, kind=dk) for s, L in enumerate(self.Ls)]
    self.xres = [self.dram("xres%d" % s, [L, D], F32, kind=dk) for s, L in enumerate(self.Ls)]
    self.qT = [self.dram("qT%d" % s, [512, L], BF16, kind=dk) for s, L in enumerate(self.Ls)]
    self.kT = [self.dram("kT%d" % s, [128, L], BF16, kind=dk) for s, L in enumerate(self.Ls)]
    self.vaug = [self.dram("vaug%d" % s, [2, 128, L // 128, 192], BF16, kind=dk) for s, L in enumerate(self.Ls)]
    self.lh = [self.dram("lh%d" % s, [1280, L], F32, kind=dk) for s, L in enumerate(self.Ls)]
    self.catT = [self.dram("catT%d" % s, [1024, L], BF16, kind=dk) for s, L in enumerate(self.Ls)]
    self.hc = [self.dram("hc%d" % s, [768, L], F32, kind=dk) for s, L in enumerate(self.Ls)]
    self.kfull = [self.dram("kfull%d" % s, [2, 256, 2 * L], F32, kind=dk) for s, L in enumerate(self.Ls)]
    self.kfs = [self.dram("kfs%d" % s, [2, 128, 256, 2, L // 128 + 1], BF16, kind=dk) for s, L in enumerate(self.Ls)]
    self.z1 = [self.dram("z1_%d" % s, [256, L], F32, kind=dk) for s, L in enumerate(self.Ls)]
    self.rnd = self.dram("rnd", [2, 256], F32, kind=dk)
    self.init_arena()
    self.ident = self.sb("ident", 128, BF16, glob=True)
    self.csT = self.sb("csT", 16, F32, glob=True)
    self.modT = self.sb("modT", 96, F32, glob=True)
    self.grow = [self.sb("grow%d" % i, 1024, F32, glob=True) for i in range(4)]
    self.epsq = self.sb("epsq", 1, F32, glob=True)
    self.epsl = self.sb("epsl", 1, F32, glob=True)
    self.phase_begin()
    self.dma("pool", self.ident.t, self.din["ident"], wr=[self.ident.b])
    ct = self.sb("ct", 16, F32)
    self.dma("sp", ct.t, self.cT.rearrange("p k s -> p (k s)"), wr=[ct.b])
    self.act(self.csT.t, ct.t, AF.Silu, rd=[ct.b], wr=[self.csT.b])
    self.memset("dve", self.epsq.t, QK_EPS, wr=[self.epsq.b])
    self.memset("dve", self.epsl.t, LN_EPS, wr=[self.epsl.b])
    self.phase_end()


def _phase0(self, l):
    self.wctx = ExitStack()
    wi = self.sb("wi", 8 * 2048, BF16, ctx=self.wctx)
    wi3 = wi.t.rearrange("p (k n) -> p k n", k=8)
    wib = [self.s.buf("wib%d" % k) for k in range(8)]
    self.phase_begin()
    for kc in range(8):
        self.dma("pool", wi3[:, kc, :], self.din["w_in"][l, kc * 128:(kc + 1) * 128, :], wr=[wib[kc]])
    self.pre_wi = (wi, wib)
    adab = self.sb("adab", 48, F32)
    self.dma("sp", adab.t, self.din["ada_bT"][l], wr=[adab.b])
    self.csrep = self.sb("csrep", 16 * 128, F32)
    self.cp("dve", self.csrep.t.rearrange("p (k n) -> p k n", n=128),
            self.csT.t.unsqueeze(2).to_broadcast([128, 16, 128]), rd=[self.csT.b], wr=[self.csrep.b])
    was = [self.sb("wa%d" % i, 8 * 512, F32) for i in range(2)]
    brow = [self.sb("brow%d" % i, 512, F32) for i in range(2)]
    pm = self.ps("pm", 96)
    pgs = [self.ps("pg%d" % i) for i in range(2)]
    npg = 0
    aw = self.din["ada_w"]
    for gi in range(12):
        wa = was[gi % 2]
        src = dap(aw, l * 1024 * 6144 + gi * 512, [(6144, 128), (128 * 6144, 8), (1, 512)])
        self.dma("sp", wa.t.rearrange("p (k c) -> p k c", k=8), src, wr=[wa.b])
        wa3 = wa.t.rearrange("p (k c) -> p k c", k=8)
        for m in range(4):
            ch = gi * 4 + m
            for kc in range(8):
                self.mm(pm.t[:, ch * 2:ch * 2 + 2], wa3[:, kc, m * 128:(m + 1) * 128],
                        self.csT.t[:, kc * 2:kc * 2 + 2], kc == 0, kc == 7, rd=[wa.b, self.csT.b], wr=[pm.b])
        if gi in (4, 5, 10, 11):
            g = 0 if gi < 6 else 1
            half = gi % 2 if gi < 6 else (gi - 10)
            br = brow[half]
            self.dma("sp", br.t, dap(self.din["ada_b"], l * 6144 + gi * 512, [(0, 128), (1, 512)]), wr=[br.b])
            cr4 = self.csrep.t.rearrange("p (k s n) -> p k s n", k=8, s=2)
            for sq in range(2):
                pg = pgs[npg % 2]
                npg += 1
                for kc in range(8):
                    self.mm(pg.t, cr4[:, kc, sq, :], wa3[:, kc, :], kc == 0, kc == 7,
                            rd=[self.csrep.b, wa.b], wr=[pg.b])
                gr = self.grow[g * 2 + sq]
                self.tt("dve", gr.t[:, half * 512:(half + 1) * 512], pg.t, br.t, ALU.add,
                        rd=[pg.b, br.b], wr=[gr.b])
    m3 = self.modT.t.rearrange("p (c s) -> p c s", s=2)
    self.tt("dve", m3, pm.t.rearrange("p (c s) -> p c s", s=2),
            adab.t.unsqueeze(2).to_broadcast([128, 48, 2]), ALU.add, rd=[pm.b, adab.b], wr=[self.modT.b])
    for c0 in (8, 32):
        self.ts("dve", m3[:, c0:c0 + 8, :], m3[:, c0:c0 + 8, :], 1.0, None, ALU.add, None,
                rd=[self.modT.b], wr=[self.modT.b])
    self.phase_end()


def _phase1(self, l, xsrc):
    self.phase_begin()
    wi, wib = self.pre_wi
    wi3 = wi.t.rearrange("p (k n) -> p k n", k=8)
    gain = self.sb("gain", 640, F32)
    self.dma("sp", gain.t, dap(self.din["qkg"], l * 640, [(0, 128), (1, 640)]), wr=[gain.b])
    self.rcos = self.sb("rcos", 64 * 32, F32)
    self.rsin = self.sb("rsin", 64 * 32, F32)
    self.dma("sp", self.rcos.t, self.din["rope_cos"].rearrange("p t c -> p (t c)"), wr=[self.rcos.b])
    self.dma("sp", self.rsin.t, self.din["rope_sin"].rearrange("p t c -> p (t c)"), wr=[self.rsin.b])
    xbs = [self.sb("xb%d" % i, 4 * 1024, BF16) for i in range(2)]
    uTs = [self.sb("uT%d" % i, 8 * 512, BF16) for i in range(2)]
    lhs_ = [self.sb("lhs%d" % i, 10 * 512, F32) for i in range(2)]
    lhb = [[self.s.buf() for m in range(10)] for i in range(2)]
    sq_ = [self.sb("sq%d" % i, 640, F32) for i in range(2)]
    ss_ = [self.sb("ss%d" % i, 10, F32) for i in range(2)]
    rstd_ = [self.sb("rstd%d" % i, 10, F32) for i in range(2)]
    qn_ = [self.sb("qn%d" % i, 640, F32) for i in range(2)]
    tmp_ = [[self.sb("tmp%d%d" % (b, i), 320, F32) for i in range(4)] for b in range(2)]
    qkb_ = [self.sb("qkb%d" % i, 640, BF16) for i in range(2)]
    jn = 0
    qkTs = [self.sb("qkT%d" % i, 5 * 512, BF16) for i in range(2)]
    vgs = [self.sb("vg%d" % i, 2 * 4 * 192, BF16) for i in range(2)]
    for vg in vgs:
        self.memset("dve", vg.t, 1.0, wr=[vg.b])
    pts = [self.ps("pt%d" % i, 512, BF16) for i in range(2)]
    pfs = [self.ps("pf%d" % i) for i in range(2)]
    psq_ = [self.ps("psq%d" % i) for i in range(2)]
    pskv = self.ps("pskv", 256)
    pT = self.ps("pT", 640, BF16)
    m3 = self.modT.t.rearrange("p (c s) -> p c s", s=2)
    rc3 = self.rcos.t.rearrange("p (t c) -> p t c", c=32)
    rs3 = self.rsin.t.rearrange("p (t c) -> p t c", c=32)
    it = 0
    for s in range(2):
        L = self.Ls[s]
        NT = L // 128
        for w in range(L // 512):
            xb, uT, lh, qkT, vg = xbs[it % 2], uTs[it % 2], lhs_[it % 2], qkTs[it % 2], vgs[it % 2]
            lb = lhb[it % 2]
            it += 1
            xb3 = xb.t.rearrange("p (j c) -> p j c", j=4)
            uT3 = uT.t.rearrange("p (k t) -> p k t", k=8)
            lh3 = lh.t.rearrange("p (m t) -> p m t", m=10)
            qkT3 = qkT.t.rearrange("p (c t) -> p c t", c=5)
            vg4 = vg.t.rearrange("p (g j c) -> p g j c", g=2, j=4)
            self.dma("pool", xb3, dap(xsrc[s], w * 512 * D, [(D, 128), (128 * D, 4), (1, D)]), wr=[xb.b])
            for kc in range(8):
                pt = pts[kc % 2]
                for j in range(4):
                    self.tr(pt.t[:, j * 128:(j + 1) * 128], xb3[:, j, kc * 128:(kc + 1) * 128], self.ident.t,
                            rd=[xb.b, self.ident.b], wr=[pt.b])
                self.act(uT3[:, kc, :], pt.t, AF.Identity, rd=[pt.b, self.modT.b], wr=[uT.b],
                         scale=m3[:, 8 + kc, s:s + 1], bias=m3[:, kc, s:s + 1])
            for m in range(10):
                pf = pfs[m % 2]
                for kc in range(8):
                    self.mm(pf.t, wi3[:, kc, 768 + m * 128:768 + (m + 1) * 128], uT3[:, kc, :], kc == 0, kc == 7,
                            rd=[wib[kc], uT.b], wr=[pf.b])
                self.cp("act" if m % 2 else "dve", lh3[:, m, :], pf.t, rd=[pf.b], wr=[lb[m]])
            self.dma("sp", dap(self.lh[s], w * 512, [(L, 128), (128 * L, 10), (1, 512)]), lh3, rd=lb)
            for j in range(4):
                T = w * 4 + j
                jb = jn % 2
                jn += 1
                sq, ss, rstd, qn, tmp, qkb, psq = sq_[jb], ss_[jb], rstd_[jb], qn_[jb], tmp_[jb], qkb_[jb], psq_[jb]
                for kc in range(8):
                    self.mm(psq.t, uT3[:, kc, j * 128:(j + 1) * 128], wi3[:, kc, 0:512], kc == 0, kc == 7,
                            rd=[wib[kc], uT.b], wr=[psq.b])
                for kc in range(8):
                    self.mm(pskv.t, uT3[:, kc, j * 128:(j + 1) * 128], wi3[:, kc, 512:768], kc == 0, kc == 7,
                            rd=[wib[kc], uT.b], wr=[pskv.b])
                self.act(sq.t[:, 0:512], psq.t, AF.Square, rd=[psq.b], wr=[sq.b])
                self.act(sq.t[:, 512:640], pskv.t[:, 0:128], AF.Square, rd=[pskv.b], wr=[sq.b])
                self.s.op("dve", (lambda e, o=ss.t, i=sq.t.rearrange("p (h d) -> p h d", d=64):
                                  e.tensor_reduce(out=o, in_=i, axis=AX.X, op=ALU.add)), rd=[sq.b], wr=[ss.b])
                self.act(rstd.t, ss.t, AF.Sqrt, rd=[ss.b, self.epsq.b], wr=[rstd.b], scale=1.0 / 64, bias=self.epsq.t)
                self.recip(rstd.t, rstd.t, rd=[rstd.b], wr=[rstd.b])
                qn3 = qn.t.rearrange("p (h d) -> p h d", d=64)
                self.tt("dve", qn3[:, 0:8, :], psq.t.rearrange("p (h d) -> p h d", d=64),
                        rstd.t[:, 0:8].unsqueeze(2).to_broadcast([128, 8, 64]), ALU.mult,
                        rd=[psq.b, rstd.b], wr=[qn.b])
                self.tt("dve", qn3[:, 8:10, :], pskv.t[:, 0:128].rearrange("p (h d) -> p h d", d=64),
                        rstd.t[:, 8:10].unsqueeze(2).to_broadcast([128, 2, 64]), ALU.mult,
                        rd=[pskv.b, rstd.b], wr=[qn.b])
                self.tt("dve", qn.t, qn.t, gain.t, ALU.mult, rd=[qn.b, gain.b], wr=[qn.b])
                for c0 in (0, 128):
                    self.cp("act", vg4[:, :, j, c0:c0 + 64], pskv.t[:, 128:256].rearrange("p (g d) -> p g d", g=2),
                            rd=[pskv.b], wr=[vg.b])
                qn5 = qn.t.rearrange("p (h a two f) -> p h a two f", h=10, a=2, two=2)
                qb5 = qkb.t.rearrange("p (h a two f) -> p h a two f", h=10, a=2, two=2)
                x1, x2 = qn5[:, :, :, 0, :], qn5[:, :, :, 1, :]
                cb = rc3[:, T, :].rearrange("p (a f) -> p a f", a=2).unsqueeze(1).to_broadcast([128, 10, 2, 16])
                sb_ = rs3[:, T, :].rearrange("p (a f) -> p a f", a=2).unsqueeze(1).to_broadcast([128, 10, 2, 16])
                t4 = [t.t.rearrange("p (h a f) -> p h a f", h=10, a=2) for t in tmp]
                self.tt("dve", t4[0], x1, cb, ALU.mult, rd=[qn.b, self.rcos.b], wr=[tmp[0].b])
                self.tt("dve", t4[1], x2, sb_, ALU.mult, rd=[qn.b, self.rsin.b], wr=[tmp[1].b])
                self.tt("dve", qb5[:, :, :, 0, :], t4[0], t4[1], ALU.subtract, rd=[tmp[0].b, tmp[1].b], wr=[qkb.b])
                self.tt("dve", t4[2], x2, cb, ALU.mult, rd=[qn.b, self.rcos.b], wr=[tmp[2].b])
                self.tt("dve", t4[3], x1, sb_, ALU.mult, rd=[qn.b, self.rsin.b], wr=[tmp[3].b])
                self.tt("dve", qb5[:, :, :, 1, :], t4[2], t4[3], ALU.add, rd=[tmp[2].b, tmp[3].b], wr=[qkb.b])
                for c in range(5):
                    self.tr(pT.t[:, c * 128:(c + 1) * 128], qkb.t[:, c * 128:(c + 1) * 128], self.ident.t,
                            rd=[qkb.b, self.ident.b], wr=[pT.b])
                self.cp("act", qkT3[:, :, j * 128:(j + 1) * 128], pT.t.rearrange("p (c t) -> p c t", c=5),
                        rd=[pT.b], wr=[qkT.b])
            self.dma("sp", dap(self.qT[s], w * 512, [(L, 128), (128 * L, 4), (1, 512)]), qkT3[:, 0:4, :], rd=[qkT.b])
            self.dma("sp", self.kT[s][:, w * 512:(w + 1) * 512], qkT3[:, 4, :], rd=[qkT.b])
            self.dma("sp", dap(self.vaug[s], w * 4 * 192, [(NT * 192, 128), (128 * NT * 192, 2), (1, 768)]),
                     vg.t.rearrange("p (g c) -> p g c", g=2), rd=[vg.b])
    self.phase_end()
    self.wctx.close()


K.setup = _setup
K.phase0 = _phase0
K.phase1 = _phase1


def _phase2(self, l):
    self.phase_begin()
    Kd = self.sb("Kd", LS, BF16)
    Va = self.sb("Va", 64 * 192, BF16)
    Q2 = self.sb("Q2", 2 * LS, BF16)
    PAB = [self.sb("PAB%d" % i, 1024, BF16) for i in range(3)]
    rr = self.sb("rr", 512, F32)
    rAb, rBb = self.s.buf("rA"), self.s.buf("rB")
    atts = [self.sb("att%d" % i, 512, BF16) for i in range(2)]
    psAB = [self.ps2("psAB%d" % i) for i in range(2)]
    oA = [self.ps("oA%d" % i) for i in range(2)]
    oB = [self.ps("oB%d" % i) for i in range(2)]
    nblk = 0
    for s in range(2):
        L = self.Ls[s]
        NT = L // 128
        for g in range(2):
            self.dma("sp", Kd.t[0:64, 0:L], self.kT[s][g * 64:(g + 1) * 64, :], wr=[Kd.b])
            self.dma("sp", Kd.t[64:128, 0:L], self.kT[s][g * 64:(g + 1) * 64, :], wr=[Kd.b])
            self.dma("sp", Va.t[:, 0:NT * 192], self.vaug[s][g].rearrange("p t c -> p (t c)"), wr=[Va.b])
            Q3 = Q2.t.rearrange("p (h t) -> p h t", h=2)
            self.dma("sp", Q3[:, :, 0:L], dap(self.qT[s], 2 * g * 128 * L, [(L, 128), (128 * L, 2), (1, L)]), wr=[Q2.b])
            Va3 = Va.t.rearrange("p (t c) -> p t c", c=192)
            steps = [(qc, hp, st) for qc in range(L // 512) for hp in range(2) for st in range(NT)]

            def qk(i):
                qc, hp, st = steps[i]
                ib = i % 2
                self.mm(psAB[ib].t[:, 0:512], Kd.t[0:64, st * 128:(st + 1) * 128], Q3[0:64, hp, qc * 512:(qc + 1) * 512],
                        True, True, rd=[Kd.b, Q2.b], wr=[psAB[ib].b], tp=(0, 0))
                self.mm(psAB[ib].t[:, 512:1024], Kd.t[64:128, st * 128:(st + 1) * 128], Q3[64:128, hp, qc * 512:(qc + 1) * 512],
                        True, True, rd=[Kd.b, Q2.b], wr=[psAB[ib].b], tp=(64, 0))
            qk(0)
            for i, (qc, hp, st) in enumerate(steps):
                if i + 1 < len(steps):
                    qk(i + 1)
                ib, ip = i % 2, i % 3
                if st == 0:
                    nblk += 1
                io = nblk % 2
                self.act(PAB[ip].t, psAB[ib].t, AF.Exp, rd=[psAB[ib].b], wr=[PAB[ip].b], scale=0.125)
                self.mm(oA[io].t, Va3[:, st, 0:128], PAB[ip].t[:, 0:512], st == 0, st == NT - 1, rd=[Va.b, PAB[ip].b], wr=[oA[io].b])
                self.mm(oB[io].t, Va3[:, st, 64:192], PAB[ip].t[:, 512:1024], st == 0, st == NT - 1, rd=[Va.b, PAB[ip].b], wr=[oB[io].b])
                if st == NT - 1:
                    att = atts[io]
                    self.recip(rr.t[64:128, :], oA[io].t[64:128, :], rd=[oA[io].b], wr=[rAb])
                    self.tt("dve", att.t[0:64, :], oA[io].t[0:64, :], rr.t[64:128, :], ALU.mult,
                            rd=[oA[io].b, rAb], wr=[att.b])
                    self.recip(rr.t[0:64, :], oB[io].t[0:64, :], rd=[oB[io].b], wr=[rBb])
                    self.tt("dve", att.t[64:128, :], oB[io].t[64:128, :], rr.t[0:64, :], ALU.mult,
                            rd=[oB[io].b, rBb], wr=[att.b])
                    self.dma("sp", self.catT[s][(2 * g + hp) * 128:(2 * g + hp + 1) * 128, qc * 512:(qc + 1) * 512],
                             att.t, rd=[att.b])
    self.phase_end()


K.phase2 = _phase2


def _phase3(self, l):
    self.phase_begin()
    TC = 1024
    XC = self.sb("XC", LS, F32)
    XCB = self.sb("XCB", LS, BF16)
    HF = self.sb("HF", LS, F32)
    WA = self.sb("WA", 8 * 128, BF16)
    WA5 = WA.t.rearrange("p (d g c n) -> p d g c n", d=2, g=2, c=2)
    self.dma("pool", WA5, self.din["lruW"][l].rearrange("d g c k n -> k d g c n"), wr=[WA.b])
    cw = self.sb("cw", 10, F32)
    self.dma("sp", cw.t, self.din["lruw"][l].rearrange("p c k -> p (c k)"), wr=[cw.b])
    lb = self.sb("lb", 12, F32)
    self.dma("sp", lb.t, self.din["lrub"][l].rearrange("p c d k -> p (c d k)"), wr=[lb.b])
    lb4 = lb.t.rearrange("p (c d k) -> p c d k", c=2, d=2)
    cw3 = cw.t.rearrange("p (c k) -> p c k", c=2)
    sp_ = self.sb("sp", 4, F32)
    c12 = self.sb("c12", 8, F32)
    sp3 = sp_.t.rearrange("p (c d) -> p c d", c=2)
    self.act(sp3, lb4[:, :, :, 2], AF.Exp, rd=[lb.b], wr=[sp_.b], scale=-1.0)
    self.act(sp_.t, sp_.t, AF.Ln, rd=[sp_.b], wr=[sp_.b], bias=1.0)
    self.ts("dve", c12.t[:, 0:4], sp_.t, -8.0, None, ALU.mult, None, rd=[sp_.b], wr=[c12.b])
    self.ts("dve", c12.t[:, 4:8], sp_.t, -16.0, None, ALU.mult, None, rd=[sp_.b], wr=[c12.b])
    c4 = c12.t.rearrange("p (k c d) -> p k c d", k=2, c=2)
    xhs = [self.sb("xh%d" % i, TC + 3, F32) for i in range(2)]
    tR = [self.sb("tR%d" % i, TC, F32) for i in range(2)]
    tI = [self.sb("tI%d" % i, TC, F32) for i in range(2)]
    tA = [self.sb("tA%d" % i, TC, F32) for i in range(2)]
    tT = [self.sb("tT%d" % i, TC, F32) for i in range(2)]
    tB = [self.sb("tB%d" % i, TC, F32) for i in range(2)]
    tH = [self.sb("tH%d" % i, TC, F32) for i in range(2)]
    gs = [self.sb("g%d" % i, TC, F32) for i in range(2)]
    ggs = [self.sb("gg%d" % i, TC, F32) for i in range(2)]
    obs = [self.sb("ob%d" % i, TC, BF16) for i in range(2)]
    carry = self.sb("carry", 1, F32)
    prs = [self.ps("pr%d" % i) for i in range(4)]
    pis = [self.ps("pi%d" % i) for i in range(4)]
    it = 0
    nps = 0
    for s in range(2):
        L = self.Ls[s]
        NCH = L // TC
        for cc in range(2):
            for c in range(NCH):
                xh = xhs[it % 2]
                it += 1
                t0 = c * TC
                lo = max(t0 - 2, 0)
                hi = min(t0 + TC + 1, L)
                if c == 0:
                    self.memset("dve", xh.t[:, 0:2], 0.0, wr=[xh.b])
                if c == NCH - 1:
                    self.memset("dve", xh.t[:, TC + 2:TC + 3], 0.0, wr=[xh.b])
                self.dma("sp", xh.t[:, lo - (t0 - 2):hi - (t0 - 2)], self.lh[s][cc * 128:(cc + 1) * 128, lo:hi], wr=[xh.b])
                xc = XC.t[:, t0:t0 + TC]
                self.ts("dve", xc, xh.t[:, 0:TC], cw3[:, cc, 0:1], cw3[:, cc, 4:5], ALU.mult, ALU.add,
                        rd=[xh.b, cw.b], wr=[XC.b])
                for k in range(1, 4):
                    self.stt("dve", xc, xh.t[:, k:k + TC], cw3[:, cc, k:k + 1], xc, ALU.mult, ALU.add,
                             rd=[xh.b, cw.b, XC.b], wr=[XC.b])
                self.cp("act", XCB.t[:, t0:t0 + TC], xc, rd=[XC.b], wr=[XCB.b])
            for d in range(2):
                order = list(range(NCH)) if d == 0 else list(range(NCH - 1, -1, -1))
                for ci, c in enumerate(order):
                    i2 = it % 2
                    it += 1
                    t0 = c * TC
                    for sub in range(2):
                        pr, pi = prs[nps % 4], pis[nps % 4]
                        nps += 1
                        cols = slice(t0 + sub * 512, t0 + (sub + 1) * 512)
                        self.mm(pr.t, WA5[:, d, 0, cc, :], XCB.t[:, cols], True, True, rd=[WA.b, XCB.b], wr=[pr.b])
                        self.mm(pi.t, WA5[:, d, 1, cc, :], XCB.t[:, cols], True, True, rd=[WA.b, XCB.b], wr=[pi.b])
                        self.act(tR[i2].t[:, sub * 512:(sub + 1) * 512], pr.t, AF.Sigmoid, rd=[pr.b, lb.b], wr=[tR[i2].b],
                                 bias=lb4[:, cc, d, 0:1])
                        self.act(tI[i2].t[:, sub * 512:(sub + 1) * 512], pi.t, AF.Sigmoid, rd=[pi.b, lb.b], wr=[tI[i2].b],
                                 bias=lb4[:, cc, d, 1:2])
                    self.act(tA[i2].t, tR[i2].t, AF.Exp, rd=[tR[i2].b, c12.b], wr=[tA[i2].b], scale=c4[:, 0, cc, d:d + 1])
                    self.act(tT[i2].t, tR[i2].t, AF.Exp, rd=[tR[i2].b, c12.b], wr=[tT[i2].b], scale=c4[:, 1, cc, d:d + 1])
                    self.act(tT[i2].t, tT[i2].t, AF.Sqrt, rd=[tT[i2].b], wr=[tT[i2].b], scale=-1.0, bias=1.0)
                    self.tt("dve", tB[i2].t, tI[i2].t, XC.t[:, t0:t0 + TC], ALU.mult, rd=[tI[i2].b, XC.b], wr=[tB[i2].b])
                    self.tt("dve", tB[i2].t, tB[i2].t, tT[i2].t, ALU.mult, rd=[tB[i2].b, tT[i2].b], wr=[tB[i2].b])
                    if d == 0:
                        init = 0.0 if ci == 0 else HF.t[:, t0 - 1:t0]
                        self.s.op("dve", (lambda e, o=HF.t[:, t0:t0 + TC], a=tA[i2].t, b=tB[i2].t, i0=init:
                                          e.tensor_tensor_scan(out=o, data0=a, data1=b, initial=i0, op0=ALU.mult, op1=ALU.add)),
                                  rd=[tA[i2].b, tB[i2].b, HF.b], wr=[HF.b])
                    else:
                        init = 0.0 if ci == 0 else carry.t
                        self.s.op("dve", (lambda e, o=rev(tH[i2].t), a=rev(tA[i2].t), b=rev(tB[i2].t), i0=init:
                                          e.tensor_tensor_scan(out=o, data0=a, data1=b, initial=i0, op0=ALU.mult, op1=ALU.add)),
                                  rd=[tA[i2].b, tB[i2].b, carry.b], wr=[tH[i2].b])
                        self.cp("dve", carry.t, tH[i2].t[:, 0:1], rd=[tH[i2].b], wr=[carry.b])
                        self.tt("dve", HF.t[:, t0:t0 + TC], HF.t[:, t0:t0 + TC], tH[i2].t, ALU.add,
                                rd=[HF.b, tH[i2].b], wr=[HF.b])
            for c in range(NCH):
                i2 = it % 2
                it += 1
                t0 = c * TC
                self.dma("sp", gs[i2].t, self.lh[s][256 + cc * 128:256 + (cc + 1) * 128, t0:t0 + TC], wr=[gs[i2].b])
                self.act(ggs[i2].t, gs[i2].t, AF.Gelu, rd=[gs[i2].b], wr=[ggs[i2].b])
                self.tt("dve", obs[i2].t, HF.t[:, t0:t0 + TC], ggs[i2].t, ALU.mult, rd=[HF.b, ggs[i2].b], wr=[obs[i2].b])
                self.dma("sp", self.catT[s][512 + cc * 128:512 + (cc + 1) * 128, t0:t0 + TC], obs[i2].t, rd=[obs[i2].b])
    self.phase_end()


K.phase3 = _phase3


def _ln_tail(self, po, n, xsrc_rows, gate, lg, lbias, dst_rows, T):
    i2 = T["i"] % 2
    T["i"] += 1
    xr, ysb, st, mv, rs, nm = T["xr"][i2], T["y"][i2], T["st"][i2], T["mv"][i2], T["rs"][i2], T["nm"][i2]
    self.dma("pool", xr.t[0:n, :], xsrc_rows, wr=[xr.b])
    for h in range(2):
        self.tt("dve", ysb.t[0:n, h * 512:(h + 1) * 512], po[h].t[0:n, :], gate.t[0:n, h * 512:(h + 1) * 512], ALU.mult,
                rd=[po[h].b, gate.b], wr=[ysb.b])
    self.stt("dve", ysb.t[0:n, :], xr.t[0:n, :], ALPHA, ysb.t[0:n, :], ALU.mult, ALU.add, rd=[xr.b, ysb.b], wr=[ysb.b])
    st3 = st.t.rearrange("p (c k) -> p c k", c=2)
    for h in range(2):
        self.s.op("dve", (lambda e, o=st3[0:n, h, :], i=ysb.t[0:n, h * 512:(h + 1) * 512]: e.bn_stats(out=o, in_=i)),
                  rd=[ysb.b], wr=[st.b])
    self.s.op("dve", (lambda e, o=mv.t[0:n, :], i=st3[0:n, :, :]: e.bn_aggr(out=o, in_=i)), rd=[st.b], wr=[mv.b])
    self.act(rs.t[0:n, :], mv.t[0:n, 1:2], AF.Sqrt, rd=[mv.b, self.epsl.b], wr=[rs.b], bias=self.epsl.t[0:n, :])
    self.recip(rs.t[0:n, :], rs.t[0:n, :], rd=[rs.b], wr=[rs.b])
    self.ts("dve", nm.t[0:n, :], mv.t[0:n, 0:1], -1.0, rs.t[0:n, :], ALU.mult, ALU.mult, rd=[mv.b, rs.b], wr=[nm.b])
    self.act(ysb.t[0:n, :], ysb.t[0:n, :], AF.Identity, rd=[ysb.b, rs.b, nm.b], wr=[ysb.b],
             scale=rs.t[0:n, :], bias=nm.t[0:n, :])
    self.tt("pool", ysb.t[0:n, :], ysb.t[0:n, :], lg.t[0:n, :], ALU.mult, rd=[ysb.b, lg.b], wr=[ysb.b])
    self.tt("pool", xr.t[0:n, :], ysb.t[0:n, :], lbias.t[0:n, :], ALU.add, rd=[ysb.b, lbias.b], wr=[xr.b])
    self.dma("sp", dst_rows, xr.t[0:n, :], rd=[xr.b])


def _ln_bufs(self):
    return {"i": 0, "xr": [self.sb("lnx%d" % i, 1024, F32) for i in range(2)],
            "y": [self.sb("lny%d" % i, 1024, F32) for i in range(2)],
            "st": [self.sb("lnst%d" % i, 12, F32) for i in range(2)], "mv": [self.sb("lnmv%d" % i, 2, F32) for i in range(2)],
            "rs": [self.sb("lnrs%d" % i, 1, F32) for i in range(2)], "nm": [self.sb("lnnm%d" % i, 1, F32) for i in range(2)]}


def _phase5(self, l, xsrc):
    self.wctx = ExitStack()
    wu = self.sb("wu", 8 * 2 * DFF, BF16, ctx=self.wctx)
    wu3 = wu.t.rearrange("p (k n) -> p k n", k=8)
    wub = [self.s.buf() for k in range(8)]
    self.pre_wu = (wu, wub)
    self.phase_begin()
    wo = self.sb("wo", 8 * 1024, BF16)
    wo3 = wo.t.rearrange("p (k n) -> p k n", k=8)
    wob = [self.s.buf() for k in range(8)]
    for kc in range(8):
        self.dma("pool", wo3[:, kc, :], self.din["w_out"][l, kc * 128:(kc + 1) * 128, :], wr=[wob[kc]])
    lg = self.sb("lg", 1024, F32)
    lb = self.sb("lb", 1024, F32)
    self.dma("sp", lg.t, dap(self.din["ln1_g"], l * 1024, [(0, 128), (1, 1024)]), wr=[lg.b])
    self.dma("sp", lb.t, dap(self.din["ln1_b"], l * 1024, [(0, 128), (1, 1024)]), wr=[lb.b])
    for kc in range(8):
        self.dma("pool", wu3[:, kc, :], self.din["w_up"][l, kc * 128:(kc + 1) * 128, :], wr=[wub[kc]])
    T = self.ln_bufs()
    cts = [self.sb("ct%d" % i, 8 * 512, BF16) for i in range(2)]
    pos = [[self.ps("po%d%d" % (i, h)) for h in range(2)] for i in range(2)]
    it = 0
    nj = 0
    for s in range(2):
        L = self.Ls[s]
        for w in range(L // 512):
            ct = cts[it % 2]
            it += 1
            ct3 = ct.t.rearrange("p (k t) -> p k t", k=8)
            self.dma("sp", ct3, dap(self.catT[s], w * 512, [(L, 128), (128 * L, 8), (1, 512)]), wr=[ct.b])
            for j in range(4):
                po = pos[nj % 2]
                nj += 1
                for h in range(2):
                    for kc in range(8):
                        self.mm(po[h].t, ct3[:, kc, j * 128:(j + 1) * 128], wo3[:, kc, h * 512:(h + 1) * 512],
                                kc == 0, kc == 7, rd=[ct.b, wob[kc]], wr=[po[h].b])
                r0 = w * 512 + j * 128
                self.ln_tail(po, 128, xsrc[s][r0:r0 + 128, :], self.grow[0 * 2 + s], lg, lb,
                             self.xres[s][r0:r0 + 128, :], T)
    self.phase_end()


def _phase6(self, l, dst):
    self.phase_begin()
    WN = 254
    wu, wub = self.pre_wu
    wu3 = wu.t.rearrange("p (k n) -> p k n", k=8)
    wd = self.sb("wd", 22 * 1024, BF16)
    wd3 = wd.t.rearrange("p (k n) -> p k n", k=22)
    wdb = [self.s.buf() for k in range(22)]
    for kc in range(22):
        self.dma("pool", wd3[:, kc, :], self.din["w_down"][l, kc * 128:(kc + 1) * 128, :], wr=[wdb[kc]])
    cw = self.sb("fcw", 44 * 4, F32)
    self.dma("sp", cw.t, self.din["ffw"][l].rearrange("p c k -> p (c k)"), wr=[cw.b])
    cw3 = cw.t.rearrange("p (c k) -> p c k", k=4)
    lg = self.sb("lg", 1024, F32)
    lb = self.sb("lb", 1024, F32)
    self.dma("sp", lg.t, dap(self.din["ln2_g"], l * 1024, [(0, 128), (1, 1024)]), wr=[lg.b])
    self.dma("sp", lb.t, dap(self.din["ln2_b"], l * 1024, [(0, 128), (1, 1024)]), wr=[lb.b])
    T = self.ln_bufs()
    xbs = [self.sb("fxb%d" % i, 2 * 1024, BF16) for i in range(2)]
    uT = self.sb("fuT", 8 * 256, BF16)
    uT3 = uT.t.rearrange("p (k t) -> p k t", k=8)
    g = self.sb("fg", 22 * 256, BF16)
    g3 = g.t.rearrange("p (k t) -> p k t", k=22)
    tgs = [self.sb("tg%d" % i, 256, F32) for i in range(2)]
    tvs = [self.sb("tv%d" % i, 256, F32) for i in range(2)]
    pts = [self.ps("fpt%d" % i, 256, BF16) for i in range(2)]
    pgs = [self.ps("fpg%d" % i, 256) for i in range(2)]
    pvs = [self.ps("fpv%d" % i, 256) for i in range(2)]
    po = [self.ps("fpo%d" % h) for h in range(2)]
    m3 = self.modT.t.rearrange("p (c s) -> p c s", s=2)
    wins = [(s, w) for s in range(2) for w in range((self.Ls[s] + WN - 1) // WN)]

    def load_win(idx):
        s, w = wins[idx]
        L = self.Ls[s]
        t0 = w * WN
        xb = xbs[idx % 2]
        xb3 = xb.t.rearrange("p (j c) -> p j c", j=2)
        if (w == 0) or (t0 + 255 > L):
            self.memset("dve", xb.t, 0.0, wr=[xb.b])
        for j in range(2):
            a = t0 - 1 + 128 * j
            lo, hi = max(a, 0), min(a + 128, L)
            if hi > lo:
                self.dma("pool", xb3[lo - a:hi - a, j, :], self.xres[s][lo:hi, :], wr=[xb.b])

    load_win(0)
    for idx, (s, w) in enumerate(wins):
        if True:
            L = self.Ls[s]
            xs_ = self.xres[s]
            t0 = w * WN
            nv = min(WN, L - t0)
            xb = xbs[idx % 2]
            xb3 = xb.t.rearrange("p (j c) -> p j c", j=2)
            for kc in range(8):
                pt = pts[kc % 2]
                for j in range(2):
                    self.tr(pt.t[:, j * 128:(j + 1) * 128], xb3[:, j, kc * 128:(kc + 1) * 128], self.ident.t,
                            rd=[xb.b, self.ident.b], wr=[pt.b])
                self.act(uT3[:, kc, :], pt.t, AF.Identity, rd=[pt.b, self.modT.b], wr=[uT.b],
                         scale=m3[:, 32 + kc, s:s + 1], bias=m3[:, 24 + kc, s:s + 1])
            if idx + 1 < len(wins):
                load_win(idx + 1)
            if w == 0:
                self.memset("dve", uT3[:, :, 0:1], 0.0, wr=[uT.b])
            if t0 + nv >= L:
                c0 = L - (t0 - 1)
                self.memset("dve", uT3[:, :, c0:256], 0.0, wr=[uT.b])
            for jc in range(22):
                pg, pv = pgs[jc % 2], pvs[jc % 2]
                tg, tv = tgs[jc % 2], tvs[jc % 2]
                for kc in range(8):
                    self.mm(pg.t, wu3[:, kc, jc * 128:(jc + 1) * 128], uT3[:, kc, :], kc == 0, kc == 7,
                            rd=[wub[kc], uT.b], wr=[pg.b])
                for kc in range(8):
                    self.mm(pv.t, wu3[:, kc, DFF + jc * 128:DFF + (jc + 1) * 128], uT3[:, kc, :], kc == 0, kc == 7,
                            rd=[wub[kc], uT.b], wr=[pv.b])
                for (pp, tt_, ch) in ((pg, tg, jc), (pv, tv, 22 + jc)):
                    self.act(tt_.t[:, 0:WN], pp.t[:, 0:WN], AF.Identity, rd=[pp.b, cw.b], wr=[tt_.b],
                             scale=cw3[:, ch, 0:1], bias=cw3[:, ch, 3:4])
                    for k in (1, 2):
                        self.stt("dve", tt_.t[:, 0:WN], pp.t[:, k:k + WN], cw3[:, ch, k:k + 1], tt_.t[:, 0:WN],
                                 ALU.mult, ALU.add, rd=[pp.b, cw.b, tt_.b], wr=[tt_.b])
                self.act(tg.t[:, 0:WN], tg.t[:, 0:WN], AF.Gelu, rd=[tg.b], wr=[tg.b])
                self.tt("dve", g3[:, jc, 0:WN], tg.t[:, 0:WN], tv.t[:, 0:WN], ALU.mult, rd=[tg.b, tv.b], wr=[g.b])
            for c0 in (0, 128):
                n = min(128, nv - c0)
                if n <= 0:
                    continue
                for h in range(2):
                    for kc in range(22):
                        self.mm(po[h].t[0:n, :], g3[:, kc, c0:c0 + n], wd3[:, kc, h * 512:(h + 1) * 512],
                                kc == 0, kc == 21, rd=[g.b, wdb[kc]], wr=[po[h].b])
                r0 = t0 + c0
                self.ln_tail(po, n, xs_[r0:r0 + n, :], self.grow[1 * 2 + s], lg, lb, dst[s][r0:r0 + n, :], T)
    self.phase_end()
    self.wctx.close()


K.ln_tail = _ln_tail
K.ln_bufs = _ln_bufs
K.phase5 = _phase5
K.phase6 = _phase6


def _sync(self):
    self.s.barrier()


def _phase4(self, l):
    self.phase_begin()
    TWO_PI = 2.0 * math.pi
    F128 = self.sb("F128", 3 * 128, BF16)
    self.dma("pool", F128.t, self.din["hF128"].rearrange("p r k -> p (r k)"), wr=[F128.b])
    F3 = F128.t.rearrange("p (r k) -> p r k", r=3)
    G128 = self.sb("G128", 2 * 256, BF16)
    self.dma("pool", G128.t, self.din["hG128"].rearrange("p r k -> p (r k)"), wr=[G128.b])
    G3 = G128.t.rearrange("p (r k) -> p r k", r=2)
    hyw = self.sb("hyw", 24, F32)
    self.dma("sp", hyw.t, self.din["hyw"][l].rearrange("p c k -> p (c k)"), wr=[hyw.b])
    hyw3 = hyw.t.rearrange("p (c k) -> p c k", k=4)
    drow = self.sb("drow", 512, F32)
    self.dma("sp", drow.t, dap(self.din["hy_bias"], l * 512, [(0, 128), (1, 512)]), wr=[drow.b])
    negd = self.sb("negd", 2, F32)
    self.dma("sp", negd.t, self.din["hnegd"], wr=[negd.b])
    w1 = self.sb("hw1", 64, F32)
    w2 = self.sb("hw2", 64, F32)
    w3 = self.sb("hw3", 64, F32)
    wout = self.sb("hwout", 1024, F32)
    hyb = self.sb("hyb", 4, F32)
    self.dma("sp", w1.t[0:33, :], self.din["hy_w1"][l], wr=[w1.b])
    self.dma("sp", w2.t[0:64, :], self.din["hy_w2"][l], wr=[w2.b])
    self.dma("sp", w3.t[0:64, :], self.din["hy_w3"][l], wr=[w3.b])
    self.dma("sp", wout.t[0:64, :], self.din["hy_wout"][l], wr=[wout.b])
    self.dma("sp", hyb.t[0:64, :], self.din["hyb"][l], wr=[hyb.b])
    bfq = self.sb("bfq", 3, F32)
    self.ts("dve", bfq.t[0:64, :], hyb.t[0:64, 0:3], hyb.t[0:64, 3:4], None, ALU.mult, None, rd=[hyb.b], wr=[bfq.b])
    zero1 = self.sb("zero1", 1, F32)
    self.memset("dve", zero1.t, 0.0, wr=[zero1.b])
    FN1 = self.sb("FN1", 256, BF16)
    Tt = self.sb("Tt", 256, F32)
    T2 = self.sb("T2", 256, F32)
    GN1 = self.sb("GN1", 128, BF16)
    rnrow = self.sb("rnrow", 512, F32)
    xhs = [self.sb("hxh%d" % i, 2050, F32) for i in range(2)]
    cos_ = [self.sb("hco%d" % i, 2048, F32) for i in range(2)]
    zcs = [self.sb("hzc%d" % i, 512, F32) for i in range(2)]
    tls = [self.sb("htl%d" % i, 512, F32) for i in range(2)]
    decs = [[self.sb("hdec%d%d" % (i, c), 512, F32) for c in range(2)] for i in range(2)]
    ya = self.sb("hya", 512, F32)
    kk = self.sb("hkk", 512, F32)
    hks = [self.sb("hk%d" % i, 512, F32) for i in range(3)]
    kts = [self.sb("hkt%d" % i, 512, F32) for i in range(3)]
    kab = self.sb("hkab", 512, F32)
    acc = self.sb("hacc", 8 * 16, F32)
    kb0 = self.sb("hkb0", 4, F32)
    nrm = self.sb("hnrm", 4, F32)
    Xbs = [self.sb("hXb%d" % i, 512, BF16) for i in range(2)]
    zfs = [self.sb("hzf%d" % i, 512, F32) for i in range(4)]
    gfs = [self.sb("hgf%d" % i, 512, F32) for i in range(4)]
    kfts = [self.sb("hkft%d" % i, 4 * 2 * 128, BF16) for i in range(2)]
    Apre = self.sb("hApre", 512, BF16)
    Apim = self.sb("hApim", 512, BF16)
    Yre = self.sb("hYre", 512, BF16)
    Yim = self.sb("hYim", 512, BF16)
    Bpre = self.sb("hBpre", 512, BF16)
    Bpim = self.sb("hBpim", 512, BF16)
    tq = [self.sb("htq%d" % i, 512, F32) for i in range(4)]
    outf = [self.sb("houtf%d" % i, 512, F32) for i in range(2)]
    outb = [self.sb("houtb%d" % i, 512, BF16) for i in range(2)]
    ps1 = self.ps2("hps1")
    pXre = self.ps("hpXre")
    pXim = self.ps("hpXim")
    pB = self.ps2("hpB")
    py = self.ps("hpy")
    it = 0

    def fft_s1(Xb, Kp, N1):
        NH = N1 // 2 + 1
        cw = 2 * NH
        CS = 256 if N1 == 128 else 128
        X3 = Xb.t.rearrange("p (c i) -> p c i", c=4)
        for ch in range(4):
            off = ch * CS
            self.mm(ps1.t[:, off:off + cw], X3[0:Kp, ch, :], FN1.t[0:Kp, 0:cw], True, True,
                    rd=[Xb.b, FN1.b], wr=[ps1.b])

    def fft_d1s2(N1):
        NH = N1 // 2 + 1
        cw = 2 * NH
        CS = 256 if N1 == 128 else 128
        Tt3 = Tt.t[:, 0:cw].rearrange("p (r k) -> p r k", r=2)
        Ar3 = Apre.t[:, 0:4 * NH].rearrange("p (c k) -> p c k", c=4)
        Ai3 = Apim.t[:, 0:4 * NH].rearrange("p (c k) -> p c k", c=4)
        v = ps1.t[:, 0:4 * CS].rearrange("p (c w) -> p c w", c=4)
        Are, Aim = v[:, :, 0:NH], v[:, :, NH:cw]
        Tre = Tt3[:, 0, :].unsqueeze(1).to_broadcast([128, 4, NH])
        Tim = Tt3[:, 1, :].unsqueeze(1).to_broadcast([128, 4, NH])
        t = [q.t[:, 0:4 * NH].rearrange("p (c k) -> p c k", c=4) for q in tq]
        self.tt("dve", t[0], Are, Tre, ALU.mult, rd=[ps1.b, Tt.b], wr=[tq[0].b])
        self.tt("dve", t[1], Aim, Tim, ALU.mult, rd=[ps1.b, Tt.b], wr=[tq[1].b])
        self.tt("dve", Ar3, t[0], t[1], ALU.subtract, rd=[tq[0].b, tq[1].b], wr=[Apre.b])
        self.tt("dve", t[2], Are, Tim, ALU.mult, rd=[ps1.b, Tt.b], wr=[tq[2].b])
        self.tt("dve", t[3], Aim, Tre, ALU.mult, rd=[ps1.b, Tt.b], wr=[tq[3].b])
        self.tt("dve", Ai3, t[2], t[3], ALU.add, rd=[tq[2].b, tq[3].b], wr=[Apim.b])
        W = 4 * NH
        self.mm(pXre.t[:, 0:W], F3[:, 0, :], Apre.t[:, 0:W], True, False, rd=[F128.b, Apre.b], wr=[pXre.b])
        self.mm(pXre.t[:, 0:W], F3[:, 2, :], Apim.t[:, 0:W], False, True, rd=[F128.b, Apim.b], wr=[pXre.b])
        self.mm(pXim.t[:, 0:W], F3[:, 0, :], Apim.t[:, 0:W], True, False, rd=[F128.b, Apim.b], wr=[pXim.b])
        self.mm(pXim.t[:, 0:W], F3[:, 1, :], Apre.t[:, 0:W], False, True, rd=[F128.b, Apre.b], wr=[pXim.b])

    for s in range(2):
        L = self.Ls[s]
        NB = L // 128
        N1 = 2 * NB
        NH = N1 // 2 + 1
        self.dma("pool", FN1.t[0:N1, 0:2 * NH], self.din["hFN1_%d" % s], wr=[FN1.b])
        self.dma("sp", Tt.t[:, 0:2 * NH], self.din["hT_%d" % s].rearrange("p r k -> p (r k)"), wr=[Tt.b])
        self.dma("sp", T2.t[0:NH, :], self.din["hT2_%d" % s].rearrange("p r k -> p (r k)"), wr=[T2.b])
        self.dma("pool", GN1.t[0:NH, 0:2 * NB], self.din["hGN1_%d" % s].rearrange("p r k -> p (r k)"), wr=[GN1.b])
        TCc = 2048
        for m in range(6):
            for c in range(L // TCc):
                xh, co = xhs[it % 2], cos_[it % 2]
                it += 1
                t0 = c * TCc
                lo, hi = max(t0 - 1, 0), min(t0 + TCc + 1, L)
                if c == 0:
                    self.memset("dve", xh.t[:, 0:1], 0.0, wr=[xh.b])
                if hi == L:
                    self.memset("dve", xh.t[:, TCc + 1:TCc + 2], 0.0, wr=[xh.b])
                self.dma("sp", xh.t[:, lo - (t0 - 1):hi - (t0 - 1)], self.lh[s][512 + m * 128:512 + (m + 1) * 128, lo:hi], wr=[xh.b])
                self.ts("dve", co.t, xh.t[:, 0:TCc], hyw3[:, m, 0:1], hyw3[:, m, 3:4], ALU.mult, ALU.add,
                        rd=[xh.b, hyw.b], wr=[co.b])
                for k in (1, 2):
                    self.stt("dve", co.t, xh.t[:, k:k + TCc], hyw3[:, m, k:k + 1], co.t, ALU.mult, ALU.add,
                             rd=[xh.b, hyw.b, co.b], wr=[co.b])
                self.dma("sp", self.hc[s][m * 128:(m + 1) * 128, t0:t0 + TCc], co.t, rd=[co.b])
        NCH = L // 512
        self.memset("dve", acc.t, 0.0, wr=[acc.b])
        acc3 = acc.t.rearrange("p (m c) -> p m c", m=8)
        for c in range(NCH):
            zc, tl = zcs[c % 2], tls[c % 2]
            dec = decs[c % 2]
            t0 = c * 512
            self.dma("sp", zc.t[0:33, :], self.din["hz%d" % s][:, t0:t0 + 512], wr=[zc.b])
            self.dma("sp", tl.t, dap(self.din["htl%d" % s], t0, [(0, 128), (1, 512)]), wr=[tl.b])
            for cc in range(2):
                self.act(dec[cc].t, tl.t, AF.Exp, rd=[tl.b, negd.b], wr=[dec[cc].b], scale=negd.t[:, cc:cc + 1])
            h, hK = zc, 33
            for k, wk in enumerate((w1, w2, w3)):
                ph = ps1
                pho = (k % 2) * 512
                self.mm(ph.t[0:64, pho:pho + 512], wk.t[0:hK, :], h.t[0:hK, :], True, True, rd=[wk.b, h.b], wr=[ph.b])
                self.ts("dve", ya.t[0:64, :], ph.t[0:64, pho:pho + 512], hyb.t[0:64, 3:4], bfq.t[0:64, k:k + 1], ALU.mult, ALU.add,
                        rd=[ph.b, hyb.b, bfq.b], wr=[ya.b])
                self.ts("dve", kk.t[0:64, :], ya.t[0:64, :], 1.0 / TWO_PI, MAGIC, ALU.mult, ALU.add, rd=[ya.b], wr=[kk.b])
                self.ts("dve", kk.t[0:64, :], kk.t[0:64, :], -MAGIC, -TWO_PI, ALU.add, ALU.mult, rd=[kk.b], wr=[kk.b])
                self.tt("dve", ya.t[0:64, :], kk.t[0:64, :], ya.t[0:64, :], ALU.add, rd=[kk.b, ya.b], wr=[ya.b])
                self.ts("dve", ya.t[0:64, :], ya.t[0:64, :], math.pi, -math.pi, ALU.min, ALU.max, rd=[ya.b], wr=[ya.b])
                hk = hks[k]
                self.act(hk.t[0:64, :], ya.t[0:64, :], AF.Sin, rd=[ya.b], wr=[hk.b])
                h, hK = hk, 64
            nk = 0
            for o in range(2):
                for cc in range(2):
                    for dr in (1, 0):
                        m = o * 2 + dr
                        col0 = (m * 2 + cc) * 128
                        pk = (pXre, pXim)[nk % 2]
                        kt = kts[nk % 3]
                        nk += 1
                        self.mm(pk.t, wout.t[0:64, col0:col0 + 128], h.t[0:64, :], True, True, rd=[wout.b, h.b], wr=[pk.b])
                        row0 = cc * 128
                        ai = (o * 2 + cc) * 2 + dr
                        kdst = self.kfull[s][o, row0:row0 + 128, :]
                        if dr == 1:
                            self.tt("dve", rev(kt.t), pk.t, dec[cc].t, ALU.mult, rd=[pk.b, dec[cc].b], wr=[kt.b])
                            if c == 0:
                                self.cp("dve", kb0.t[:, o * 2 + cc:o * 2 + cc + 1], kt.t[:, 511:512], rd=[kt.b], wr=[kb0.b])
                                n_ = 511
                                self.dma("sp", kdst[:, 2 * L - 511:2 * L], kt.t[:, 0:511], rd=[kt.b])
                            else:
                                n_ = 512
                                self.dma("sp", kdst[:, 2 * L - t0 - 511:2 * L - t0 + 1], kt.t, rd=[kt.b])
                        else:
                            self.tt("dve", kt.t, pk.t, dec[cc].t, ALU.mult, rd=[pk.b, dec[cc].b], wr=[kt.b])
                            if c == 0:
                                self.tt("dve", kt.t[:, 0:1], kt.t[:, 0:1], kb0.t[:, o * 2 + cc:o * 2 + cc + 1], ALU.add,
                                        rd=[kt.b, kb0.b], wr=[kt.b])
                            n_ = 512
                            self.dma("sp", kdst[:, t0:t0 + 512], kt.t, rd=[kt.b])
                        self.act(kab.t[:, 0:n_], kt.t[:, 0:n_], AF.Abs, rd=[kt.b, acc.b], wr=[kab.b, acc.b],
                                 accum=acc3[:, ai, c:c + 1])
        self.s.op("dve", (lambda e, o_=nrm.t, i_=acc.t.rearrange("p (m c) -> p m c", m=4):
                          e.tensor_reduce(out=o_, in_=i_, axis=AX.X, op=ALU.add)), rd=[acc.b], wr=[nrm.b])
        self.recip(nrm.t, nrm.t, rd=[nrm.b], wr=[nrm.b])
        for o in range(2):
            for cc in range(2):
                self.dma("sp", dap(self.rnd, o * 256 + cc * 128, [(1, 128), (1, 1)]), nrm.t[:, o * 2 + cc:o * 2 + cc + 1], rd=[nrm.b], slow=True)
                self.dma("sp", dap(self.kfull[s], (o * 256 + cc * 128) * 2 * L + L, [(2 * L, 128), (1, 1)]), zero1.t, rd=[zero1.b], slow=True)
        self.sync()
        self.dma("sp", rnrow.t, dap(self.rnd, 0, [(0, 128), (1, 512)]), wr=[rnrow.b])
        def h3_a(g):
            o, c0, i2 = g
            Xb = Xbs[i2 % 2]
            self.dma("pool", Xb.t[0:N1, :].rearrange("p (c i) -> p c i", c=4),
                     dap(self.kfull[s], (o * 256 + c0) * 2 * L, [(128, N1), (2 * L, 4), (1, 128)]), wr=[Xb.b])
            fft_s1(Xb, N1, N1)

        def h3_c(g):
            o, c0, i2 = g
            kft = kfts[i2 % 2]
            k4 = kft.t[:, 0:8 * NH].rearrange("p (c r k) -> p c r k", c=4, r=2)
            rb = rnrow.t[:, o * 256 + c0:o * 256 + c0 + 4].unsqueeze(2).to_broadcast([128, 4, NH])
            self.tt("dve", k4[:, :, 0, :], pXre.t[:, 0:4 * NH].rearrange("p (c k) -> p c k", c=4), rb, ALU.mult,
                    rd=[pXre.b, rnrow.b], wr=[kft.b])
            self.tt("dve", k4[:, :, 1, :], pXim.t[:, 0:4 * NH].rearrange("p (c k) -> p c k", c=4), rb, ALU.mult,
                    rd=[pXim.b, rnrow.b], wr=[kft.b])
            self.dma("sp", dap(self.kfs[s], (o * 128 * 256 + c0) * 2 * NH, [(256 * 2 * NH, 128), (1, 8 * NH)]),
                     kft.t[:, 0:8 * NH], rd=[kft.b])

        G = [(o, grp * 4, o * 64 + grp) for o in range(2) for grp in range(64)]
        for n in range(len(G) + 1):
            if n < len(G):
                h3_a(G[n])
            if n >= 1:
                h3_c(G[n - 1])
            if n < len(G):
                fft_d1s2(N1)
        self.sync()
        T23 = T2.t.rearrange("p (r i) -> p r i", r=2)
        GN3 = GN1.t[:, 0:2 * NB].rearrange("p (r a) -> p r a", r=2)
        W = 4 * NH
        def h4_a(g):
            o, c0, i2 = g
            zsrc = self.hc[s] if o == 0 else self.z1[s]
            Xb, kft, zf, gf = Xbs[i2 % 2], kfts[i2 % 2], zfs[i2 % 4], gfs[i2 % 4]
            zap = dap(zsrc, c0 * L, [(128, NB), (L, 4), (1, 128)])
            self.dma("pool", Xb.t[0:NB, :].rearrange("p (c i) -> p c i", c=4), zap, wr=[Xb.b])
            self.dma("sp", zf.t[0:NB, :].rearrange("p (c i) -> p c i", c=4), zap, wr=[zf.b])
            self.dma("sp", gf.t[0:NB, :].rearrange("p (c i) -> p c i", c=4),
                     dap(self.hc[s], (256 * (o + 1) + c0) * L, [(128, NB), (L, 4), (1, 128)]), wr=[gf.b])
            self.dma("sp", kft.t[:, 0:8 * NH],
                     dap(self.kfs[s], (o * 128 * 256 + c0) * 2 * NH, [(256 * 2 * NH, 128), (1, 8 * NH)]), wr=[kft.b])
            fft_s1(Xb, NB, N1)

        def h4_d2s3(g):
            o, c0, i2 = g
            kft = kfts[i2 % 2]
            k4 = kft.t[:, 0:8 * NH].rearrange("p (c r k) -> p c r k", c=4, r=2)
            Kre, Kim = k4[:, :, 0, :], k4[:, :, 1, :]
            Xr = pXre.t[:, 0:W].rearrange("p (c k) -> p c k", c=4)
            Xi = pXim.t[:, 0:W].rearrange("p (c k) -> p c k", c=4)
            t = [q.t[:, 0:W].rearrange("p (c k) -> p c k", c=4) for q in tq]
            self.tt("dve", t[0], Xr, Kre, ALU.mult, rd=[pXre.b, kft.b], wr=[tq[0].b])
            self.tt("dve", t[1], Xi, Kim, ALU.mult, rd=[pXim.b, kft.b], wr=[tq[1].b])
            self.tt("dve", Yre.t[:, 0:W].rearrange("p (c k) -> p c k", c=4), t[0], t[1], ALU.subtract,
                    rd=[tq[0].b, tq[1].b], wr=[Yre.b])
            self.tt("dve", t[2], Xr, Kim, ALU.mult, rd=[pXre.b, kft.b], wr=[tq[2].b])
            self.tt("dve", t[3], Xi, Kre, ALU.mult, rd=[pXim.b, kft.b], wr=[tq[3].b])
            self.tt("dve", Yim.t[:, 0:W].rearrange("p (c k) -> p c k", c=4), t[2], t[3], ALU.add,
                    rd=[tq[2].b, tq[3].b], wr=[Yim.b])
            Yr3 = Yre.t[:, 0:W].rearrange("p (c k) -> p c k", c=4)
            Yi3 = Yim.t[:, 0:W].rearrange("p (c k) -> p c k", c=4)
            for ch in range(4):
                off = ch * 256
                self.mm(pB.t[0:NH, off:off + 256], Yr3[:, ch, :], G3[:, 0, :], True, False,
                        rd=[Yre.b, G128.b], wr=[pB.b])
                self.mm(pB.t[0:NH, off:off + 256], Yi3[:, ch, :], G3[:, 1, :], False, True,
                        rd=[Yim.b, G128.b], wr=[pB.b])

        def h4_d3s4(g):
            Br4 = Bpre.t[0:NH, :].rearrange("p (c i) -> p c i", c=4)
            Bi4 = Bpim.t[0:NH, :].rearrange("p (c i) -> p c i", c=4)
            v = pB.t[0:NH, :].rearrange("p (c r i) -> p c r i", c=4, r=2)
            Bre, Bim = v[:, :, 0, :], v[:, :, 1, :]
            Tre = T23[0:NH, 0, :].unsqueeze(1).to_broadcast([NH, 4, 128])
            Tim = T23[0:NH, 1, :].unsqueeze(1).to_broadcast([NH, 4, 128])
            t = [q.t[0:NH, 0:512].rearrange("p (c i) -> p c i", c=4) for q in tq]
            self.tt("dve", t[0], Bre, Tre, ALU.mult, rd=[pB.b, T2.b], wr=[tq[0].b])
            self.tt("dve", t[1], Bim, Tim, ALU.mult, rd=[pB.b, T2.b], wr=[tq[1].b])
            self.tt("dve", Br4, t[0], t[1], ALU.subtract, rd=[tq[0].b, tq[1].b], wr=[Bpre.b])
            self.tt("dve", t[2], Bre, Tim, ALU.mult, rd=[pB.b, T2.b], wr=[tq[2].b])
            self.tt("dve", t[3], Bim, Tre, ALU.mult, rd=[pB.b, T2.b], wr=[tq[3].b])
            self.tt("dve", Bi4, t[2], t[3], ALU.add, rd=[tq[2].b, tq[3].b], wr=[Bpim.b])
            self.mm(py.t[0:NB, :], GN3[0:NH, 0, :], Bpre.t[0:NH, :], True, False, rd=[GN1.b, Bpre.b], wr=[py.b])
            self.mm(py.t[0:NB, :], GN3[0:NH, 1, :], Bpim.t[0:NH, :], False, True, rd=[GN1.b, Bpim.b], wr=[py.b])

        def h4_d4(g):
            o, c0, i2 = g
            zf, gf = zfs[i2 % 4], gfs[i2 % 4]
            t0_ = tq[0].t[0:NB, :]
            db = drow.t[0:NB, o * 256 + c0:o * 256 + c0 + 4].unsqueeze(2).to_broadcast([NB, 4, 128])
            self.tt("dve", t0_.rearrange("p (c i) -> p c i", c=4), zf.t[0:NB, :].rearrange("p (c i) -> p c i", c=4), db,
                    ALU.mult, rd=[zf.b, drow.b], wr=[tq[0].b])
            self.tt("dve", t0_, t0_, py.t[0:NB, :], ALU.add, rd=[tq[0].b, py.b], wr=[tq[0].b])
            if o == 0:
                ob = outf[i2 % 2]
                self.tt("dve", ob.t[0:NB, :], t0_, gf.t[0:NB, :], ALU.mult, rd=[tq[0].b, gf.b], wr=[ob.b])
                self.dma("sp", dap(self.z1[s], c0 * L, [(128, NB), (L, 4), (1, 128)]),
                         ob.t[0:NB, :].rearrange("p (c i) -> p c i", c=4), rd=[ob.b])
            else:
                ob = outb[i2 % 2]
                self.tt("dve", ob.t[0:NB, :], t0_, gf.t[0:NB, :], ALU.mult, rd=[tq[0].b, gf.b], wr=[ob.b])
                self.dma("sp", dap(self.catT[s], (768 + c0) * L, [(128, NB), (L, 4), (1, 128)]),
                         ob.t[0:NB, :].rearrange("p (c i) -> p c i", c=4), rd=[ob.b])

        for o in range(2):
            G = [(o, grp * 4, grp) for grp in range(64)]
            ng = len(G)
            for n in range(ng + 3):
                if n < ng:
                    h4_a(G[n])
                if 0 <= n - 3 < ng:
                    h4_d4(G[n - 3])
                if 0 <= n - 2 < ng:
                    h4_d3s4(G[n - 2])
                if 0 <= n - 1 < ng:
                    h4_d2s3(G[n - 1])
                if n < ng:
                    fft_d1s2(N1)
            self.sync()
    self.phase_end()


K.sync = _sync
K.phase4 = _phase4


def build_program(nlayers=DEPTH, dump=False):
    nc = bass.Bass("TRN2", target_bir_lowering=False)
    with ExitStack() as ctx:
        s = Sched(nc, ctx)
        k = K(nc, s, ctx, nlayers=nlayers, dbg={"dump": dump})
        k.setup()
        for l in range(nlayers):
            xsrc = k.xin if l == 0 else k.xa
            dst = k.yout if l == nlayers - 1 else k.xa
            k.phase0(l)
            k.phase1(l, xsrc)
            k.phase2(l)
            k.phase3(l)
            k.phase4(l)
            k.phase5(l, xsrc)
            k.phase6(l, dst)
        s.barrier()
        s.emit()
    return nc


_NC_CACHE = {}


def kernel(**inputs):
    inp = {k: np.asarray(v) for k, v in inputs.items()}
    if "nc" not in _NC_CACHE:
        _NC_CACHE["nc"] = build_program()
    nc = _NC_CACHE["nc"]
    sh = host_prep(inp)
    in_maps = []
    for i in range(8):
        m = dict(sh)
        m.update(core_prep(inp, i))
        in_maps.append(m)
    res = run_bass_kernel_spmd(nc, in_maps, core_ids=list(range(8)))
    yp = np.stack([np.asarray(r["yp"], dtype=np.float32) for r in res.results], axis=0)
    ys = np.stack([np.asarray(r["ys"], dtype=np.float32) for r in res.results], axis=0)
    return (yp, ys)
```

```python
import math
import numpy as np
from contextlib import ExitStack
import concourse.bass as bass
import concourse.mybir as mybir
from concourse.bass_types import AP as APc
from concourse.bass_utils import run_bass_kernel_spmd

F32 = mybir.dt.float32
BF16 = mybir.dt.bfloat16
AF = mybir.ActivationFunctionType
ALU = mybir.AluOpType
AX = mybir.AxisListType

D = 1024
DEPTH = 4
LP = 4096
LS = 8192
DFF = 2816
ALPHA = (2 * DEPTH) ** 0.25
LN_EPS = 1e-5
QK_EPS = 1e-6
HY_MIN_DECAY = abs(math.log(1e-2)) / 1.5
HY_MAX_DECAY = abs(math.log(1e-2)) / 0.3
MAGIC = 12582912.0

ENGS = ("pe", "act", "dve", "pool", "sp")
EMBED_WAITS = True


class Buf:
    __slots__ = ("name", "w", "r")

    def __init__(self, name):
        self.name = name
        self.w = None
        self.r = []


class Sched:
    NDMA = 8

    def __init__(self, nc, ctx):
        self.nc = nc
        self.streams = {e: [] for e in ENGS}
        self.sems = {}
        self.cnt = {}
        for e in ENGS:
            self.sems[e] = ctx.enter_context(nc.semaphore("s_" + e))
            self.cnt[e] = 0
        self.dnext = {}
        for q in ("sp", "act", "pool"):
            for i in range(self.NDMA):
                k = "d_%s%d" % (q, i)
                self.sems[k] = ctx.enter_context(nc.semaphore(k))
                self.cnt[k] = 0
            self.dnext[q] = 0
        self.waited = {e: {} for e in ENGS}
        self.nbuf = 0

    def buf(self, name=None):
        self.nbuf += 1
        return Buf(name or ("b%d" % self.nbuf))

    def _need(self, eng, ev, same_ok):
        if ev is None:
            return
        k, v, src = ev
        if src == eng and same_ok:
            return
        if self.waited[eng].get(k, 0) >= v:
            return
        self.waited[eng][k] = v
        self.streams[eng].append(("w", k, v))

    def _deps(self, eng, rd, wr):
        pe = (eng == "pe")
        for b in rd:
            self._need(eng, b.w, pe)
        for b in wr:
            self._need(eng, b.w, pe)
            for ev in b.r:
                self._need(eng, ev, True)

    def _mark(self, ev, rd, wr):
        for b in rd:
            b.r.append(ev)
            if len(b.r) > 64:
                last = {}
                for e2 in b.r:
                    if e2[0] not in last or last[e2[0]][1] < e2[1]:
                        last[e2[0]] = e2
                b.r = list(last.values())
        for b in wr:
            b.w = ev
            b.r = []

    def op(self, eng, fn, rd=(), wr=()):
        self._deps(eng, rd, wr)
        self.cnt[eng] += 1
        ev = (eng, self.cnt[eng], eng)
        self.streams[eng].append(("o", fn, eng, 1))
        self._mark(ev, rd, wr)
        return ev

    def dma(self, q, out, in_, rd=(), wr=(), slow=False):
        self._deps(q, rd, wr)
        i = self.dnext[q]
        self.dnext[q] = (i + 1) % self.NDMA
        k = "d_%s%d" % (q, i)
        if self.cnt[k] > 0:
            self._need(q, (k, self.cnt[k], None), False)
        self.cnt[k] += 16
        ev = (k, self.cnt[k], None)
        if slow:
            self.streams[q].append(("o", (lambda e: e.dma_start(out=out, in_=in_, allow_slow_non_contiguous=True)), k, 16))
        else:
            self.streams[q].append(("o", (lambda e: e.dma_start(out=out, in_=in_)), k, 16))
        self._mark(ev, rd, wr)
        return ev

    def barrier(self):
        for e in ENGS:
            for k, v in self.cnt.items():
                if v > 0 and k != e:
                    self._need(e, (k, v, None), False)

    def emit(self):
        nc = self.nc
        if not any(self.streams[e] for e in ENGS):
            return
        engobj = {"pe": "tensor", "act": "scalar", "dve": "vector", "pool": "gpsimd", "sp": "sync"}
        with nc.Block() as block:
            for e in ENGS:
                items = self.streams[e]
                sems = self.sems

                def body(eng, items=items, sems=sems):
                    n = len(items)
                    i = 0
                    while i < n:
                        it = items[i]
                        if it[0] == "w":
                            if EMBED_WAITS and i + 1 < n and items[i + 1][0] == "o":
                                nx = items[i + 1]
                                ins = nx[1](eng)
                                ins._wait_ge(sems[it[1]], it[2])
                                ins.then_inc(sems[nx[2]], nx[3])
                                i += 2
                                continue
                            eng.wait_ge(sems[it[1]], it[2])
                        else:
                            it[1](eng).then_inc(sems[it[2]], it[3])
                        i += 1
                getattr(block, engobj[e])(body)
        self.streams = {e: [] for e in ENGS}


class TB:
    __slots__ = ("t", "b")

    def __init__(self, t, b):
        self.t = t
        self.b = b

    def __getitem__(self, k):
        return self.t[k]


def rev(ap2d):
    (ps, pn), (fs, fn) = ap2d.ap
    return APc(ap2d.tensor, ap2d.offset + (fn - 1) * fs, [[ps, pn], [-fs, fn]])


def dap(t, off, dims):
    return APc(t.tensor, t.offset + off, [[a, b] for a, b in dims])


class K:
    def __init__(self, nc, s, ctx, nlayers=DEPTH, dbg=None):
        self.nc, self.s, self.gctx = nc, s, ctx
        self.nlayers = nlayers
        self.dbg = dbg or {}
        self.pctx = None
        self.uid = 0

    def _nm(self, n):
        self.uid += 1
        return "%s_%d" % (n, self.uid)

    def sb(self, name, shape, dt, glob=False):
        c = self.gctx if glob else self.pctx
        t = c.enter_context(self.nc.sbuf_tensor(self._nm(name), list(shape), dt))
        return TB(t, self.s.buf(name))

    def ps(self, name, shape, dt=F32):
        t = self.pctx.enter_context(self.nc.psum_tensor(self._nm(name), list(shape), dt))
        return TB(t, self.s.buf(name))

    def dram(self, name, shape, dt, kind="Internal"):
        return self.nc.dram_tensor(name, list(shape), dt, kind=kind).ap()

    def mm(self, out, lhsT, rhs, start, stop, rd, wr, tp=None):
        if tp is None:
            f = lambda e: e.matmul(out=out, lhsT=lhsT, rhs=rhs, start=start, stop=stop)
        else:
            f = lambda e: e.matmul(out=out, lhsT=lhsT, rhs=rhs, start=start, stop=stop, tile_position=tp)
        return self.s.op("pe", f, rd=rd, wr=wr)

    def tr(self, out, in_, ident, rd, wr):
        return self.s.op("pe", lambda e: e.transpose(out=out, in_=in_, identity=ident), rd=rd, wr=wr)

    def act(self, out, in_, func, rd, wr, bias=None, scale=None, accum=None):
        kw = {}
        if bias is not None:
            kw["bias"] = bias
        if scale is not None:
            kw["scale"] = scale
        if accum is not None:
            kw["accum_out"] = accum
        return self.s.op("act", lambda e: e.activation(out=out, in_=in_, func=func, **kw), rd=rd, wr=wr)

    def tt(self, eng, out, in0, in1, op, rd, wr):
        return self.s.op(eng, lambda e: e.tensor_tensor(out=out, in0=in0, in1=in1, op=op), rd=rd, wr=wr)

    def ts(self, eng, out, in0, s1, s2, op0, op1, rd, wr):
        if op1 is None:
            f = lambda e: e.tensor_scalar(out=out, in0=in0, scalar1=s1, scalar2=None, op0=op0)
        else:
            f = lambda e: e.tensor_scalar(out=out, in0=in0, scalar1=s1, scalar2=s2, op0=op0, op1=op1)
        return self.s.op(eng, f, rd=rd, wr=wr)

    def stt(self, eng, out, in0, scalar, in1, op0, op1, rd, wr):
        return self.s.op(eng, lambda e: e.scalar_tensor_tensor(out=out, in0=in0, scalar=scalar, in1=in1, op0=op0, op1=op1), rd=rd, wr=wr)

    def cp(self, eng, out, in_, rd, wr):
        if eng == "act":
            return self.s.op("act", lambda e: e.copy(out=out, in_=in_), rd=rd, wr=wr)
        return self.s.op(eng, lambda e: e.tensor_copy(out=out, in_=in_), rd=rd, wr=wr)

    def memset(self, eng, ap, val, wr):
        return self.s.op(eng, lambda e: e.memset(ap, val), rd=(), wr=wr)

    def recip(self, out, in_, rd, wr):
        return self.s.op("dve", lambda e: e.reciprocal(out=out, in_=in_), rd=rd, wr=wr)

    def dma(self, q, out, in_, rd=(), wr=(), slow=False):
        return self.s.dma(q, out, in_, rd=rd, wr=wr, slow=slow)

    def phase_begin(self):
        self.pctx = ExitStack()

    def phase_end(self):
        self.s.barrier()
        self.pctx.close()
        self.pctx = None


def _init_arena(self):
    self.pctx = None


def _sb(self, name, cols, dt=F32, glob=False, parts=128, ctx=None):
    c = ctx if ctx is not None else (self.gctx if glob else self.pctx)
    t = c.enter_context(self.nc.sbuf_tensor(self._nm(name), [128, cols], dt))
    return TB(t[0:parts, :], self.s.buf(name))


def _ps(self, name, cols=512, dt=F32, parts=128):
    full = 512 if dt == F32 else 1024
    t = self.pctx.enter_context(self.nc.psum_tensor(self._nm(name), [128, full], dt))
    return TB(t[0:parts, 0:cols], self.s.buf(name))


def _ps2(self, name):
    t = self.pctx.enter_context(self.nc.psum_tensor(self._nm(name), [128, 1024], F32))
    return TB(t[:, :], self.s.buf(name))


def _phase_begin(self):
    self.pctx = ExitStack()


def _phase_end(self):
    self.s.barrier()
    self.s.emit()
    self.pctx.close()
    self.pctx = None


K.init_arena = _init_arena
K.sb = _sb
K.ps = _ps
K.ps2 = _ps2
K.phase_begin = _phase_begin
K.phase_end = _phase_end


def _consts():
    c = {}
    c["ident"] = np.eye(128, dtype=np.float32)
    t = np.arange(LS)
    row = (t // 64).astype(np.float32)
    col = (t % 64).astype(np.float32)
    inv = (10000.0 ** (-np.arange(16, dtype=np.float32) / 16)).astype(np.float32)
    ang = np.stack([row[:, None] * inv, col[:, None] * inv], axis=1).astype(np.float32)
    cs = np.cos(ang).reshape(LS // 128, 128, 32).transpose(1, 0, 2)
    sn = np.sin(ang).reshape(LS // 128, 128, 32).transpose(1, 0, 2)
    for s, L in enumerate((LP, LS)):
        f32 = np.float32
        t = np.linspace(0.0, 1.0, L, dtype=f32)[:, None]
        w = (f32(2.0 * math.pi) * np.arange(L, dtype=f32)[:, None] / f32(L)).astype(f32)
        f = np.linspace(1e-4, 15, 16, dtype=f32)[None, :]
        z = np.concatenate([t, np.cos(f * w), -np.sin(f * w)], axis=-1).astype(f32)
        c["hz%d" % s] = np.ascontiguousarray(z.T)
        c["htl%d" % s] = np.ascontiguousarray(t.T)
        NB = L // 128
        N1 = 2 * NB
        N = 2 * L
        NH = N1 // 2 + 1
        a = np.arange(N1, dtype=np.float64)[:, None]
        kl = np.arange(NH, dtype=np.float64)[None, :]
        th = 2 * np.pi * a * kl / N1
        c["hFN1_%d" % s] = np.concatenate([np.cos(th), -np.sin(th)], axis=1).astype(f32)
        i = np.arange(128, dtype=np.float64)[:, None]
        th = 2 * np.pi * i * kl / N
        c["hT_%d" % s] = np.stack([np.cos(th), -np.sin(th)], axis=1).astype(f32)
        c["hT2_%d" % s] = np.stack([np.cos(th.T), np.sin(th.T)], axis=1).astype(f32)
        aa = np.arange(NB, dtype=np.float64)[None, :]
        klc = np.arange(NH, dtype=np.float64)[:, None]
        th = 2 * np.pi * klc * aa / N1
        wgt = np.full((NH, 1), 2.0)
        wgt[0, 0] = 1.0
        wgt[NH - 1, 0] = 1.0
        c["hGN1_%d" % s] = np.stack([wgt * np.cos(th) / N, -wgt * np.sin(th) / N], axis=1).astype(f32)
    i = np.arange(128, dtype=np.float64)[:, None]
    kh = np.arange(128, dtype=np.float64)[None, :]
    th = 2 * np.pi * i * kh / 128
    c["hF128"] = np.stack([np.cos(th), -np.sin(th), np.sin(th)], axis=1).astype(np.float32)
    c["hG128"] = np.stack([np.concatenate([np.cos(th), np.sin(th)], axis=1),
                           np.concatenate([-np.sin(th), np.cos(th)], axis=1)], axis=1).astype(np.float32)
    dl = np.linspace(HY_MIN_DECAY, HY_MAX_DECAY, 256, dtype=np.float32)
    c["hnegd"] = np.ascontiguousarray((-dl).reshape(2, 128).T)
    c["rope_cos"] = np.ascontiguousarray(cs, dtype=np.float32)
    c["rope_sin"] = np.ascontiguousarray(sn, dtype=np.float32)
    return c


def host_prep(inp):
    f32 = np.float32
    A = lambda a: np.ascontiguousarray(a, dtype=f32)
    sh = {}
    sh["ada_w"] = A(inp["ada_w"])
    sh["ada_b"] = A(inp["ada_b"])
    sh["ada_bT"] = A(inp["ada_b"].reshape(DEPTH, 48, 128).transpose(0, 2, 1))
    sh["w_in"] = A(inp["w_in"])
    sh["w_out"] = A(inp["w_out"])
    sh["w_up"] = A(inp["ffn_w_up"])
    sh["w_down"] = A(inp["ffn_w_down"])
    sh["qkg"] = A(np.concatenate([np.tile(inp["q_gain"], (1, 8)), np.tile(inp["k_gain"], (1, 2))], axis=1))
    lw = np.concatenate([inp["lru_conv_w"], inp["lru_conv_b"][:, None, :]], axis=1)
    sh["lruw"] = A(lw.reshape(DEPTH, 5, 2, 128).transpose(0, 3, 2, 1))
    W = np.zeros((DEPTH, 2, 2, 2, 128, 128), f32)
    for gi, nm in enumerate(("lru_wa", "lru_wx")):
        w = np.asarray(inp[nm])
        for cc in range(2):
            for h2 in range(2):
                W[:, :, gi, cc, h2 * 64:(h2 + 1) * 64, h2 * 64:(h2 + 1) * 64] = w[:, :, 2 * cc + h2]
    sh["lruW"] = W
    lb = np.stack([inp["lru_ba"], inp["lru_bx"], inp["lru_lambda"]], axis=-1)
    sh["lrub"] = A(lb.reshape(DEPTH, 2, 2, 128, 3).transpose(0, 3, 2, 1, 4))
    fw = np.concatenate([inp["ffn_conv_w"], inp["ffn_conv_b"][:, None, :]], axis=1)
    sh["ffw"] = A(fw.reshape(DEPTH, 4, 44, 128).transpose(0, 3, 2, 1))
    sh["ln1_g"] = A(inp["ln1_g"]); sh["ln1_b"] = A(inp["ln1_b"])
    sh["ln2_g"] = A(inp["ln2_g"]); sh["ln2_b"] = A(inp["ln2_b"])
    hw = np.concatenate([inp["hy_conv_w"], inp["hy_conv_b"][:, None, :]], axis=1)
    sh["hyw"] = A(hw.reshape(DEPTH, 4, 6, 128).transpose(0, 3, 2, 1))
    sh["hy_w1"] = A(inp["hy_w1"]); sh["hy_w2"] = A(inp["hy_w2"]); sh["hy_w3"] = A(inp["hy_w3"])
    sh["hy_wout"] = A(inp["hy_wout"])
    sh["hyb"] = A(np.stack([inp["hy_b1"], inp["hy_b2"], inp["hy_b3"], inp["hy_freq"]], axis=-1))
    sh["hy_bias"] = A(inp["hy_bias"])
    sh.update(_consts())
    return sh


SHARED_SHAPES = {
    "ada_w": [DEPTH, 1024, 6144], "ada_b": [DEPTH, 6144], "ada_bT": [DEPTH, 128, 48],
    "w_in": [DEPTH, 1024, 2048], "w_out": [DEPTH, 1024, 1024], "w_up": [DEPTH, 1024, 2 * DFF],
    "w_down": [DEPTH, DFF, 1024], "qkg": [DEPTH, 640],
    "lruw": [DEPTH, 128, 2, 5], "lruW": [DEPTH, 2, 2, 2, 128, 128], "lrub": [DEPTH, 128, 2, 2, 3],
    "ffw": [DEPTH, 128, 44, 4], "ln1_g": [DEPTH, 1024], "ln1_b": [DEPTH, 1024], "ln2_g": [DEPTH, 1024], "ln2_b": [DEPTH, 1024],
    "hyw": [DEPTH, 128, 6, 4], "hy_w1": [DEPTH, 33, 64], "hy_w2": [DEPTH, 64, 64], "hy_w3": [DEPTH, 64, 64],
    "hy_wout": [DEPTH, 64, 1024], "hyb": [DEPTH, 64, 4], "hy_bias": [DEPTH, 2, 256],
    "hz0": [33, LP], "hz1": [33, LS], "htl0": [1, LP], "htl1": [1, LS],
    "hFN1_0": [64, 66], "hFN1_1": [128, 130], "hT_0": [128, 2, 33], "hT_1": [128, 2, 65],
    "hT2_0": [33, 2, 128], "hT2_1": [65, 2, 128], "hGN1_0": [33, 2, 32], "hGN1_1": [65, 2, 64],
    "hF128": [128, 3, 128], "hG128": [128, 2, 256], "hnegd": [128, 2],
    "ident": [128, 128], "rope_cos": [128, 64, 32], "rope_sin": [128, 64, 32],
}


def core_prep(inp, i):
    c = np.stack([inp["c_prompt"][i], inp["c_sample"][i]], axis=0)
    cT = np.ascontiguousarray(c.reshape(2, 8, 128).transpose(2, 1, 0), dtype=np.float32)
    return {"xp": np.ascontiguousarray(inp["x_prompt"][i], dtype=np.float32),
            "xs": np.ascontiguousarray(inp["x_sample"][i], dtype=np.float32),
            "cT": cT}


def _setup(self):
    nc = self.nc
    dk = "ExternalOutput" if self.dbg.get("dump") else "Internal"
    self.din = {}
    for k, shp in SHARED_SHAPES.items():
        self.din[k] = self.dram(k, shp, F32, kind="ExternalInput")
    self.xin = [self.dram("xp", [LP, D], F32, kind="ExternalInput"),
                self.dram("xs", [LS, D], F32, kind="ExternalInput")]
    self.cT = self.dram("cT", [128, 8, 2], F32, kind="ExternalInput")
    self.yout = [self.dram("yp", [LP, D], F32, kind="ExternalOutput"),
                 self.dram("ys", [LS, D], F32, kind="ExternalOutput")]
    self.Ls = [LP, LS]
    self.xa = [self.dram("xa%d" % s, [L, D], F32, kind=dk) for s, L in enumerate(self.Ls)]
    self.xres = [self.dram("xres%d" % s, [L, D], F32, kind=dk) for s, L in enumerate(self.Ls)]
    self.qT = [self.dram("qT%d" % s, [512, L], BF16, kind=dk) for s, L in enumerate(self.Ls)]
    self.kT = [self.dram("kT%d" % s, [128, L], BF16, kind=dk) for s, L in enumerate(self.Ls)]
    self.vaug = [self.dram("vaug%d" % s, [2, 128, L // 128, 192], BF16, kind=dk) for s, L in enumerate(self.Ls)]
    self.lh = [self.dram("lh%d" % s, [1280, L], F32, kind=dk) for s, L in enumerate(self.Ls)]
    self.catT = [self.dram("catT%d" % s, [1024, L], BF16, kind=dk) for s, L in enumerate(self.Ls)]
    self.hc = [self.dram("hc%d" % s, [768, L], F32, kind=dk) for s, L in enumerate(self.Ls)]
    self.kfull = [self.dram("kfull%d" % s, [2, 256, 2 * L], F32, kind=dk) for s, L in enumerate(self.Ls)]
    self.kfs = [self.dram("kfs%d" % s, [2, 128, 256, 2, L // 128 + 1], BF16, kind=dk) for s, L in enumerate(self.Ls)]
    self.z1 = [self.dram("z1_%d" % s, [256, L], F32, kind=dk) for s, L in enumerate(self.Ls)]
    self.rnd = self.dram("rnd", [2, 256], F32, kind=dk)
    self.init_arena()
    self.ident = self.sb("ident", 128, BF16, glob=True)
    self.csT = self.sb("csT", 16, F32, glob=True)
    self.modT = self.sb("modT", 96, F32, glob=True)
    self.grow = [self.sb("grow%d" % i, 1024, F32, glob=True) for i in range(4)]
    self.epsq = self.sb("epsq", 1, F32, glob=True)
    self.epsl = self.sb("epsl", 1, F32, glob=True)
    self.phase_begin()
    self.dma("pool", self.ident.t, self.din["ident"], wr=[self.ident.b])
    ct = self.sb("ct", 16, F32)
    self.dma("sp", ct.t, self.cT.rearrange("p k s -> p (k s)"), wr=[ct.b])
    self.act(self.csT.t, ct.t, AF.Silu, rd=[ct.b], wr=[self.csT.b])
    self.memset("dve", self.epsq.t, QK_EPS, wr=[self.epsq.b])
    self.memset("dve", self.epsl.t, LN_EPS, wr=[self.epsl.b])
    self.phase_end()


def _phase0(self, l):
    self.wctx = ExitStack()
    wi = self.sb("wi", 8 * 2048, BF16, ctx=self.wctx)
    wi3 = wi.t.rearrange("p (k n) -> p k n", k=8)
    wib = [self.s.buf("wib%d" % k) for k in range(8)]
    self.phase_begin()
    for kc in range(8):
        self.dma("pool", wi3[:, kc, :], self.din["w_in"][l, kc * 128:(kc + 1) * 128, :], wr=[wib[kc]])
    self.pre_wi = (wi, wib)
    adab = self.sb("adab", 48, F32)
    self.dma("sp", adab.t, self.din["ada_bT"][l], wr=[adab.b])
    self.csrep = self.sb("csrep", 16 * 128, F32)
    self.cp("dve", self.csrep.t.rearrange("p (k n) -> p k n", n=128),
            self.csT.t.unsqueeze(2).to_broadcast([128, 16, 128]), rd=[self.csT.b], wr=[self.csrep.b])
    was = [self.sb("wa%d" % i, 8 * 512, F32) for i in range(2)]
    brow = [self.sb("brow%d" % i, 512, F32) for i in range(2)]
    pm = self.ps("pm", 96)
    pgs = [self.ps("pg%d" % i) for i in range(2)]
    npg = 0
    aw = self.din["ada_w"]
    for gi in range(12):
        wa = was[gi % 2]
        src = dap(aw, l * 1024 * 6144 + gi * 512, [(6144, 128), (128 * 6144, 8), (1, 512)])
        self.dma("sp", wa.t.rearrange("p (k c) -> p k c", k=8), src, wr=[wa.b])
        wa3 = wa.t.rearrange("p (k c) -> p k c", k=8)
        for m in range(4):
            ch = gi * 4 + m
            for kc in range(8):
                self.mm(pm.t[:, ch * 2:ch * 2 + 2], wa3[:, kc, m * 128:(m + 1) * 128],
                        self.csT.t[:, kc * 2:kc * 2 + 2], kc == 0, kc == 7, rd=[wa.b, self.csT.b], wr=[pm.b])
        if gi in (4, 5, 10, 11):
            g = 0 if gi < 6 else 1
            half = gi % 2 if gi < 6 else (gi - 10)
            br = brow[half]
            self.dma("sp", br.t, dap(self.din["ada_b"], l * 6144 + gi * 512, [(0, 128), (1, 512)]), wr=[br.b])
            cr4 = self.csrep.t.rearrange("p (k s n) -> p k s n", k=8, s=2)
            for sq in range(2):
                pg = pgs[npg % 2]
                npg += 1
                for kc in range(8):
                    self.mm(pg.t, cr4[:, kc, sq, :], wa3[:, kc, :], kc == 0, kc == 7,
                            rd=[self.csrep.b, wa.b], wr=[pg.b])
                gr = self.grow[g * 2 + sq]
                self.tt("dve", gr.t[:, half * 512:(half + 1) * 512], pg.t, br.t, ALU.add,
                        rd=[pg.b, br.b], wr=[gr.b])
    m3 = self.modT.t.rearrange("p (c s) -> p c s", s=2)
    self.tt("dve", m3, pm.t.rearrange("p (c s) -> p c s", s=2),
            adab.t.unsqueeze(2).to_broadcast([128, 48, 2]), ALU.add, rd=[pm.b, adab.b], wr=[self.modT.b])
    for c0 in (8, 32):
        self.ts("dve", m3[:, c0:c0 + 8, :], m3[:, c0:c0 + 8, :], 1.0, None, ALU.add, None,
                rd=[self.modT.b], wr=[self.modT.b])
    self.phase_end()


def _phase1(self, l, xsrc):
    self.phase_begin()
    wi, wib = self.pre_wi
    wi3 = wi.t.rearrange("p (k n) -> p k n", k=8)
    gain = self.sb("gain", 640, F32)
    self.dma("sp", gain.t, dap(self.din["qkg"], l * 640, [(0, 128), (1, 640)]), wr=[gain.b])
    self.rcos = self.sb("rcos", 64 * 32, F32)
    self.rsin = self.sb("rsin", 64 * 32, F32)
    self.dma("sp", self.rcos.t, self.din["rope_cos"].rearrange("p t c -> p (t c)"), wr=[self.rcos.b])
    self.dma("sp", self.rsin.t, self.din["rope_sin"].rearrange("p t c -> p (t c)"), wr=[self.rsin.b])
    xbs = [self.sb("xb%d" % i, 4 * 1024, BF16) for i in range(2)]
    uTs = [self.sb("uT%d" % i, 8 * 512, BF16) for i in range(2)]
    lhs_ = [self.sb("lhs%d" % i, 10 * 512, F32) for i in range(2)]
    lhb = [[self.s.buf() for m in range(10)] for i in range(2)]
    sq_ = [self.sb("sq%d" % i, 640, F32) for i in range(2)]
    ss_ = [self.sb("ss%d" % i, 10, F32) for i in range(2)]
    rstd_ = [self.sb("rstd%d" % i, 10, F32) for i in range(2)]
    qn_ = [self.sb("qn%d" % i, 640, F32) for i in range(2)]
    tmp_ = [[self.sb("tmp%d%d" % (b, i), 320, F32) for i in range(4)] for b in range(2)]
    qkb_ = [self.sb("qkb%d" % i, 640, BF16) for i in range(2)]
    jn = 0
    qkTs = [self.sb("qkT%d" % i, 5 * 512, BF16) for i in range(2)]
    vgs = [self.sb("vg%d" % i, 2 * 4 * 192, BF16) for i in range(2)]
    for vg in vgs:
        self.memset("dve", vg.t, 1.0, wr=[vg.b])
    pts = [self.ps("pt%d" % i, 512, BF16) for i in range(2)]
    pfs = [self.ps("pf%d" % i) for i in range(2)]
    psq_ = [self.ps("psq%d" % i) for i in range(2)]
    pskv = self.ps("pskv", 256)
    pT = self.ps("pT", 640, BF16)
    m3 = self.modT.t.rearrange("p (c s) -> p c s", s=2)
    rc3 = self.rcos.t.rearrange("p (t c) -> p t c", c=32)
    rs3 = self.rsin.t.rearrange("p (t c) -> p t c", c=32)
    it = 0
    for s in range(2):
        L = self.Ls[s]
        NT = L // 128
        for w in range(L // 512):
            xb, uT, lh, qkT, vg = xbs[it % 2], uTs[it % 2], lhs_[it % 2], qkTs[it % 2], vgs[it % 2]
            lb = lhb[it % 2]
            it += 1
            xb3 = xb.t.rearrange("p (j c) -> p j c", j=4)
            uT3 = uT.t.rearrange("p (k t) -> p k t", k=8)
            lh3 = lh.t.rearrange("p (m t) -> p m t", m=10)
            qkT3 = qkT.t.rearrange("p (c t) -> p c t", c=5)
            vg4 = vg.t.rearrange("p (g j c) -> p g j c", g=2, j=4)
            self.dma("pool", xb3, dap(xsrc[s], w * 512 * D, [(D, 128), (128 * D, 4), (1, D)]), wr=[xb.b])
            for kc in range(8):
                pt = pts[kc % 2]
                for j in range(4):
                    self.tr(pt.t[:, j * 128:(j + 1) * 128], xb3[:, j, kc * 128:(kc + 1) * 128], self.ident.t,
                            rd=[xb.b, self.ident.b], wr=[pt.b])
                self.act(uT3[:, kc, :], pt.t, AF.Identity, rd=[pt.b, self.modT.b], wr=[uT.b],
                         scale=m3[:, 8 + kc, s:s + 1], bias=m3[:, kc, s:s + 1])
            for m in range(10):
                pf = pfs[m % 2]
                for kc in range(8):
                    self.mm(pf.t, wi3[:, kc, 768 + m * 128:768 + (m + 1) * 128], uT3[:, kc, :], kc == 0, kc == 7,
                            rd=[wib[kc], uT.b], wr=[pf.b])
                self.cp("act" if m % 2 else "dve", lh3[:, m, :], pf.t, rd=[pf.b], wr=[lb[m]])
            self.dma("sp", dap(self.lh[s], w * 512, [(L, 128), (128 * L, 10), (1, 512)]), lh3, rd=lb)
            for j in range(4):
                T = w * 4 + j
                jb = jn % 2
                jn += 1
                sq, ss, rstd, qn, tmp, qkb, psq = sq_[jb], ss_[jb], rstd_[jb], qn_[jb], tmp_[jb], qkb_[jb], psq_[jb]
                for kc in range(8):
                    self.mm(psq.t, uT3[:, kc, j * 128:(j + 1) * 128], wi3[:, kc, 0:512], kc == 0, kc == 7,
                            rd=[wib[kc], uT.b], wr=[psq.b])
                for kc in range(8):
                    self.mm(pskv.t, uT3[:, kc, j * 128:(j + 1) * 128], wi3[:, kc, 512:768], kc == 0, kc == 7,
                            rd=[wib[kc], uT.b], wr=[pskv.b])
                self.act(sq.t[:, 0:512], psq.t, AF.Square, rd=[psq.b], wr=[sq.b])
                self.act(sq.t[:, 512:640], pskv.t[:, 0:128], AF.Square, rd=[pskv.b], wr=[sq.b])
                self.s.op("dve", (lambda e, o=ss.t, i=sq.t.rearrange("p (h d) -> p h d", d=64):
                                  e.tensor_reduce(out=o, in_=i, axis=AX.X, op=ALU.add)), rd=[sq.b], wr=[ss.b])
                self.act(rstd.t, ss.t, AF.Sqrt, rd=[ss.b, self.epsq.b], wr=[rstd.b], scale=1.0 / 64, bias=self.epsq.t)
                self.recip(rstd.t, rstd.t, rd=[rstd.b], wr=[rstd.b])
                qn3 = qn.t.rearrange("p (h d) -> p h d", d=64)
                self.tt("dve", qn3[:, 0:8, :], psq.t.rearrange("p (h d) -> p h d", d=64),
                        rstd.t[:, 0:8].unsqueeze(2).to_broadcast([128, 8, 64]), ALU.mult,
                        rd=[psq.b, rstd.b], wr=[qn.b])
                self.tt("dve", qn3[:, 8:10, :], pskv.t[:, 0:128].rearrange("p (h d) -> p h d", d=64),
                        rstd.t[:, 8:10].unsqueeze(2).to_broadcast([128, 2, 64]), ALU.mult,
                        rd=[pskv.b, rstd.b], wr=[qn.b])
                self.tt("dve", qn.t, qn.t, gain.t, ALU.mult, rd=[qn.b, gain.b], wr=[qn.b])
                for c0 in (0, 128):
                    self.cp("act", vg4[:, :, j, c0:c0 + 64], pskv.t[:, 128:256].rearrange("p (g d) -> p g d", g=2),
                            rd=[pskv.b], wr=[vg.b])
                qn5 = qn.t.rearrange("p (h a two f) -> p h a two f", h=10, a=2, two=2)
                qb5 = qkb.t.rearrange("p (h a two f) -> p h a two f", h=10, a=2, two=2)
                x1, x2 = qn5[:, :, :, 0, :], qn5[:, :, :, 1, :]
                cb = rc3[:, T, :].rearrange("p (a f) -> p a f", a=2).unsqueeze(1).to_broadcast([128, 10, 2, 16])
                sb_ = rs3[:, T, :].rearrange("p (a f) -> p a f", a=2).unsqueeze(1).to_broadcast([128, 10, 2, 16])
                t4 = [t.t.rearrange("p (h a f) -> p h a f", h=10, a=2) for t in tmp]
                self.tt("dve", t4[0], x1, cb, ALU.mult, rd=[qn.b, self.rcos.b], wr=[tmp[0].b])
                self.tt("dve", t4[1], x2, sb_, ALU.mult, rd=[qn.b, self.rsin.b], wr=[tmp[1].b])
                self.tt("dve", qb5[:, :, :, 0, :], t4[0], t4[1], ALU.subtract, rd=[tmp[0].b, tmp[1].b], wr=[qkb.b])
                self.tt("dve", t4[2], x2, cb, ALU.mult, rd=[qn.b, self.rcos.b], wr=[tmp[2].b])
                self.tt("dve", t4[3], x1, sb_, ALU.mult, rd=[qn.b, self.rsin.b], wr=[tmp[3].b])
                self.tt("dve", qb5[:, :, :, 1, :], t4[2], t4[3], ALU.add, rd=[tmp[2].b, tmp[3].b], wr=[qkb.b])
                for c in range(5):
                    self.tr(pT.t[:, c * 128:(c + 1) * 128], qkb.t[:, c * 128:(c + 1) * 128], self.ident.t,
                            rd=[qkb.b, self.ident.b], wr=[pT.b])
                self.cp("act", qkT3[:, :, j * 128:(j + 1) * 128], pT.t.rearrange("p (c t) -> p c t", c=5),
                        rd=[pT.b], wr=[qkT.b])
            self.dma("sp", dap(self.qT[s], w * 512, [(L, 128), (128 * L, 4), (1, 512)]), qkT3[:, 0:4, :], rd=[qkT.b])
            self.dma("sp", self.kT[s][:, w * 512:(w + 1) * 512], qkT3[:, 4, :], rd=[qkT.b])
            self.dma("sp", dap(self.vaug[s], w * 4 * 192, [(NT * 192, 128), (128 * NT * 192, 2), (1, 768)]),
                     vg.t.rearrange("p (g c) -> p g c", g=2), rd=[vg.b])
    self.phase_end()
    self.wctx.close()


K.setup = _setup
K.phase0 = _phase0
K.phase1 = _phase1


def _phase2(self, l):
    self.phase_begin()
    Kds = [self.sb("Kd%d" % i, LS, BF16) for i in range(2)]
    Vas = [self.sb("Va%d" % i, 64 * 192, BF16) for i in range(2)]
    Q2s = [self.sb("Q2%d" % i, 2 * LS, BF16) for i in range(2)]
    blocks = [(s, g) for s in range(2) for g in range(2)]

    def load_blk(bi):
        s, g = blocks[bi]
        L = self.Ls[s]
        NT = L // 128
        Kd, Va, Q2 = Kds[bi % 2], Vas[bi % 2], Q2s[bi % 2]
        self.dma("sp", Kd.t[0:64, 0:L], self.kT[s][g * 64:(g + 1) * 64, :], wr=[Kd.b])
        self.dma("sp", Kd.t[64:128, 0:L], self.kT[s][g * 64:(g + 1) * 64, :], wr=[Kd.b])
        self.dma("sp", Va.t[:, 0:NT * 192], self.vaug[s][g].rearrange("p t c -> p (t c)"), wr=[Va.b])
        Q3 = Q2.t.rearrange("p (h t) -> p h t", h=2)
        self.dma("sp", Q3[:, :, 0:L], dap(self.qT[s], 2 * g * 128 * L, [(L, 128), (128 * L, 2), (1, L)]), wr=[Q2.b])
    PAB = [self.sb("PAB%d" % i, 1024, BF16) for i in range(3)]
    rr = self.sb("rr", 512, F32)
    rAb, rBb = self.s.buf("rA"), self.s.buf("rB")
    atts = [self.sb("att%d" % i, 512, BF16) for i in range(2)]
    psAB = [self.ps2("psAB%d" % i) for i in range(2)]
    oA = [self.ps("oA%d" % i) for i in range(2)]
    oB = [self.ps("oB%d" % i) for i in range(2)]
    nblk = 0
    load_blk(0)
    for bi, (s, g) in enumerate(blocks):
        if True:
            L = self.Ls[s]
            NT = L // 128
            Kd, Va, Q2 = Kds[bi % 2], Vas[bi % 2], Q2s[bi % 2]
            Q3 = Q2.t.rearrange("p (h t) -> p h t", h=2)
            if bi + 1 < len(blocks):
                load_blk(bi + 1)
            Va3 = Va.t.rearrange("p (t c) -> p t c", c=192)
            steps = [(qc, hp, st) for qc in range(L // 512) for hp in range(2) for st in range(NT)]

            def qk(i):
                qc, hp, st = steps[i]
                ib = i % 2
                self.mm(psAB[ib].t[:, 0:512], Kd.t[0:64, st * 128:(st + 1) * 128], Q3[0:64, hp, qc * 512:(qc + 1) * 512],
                        True, True, rd=[Kd.b, Q2.b], wr=[psAB[ib].b], tp=(0, 0))
                self.mm(psAB[ib].t[:, 512:1024], Kd.t[64:128, st * 128:(st + 1) * 128], Q3[64:128, hp, qc * 512:(qc + 1) * 512],
                        True, True, rd=[Kd.b, Q2.b], wr=[psAB[ib].b], tp=(64, 0))
            qk(0)
            for i, (qc, hp, st) in enumerate(steps):
                if i + 1 < len(steps):
                    qk(i + 1)
                ib, ip = i % 2, i % 3
                if st == 0:
                    nblk += 1
                io = nblk % 2
                self.act(PAB[ip].t, psAB[ib].t, AF.Exp, rd=[psAB[ib].b], wr=[PAB[ip].b], scale=0.125)
                self.mm(oA[io].t, Va3[:, st, 0:128], PAB[ip].t[:, 0:512], st == 0, st == NT - 1, rd=[Va.b, PAB[ip].b], wr=[oA[io].b])
                self.mm(oB[io].t, Va3[:, st, 64:192], PAB[ip].t[:, 512:1024], st == 0, st == NT - 1, rd=[Va.b, PAB[ip].b], wr=[oB[io].b])
                if st == NT - 1:
                    att = atts[io]
                    self.recip(rr.t[64:128, :], oA[io].t[64:128, :], rd=[oA[io].b], wr=[rAb])
                    self.tt("dve", att.t[0:64, :], oA[io].t[0:64, :], rr.t[64:128, :], ALU.mult,
                            rd=[oA[io].b, rAb], wr=[att.b])
                    self.recip(rr.t[0:64, :], oB[io].t[0:64, :], rd=[oB[io].b], wr=[rBb])
                    self.tt("dve", att.t[64:128, :], oB[io].t[64:128, :], rr.t[0:64, :], ALU.mult,
                            rd=[oB[io].b, rBb], wr=[att.b])
                    self.dma("sp", self.catT[s][(2 * g + hp) * 128:(2 * g + hp + 1) * 128, qc * 512:(qc + 1) * 512],
                             att.t, rd=[att.b])
    self.phase_end()


K.phase2 = _phase2


def _phase3(self, l):
    self.phase_begin()
    TC = 1024
    XC = self.sb("XC", LS, F32)
    XCB = self.sb("XCB", LS, BF16)
    HF = self.sb("HF", LS, F32)
    WA = self.sb("WA", 8 * 128, BF16)
    WA5 = WA.t.rearrange("p (d g c n) -> p d g c n", d=2, g=2, c=2)
    self.dma("pool", WA5, self.din["lruW"][l].rearrange("d g c k n -> k d g c n"), wr=[WA.b])
    cw = self.sb("cw", 10, F32)
    self.dma("sp", cw.t, self.din["lruw"][l].rearrange("p c k -> p (c k)"), wr=[cw.b])
    lb = self.sb("lb", 12, F32)
    self.dma("sp", lb.t, self.din["lrub"][l].rearrange("p c d k -> p (c d k)"), wr=[lb.b])
    lb4 = lb.t.rearrange("p (c d k) -> p c d k", c=2, d=2)
    cw3 = cw.t.rearrange("p (c k) -> p c k", c=2)
    sp_ = self.sb("sp", 4, F32)
    c12 = self.sb("c12", 8, F32)
    sp3 = sp_.t.rearrange("p (c d) -> p c d", c=2)
    self.act(sp3, lb4[:, :, :, 2], AF.Exp, rd=[lb.b], wr=[sp_.b], scale=-1.0)
    self.act(sp_.t, sp_.t, AF.Ln, rd=[sp_.b], wr=[sp_.b], bias=1.0)
    self.ts("dve", c12.t[:, 0:4], sp_.t, -8.0, None, ALU.mult, None, rd=[sp_.b], wr=[c12.b])
    self.ts("dve", c12.t[:, 4:8], sp_.t, -16.0, None, ALU.mult, None, rd=[sp_.b], wr=[c12.b])
    c4 = c12.t.rearrange("p (k c d) -> p k c d", k=2, c=2)
    xhs = [self.sb("xh%d" % i, TC + 3, F32) for i in range(2)]
    tR = [self.sb("tR%d" % i, TC, F32) for i in range(2)]
    tI = [self.sb("tI%d" % i, TC, F32) for i in range(2)]
    tA = [self.sb("tA%d" % i, TC, F32) for i in range(2)]
    tT = [self.sb("tT%d" % i, TC, F32) for i in range(2)]
    tB = [self.sb("tB%d" % i, TC, F32) for i in range(2)]
    tH = [self.sb("tH%d" % i, TC, F32) for i in range(2)]
    gs = [self.sb("g%d" % i, TC, F32) for i in range(2)]
    ggs = [self.sb("gg%d" % i, TC, F32) for i in range(2)]
    obs = [self.sb("ob%d" % i, TC, BF16) for i in range(2)]
    carry = self.sb("carry", 1, F32)
    prs = [self.ps("pr%d" % i) for i in range(4)]
    pis = [self.ps("pi%d" % i) for i in range(4)]
    it = 0
    nps = 0
    for s in range(2):
        L = self.Ls[s]
        NCH = L // TC
        for cc in range(2):
            for c in range(NCH):
                xh = xhs[it % 2]
                it += 1
                t0 = c * TC
                lo = max(t0 - 2, 0)
                hi = min(t0 + TC + 1, L)
                if c == 0:
                    self.memset("dve", xh.t[:, 0:2], 0.0, wr=[xh.b])
                if c == NCH - 1:
                    self.memset("dve", xh.t[:, TC + 2:TC + 3], 0.0, wr=[xh.b])
                self.dma("pool", xh.t[:, lo - (t0 - 2):hi - (t0 - 2)], self.lh[s][cc * 128:(cc + 1) * 128, lo:hi], wr=[xh.b])
                xc = XC.t[:, t0:t0 + TC]
                self.ts("dve", xc, xh.t[:, 0:TC], cw3[:, cc, 0:1], cw3[:, cc, 4:5], ALU.mult, ALU.add,
                        rd=[xh.b, cw.b], wr=[XC.b])
                for k in range(1, 4):
                    self.stt("dve", xc, xh.t[:, k:k + TC], cw3[:, cc, k:k + 1], xc, ALU.mult, ALU.add,
                             rd=[xh.b, cw.b, XC.b], wr=[XC.b])
                self.cp("act", XCB.t[:, t0:t0 + TC], xc, rd=[XC.b], wr=[XCB.b])
            for d in range(2):
                order = list(range(NCH)) if d == 0 else list(range(NCH - 1, -1, -1))
                for ci, c in enumerate(order):
                    i2 = it % 2
                    it += 1
                    t0 = c * TC
                    for sub in range(2):
                        pr, pi = prs[nps % 4], pis[nps % 4]
                        nps += 1
                        cols = slice(t0 + sub * 512, t0 + (sub + 1) * 512)
                        self.mm(pr.t, WA5[:, d, 0, cc, :], XCB.t[:, cols], True, True, rd=[WA.b, XCB.b], wr=[pr.b])
                        self.mm(pi.t, WA5[:, d, 1, cc, :], XCB.t[:, cols], True, True, rd=[WA.b, XCB.b], wr=[pi.b])
                        self.act(tR[i2].t[:, sub * 512:(sub + 1) * 512], pr.t, AF.Sigmoid, rd=[pr.b, lb.b], wr=[tR[i2].b],
                                 bias=lb4[:, cc, d, 0:1])
                        self.act(tI[i2].t[:, sub * 512:(sub + 1) * 512], pi.t, AF.Sigmoid, rd=[pi.b, lb.b], wr=[tI[i2].b],
                                 bias=lb4[:, cc, d, 1:2])
                    self.act(tA[i2].t, tR[i2].t, AF.Exp, rd=[tR[i2].b, c12.b], wr=[tA[i2].b], scale=c4[:, 0, cc, d:d + 1])
                    self.act(tT[i2].t, tR[i2].t, AF.Exp, rd=[tR[i2].b, c12.b], wr=[tT[i2].b], scale=c4[:, 1, cc, d:d + 1])
                    self.act(tT[i2].t, tT[i2].t, AF.Sqrt, rd=[tT[i2].b], wr=[tT[i2].b], scale=-1.0, bias=1.0)
                    self.tt("dve", tB[i2].t, tI[i2].t, XC.t[:, t0:t0 + TC], ALU.mult, rd=[tI[i2].b, XC.b], wr=[tB[i2].b])
                    self.tt("dve", tB[i2].t, tB[i2].t, tT[i2].t, ALU.mult, rd=[tB[i2].b, tT[i2].b], wr=[tB[i2].b])
                    if d == 0:
                        init = 0.0 if ci == 0 else HF.t[:, t0 - 1:t0]
                        self.s.op("dve", (lambda e, o=HF.t[:, t0:t0 + TC], a=tA[i2].t, b=tB[i2].t, i0=init:
                                          e.tensor_tensor_scan(out=o, data0=a, data1=b, initial=i0, op0=ALU.mult, op1=ALU.add)),
                                  rd=[tA[i2].b, tB[i2].b, HF.b], wr=[HF.b])
                    else:
                        init = 0.0 if ci == 0 else carry.t
                        self.s.op("dve", (lambda e, o=rev(tH[i2].t), a=rev(tA[i2].t), b=rev(tB[i2].t), i0=init:
                                          e.tensor_tensor_scan(out=o, data0=a, data1=b, initial=i0, op0=ALU.mult, op1=ALU.add)),
                                  rd=[tA[i2].b, tB[i2].b, carry.b], wr=[tH[i2].b])
                        self.cp("dve", carry.t, tH[i2].t[:, 0:1], rd=[tH[i2].b], wr=[carry.b])
                        self.tt("dve", HF.t[:, t0:t0 + TC], HF.t[:, t0:t0 + TC], tH[i2].t, ALU.add,
                                rd=[HF.b, tH[i2].b], wr=[HF.b])
            for c in range(NCH):
                i2 = it % 2
                it += 1
                t0 = c * TC
                self.dma("pool", gs[i2].t, self.lh[s][256 + cc * 128:256 + (cc + 1) * 128, t0:t0 + TC], wr=[gs[i2].b])
                self.act(ggs[i2].t, gs[i2].t, AF.Gelu, rd=[gs[i2].b], wr=[ggs[i2].b])
                self.tt("dve", obs[i2].t, HF.t[:, t0:t0 + TC], ggs[i2].t, ALU.mult, rd=[HF.b, ggs[i2].b], wr=[obs[i2].b])
                self.dma("sp", self.catT[s][512 + cc * 128:512 + (cc + 1) * 128, t0:t0 + TC], obs[i2].t, rd=[obs[i2].b])
    self.phase_end()


K.phase3 = _phase3


def _ln_tail(self, po, n, xsrc_rows, gate, lg, lbias, dst_rows, T):
    i2 = T["i"] % 2
    T["i"] += 1
    xr, ysb, st, mv, rs, nm = T["xr"][i2], T["y"][i2], T["st"][i2], T["mv"][i2], T["rs"][i2], T["nm"][i2]
    self.dma("pool", xr.t[0:n, :], xsrc_rows, wr=[xr.b])
    for h in range(2):
        self.tt("dve", ysb.t[0:n, h * 512:(h + 1) * 512], po[h].t[0:n, :], gate.t[0:n, h * 512:(h + 1) * 512], ALU.mult,
                rd=[po[h].b, gate.b], wr=[ysb.b])
    self.stt("dve", ysb.t[0:n, :], xr.t[0:n, :], ALPHA, ysb.t[0:n, :], ALU.mult, ALU.add, rd=[xr.b, ysb.b], wr=[ysb.b])
    st3 = st.t.rearrange("p (c k) -> p c k", c=2)
    for h in range(2):
        self.s.op("dve", (lambda e, o=st3[0:n, h, :], i=ysb.t[0:n, h * 512:(h + 1) * 512]: e.bn_stats(out=o, in_=i)),
                  rd=[ysb.b], wr=[st.b])
    self.s.op("dve", (lambda e, o=mv.t[0:n, :], i=st3[0:n, :, :]: e.bn_aggr(out=o, in_=i)), rd=[st.b], wr=[mv.b])
    self.act(rs.t[0:n, :], mv.t[0:n, 1:2], AF.Sqrt, rd=[mv.b, self.epsl.b], wr=[rs.b], bias=self.epsl.t[0:n, :])
    self.recip(rs.t[0:n, :], rs.t[0:n, :], rd=[rs.b], wr=[rs.b])
    self.ts("dve", nm.t[0:n, :], mv.t[0:n, 0:1], -1.0, rs.t[0:n, :], ALU.mult, ALU.mult, rd=[mv.b, rs.b], wr=[nm.b])
    self.act(ysb.t[0:n, :], ysb.t[0:n, :], AF.Identity, rd=[ysb.b, rs.b, nm.b], wr=[ysb.b],
             scale=rs.t[0:n, :], bias=nm.t[0:n, :])
    self.tt("pool", ysb.t[0:n, :], ysb.t[0:n, :], lg.t[0:n, :], ALU.mult, rd=[ysb.b, lg.b], wr=[ysb.b])
    self.tt("pool", xr.t[0:n, :], ysb.t[0:n, :], lbias.t[0:n, :], ALU.add, rd=[ysb.b, lbias.b], wr=[xr.b])
    self.dma("sp", dst_rows, xr.t[0:n, :], rd=[xr.b])


def _ln_bufs(self):
    return {"i": 0, "xr": [self.sb("lnx%d" % i, 1024, F32) for i in range(2)],
            "y": [self.sb("lny%d" % i, 1024, F32) for i in range(2)],
            "st": [self.sb("lnst%d" % i, 12, F32) for i in range(2)], "mv": [self.sb("lnmv%d" % i, 2, F32) for i in range(2)],
            "rs": [self.sb("lnrs%d" % i, 1, F32) for i in range(2)], "nm": [self.sb("lnnm%d" % i, 1, F32) for i in range(2)]}


def _phase5(self, l, xsrc):
    self.wctx_u = ExitStack()
    wu = self.sb("wu", 8 * 2 * DFF, BF16, ctx=self.wctx_u)
    wu3 = wu.t.rearrange("p (k n) -> p k n", k=8)
    wub = [self.s.buf() for k in range(8)]
    self.pre_wu = (wu, wub)
    self.phase_begin()
    wo = self.sb("wo", 8 * 1024, BF16)
    wo3 = wo.t.rearrange("p (k n) -> p k n", k=8)
    wob = [self.s.buf() for k in range(8)]
    for kc in range(8):
        self.dma("pool", wo3[:, kc, :], self.din["w_out"][l, kc * 128:(kc + 1) * 128, :], wr=[wob[kc]])
    lg = self.sb("lg", 1024, F32)
    lb = self.sb("lb", 1024, F32)
    self.dma("sp", lg.t, dap(self.din["ln1_g"], l * 1024, [(0, 128), (1, 1024)]), wr=[lg.b])
    self.dma("sp", lb.t, dap(self.din["ln1_b"], l * 1024, [(0, 128), (1, 1024)]), wr=[lb.b])
    for kc in range(8):
        self.dma("pool", wu3[:, kc, :], self.din["w_up"][l, kc * 128:(kc + 1) * 128, :], wr=[wub[kc]])
    T = self.ln_bufs()
    cts = [self.sb("ct%d" % i, 8 * 512, BF16) for i in range(2)]
    pos = [[self.ps("po%d%d" % (i, h)) for h in range(2)] for i in range(2)]
    it = 0
    nj = 0
    for s in range(2):
        L = self.Ls[s]
        for w in range(L // 512):
            ct = cts[it % 2]
            it += 1
            ct3 = ct.t.rearrange("p (k t) -> p k t", k=8)
            self.dma("sp", ct3, dap(self.catT[s], w * 512, [(L, 128), (128 * L, 8), (1, 512)]), wr=[ct.b])
            for j in range(4):
                po = pos[nj % 2]
                nj += 1
                for h in range(2):
                    for kc in range(8):
                        self.mm(po[h].t, ct3[:, kc, j * 128:(j + 1) * 128], wo3[:, kc, h * 512:(h + 1) * 512],
                                kc == 0, kc == 7, rd=[ct.b, wob[kc]], wr=[po[h].b])
                r0 = w * 512 + j * 128
                self.ln_tail(po, 128, xsrc[s][r0:r0 + 128, :], self.grow[0 * 2 + s], lg, lb,
                             self.xres[s][r0:r0 + 128, :], T)
    self.phase_end()


def _phase6(self, l, dst):
    self.phase_begin()
    WN = 254
    wu, wub = self.pre_wu
    wu3 = wu.t.rearrange("p (k n) -> p k n", k=8)
    wd = self.sb("wd", 22 * 1024, BF16)
    wd3 = wd.t.rearrange("p (k n) -> p k n", k=22)
    wdb = [self.s.buf() for k in range(22)]
    for kc in range(22):
        self.dma("pool", wd3[:, kc, :], self.din["w_down"][l, kc * 128:(kc + 1) * 128, :], wr=[wdb[kc]])
    cw = self.sb("fcw", 44 * 4, F32)
    self.dma("sp", cw.t, self.din["ffw"][l].rearrange("p c k -> p (c k)"), wr=[cw.b])
    cw3 = cw.t.rearrange("p (c k) -> p c k", k=4)
    lg = self.sb("lg", 1024, F32)
    lb = self.sb("lb", 1024, F32)
    self.dma("sp", lg.t, dap(self.din["ln2_g"], l * 1024, [(0, 128), (1, 1024)]), wr=[lg.b])
    self.dma("sp", lb.t, dap(self.din["ln2_b"], l * 1024, [(0, 128), (1, 1024)]), wr=[lb.b])
    T = self.ln_bufs()
    xbs = [self.sb("fxb%d" % i, 2 * 1024, BF16) for i in range(2)]
    uT = self.sb("fuT", 8 * 256, BF16)
    uT3 = uT.t.rearrange("p (k t) -> p k t", k=8)
    g = self.sb("fg", 22 * 256, BF16)
    g3 = g.t.rearrange("p (k t) -> p k t", k=22)
    tgs = [self.sb("tg%d" % i, 256, F32) for i in range(2)]
    tvs = [self.sb("tv%d" % i, 256, F32) for i in range(2)]
    pts = [self.ps("fpt%d" % i, 256, BF16) for i in range(2)]
    pgs = [self.ps("fpg%d" % i, 256) for i in range(2)]
    pvs = [self.ps("fpv%d" % i, 256) for i in range(2)]
    po = [self.ps("fpo%d" % h) for h in range(2)]
    m3 = self.modT.t.rearrange("p (c s) -> p c s", s=2)
    wins = [(s, w) for s in range(2) for w in range((self.Ls[s] + WN - 1) // WN)]

    def load_win(idx):
        s, w = wins[idx]
        L = self.Ls[s]
        t0 = w * WN
        xb = xbs[idx % 2]
        xb3 = xb.t.rearrange("p (j c) -> p j c", j=2)
        if (w == 0) or (t0 + 255 > L):
            self.memset("dve", xb.t, 0.0, wr=[xb.b])
        for j in range(2):
            a = t0 - 1 + 128 * j
            lo, hi = max(a, 0), min(a + 128, L)
            if hi > lo:
                self.dma("pool", xb3[lo - a:hi - a, j, :], self.xres[s][lo:hi, :], wr=[xb.b])

    load_win(0)
    for idx, (s, w) in enumerate(wins):
        if True:
            L = self.Ls[s]
            xs_ = self.xres[s]
            t0 = w * WN
            nv = min(WN, L - t0)
            xb = xbs[idx % 2]
            xb3 = xb.t.rearrange("p (j c) -> p j c", j=2)
            for kc in range(8):
                pt = pts[kc % 2]
                for j in range(2):
                    self.tr(pt.t[:, j * 128:(j + 1) * 128], xb3[:, j, kc * 128:(kc + 1) * 128], self.ident.t,
                            rd=[xb.b, self.ident.b], wr=[pt.b])
                self.act(uT3[:, kc, :], pt.t, AF.Identity, rd=[pt.b, self.modT.b], wr=[uT.b],
                         scale=m3[:, 32 + kc, s:s + 1], bias=m3[:, 24 + kc, s:s + 1])
            if idx + 1 < len(wins):
                load_win(idx + 1)
            if w == 0:
                self.memset("dve", uT3[:, :, 0:1], 0.0, wr=[uT.b])
            if t0 + nv >= L:
                c0 = L - (t0 - 1)
                self.memset("dve", uT3[:, :, c0:256], 0.0, wr=[uT.b])
            for jc in range(22):
                pg, pv = pgs[jc % 2], pvs[jc % 2]
                tg, tv = tgs[jc % 2], tvs[jc % 2]
                for kc in range(8):
                    self.mm(pg.t, wu3[:, kc, jc * 128:(jc + 1) * 128], uT3[:, kc, :], kc == 0, kc == 7,
                            rd=[wub[kc], uT.b], wr=[pg.b])
                for kc in range(8):
                    self.mm(pv.t, wu3[:, kc, DFF + jc * 128:DFF + (jc + 1) * 128], uT3[:, kc, :], kc == 0, kc == 7,
                            rd=[wub[kc], uT.b], wr=[pv.b])
                for (pp, tt_, ch) in ((pg, tg, jc), (pv, tv, 22 + jc)):
                    self.act(tt_.t[:, 0:WN], pp.t[:, 0:WN], AF.Identity, rd=[pp.b, cw.b], wr=[tt_.b],
                             scale=cw3[:, ch, 0:1], bias=cw3[:, ch, 3:4])
                    for k in (1, 2):
                        self.stt("dve", tt_.t[:, 0:WN], pp.t[:, k:k + WN], cw3[:, ch, k:k + 1], tt_.t[:, 0:WN],
                                 ALU.mult, ALU.add, rd=[pp.b, cw.b, tt_.b], wr=[tt_.b])
                self.act(tg.t[:, 0:WN], tg.t[:, 0:WN], AF.Gelu, rd=[tg.b], wr=[tg.b])
                self.tt("dve", g3[:, jc, 0:WN], tg.t[:, 0:WN], tv.t[:, 0:WN], ALU.mult, rd=[tg.b, tv.b], wr=[g.b])
            for c0 in (0, 128):
                n = min(128, nv - c0)
                if n <= 0:
                    continue
                for h in range(2):
                    for kc in range(22):
                        self.mm(po[h].t[0:n, :], g3[:, kc, c0:c0 + n], wd3[:, kc, h * 512:(h + 1) * 512],
                                kc == 0, kc == 21, rd=[g.b, wdb[kc]], wr=[po[h].b])
                r0 = t0 + c0
                self.ln_tail(po, n, xs_[r0:r0 + n, :], self.grow[1 * 2 + s], lg, lb, dst[s][r0:r0 + n, :], T)
    self.phase_end()
    self.wctx_u.close()


K.ln_tail = _ln_tail
K.ln_bufs = _ln_bufs
K.phase5 = _phase5
K.phase6 = _phase6


def _sync(self):
    self.s.barrier()


def _phase4(self, l):
    self.phase_begin()
    TWO_PI = 2.0 * math.pi
    F128 = self.sb("F128", 3 * 128, BF16)
    self.dma("pool", F128.t, self.din["hF128"].rearrange("p r k -> p (r k)"), wr=[F128.b])
    F3 = F128.t.rearrange("p (r k) -> p r k", r=3)
    G128 = self.sb("G128", 2 * 256, BF16)
    self.dma("pool", G128.t, self.din["hG128"].rearrange("p r k -> p (r k)"), wr=[G128.b])
    G3 = G128.t.rearrange("p (r k) -> p r k", r=2)
    hyw = self.sb("hyw", 24, F32)
    self.dma("sp", hyw.t, self.din["hyw"][l].rearrange("p c k -> p (c k)"), wr=[hyw.b])
    hyw3 = hyw.t.rearrange("p (c k) -> p c k", k=4)
    drow = self.sb("drow", 512, F32)
    self.dma("sp", drow.t, dap(self.din["hy_bias"], l * 512, [(0, 128), (1, 512)]), wr=[drow.b])
    negd = self.sb("negd", 2, F32)
    self.dma("sp", negd.t, self.din["hnegd"], wr=[negd.b])
    w1 = self.sb("hw1", 64, F32)
    w2 = self.sb("hw2", 64, F32)
    w3 = self.sb("hw3", 64, F32)
    wout = self.sb("hwout", 1024, F32)
    hyb = self.sb("hyb", 4, F32)
    self.dma("sp", w1.t[0:33, :], self.din["hy_w1"][l], wr=[w1.b])
    self.dma("sp", w2.t[0:64, :], self.din["hy_w2"][l], wr=[w2.b])
    self.dma("sp", w3.t[0:64, :], self.din["hy_w3"][l], wr=[w3.b])
    self.dma("sp", wout.t[0:64, :], self.din["hy_wout"][l], wr=[wout.b])
    self.dma("sp", hyb.t[0:64, :], self.din["hyb"][l], wr=[hyb.b])
    bfq = self.sb("bfq", 3, F32)
    self.ts("dve", bfq.t[0:64, :], hyb.t[0:64, 0:3], hyb.t[0:64, 3:4], None, ALU.mult, None, rd=[hyb.b], wr=[bfq.b])
    zero1 = self.sb("zero1", 1, F32)
    self.memset("dve", zero1.t, 0.0, wr=[zero1.b])
    FN1 = self.sb("FN1", 256, BF16)
    Tt = self.sb("Tt", 256, F32)
    T2 = self.sb("T2", 256, F32)
    GN1 = self.sb("GN1", 128, BF16)
    rnrow = self.sb("rnrow", 512, F32)
    xhs = [self.sb("hxh%d" % i, 2050, F32) for i in range(2)]
    cos_ = [self.sb("hco%d" % i, 2048, F32) for i in range(2)]
    zcs = [self.sb("hzc%d" % i, 512, F32) for i in range(2)]
    tls = [self.sb("htl%d" % i, 512, F32) for i in range(2)]
    decs = [[self.sb("hdec%d%d" % (i, c), 512, F32) for c in range(2)] for i in range(2)]
    ya = self.sb("hya", 512, F32)
    kk = self.sb("hkk", 512, F32)
    hks = [self.sb("hk%d" % i, 512, F32) for i in range(3)]
    kts = [self.sb("hkt%d" % i, 512, F32) for i in range(3)]
    kab = self.sb("hkab", 512, F32)
    acc = self.sb("hacc", 8 * 16, F32)
    kb0 = self.sb("hkb0", 4, F32)
    nrm = self.sb("hnrm", 4, F32)
    Xbs = [self.sb("hXb%d" % i, 512, BF16) for i in range(2)]
    zfs = [self.sb("hzf%d" % i, 512, F32) for i in range(4)]
    gfs = [self.sb("hgf%d" % i, 512, F32) for i in range(4)]
    kfts = [self.sb("hkft%d" % i, 4 * 2 * 128, BF16) for i in range(2)]
    Apre = self.sb("hApre", 512, BF16)
    Apim = self.sb("hApim", 512, BF16)
    Yre = self.sb("hYre", 512, BF16)
    Yim = self.sb("hYim", 512, BF16)
    Bpre = self.sb("hBpre", 512, BF16)
    Bpim = self.sb("hBpim", 512, BF16)
    tq = [self.sb("htq%d" % i, 512, F32) for i in range(4)]
    outf = [self.sb("houtf%d" % i, 512, F32) for i in range(2)]
    outb = [self.sb("houtb%d" % i, 512, BF16) for i in range(2)]
    ps1 = self.ps2("hps1")
    pXre = self.ps("hpXre")
    pXim = self.ps("hpXim")
    pB = self.ps2("hpB")
    py = self.ps("hpy")
    it = 0

    def fft_s1(Xb, Kp, N1):
        NH = N1 // 2 + 1
        cw = 2 * NH
        CS = 256 if N1 == 128 else 128
        X3 = Xb.t.rearrange("p (c i) -> p c i", c=4)
        for ch in range(4):
            off = ch * CS
            self.mm(ps1.t[:, off:off + cw], X3[0:Kp, ch, :], FN1.t[0:Kp, 0:cw], True, True,
                    rd=[Xb.b, FN1.b], wr=[ps1.b])

    def fft_d1s2(N1):
        NH = N1 // 2 + 1
        cw = 2 * NH
        CS = 256 if N1 == 128 else 128
        Tt3 = Tt.t[:, 0:cw].rearrange("p (r k) -> p r k", r=2)
        Ar3 = Apre.t[:, 0:4 * NH].rearrange("p (c k) -> p c k", c=4)
        Ai3 = Apim.t[:, 0:4 * NH].rearrange("p (c k) -> p c k", c=4)
        v = ps1.t[:, 0:4 * CS].rearrange("p (c w) -> p c w", c=4)
        Are, Aim = v[:, :, 0:NH], v[:, :, NH:cw]
        Tre = Tt3[:, 0, :].unsqueeze(1).to_broadcast([128, 4, NH])
        Tim = Tt3[:, 1, :].unsqueeze(1).to_broadcast([128, 4, NH])
        t = [q.t[:, 0:4 * NH].rearrange("p (c k) -> p c k", c=4) for q in tq]
        self.tt("dve", t[0], Are, Tre, ALU.mult, rd=[ps1.b, Tt.b], wr=[tq[0].b])
        self.tt("dve", t[1], Aim, Tim, ALU.mult, rd=[ps1.b, Tt.b], wr=[tq[1].b])
        self.tt("dve", Ar3, t[0], t[1], ALU.subtract, rd=[tq[0].b, tq[1].b], wr=[Apre.b])
        self.tt("dve", t[2], Are, Tim, ALU.mult, rd=[ps1.b, Tt.b], wr=[tq[2].b])
        self.tt("dve", t[3], Aim, Tre, ALU.mult, rd=[ps1.b, Tt.b], wr=[tq[3].b])
        self.tt("dve", Ai3, t[2], t[3], ALU.add, rd=[tq[2].b, tq[3].b], wr=[Apim.b])
        W = 4 * NH
        self.mm(pXre.t[:, 0:W], F3[:, 0, :], Apre.t[:, 0:W], True, False, rd=[F128.b, Apre.b], wr=[pXre.b])
        self.mm(pXre.t[:, 0:W], F3[:, 2, :], Apim.t[:, 0:W], False, True, rd=[F128.b, Apim.b], wr=[pXre.b])
        self.mm(pXim.t[:, 0:W], F3[:, 0, :], Apim.t[:, 0:W], True, False, rd=[F128.b, Apim.b], wr=[pXim.b])
        self.mm(pXim.t[:, 0:W], F3[:, 1, :], Apre.t[:, 0:W], False, True, rd=[F128.b, Apre.b], wr=[pXim.b])

    for s in range(2):
        L = self.Ls[s]
        NB = L // 128
        N1 = 2 * NB
        NH = N1 // 2 + 1
        self.dma("pool", FN1.t[0:N1, 0:2 * NH], self.din["hFN1_%d" % s], wr=[FN1.b])
        self.dma("sp", Tt.t[:, 0:2 * NH], self.din["hT_%d" % s].rearrange("p r k -> p (r k)"), wr=[Tt.b])
        self.dma("sp", T2.t[0:NH, :], self.din["hT2_%d" % s].rearrange("p r k -> p (r k)"), wr=[T2.b])
        self.dma("pool", GN1.t[0:NH, 0:2 * NB], self.din["hGN1_%d" % s].rearrange("p r k -> p (r k)"), wr=[GN1.b])
        TCc = 2048
        for m in range(6):
            for c in range(L // TCc):
                xh, co = xhs[it % 2], cos_[it % 2]
                it += 1
                t0 = c * TCc
                lo, hi = max(t0 - 1, 0), min(t0 + TCc + 1, L)
                if c == 0:
                    self.memset("dve", xh.t[:, 0:1], 0.0, wr=[xh.b])
                if hi == L:
                    self.memset("dve", xh.t[:, TCc + 1:TCc + 2], 0.0, wr=[xh.b])
                self.dma("pool", xh.t[:, lo - (t0 - 1):hi - (t0 - 1)], self.lh[s][512 + m * 128:512 + (m + 1) * 128, lo:hi], wr=[xh.b])
                self.ts("dve", co.t, xh.t[:, 0:TCc], hyw3[:, m, 0:1], hyw3[:, m, 3:4], ALU.mult, ALU.add,
                        rd=[xh.b, hyw.b], wr=[co.b])
                for k in (1, 2):
                    self.stt("dve", co.t, xh.t[:, k:k + TCc], hyw3[:, m, k:k + 1], co.t, ALU.mult, ALU.add,
                             rd=[xh.b, hyw.b, co.b], wr=[co.b])
                self.dma("sp", self.hc[s][m * 128:(m + 1) * 128, t0:t0 + TCc], co.t, rd=[co.b])
        NCH = L // 512
        self.memset("dve", acc.t, 0.0, wr=[acc.b])
        acc3 = acc.t.rearrange("p (m c) -> p m c", m=8)
        for c in range(NCH):
            zc, tl = zcs[c % 2], tls[c % 2]
            dec = decs[c % 2]
            t0 = c * 512
            self.dma("pool", zc.t[0:33, :], self.din["hz%d" % s][:, t0:t0 + 512], wr=[zc.b])
            self.dma("pool", tl.t, dap(self.din["htl%d" % s], t0, [(0, 128), (1, 512)]), wr=[tl.b])
            for cc in range(2):
                self.act(dec[cc].t, tl.t, AF.Exp, rd=[tl.b, negd.b], wr=[dec[cc].b], scale=negd.t[:, cc:cc + 1])
            h, hK = zc, 33
            for k, wk in enumerate((w1, w2, w3)):
                ph = ps1
                pho = (k % 2) * 512
                self.mm(ph.t[0:64, pho:pho + 512], wk.t[0:hK, :], h.t[0:hK, :], True, True, rd=[wk.b, h.b], wr=[ph.b])
                self.ts("dve", ya.t[0:64, :], ph.t[0:64, pho:pho + 512], hyb.t[0:64, 3:4], bfq.t[0:64, k:k + 1], ALU.mult, ALU.add,
                        rd=[ph.b, hyb.b, bfq.b], wr=[ya.b])
                self.ts("dve", kk.t[0:64, :], ya.t[0:64, :], 1.0 / TWO_PI, MAGIC, ALU.mult, ALU.add, rd=[ya.b], wr=[kk.b])
                self.ts("dve", kk.t[0:64, :], kk.t[0:64, :], -MAGIC, -TWO_PI, ALU.add, ALU.mult, rd=[kk.b], wr=[kk.b])
                self.tt("dve", ya.t[0:64, :], kk.t[0:64, :], ya.t[0:64, :], ALU.add, rd=[kk.b, ya.b], wr=[ya.b])
                self.ts("dve", ya.t[0:64, :], ya.t[0:64, :], math.pi, -math.pi, ALU.min, ALU.max, rd=[ya.b], wr=[ya.b])
                hk = hks[k]
                self.act(hk.t[0:64, :], ya.t[0:64, :], AF.Sin, rd=[ya.b], wr=[hk.b])
                h, hK = hk, 64
            nk = 0
            for o in range(2):
                for cc in range(2):
                    for dr in (1, 0):
                        m = o * 2 + dr
                        col0 = (m * 2 + cc) * 128
                        pk = (pXre, pXim)[nk % 2]
                        kt = kts[nk % 3]
                        nk += 1
                        self.mm(pk.t, wout.t[0:64, col0:col0 + 128], h.t[0:64, :], True, True, rd=[wout.b, h.b], wr=[pk.b])
                        row0 = cc * 128
                        ai = (o * 2 + cc) * 2 + dr
                        kdst = self.kfull[s][o, row0:row0 + 128, :]
                        if dr == 1:
                            self.tt("dve", rev(kt.t), pk.t, dec[cc].t, ALU.mult, rd=[pk.b, dec[cc].b], wr=[kt.b])
                            if c == 0:
                                self.cp("dve", kb0.t[:, o * 2 + cc:o * 2 + cc + 1], kt.t[:, 511:512], rd=[kt.b], wr=[kb0.b])
                                n_ = 511
                                self.dma("sp", kdst[:, 2 * L - 511:2 * L], kt.t[:, 0:511], rd=[kt.b])
                            else:
                                n_ = 512
                                self.dma("sp", kdst[:, 2 * L - t0 - 511:2 * L - t0 + 1], kt.t, rd=[kt.b])
                        else:
                            self.tt("dve", kt.t, pk.t, dec[cc].t, ALU.mult, rd=[pk.b, dec[cc].b], wr=[kt.b])
                            if c == 0:
                                self.tt("dve", kt.t[:, 0:1], kt.t[:, 0:1], kb0.t[:, o * 2 + cc:o * 2 + cc + 1], ALU.add,
                                        rd=[kt.b, kb0.b], wr=[kt.b])
                            n_ = 512
                            self.dma("sp", kdst[:, t0:t0 + 512], kt.t, rd=[kt.b])
                        self.act(kab.t[:, 0:n_], kt.t[:, 0:n_], AF.Abs, rd=[kt.b, acc.b], wr=[kab.b, acc.b],
                                 accum=acc3[:, ai, c:c + 1])
        self.s.op("dve", (lambda e, o_=nrm.t, i_=acc.t.rearrange("p (m c) -> p m c", m=4):
                          e.tensor_reduce(out=o_, in_=i_, axis=AX.X, op=ALU.add)), rd=[acc.b], wr=[nrm.b])
        self.recip(nrm.t, nrm.t, rd=[nrm.b], wr=[nrm.b])
        for o in range(2):
            for cc in range(2):
                self.dma("sp", dap(self.rnd, o * 256 + cc * 128, [(1, 128), (1, 1)]), nrm.t[:, o * 2 + cc:o * 2 + cc + 1], rd=[nrm.b], slow=True)
                self.dma("sp", dap(self.kfull[s], (o * 256 + cc * 128) * 2 * L + L, [(2 * L, 128), (1, 1)]), zero1.t, rd=[zero1.b], slow=True)
        self.sync()
        self.dma("sp", rnrow.t, dap(self.rnd, 0, [(0, 128), (1, 512)]), wr=[rnrow.b])
        def h3_a(g):
            o, c0, i2 = g
            Xb = Xbs[i2 % 2]
            self.dma("pool", Xb.t[0:N1, :].rearrange("p (c i) -> p c i", c=4),
                     dap(self.kfull[s], (o * 256 + c0) * 2 * L, [(128, N1), (2 * L, 4), (1, 128)]), wr=[Xb.b])
            fft_s1(Xb, N1, N1)

        def h3_c(g):
            o, c0, i2 = g
            kft = kfts[i2 % 2]
            k4 = kft.t[:, 0:8 * NH].rearrange("p (c r k) -> p c r k", c=4, r=2)
            rb = rnrow.t[:, o * 256 + c0:o * 256 + c0 + 4].unsqueeze(2).to_broadcast([128, 4, NH])
            self.tt("dve", k4[:, :, 0, :], pXre.t[:, 0:4 * NH].rearrange("p (c k) -> p c k", c=4), rb, ALU.mult,
                    rd=[pXre.b, rnrow.b], wr=[kft.b])
            self.tt("dve", k4[:, :, 1, :], pXim.t[:, 0:4 * NH].rearrange("p (c k) -> p c k", c=4), rb, ALU.mult,
                    rd=[pXim.b, rnrow.b], wr=[kft.b])
            self.dma("sp", dap(self.kfs[s], (o * 128 * 256 + c0) * 2 * NH, [(256 * 2 * NH, 128), (1, 8 * NH)]),
                     kft.t[:, 0:8 * NH], rd=[kft.b])

        G = [(o, grp * 4, o * 64 + grp) for o in range(2) for grp in range(64)]
        for n in range(len(G) + 1):
            if n < len(G):
                h3_a(G[n])
            if n >= 1:
                h3_c(G[n - 1])
            if n < len(G):
                fft_d1s2(N1)
        self.sync()
        T23 = T2.t.rearrange("p (r i) -> p r i", r=2)
        GN3 = GN1.t[:, 0:2 * NB].rearrange("p (r a) -> p r a", r=2)
        W = 4 * NH
        def h4_a(g):
            o, c0, i2 = g
            zsrc = self.hc[s] if o == 0 else self.z1[s]
            Xb, kft, zf, gf = Xbs[i2 % 2], kfts[i2 % 2], zfs[i2 % 4], gfs[i2 % 4]
            zap = dap(zsrc, c0 * L, [(128, NB), (L, 4), (1, 128)])
            self.dma("pool", Xb.t[0:NB, :].rearrange("p (c i) -> p c i", c=4), zap, wr=[Xb.b])
            self.dma("sp", zf.t[0:NB, :].rearrange("p (c i) -> p c i", c=4), zap, wr=[zf.b])
            self.dma("sp", gf.t[0:NB, :].rearrange("p (c i) -> p c i", c=4),
                     dap(self.hc[s], (256 * (o + 1) + c0) * L, [(128, NB), (L, 4), (1, 128)]), wr=[gf.b])
            self.dma("sp", kft.t[:, 0:8 * NH],
                     dap(self.kfs[s], (o * 128 * 256 + c0) * 2 * NH, [(256 * 2 * NH, 128), (1, 8 * NH)]), wr=[kft.b])
            fft_s1(Xb, NB, N1)

        def h4_d2s3(g):
            o, c0, i2 = g
            kft = kfts[i2 % 2]
            k4 = kft.t[:, 0:8 * NH].rearrange("p (c r k) -> p c r k", c=4, r=2)
            Kre, Kim = k4[:, :, 0, :], k4[:, :, 1, :]
            Xr = pXre.t[:, 0:W].rearrange("p (c k) -> p c k", c=4)
            Xi = pXim.t[:, 0:W].rearrange("p (c k) -> p c k", c=4)
            t = [q.t[:, 0:W].rearrange("p (c k) -> p c k", c=4) for q in tq]
            self.tt("dve", t[0], Xr, Kre, ALU.mult, rd=[pXre.b, kft.b], wr=[tq[0].b])
            self.tt("dve", t[1], Xi, Kim, ALU.mult, rd=[pXim.b, kft.b], wr=[tq[1].b])
            self.tt("dve", Yre.t[:, 0:W].rearrange("p (c k) -> p c k", c=4), t[0], t[1], ALU.subtract,
                    rd=[tq[0].b, tq[1].b], wr=[Yre.b])
            self.tt("dve", t[2], Xr, Kim, ALU.mult, rd=[pXre.b, kft.b], wr=[tq[2].b])
            self.tt("dve", t[3], Xi, Kre, ALU.mult, rd=[pXim.b, kft.b], wr=[tq[3].b])
            self.tt("dve", Yim.t[:, 0:W].rearrange("p (c k) -> p c k", c=4), t[2], t[3], ALU.add,
                    rd=[tq[2].b, tq[3].b], wr=[Yim.b])
            Yr3 = Yre.t[:, 0:W].rearrange("p (c k) -> p c k", c=4)
            Yi3 = Yim.t[:, 0:W].rearrange("p (c k) -> p c k", c=4)
            for ch in range(4):
                off = ch * 256
                self.mm(pB.t[0:NH, off:off + 256], Yr3[:, ch, :], G3[:, 0, :], True, False,
                        rd=[Yre.b, G128.b], wr=[pB.b])
                self.mm(pB.t[0:NH, off:off + 256], Yi3[:, ch, :], G3[:, 1, :], False, True,
                        rd=[Yim.b, G128.b], wr=[pB.b])

        def h4_d3s4(g):
            Br4 = Bpre.t[0:NH, :].rearrange("p (c i) -> p c i", c=4)
            Bi4 = Bpim.t[0:NH, :].rearrange("p (c i) -> p c i", c=4)
            v = pB.t[0:NH, :].rearrange("p (c r i) -> p c r i", c=4, r=2)
            Bre, Bim = v[:, :, 0, :], v[:, :, 1, :]
            Tre = T23[0:NH, 0, :].unsqueeze(1).to_broadcast([NH, 4, 128])
            Tim = T23[0:NH, 1, :].unsqueeze(1).to_broadcast([NH, 4, 128])
            t = [q.t[0:NH, 0:512].rearrange("p (c i) -> p c i", c=4) for q in tq]
            self.tt("dve", t[0], Bre, Tre, ALU.mult, rd=[pB.b, T2.b], wr=[tq[0].b])
            self.tt("dve", t[1], Bim, Tim, ALU.mult, rd=[pB.b, T2.b], wr=[tq[1].b])
            self.tt("dve", Br4, t[0], t[1], ALU.subtract, rd=[tq[0].b, tq[1].b], wr=[Bpre.b])
            self.tt("dve", t[2], Bre, Tim, ALU.mult, rd=[pB.b, T2.b], wr=[tq[2].b])
            self.tt("dve", t[3], Bim, Tre, ALU.mult, rd=[pB.b, T2.b], wr=[tq[3].b])
            self.tt("dve", Bi4, t[2], t[3], ALU.add, rd=[tq[2].b, tq[3].b], wr=[Bpim.b])
            self.mm(py.t[0:NB, :], GN3[0:NH, 0, :], Bpre.t[0:NH, :], True, False, rd=[GN1.b, Bpre.b], wr=[py.b])
            self.mm(py.t[0:NB, :], GN3[0:NH, 1, :], Bpim.t[0:NH, :], False, True, rd=[GN1.b, Bpim.b], wr=[py.b])

        def h4_d4(g):
            o, c0, i2 = g
            zf, gf = zfs[i2 % 4], gfs[i2 % 4]
            t0_ = tq[0].t[0:NB, :]
            db = drow.t[0:NB, o * 256 + c0:o * 256 + c0 + 4].unsqueeze(2).to_broadcast([NB, 4, 128])
            self.tt("dve", t0_.rearrange("p (c i) -> p c i", c=4), zf.t[0:NB, :].rearrange("p (c i) -> p c i", c=4), db,
                    ALU.mult, rd=[zf.b, drow.b], wr=[tq[0].b])
            self.tt("dve", t0_, t0_, py.t[0:NB, :], ALU.add, rd=[tq[0].b, py.b], wr=[tq[0].b])
            if o == 0:
                ob = outf[i2 % 2]
                self.tt("dve", ob.t[0:NB, :], t0_, gf.t[0:NB, :], ALU.mult, rd=[tq[0].b, gf.b], wr=[ob.b])
                self.dma("sp", dap(self.z1[s], c0 * L, [(128, NB), (L, 4), (1, 128)]),
                         ob.t[0:NB, :].rearrange("p (c i) -> p c i", c=4), rd=[ob.b])
            else:
                ob = outb[i2 % 2]
                self.tt("dve", ob.t[0:NB, :], t0_, gf.t[0:NB, :], ALU.mult, rd=[tq[0].b, gf.b], wr=[ob.b])
                self.dma("sp", dap(self.catT[s], (768 + c0) * L, [(128, NB), (L, 4), (1, 128)]),
                         ob.t[0:NB, :].rearrange("p (c i) -> p c i", c=4), rd=[ob.b])

        for o in range(2):
            G = [(o, grp * 4, grp) for grp in range(64)]
            ng = len(G)
            for n in range(ng + 3):
                if n < ng:
                    h4_a(G[n])
                if 0 <= n - 3 < ng:
                    h4_d4(G[n - 3])
                if 0 <= n - 2 < ng:
                    h4_d3s4(G[n - 2])
                if 0 <= n - 1 < ng:
                    h4_d2s3(G[n - 1])
                if n < ng:
                    fft_d1s2(N1)
            self.sync()
    self.phase_end()


K.sync = _sync
K.phase4 = _phase4


def build_program(nlayers=DEPTH, dump=False):
    nc = bass.Bass("TRN2", target_bir_lowering=False)
    with ExitStack() as ctx:
        s = Sched(nc, ctx)
        k = K(nc, s, ctx, nlayers=nlayers, dbg={"dump": dump})
        k.setup()
        for l in range(nlayers):
            xsrc = k.xin if l == 0 else k.xa
            dst = k.yout if l == nlayers - 1 else k.xa
            k.phase0(l)
            k.phase1(l, xsrc)
            k.phase2(l)
            k.phase3(l)
            k.phase4(l)
            k.phase5(l, xsrc)
            k.phase6(l, dst)
        s.barrier()
        s.emit()
    return nc


_NC_CACHE = {}


def kernel(**inputs):
    inp = {k: np.asarray(v) for k, v in inputs.items()}
    if "nc" not in _NC_CACHE:
        _NC_CACHE["nc"] = build_program()
    nc = _NC_CACHE["nc"]
    sh = host_prep(inp)
    in_maps = []
    for i in range(8):
        m = dict(sh)
        m.update(core_prep(inp, i))
        in_maps.append(m)
    res = run_bass_kernel_spmd(nc, in_maps, core_ids=list(range(8)))
    yp = np.stack([np.asarray(r["yp"], dtype=np.float32) for r in res.results], axis=0)
    ys = np.stack([np.asarray(r["ys"], dtype=np.float32) for r in res.results], axis=0)
    return (yp, ys)
```

```python
import math
import numpy as np
from contextlib import ExitStack
import concourse.bass as bass
import concourse.mybir as mybir
from concourse.bass_types import AP as APc
from concourse.bass_utils import run_bass_kernel_spmd

F32 = mybir.dt.float32
BF16 = mybir.dt.bfloat16
AF = mybir.ActivationFunctionType
ALU = mybir.AluOpType
AX = mybir.AxisListType

D = 1024
DEPTH = 4
LP = 4096
LS = 8192
DFF = 2816
ALPHA = (2 * DEPTH) ** 0.25
LN_EPS = 1e-5
QK_EPS = 1e-6
HY_MIN_DECAY = abs(math.log(1e-2)) / 1.5
HY_MAX_DECAY = abs(math.log(1e-2)) / 0.3
MAGIC = 12582912.0

ENGS = ("pe", "act", "dve", "pool", "sp")
EMBED_WAITS = True


class Buf:
    __slots__ = ("name", "w", "r")

    def __init__(self, name):
        self.name = name
        self.w = None
        self.r = []


class Sched:
    NDMA = 8

    def __init__(self, nc, ctx):
        self.nc = nc
        self.streams = {e: [] for e in ENGS}
        self.sems = {}
        self.cnt = {}
        for e in ENGS:
            self.sems[e] = ctx.enter_context(nc.semaphore("s_" + e))
            self.cnt[e] = 0
        self.dnext = {}
        for q in ("sp", "act", "pool"):
            for i in range(self.NDMA):
                k = "d_%s%d" % (q, i)
                self.sems[k] = ctx.enter_context(nc.semaphore(k))
                self.cnt[k] = 0
            self.dnext[q] = 0
        self.waited = {e: {} for e in ENGS}
        self.nbuf = 0

    def buf(self, name=None):
        self.nbuf += 1
        return Buf(name or ("b%d" % self.nbuf))

    def _need(self, eng, ev, same_ok):
        if ev is None:
            return
        k, v, src = ev
        if src == eng and same_ok:
            return
        if self.waited[eng].get(k, 0) >= v:
            return
        self.waited[eng][k] = v
        self.streams[eng].append(("w", k, v))

    def _deps(self, eng, rd, wr):
        pe = (eng == "pe")
        for b in rd:
            self._need(eng, b.w, pe)
        for b in wr:
            self._need(eng, b.w, pe)
            for ev in b.r:
                self._need(eng, ev, True)

    def _mark(self, ev, rd, wr):
        for b in rd:
            b.r.append(ev)
            if len(b.r) > 64:
                last = {}
                for e2 in b.r:
                    if e2[0] not in last or last[e2[0]][1] < e2[1]:
                        last[e2[0]] = e2
                b.r = list(last.values())
        for b in wr:
            b.w = ev
            b.r = []

    def op(self, eng, fn, rd=(), wr=()):
        self._deps(eng, rd, wr)
        self.cnt[eng] += 1
        ev = (eng, self.cnt[eng], eng)
        self.streams[eng].append(("o", fn, eng, 1))
        self._mark(ev, rd, wr)
        return ev

    def dma(self, q, out, in_, rd=(), wr=(), slow=False):
        self._deps(q, rd, wr)
        i = self.dnext[q]
        self.dnext[q] = (i + 1) % self.NDMA
        k = "d_%s%d" % (q, i)
        if self.cnt[k] > 0:
            self._need(q, (k, self.cnt[k], None), False)
        self.cnt[k] += 16
        ev = (k, self.cnt[k], None)
        if slow:
            self.streams[q].append(("o", (lambda e: e.dma_start(out=out, in_=in_, allow_slow_non_contiguous=True)), k, 16))
        else:
            self.streams[q].append(("o", (lambda e: e.dma_start(out=out, in_=in_)), k, 16))
        self._mark(ev, rd, wr)
        return ev

    def barrier(self):
        for e in ENGS:
            for k, v in self.cnt.items():
                if v > 0 and k != e:
                    self._need(e, (k, v, None), False)

    def emit(self):
        nc = self.nc
        if not any(self.streams[e] for e in ENGS):
            return
        engobj = {"pe": "tensor", "act": "scalar", "dve": "vector", "pool": "gpsimd", "sp": "sync"}
        with nc.Block() as block:
            for e in ENGS:
                items = self.streams[e]
                sems = self.sems

                def body(eng, items=items, sems=sems):
                    n = len(items)
                    i = 0
                    while i < n:
                        it = items[i]
                        if it[0] == "w":
                            if EMBED_WAITS and i + 1 < n and items[i + 1][0] == "o":
                                nx = items[i + 1]
                                ins = nx[1](eng)
                                ins._wait_ge(sems[it[1]], it[2])
                                ins.then_inc(sems[nx[2]], nx[3])
                                i += 2
                                continue
                            eng.wait_ge(sems[it[1]], it[2])
                        else:
                            it[1](eng).then_inc(sems[it[2]], it[3])
                        i += 1
                getattr(block, engobj[e])(body)
        self.streams = {e: [] for e in ENGS}


class TB:
    __slots__ = ("t", "b")

    def __init__(self, t, b):
        self.t = t
        self.b = b

    def __getitem__(self, k):
        return self.t[k]


def rev(ap2d):
    (ps, pn), (fs, fn) = ap2d.ap
    return APc(ap2d.tensor, ap2d.offset + (fn - 1) * fs, [[ps, pn], [-fs, fn]])


def dap(t, off, dims):
    return APc(t.tensor, t.offset + off, [[a, b] for a, b in dims])


class K:
    def __init__(self, nc, s, ctx, nlayers=DEPTH, dbg=None):
        self.nc, self.s, self.gctx = nc, s, ctx
        self.nlayers = nlayers
        self.dbg = dbg or {}
        self.pctx = None
        self.uid = 0

    def _nm(self, n):
        self.uid += 1
        return "%s_%d" % (n, self.uid)

    def sb(self, name, shape, dt, glob=False):
        c = self.gctx if glob else self.pctx
        t = c.enter_context(self.nc.sbuf_tensor(self._nm(name), list(shape), dt))
        return TB(t, self.s.buf(name))

    def ps(self, name, shape, dt=F32):
        t = self.pctx.enter_context(self.nc.psum_tensor(self._nm(name), list(shape), dt))
        return TB(t, self.s.buf(name))

    def dram(self, name, shape, dt, kind="Internal"):
        return self.nc.dram_tensor(name, list(shape), dt, kind=kind).ap()

    def mm(self, out, lhsT, rhs, start, stop, rd, wr, tp=None):
        if tp is None:
            f = lambda e: e.matmul(out=out, lhsT=lhsT, rhs=rhs, start=start, stop=stop)
        else:
            f = lambda e: e.matmul(out=out, lhsT=lhsT, rhs=rhs, start=start, stop=stop, tile_position=tp)
        return self.s.op("pe", f, rd=rd, wr=wr)

    def tr(self, out, in_, ident, rd, wr):
        return self.s.op("pe", lambda e: e.transpose(out=out, in_=in_, identity=ident), rd=rd, wr=wr)

    def act(self, out, in_, func, rd, wr, bias=None, scale=None, accum=None):
        kw = {}
        if bias is not None:
            kw["bias"] = bias
        if scale is not None:
            kw["scale"] = scale
        if accum is not None:
            kw["accum_out"] = accum
        return self.s.op("act", lambda e: e.activation(out=out, in_=in_, func=func, **kw), rd=rd, wr=wr)

    def tt(self, eng, out, in0, in1, op, rd, wr):
        return self.s.op(eng, lambda e: e.tensor_tensor(out=out, in0=in0, in1=in1, op=op), rd=rd, wr=wr)

    def ts(self, eng, out, in0, s1, s2, op0, op1, rd, wr):
        if op1 is None:
            f = lambda e: e.tensor_scalar(out=out, in0=in0, scalar1=s1, scalar2=None, op0=op0)
        else:
            f = lambda e: e.tensor_scalar(out=out, in0=in0, scalar1=s1, scalar2=s2, op0=op0, op1=op1)
        return self.s.op(eng, f, rd=rd, wr=wr)

    def stt(self, eng, out, in0, scalar, in1, op0, op1, rd, wr):
        return self.s.op(eng, lambda e: e.scalar_tensor_tensor(out=out, in0=in0, scalar=scalar, in1=in1, op0=op0, op1=op1), rd=rd, wr=wr)

    def cp(self, eng, out, in_, rd, wr):
        if eng == "act":
            return self.s.op("act", lambda e: e.copy(out=out, in_=in_), rd=rd, wr=wr)
        return self.s.op(eng, lambda e: e.tensor_copy(out=out, in_=in_), rd=rd, wr=wr)

    def memset(self, eng, ap, val, wr):
        return self.s.op(eng, lambda e: e.memset(ap, val), rd=(), wr=wr)

    def recip(self, out, in_, rd, wr):
        return self.s.op("dve", lambda e: e.reciprocal(out=out, in_=in_), rd=rd, wr=wr)

    def dma(self, q, out, in_, rd=(), wr=(), slow=False):
        return self.s.dma(q, out, in_, rd=rd, wr=wr, slow=slow)

    def phase_begin(self):
        self.pctx = ExitStack()

    def phase_end(self):
        self.s.barrier()
        self.pctx.close()
        self.pctx = None


def _init_arena(self):
    self.pctx = None


def _sb(self, name, cols, dt=F32, glob=False, parts=128, ctx=None):
    c = ctx if ctx is not None else (self.gctx if glob else self.pctx)
    t = c.enter_context(self.nc.sbuf_tensor(self._nm(name), [128, cols], dt))
    return TB(t[0:parts, :], self.s.buf(name))


def _ps(self, name, cols=512, dt=F32, parts=128):
    full = 512 if dt == F32 else 1024
    t = self.pctx.enter_context(self.nc.psum_tensor(self._nm(name), [128, full], dt))
    return TB(t[0:parts, 0:cols], self.s.buf(name))


def _ps2(self, name):
    t = self.pctx.enter_context(self.nc.psum_tensor(self._nm(name), [128, 1024], F32))
    return TB(t[:, :], self.s.buf(name))


def _phase_begin(self):
    self.pctx = ExitStack()


def _phase_end(self):
    self.s.barrier()
    self.s.emit()
    self.pctx.close()
    self.pctx = None


K.init_arena = _init_arena
K.sb = _sb
K.ps = _ps
K.ps2 = _ps2
K.phase_begin = _phase_begin
K.phase_end = _phase_end


def _consts():
    c = {}
    c["ident"] = np.eye(128, dtype=np.float32)
    t = np.arange(LS)
    row = (t // 64).astype(np.float32)
    col = (t % 64).astype(np.float32)
    inv = (10000.0 ** (-np.arange(16, dtype=np.float32) / 16)).astype(np.float32)
    ang = np.stack([row[:, None] * inv, col[:, None] * inv], axis=1).astype(np.float32)
    cs = np.cos(ang).reshape(LS // 128, 128, 32).transpose(1, 0, 2)
    sn = np.sin(ang).reshape(LS // 128, 128, 32).transpose(1, 0, 2)
    for s, L in enumerate((LP, LS)):
        f32 = np.float32
        t = np.linspace(0.0, 1.0, L, dtype=f32)[:, None]
        w = (f32(2.0 * math.pi) * np.arange(L, dtype=f32)[:, None] / f32(L)).astype(f32)
        f = np.linspace(1e-4, 15, 16, dtype=f32)[None, :]
        z = np.concatenate([t, np.cos(f * w), -np.sin(f * w)], axis=-1).astype(f32)
        c["hz%d" % s] = np.ascontiguousarray(z.T)
        c["htl%d" % s] = np.ascontiguousarray(t.T)
        NB = L // 128
        N1 = 2 * NB
        N = 2 * L
        NH = N1 // 2 + 1
        a = np.arange(N1, dtype=np.float64)[:, None]
        kl = np.arange(NH, dtype=np.float64)[None, :]
        th = 2 * np.pi * a * kl / N1
        c["hFN1_%d" % s] = np.concatenate([np.cos(th), -np.sin(th)], axis=1).astype(f32)
        i = np.arange(128, dtype=np.float64)[:, None]
        th = 2 * np.pi * i * kl / N
        c["hT_%d" % s] = np.stack([np.cos(th), -np.sin(th)], axis=1).astype(f32)
        c["hT2_%d" % s] = np.stack([np.cos(th.T), np.sin(th.T)], axis=1).astype(f32)
        aa = np.arange(NB, dtype=np.float64)[None, :]
        klc = np.arange(NH, dtype=np.float64)[:, None]
        th = 2 * np.pi * klc * aa / N1
        wgt = np.full((NH, 1), 2.0)
        wgt[0, 0] = 1.0
        wgt[NH - 1, 0] = 1.0
        c["hGN1_%d" % s] = np.stack([wgt * np.cos(th) / N, -wgt * np.sin(th) / N], axis=1).astype(f32)
    i = np.arange(128, dtype=np.float64)[:, None]
    kh = np.arange(128, dtype=np.float64)[None, :]
    th = 2 * np.pi * i * kh / 128
    c["hF128"] = np.stack([np.cos(th), -np.sin(th), np.sin(th)], axis=1).astype(np.float32)
    c["hG128"] = np.stack([np.concatenate([np.cos(th), np.sin(th)], axis=1),
                           np.concatenate([-np.sin(th), np.cos(th)], axis=1)], axis=1).astype(np.float32)
    dl = np.linspace(HY_MIN_DECAY, HY_MAX_DECAY, 256, dtype=np.float32)
    c["hnegd"] = np.ascontiguousarray((-dl).reshape(2, 128).T)
    c["rope_cos"] = np.ascontiguousarray(cs, dtype=np.float32)
    c["rope_sin"] = np.ascontiguousarray(sn, dtype=np.float32)
    return c


def host_prep(inp):
    f32 = np.float32
    A = lambda a: np.ascontiguousarray(a, dtype=f32)
    sh = {}
    sh["ada_w"] = A(inp["ada_w"])
    sh["ada_b"] = A(inp["ada_b"])
    sh["ada_bT"] = A(inp["ada_b"].reshape(DEPTH, 48, 128).transpose(0, 2, 1))
    sh["w_in"] = A(inp["w_in"])
    sh["w_out"] = A(inp["w_out"])
    sh["w_up"] = A(inp["ffn_w_up"])
    sh["w_down"] = A(inp["ffn_w_down"])
    sh["qkg"] = A(np.concatenate([np.tile(inp["q_gain"], (1, 8)), np.tile(inp["k_gain"], (1, 2))], axis=1))
    lw = np.concatenate([inp["lru_conv_w"], inp["lru_conv_b"][:, None, :]], axis=1)
    sh["lruw"] = A(lw.reshape(DEPTH, 5, 2, 128).transpose(0, 3, 2, 1))
    W = np.zeros((DEPTH, 2, 2, 2, 128, 128), f32)
    for gi, nm in enumerate(("lru_wa", "lru_wx")):
        w = np.asarray(inp[nm])
        for cc in range(2):
            for h2 in range(2):
                W[:, :, gi, cc, h2 * 64:(h2 + 1) * 64, h2 * 64:(h2 + 1) * 64] = w[:, :, 2 * cc + h2]
    sh["lruW"] = W
    lb = np.stack([inp["lru_ba"], inp["lru_bx"], inp["lru_lambda"]], axis=-1)
    sh["lrub"] = A(lb.reshape(DEPTH, 2, 2, 128, 3).transpose(0, 3, 2, 1, 4))
    fw = np.concatenate([inp["ffn_conv_w"], inp["ffn_conv_b"][:, None, :]], axis=1)
    sh["ffw"] = A(fw.reshape(DEPTH, 4, 44, 128).transpose(0, 3, 2, 1))
    sh["ln1_g"] = A(inp["ln1_g"]); sh["ln1_b"] = A(inp["ln1_b"])
    sh["ln2_g"] = A(inp["ln2_g"]); sh["ln2_b"] = A(inp["ln2_b"])
    hw = np.concatenate([inp["hy_conv_w"], inp["hy_conv_b"][:, None, :]], axis=1)
    sh["hyw"] = A(hw.reshape(DEPTH, 4, 6, 128).transpose(0, 3, 2, 1))
    sh["hy_w1"] = A(inp["hy_w1"]); sh["hy_w2"] = A(inp["hy_w2"]); sh["hy_w3"] = A(inp["hy_w3"])
    sh["hy_wout"] = A(inp["hy_wout"])
    sh["hyb"] = A(np.stack([inp["hy_b1"], inp["hy_b2"], inp["hy_b3"], inp["hy_freq"]], axis=-1))
    sh["hy_bias"] = A(inp["hy_bias"])
    sh.update(_consts())
    return sh


SHARED_SHAPES = {
    "ada_w": [DEPTH, 1024, 6144], "ada_b": [DEPTH, 6144], "ada_bT": [DEPTH, 128, 48],
    "w_in": [DEPTH, 1024, 2048], "w_out": [DEPTH, 1024, 1024], "w_up": [DEPTH, 1024, 2 * DFF],
    "w_down": [DEPTH, DFF, 1024], "qkg": [DEPTH, 640],
    "lruw": [DEPTH, 128, 2, 5], "lruW": [DEPTH, 2, 2, 2, 128, 128], "lrub": [DEPTH, 128, 2, 2, 3],
    "ffw": [DEPTH, 128, 44, 4], "ln1_g": [DEPTH, 1024], "ln1_b": [DEPTH, 1024], "ln2_g": [DEPTH, 1024], "ln2_b": [DEPTH, 1024],
    "hyw": [DEPTH, 128, 6, 4], "hy_w1": [DEPTH, 33, 64], "hy_w2": [DEPTH, 64, 64], "hy_w3": [DEPTH, 64, 64],
    "hy_wout": [DEPTH, 64, 1024], "hyb": [DEPTH, 64, 4], "hy_bias": [DEPTH, 2, 256],
    "hz0": [33, LP], "hz1": [33, LS], "htl0": [1, LP], "htl1": [1, LS],
    "hFN1_0": [64, 66], "hFN1_1": [128, 130], "hT_0": [128, 2, 33], "hT_1": [128, 2, 65],
    "hT2_0": [33, 2, 128], "hT2_1": [65, 2, 128], "hGN1_0": [33, 2, 32], "hGN1_1": [65, 2, 64],
    "hF128": [128, 3, 128], "hG128": [128, 2, 256], "hnegd": [128, 2],
    "ident": [128, 128], "rope_cos": [128, 64, 32], "rope_sin": [128, 64, 32],
}


def core_prep(inp, i):
    c = np.stack([inp["c_prompt"][i], inp["c_sample"][i]], axis=0)
    cT = np.ascontiguousarray(c.reshape(2, 8, 128).transpose(2, 1, 0), dtype=np.float32)
    return {"xp": np.ascontiguousarray(inp["x_prompt"][i], dtype=np.float32),
            "xs": np.ascontiguousarray(inp["x_sample"][i], dtype=np.float32),
            "cT": cT}


def _setup(self):
    nc = self.nc
    dk = "ExternalOutput" if self.dbg.get("dump") else "Internal"
    self.din = {}
    for k, shp in SHARED_SHAPES.items():
        self.din[k] = self.dram(k, shp, F32, kind="ExternalInput")
    self.xin = [self.dram("xp", [LP, D], F32, kind="ExternalInput"),
                self.dram("xs", [LS, D], F32, kind="ExternalInput")]
    self.cT = self.dram("cT", [128, 8, 2], F32, kind="ExternalInput")
    self.yout = [self.dram("yp", [LP, D], F32, kind="ExternalOutput"),
                 self.dram("ys", [LS, D], F32, kind="ExternalOutput")]
    self.Ls = [LP, LS]
    self.xa = [self.dram("xa%d" % s, [L, D], F32, kind=dk) for s, L in enumerate(self.Ls)]
    self.xres = [self.dram("xres%d" % s, [L, D], F32, kind=dk) for s, L in enumerate(self.Ls)]
    self.qT = [self.dram("qT%d" % s, [512, L], BF16, kind=dk) for s, L in enumerate(self.Ls)]
    self.kT = [self.dram("kT%d" % s, [128, L], BF16, kind=dk) for s, L in enumerate(self.Ls)]
    self.vaug = [self.dram("vaug%d" % s, [2, 128, L // 128, 192], BF16, kind=dk) for s, L in enumerate(self.Ls)]
    self.lh = [self.dram("lh%d" % s, [1280, L], F32, kind=dk) for s, L in enumerate(self.Ls)]
    self.catT = [self.dram("catT%d" % s, [1024, L], BF16, kind=dk) for s, L in enumerate(self.Ls)]
    self.hc = [self.dram("hc%d" % s, [768, L], F32, kind=dk) for s, L in enumerate(self.Ls)]
    self.kfull = [self.dram("kfull%d" % s, [2, 256, 2 * L], F32, kind=dk) for s, L in enumerate(self.Ls)]
    self.kfs = [self.dram("kfs%d" % s, [2, 128, 256, 2, L // 128 + 1], BF16, kind=dk) for s, L in enumerate(self.Ls)]
    self.z1 = [self.dram("z1_%d" % s, [256, L], F32, kind=dk) for s, L in enumerate(self.Ls)]
    self.rnd = self.dram("rnd", [2, 256], F32, kind=dk)
    self.init_arena()
    self.ident = self.sb("ident", 128, BF16, glob=True)
    self.csT = self.sb("csT", 16, F32, glob=True)
    self.modT = self.sb("modT", 96, F32, glob=True)
    self.grow = [self.sb("grow%d" % i, 1024, F32, glob=True) for i in range(4)]
    self.epsq = self.sb("epsq", 1, F32, glob=True)
    self.epsl = self.sb("epsl", 1, F32, glob=True)
    self.phase_begin()
    self.dma("pool", self.ident.t, self.din["ident"], wr=[self.ident.b])
    ct = self.sb("ct", 16, F32)
    self.dma("sp", ct.t, self.cT.rearrange("p k s -> p (k s)"), wr=[ct.b])
    self.act(self.csT.t, ct.t, AF.Silu, rd=[ct.b], wr=[self.csT.b])
    self.memset("dve", self.epsq.t, QK_EPS, wr=[self.epsq.b])
    self.memset("dve", self.epsl.t, LN_EPS, wr=[self.epsl.b])
    self.phase_end()


def _phase0(self, l):
    self.wctx = ExitStack()
    wi = self.sb("wi", 8 * 2048, BF16, ctx=self.wctx)
    wi3 = wi.t.rearrange("p (k n) -> p k n", k=8)
    wib = [self.s.buf("wib%d" % k) for k in range(8)]
    self.phase_begin()
    for kc in range(8):
        self.dma("pool", wi3[:, kc, :], self.din["w_in"][l, kc * 128:(kc + 1) * 128, :], wr=[wib[kc]])
    self.pre_wi = (wi, wib)
    adab = self.sb("adab", 48, F32)
    self.dma("sp", adab.t, self.din["ada_bT"][l], wr=[adab.b])
    self.csrep = self.sb("csrep", 16 * 128, F32)
    self.cp("dve", self.csrep.t.rearrange("p (k n) -> p k n", n=128),
            self.csT.t.unsqueeze(2).to_broadcast([128, 16, 128]), rd=[self.csT.b], wr=[self.csrep.b])
    was = [self.sb("wa%d" % i, 8 * 512, F32) for i in range(2)]
    brow = [self.sb("brow%d" % i, 512, F32) for i in range(2)]
    pm = self.ps("pm", 96)
    pgs = [self.ps("pg%d" % i) for i in range(2)]
    npg = 0
    aw = self.din["ada_w"]
    for gi in range(12):
        wa = was[gi % 2]
        src = dap(aw, l * 1024 * 6144 + gi * 512, [(6144, 128), (128 * 6144, 8), (1, 512)])
        self.dma("sp", wa.t.rearrange("p (k c) -> p k c", k=8), src, wr=[wa.b])
        wa3 = wa.t.rearrange("p (k c) -> p k c", k=8)
        for m in range(4):
            ch = gi * 4 + m
            for kc in range(8):
                self.mm(pm.t[:, ch * 2:ch * 2 + 2], wa3[:, kc, m * 128:(m + 1) * 128],
                        self.csT.t[:, kc * 2:kc * 2 + 2], kc == 0, kc == 7, rd=[wa.b, self.csT.b], wr=[pm.b])
        if gi in (4, 5, 10, 11):
            g = 0 if gi < 6 else 1
            half = gi % 2 if gi < 6 else (gi - 10)
            br = brow[half]
            self.dma("sp", br.t, dap(self.din["ada_b"], l * 6144 + gi * 512, [(0, 128), (1, 512)]), wr=[br.b])
            cr4 = self.csrep.t.rearrange("p (k s n) -> p k s n", k=8, s=2)
            for sq in range(2):
                pg = pgs[npg % 2]
                npg += 1
                for kc in range(8):
                    self.mm(pg.t, cr4[:, kc, sq, :], wa3[:, kc, :], kc == 0, kc == 7,
                            rd=[self.csrep.b, wa.b], wr=[pg.b])
                gr = self.grow[g * 2 + sq]
                self.tt("dve", gr.t[:, half * 512:(half + 1) * 512], pg.t, br.t, ALU.add,
                        rd=[pg.b, br.b], wr=[gr.b])
    m3 = self.modT.t.rearrange("p (c s) -> p c s", s=2)
    self.tt("dve", m3, pm.t.rearrange("p (c s) -> p c s", s=2),
            adab.t.unsqueeze(2).to_broadcast([128, 48, 2]), ALU.add, rd=[pm.b, adab.b], wr=[self.modT.b])
    for c0 in (8, 32):
        self.ts("dve", m3[:, c0:c0 + 8, :], m3[:, c0:c0 + 8, :], 1.0, None, ALU.add, None,
                rd=[self.modT.b], wr=[self.modT.b])
    self.phase_end()


def _phase1(self, l, xsrc):
    self.phase_begin()
    wi, wib = self.pre_wi
    wi3 = wi.t.rearrange("p (k n) -> p k n", k=8)
    gain = self.sb("gain", 640, F32)
    self.dma("sp", gain.t, dap(self.din["qkg"], l * 640, [(0, 128), (1, 640)]), wr=[gain.b])
    self.rcos = self.sb("rcos", 64 * 32, F32)
    self.rsin = self.sb("rsin", 64 * 32, F32)
    self.dma("sp", self.rcos.t, self.din["rope_cos"].rearrange("p t c -> p (t c)"), wr=[self.rcos.b])
    self.dma("sp", self.rsin.t, self.din["rope_sin"].rearrange("p t c -> p (t c)"), wr=[self.rsin.b])
    xbs = [self.sb("xb%d" % i, 4 * 1024, BF16) for i in range(2)]
    uTs = [self.sb("uT%d" % i, 8 * 512, BF16) for i in range(2)]
    lhs_ = [self.sb("lhs%d" % i, 10 * 512, F32) for i in range(2)]
    lhb = [[self.s.buf() for m in range(10)] for i in range(2)]
    sq_ = [self.sb("sq%d" % i, 640, F32) for i in range(2)]
    ss_ = [self.sb("ss%d" % i, 10, F32) for i in range(2)]
    rstd_ = [self.sb("rstd%d" % i, 10, F32) for i in range(2)]
    qn_ = [self.sb("qn%d" % i, 640, F32) for i in range(2)]
    tmp_ = [[self.sb("tmp%d%d" % (b, i), 320, F32) for i in range(4)] for b in range(2)]
    qkb_ = [self.sb("qkb%d" % i, 640, BF16) for i in range(2)]
    jn = 0
    qkTs = [self.sb("qkT%d" % i, 5 * 512, BF16) for i in range(2)]
    vgs = [self.sb("vg%d" % i, 2 * 4 * 192, BF16) for i in range(2)]
    for vg in vgs:
        self.memset("dve", vg.t, 1.0, wr=[vg.b])
    pts = [self.ps("pt%d" % i, 512, BF16) for i in range(2)]
    pfs = [self.ps("pf%d" % i) for i in range(2)]
    psq_ = [self.ps("psq%d" % i) for i in range(2)]
    pskv = self.ps("pskv", 256)
    pT = self.ps("pT", 640, BF16)
    m3 = self.modT.t.rearrange("p (c s) -> p c s", s=2)
    rc3 = self.rcos.t.rearrange("p (t c) -> p t c", c=32)
    rs3 = self.rsin.t.rearrange("p (t c) -> p t c", c=32)
    it = 0
    for s in range(2):
        L = self.Ls[s]
        NT = L // 128
        for w in range(L // 512):
            xb, uT, lh, qkT, vg = xbs[it % 2], uTs[it % 2], lhs_[it % 2], qkTs[it % 2], vgs[it % 2]
            lb = lhb[it % 2]
            it += 1
            xb3 = xb.t.rearrange("p (j c) -> p j c", j=4)
            uT3 = uT.t.rearrange("p (k t) -> p k t", k=8)
            lh3 = lh.t.rearrange("p (m t) -> p m t", m=10)
            qkT3 = qkT.t.rearrange("p (c t) -> p c t", c=5)
            vg4 = vg.t.rearrange("p (g j c) -> p g j c", g=2, j=4)
            self.dma("pool", xb3, dap(xsrc[s], w * 512 * D, [(D, 128), (128 * D, 4), (1, D)]), wr=[xb.b])
            for kc in range(8):
                pt = pts[kc % 2]
                for j in range(4):
                    self.tr(pt.t[:, j * 128:(j + 1) * 128], xb3[:, j, kc * 128:(kc + 1) * 128], self.ident.t,
                            rd=[xb.b, self.ident.b], wr=[pt.b])
                self.act(uT3[:, kc, :], pt.t, AF.Identity, rd=[pt.b, self.modT.b], wr=[uT.b],
                         scale=m3[:, 8 + kc, s:s + 1], bias=m3[:, kc, s:s + 1])
            for m in range(10):
                pf = pfs[m % 2]
                for kc in range(8):
                    self.mm(pf.t, wi3[:, kc, 768 + m * 128:768 + (m + 1) * 128], uT3[:, kc, :], kc == 0, kc == 7,
                            rd=[wib[kc], uT.b], wr=[pf.b])
                self.cp("act" if m % 2 else "dve", lh3[:, m, :], pf.t, rd=[pf.b], wr=[lb[m]])
            self.dma("sp", dap(self.lh[s], w * 512, [(L, 128), (128 * L, 10), (1, 512)]), lh3, rd=lb)
            for j in range(4):
                T = w * 4 + j
                jb = jn % 2
                jn += 1
                sq, ss, rstd, qn, tmp, qkb, psq = sq_[jb], ss_[jb], rstd_[jb], qn_[jb], tmp_[jb], qkb_[jb], psq_[jb]
                for kc in range(8):
                    self.mm(psq.t, uT3[:, kc, j * 128:(j + 1) * 128], wi3[:, kc, 0:512], kc == 0, kc == 7,
                            rd=[wib[kc], uT.b], wr=[psq.b])
                for kc in range(8):
                    self.mm(pskv.t, uT3[:, kc, j * 128:(j + 1) * 128], wi3[:, kc, 512:768], kc == 0, kc == 7,
                            rd=[wib[kc], uT.b], wr=[pskv.b])
                self.act(sq.t[:, 0:512], psq.t, AF.Square, rd=[psq.b], wr=[sq.b])
                self.act(sq.t[:, 512:640], pskv.t[:, 0:128], AF.Square, rd=[pskv.b], wr=[sq.b])
                self.s.op("dve", (lambda e, o=ss.t, i=sq.t.rearrange("p (h d) -> p h d", d=64):
                                  e.tensor_reduce(out=o, in_=i, axis=AX.X, op=ALU.add)), rd=[sq.b], wr=[ss.b])
                self.act(rstd.t, ss.t, AF.Sqrt, rd=[ss.b, self.epsq.b], wr=[rstd.b], scale=1.0 / 64, bias=self.epsq.t)
                self.recip(rstd.t, rstd.t, rd=[rstd.b], wr=[rstd.b])
                qn3 = qn.t.rearrange("p (h d) -> p h d", d=64)
                self.tt("dve", qn3[:, 0:8, :], psq.t.rearrange("p (h d) -> p h d", d=64),
                        rstd.t[:, 0:8].unsqueeze(2).to_broadcast([128, 8, 64]), ALU.mult,
                        rd=[psq.b, rstd.b], wr=[qn.b])
                self.tt("dve", qn3[:, 8:10, :], pskv.t[:, 0:128].rearrange("p (h d) -> p h d", d=64),
                        rstd.t[:, 8:10].unsqueeze(2).to_broadcast([128, 2, 64]), ALU.mult,
                        rd=[pskv.b, rstd.b], wr=[qn.b])
                self.tt("dve", qn.t, qn.t, gain.t, ALU.mult, rd=[qn.b, gain.b], wr=[qn.b])
                for c0 in (0, 128):
                    self.cp("act", vg4[:, :, j, c0:c0 + 64], pskv.t[:, 128:256].rearrange("p (g d) -> p g d", g=2),
                            rd=[pskv.b], wr=[vg.b])
                qn5 = qn.t.rearrange("p (h a two f) -> p h a two f", h=10, a=2, two=2)
                qb5 = qkb.t.rearrange("p (h a two f) -> p h a two f", h=10, a=2, two=2)
                x1, x2 = qn5[:, :, :, 0, :], qn5[:, :, :, 1, :]
                cb = rc3[:, T, :].rearrange("p (a f) -> p a f", a=2).unsqueeze(1).to_broadcast([128, 10, 2, 16])
                sb_ = rs3[:, T, :].rearrange("p (a f) -> p a f", a=2).unsqueeze(1).to_broadcast([128, 10, 2, 16])
                t4 = [t.t.rearrange("p (h a f) -> p h a f", h=10, a=2) for t in tmp]
                self.tt("dve", t4[0], x1, cb, ALU.mult, rd=[qn.b, self.rcos.b], wr=[tmp[0].b])
                self.tt("dve", t4[1], x2, sb_, ALU.mult, rd=[qn.b, self.rsin.b], wr=[tmp[1].b])
                self.tt("dve", qb5[:, :, :, 0, :], t4[0], t4[1], ALU.subtract, rd=[tmp[0].b, tmp[1].b], wr=[qkb.b])
                self.tt("dve", t4[2], x2, cb, ALU.mult, rd=[qn.b, self.rcos.b], wr=[tmp[2].b])
                self.tt("dve", t4[3], x1, sb_, ALU.mult, rd=[qn.b, self.rsin.b], wr=[tmp[3].b])
                self.tt("dve", qb5[:, :, :, 1, :], t4[2], t4[3], ALU.add, rd=[tmp[2].b, tmp[3].b], wr=[qkb.b])
                for c in range(5):
                    self.tr(pT.t[:, c * 128:(c + 1) * 128], qkb.t[:, c * 128:(c + 1) * 128], self.ident.t,
                            rd=[qkb.b, self.ident.b], wr=[pT.b])
                self.cp("act", qkT3[:, :, j * 128:(j + 1) * 128], pT.t.rearrange("p (c t) -> p c t", c=5),
                        rd=[pT.b], wr=[qkT.b])
            self.dma("sp", dap(self.qT[s], w * 512, [(L, 128), (128 * L, 4), (1, 512)]), qkT3[:, 0:4, :], rd=[qkT.b])
            self.dma("sp", self.kT[s][:, w * 512:(w + 1) * 512], qkT3[:, 4, :], rd=[qkT.b])
            self.dma("sp", dap(self.vaug[s], w * 4 * 192, [(NT * 192, 128), (128 * NT * 192, 2), (1, 768)]),
                     vg.t.rearrange("p (g c) -> p g c", g=2), rd=[vg.b])
    self.phase_end()
    self.wctx.close()


K.setup = _setup
K.phase0 = _phase0
K.phase1 = _phase1


def _phase2(self, l):
    self.phase_begin()
    Kds = [self.sb("Kd%d" % i, LS, BF16) for i in range(2)]
    Vas = [self.sb("Va%d" % i, 64 * 192, BF16) for i in range(2)]
    Q2s = [self.sb("Q2%d" % i, 2 * LS, BF16) for i in range(2)]
    blocks = [(s, g) for s in range(2) for g in range(2)]

    def load_blk(bi):
        s, g = blocks[bi]
        L = self.Ls[s]
        NT = L // 128
        Kd, Va, Q2 = Kds[bi % 2], Vas[bi % 2], Q2s[bi % 2]
        self.dma("sp", Kd.t[0:64, 0:L], self.kT[s][g * 64:(g + 1) * 64, :], wr=[Kd.b])
        self.dma("sp", Kd.t[64:128, 0:L], self.kT[s][g * 64:(g + 1) * 64, :], wr=[Kd.b])
        self.dma("sp", Va.t[:, 0:NT * 192], self.vaug[s][g].rearrange("p t c -> p (t c)"), wr=[Va.b])
        Q3 = Q2.t.rearrange("p (h t) -> p h t", h=2)
        self.dma("sp", Q3[:, :, 0:L], dap(self.qT[s], 2 * g * 128 * L, [(L, 128), (128 * L, 2), (1, L)]), wr=[Q2.b])
    PAB = [self.sb("PAB%d" % i, 1024, BF16) for i in range(3)]
    rr = self.sb("rr", 512, F32)
    rAb, rBb = self.s.buf("rA"), self.s.buf("rB")
    atts = [self.sb("att%d" % i, 512, BF16) for i in range(2)]
    psAB = [self.ps2("psAB%d" % i) for i in range(2)]
    oA = [self.ps("oA%d" % i) for i in range(2)]
    oB = [self.ps("oB%d" % i) for i in range(2)]
    nblk = 0
    load_blk(0)
    for bi, (s, g) in enumerate(blocks):
        if True:
            L = self.Ls[s]
            NT = L // 128
            Kd, Va, Q2 = Kds[bi % 2], Vas[bi % 2], Q2s[bi % 2]
            Q3 = Q2.t.rearrange("p (h t) -> p h t", h=2)
            if bi + 1 < len(blocks):
                load_blk(bi + 1)
            Va3 = Va.t.rearrange("p (t c) -> p t c", c=192)
            steps = [(qc, hp, st) for qc in range(L // 512) for hp in range(2) for st in range(NT)]

            def qk(i):
                qc, hp, st = steps[i]
                ib = i % 2
                self.mm(psAB[ib].t[:, 0:512], Kd.t[0:64, st * 128:(st + 1) * 128], Q3[0:64, hp, qc * 512:(qc + 1) * 512],
                        True, True, rd=[Kd.b, Q2.b], wr=[psAB[ib].b], tp=(0, 0))
                self.mm(psAB[ib].t[:, 512:1024], Kd.t[64:128, st * 128:(st + 1) * 128], Q3[64:128, hp, qc * 512:(qc + 1) * 512],
                        True, True, rd=[Kd.b, Q2.b], wr=[psAB[ib].b], tp=(64, 0))
            qk(0)
            for i, (qc, hp, st) in enumerate(steps):
                if i + 1 < len(steps):
                    qk(i + 1)
                ib, ip = i % 2, i % 3
                if st == 0:
                    nblk += 1
                io = nblk % 2
                self.act(PAB[ip].t, psAB[ib].t, AF.Exp, rd=[psAB[ib].b], wr=[PAB[ip].b], scale=0.125)
                self.mm(oA[io].t, Va3[:, st, 0:128], PAB[ip].t[:, 0:512], st == 0, st == NT - 1, rd=[Va.b, PAB[ip].b], wr=[oA[io].b])
                self.mm(oB[io].t, Va3[:, st, 64:192], PAB[ip].t[:, 512:1024], st == 0, st == NT - 1, rd=[Va.b, PAB[ip].b], wr=[oB[io].b])
                if st == NT - 1:
                    att = atts[io]
                    self.recip(rr.t[64:128, :], oA[io].t[64:128, :], rd=[oA[io].b], wr=[rAb])
                    self.tt("dve", att.t[0:64, :], oA[io].t[0:64, :], rr.t[64:128, :], ALU.mult,
                            rd=[oA[io].b, rAb], wr=[att.b])
                    self.recip(rr.t[0:64, :], oB[io].t[0:64, :], rd=[oB[io].b], wr=[rBb])
                    self.tt("dve", att.t[64:128, :], oB[io].t[64:128, :], rr.t[0:64, :], ALU.mult,
                            rd=[oB[io].b, rBb], wr=[att.b])
                    self.dma("sp", self.catT[s][(2 * g + hp) * 128:(2 * g + hp + 1) * 128, qc * 512:(qc + 1) * 512],
                             att.t, rd=[att.b])
    self.phase_end()


K.phase2 = _phase2


def _phase3(self, l):
    self.phase_begin()
    TC = 1024
    XC = self.sb("XC", LS, F32)
    XCB = self.sb("XCB", LS, BF16)
    HF = self.sb("HF", LS, F32)
    WA = self.sb("WA", 8 * 128, BF16)
    WA5 = WA.t.rearrange("p (d g c n) -> p d g c n", d=2, g=2, c=2)
    self.dma("pool", WA5, self.din["lruW"][l].rearrange("d g c k n -> k d g c n"), wr=[WA.b])
    cw = self.sb("cw", 10, F32)
    self.dma("sp", cw.t, self.din["lruw"][l].rearrange("p c k -> p (c k)"), wr=[cw.b])
    lb = self.sb("lb", 12, F32)
    self.dma("sp", lb.t, self.din["lrub"][l].rearrange("p c d k -> p (c d k)"), wr=[lb.b])
    lb4 = lb.t.rearrange("p (c d k) -> p c d k", c=2, d=2)
    cw3 = cw.t.rearrange("p (c k) -> p c k", c=2)
    sp_ = self.sb("sp", 4, F32)
    c12 = self.sb("c12", 8, F32)
    sp3 = sp_.t.rearrange("p (c d) -> p c d", c=2)
    self.act(sp3, lb4[:, :, :, 2], AF.Exp, rd=[lb.b], wr=[sp_.b], scale=-1.0)
    self.act(sp_.t, sp_.t, AF.Ln, rd=[sp_.b], wr=[sp_.b], bias=1.0)
    self.ts("dve", c12.t[:, 0:4], sp_.t, -8.0, None, ALU.mult, None, rd=[sp_.b], wr=[c12.b])
    self.ts("dve", c12.t[:, 4:8], sp_.t, -16.0, None, ALU.mult, None, rd=[sp_.b], wr=[c12.b])
    c4 = c12.t.rearrange("p (k c d) -> p k c d", k=2, c=2)
    xhs = [self.sb("xh%d" % i, TC + 3, F32) for i in range(2)]
    tR = [self.sb("tR%d" % i, TC, F32) for i in range(2)]
    tI = [self.sb("tI%d" % i, TC, F32) for i in range(2)]
    tA = [self.sb("tA%d" % i, TC, F32) for i in range(2)]
    tT = [self.sb("tT%d" % i, TC, F32) for i in range(2)]
    tB = [self.sb("tB%d" % i, TC, F32) for i in range(2)]
    tH = [self.sb("tH%d" % i, TC, F32) for i in range(2)]
    gs = [self.sb("g%d" % i, TC, F32) for i in range(2)]
    ggs = [self.sb("gg%d" % i, TC, F32) for i in range(2)]
    obs = [self.sb("ob%d" % i, TC, BF16) for i in range(2)]
    carry = self.sb("carry", 1, F32)
    prs = [self.ps("pr%d" % i) for i in range(4)]
    pis = [self.ps("pi%d" % i) for i in range(4)]
    it = 0
    nps = 0
    for s in range(2):
        L = self.Ls[s]
        NCH = L // TC
        for cc in range(2):
            for c in range(NCH):
                xh = xhs[it % 2]
                it += 1
                t0 = c * TC
                lo = max(t0 - 2, 0)
                hi = min(t0 + TC + 1, L)
                if c == 0:
                    self.memset("dve", xh.t[:, 0:2], 0.0, wr=[xh.b])
                if c == NCH - 1:
                    self.memset("dve", xh.t[:, TC + 2:TC + 3], 0.0, wr=[xh.b])
                self.dma("pool", xh.t[:, lo - (t0 - 2):hi - (t0 - 2)], self.lh[s][cc * 128:(cc + 1) * 128, lo:hi], wr=[xh.b])
                xc = XC.t[:, t0:t0 + TC]
                self.ts("dve", xc, xh.t[:, 0:TC], cw3[:, cc, 0:1], cw3[:, cc, 4:5], ALU.mult, ALU.add,
                        rd=[xh.b, cw.b], wr=[XC.b])
                for k in range(1, 4):
                    self.stt("dve", xc, xh.t[:, k:k + TC], cw3[:, cc, k:k + 1], xc, ALU.mult, ALU.add,
                             rd=[xh.b, cw.b, XC.b], wr=[XC.b])
                self.cp("act", XCB.t[:, t0:t0 + TC], xc, rd=[XC.b], wr=[XCB.b])
            for d in range(2):
                order = list(range(NCH)) if d == 0 else list(range(NCH - 1, -1, -1))
                for ci, c in enumerate(order):
                    i2 = it % 2
                    it += 1
                    t0 = c * TC
                    for sub in range(2):
                        pr, pi = prs[nps % 4], pis[nps % 4]
                        nps += 1
                        cols = slice(t0 + sub * 512, t0 + (sub + 1) * 512)
                        self.mm(pr.t, WA5[:, d, 0, cc, :], XCB.t[:, cols], True, True, rd=[WA.b, XCB.b], wr=[pr.b])
                        self.mm(pi.t, WA5[:, d, 1, cc, :], XCB.t[:, cols], True, True, rd=[WA.b, XCB.b], wr=[pi.b])
                        self.act(tR[i2].t[:, sub * 512:(sub + 1) * 512], pr.t, AF.Sigmoid, rd=[pr.b, lb.b], wr=[tR[i2].b],
                                 bias=lb4[:, cc, d, 0:1])
                        self.act(tI[i2].t[:, sub * 512:(sub + 1) * 512], pi.t, AF.Sigmoid, rd=[pi.b, lb.b], wr=[tI[i2].b],
                                 bias=lb4[:, cc, d, 1:2])
                    self.act(tA[i2].t, tR[i2].t, AF.Exp, rd=[tR[i2].b, c12.b], wr=[tA[i2].b], scale=c4[:, 0, cc, d:d + 1])
                    self.act(tT[i2].t, tR[i2].t, AF.Exp, rd=[tR[i2].b, c12.b], wr=[tT[i2].b], scale=c4[:, 1, cc, d:d + 1])
                    self.act(tT[i2].t, tT[i2].t, AF.Sqrt, rd=[tT[i2].b], wr=[tT[i2].b], scale=-1.0, bias=1.0)
                    self.tt("dve", tB[i2].t, tI[i2].t, XC.t[:, t0:t0 + TC], ALU.mult, rd=[tI[i2].b, XC.b], wr=[tB[i2].b])
                    self.tt("dve", tB[i2].t, tB[i2].t, tT[i2].t, ALU.mult, rd=[tB[i2].b, tT[i2].b], wr=[tB[i2].b])
                    if d == 0:
                        init = 0.0 if ci == 0 else HF.t[:, t0 - 1:t0]
                        self.s.op("dve", (lambda e, o=HF.t[:, t0:t0 + TC], a=tA[i2].t, b=tB[i2].t, i0=init:
                                          e.tensor_tensor_scan(out=o, data0=a, data1=b, initial=i0, op0=ALU.mult, op1=ALU.add)),
                                  rd=[tA[i2].b, tB[i2].b, HF.b], wr=[HF.b])
                    else:
                        init = 0.0 if ci == 0 else carry.t
                        self.s.op("dve", (lambda e, o=rev(tH[i2].t), a=rev(tA[i2].t), b=rev(tB[i2].t), i0=init:
                                          e.tensor_tensor_scan(out=o, data0=a, data1=b, initial=i0, op0=ALU.mult, op1=ALU.add)),
                                  rd=[tA[i2].b, tB[i2].b, carry.b], wr=[tH[i2].b])
                        self.cp("dve", carry.t, tH[i2].t[:, 0:1], rd=[tH[i2].b], wr=[carry.b])
                        self.tt("dve", HF.t[:, t0:t0 + TC], HF.t[:, t0:t0 + TC], tH[i2].t, ALU.add,
                                rd=[HF.b, tH[i2].b], wr=[HF.b])
            for c in range(NCH):
                i2 = it % 2
                it += 1
                t0 = c * TC
                self.dma("pool", gs[i2].t, self.lh[s][256 + cc * 128:256 + (cc + 1) * 128, t0:t0 + TC], wr=[gs[i2].b])
                self.act(ggs[i2].t, gs[i2].t, AF.Gelu, rd=[gs[i2].b], wr=[ggs[i2].b])
                self.tt("dve", obs[i2].t, HF.t[:, t0:t0 + TC], ggs[i2].t, ALU.mult, rd=[HF.b, ggs[i2].b], wr=[obs[i2].b])
                self.dma("sp", self.catT[s][512 + cc * 128:512 + (cc + 1) * 128, t0:t0 + TC], obs[i2].t, rd=[obs[i2].b])
    self.phase_end()


K.phase3 = _phase3


def _ln_tail(self, po, n, xsrc_rows, gate, lg, lbias, dst_rows, T):
    i2 = T["i"] % 2
    T["i"] += 1
    xr, ysb, st, mv, rs, nm = T["xr"][i2], T["y"][i2], T["st"][i2], T["mv"][i2], T["rs"][i2], T["nm"][i2]
    self.dma("pool", xr.t[0:n, :], xsrc_rows, wr=[xr.b])
    for h in range(2):
        self.tt("dve", ysb.t[0:n, h * 512:(h + 1) * 512], po[h].t[0:n, :], gate.t[0:n, h * 512:(h + 1) * 512], ALU.mult,
                rd=[po[h].b, gate.b], wr=[ysb.b])
    self.stt("dve", ysb.t[0:n, :], xr.t[0:n, :], ALPHA, ysb.t[0:n, :], ALU.mult, ALU.add, rd=[xr.b, ysb.b], wr=[ysb.b])
    st3 = st.t.rearrange("p (c k) -> p c k", c=2)
    for h in range(2):
        self.s.op("dve", (lambda e, o=st3[0:n, h, :], i=ysb.t[0:n, h * 512:(h + 1) * 512]: e.bn_stats(out=o, in_=i)),
                  rd=[ysb.b], wr=[st.b])
    self.s.op("dve", (lambda e, o=mv.t[0:n, :], i=st3[0:n, :, :]: e.bn_aggr(out=o, in_=i)), rd=[st.b], wr=[mv.b])
    self.act(rs.t[0:n, :], mv.t[0:n, 1:2], AF.Sqrt, rd=[mv.b, self.epsl.b], wr=[rs.b], bias=self.epsl.t[0:n, :])
    self.recip(rs.t[0:n, :], rs.t[0:n, :], rd=[rs.b], wr=[rs.b])
    self.ts("dve", nm.t[0:n, :], mv.t[0:n, 0:1], -1.0, rs.t[0:n, :], ALU.mult, ALU.mult, rd=[mv.b, rs.b], wr=[nm.b])
    self.act(ysb.t[0:n, :], ysb.t[0:n, :], AF.Identity, rd=[ysb.b, rs.b, nm.b], wr=[ysb.b],
             scale=rs.t[0:n, :], bias=nm.t[0:n, :])
    self.tt("pool", ysb.t[0:n, :], ysb.t[0:n, :], lg.t[0:n, :], ALU.mult, rd=[ysb.b, lg.b], wr=[ysb.b])
    self.tt("pool", xr.t[0:n, :], ysb.t[0:n, :], lbias.t[0:n, :], ALU.add, rd=[ysb.b, lbias.b], wr=[xr.b])
    self.dma("sp", dst_rows, xr.t[0:n, :], rd=[xr.b])


def _ln_bufs(self):
    return {"i": 0, "xr": [self.sb("lnx%d" % i, 1024, F32) for i in range(2)],
            "y": [self.sb("lny%d" % i, 1024, F32) for i in range(2)],
            "st": [self.sb("lnst%d" % i, 12, F32) for i in range(2)], "mv": [self.sb("lnmv%d" % i, 2, F32) for i in range(2)],
            "rs": [self.sb("lnrs%d" % i, 1, F32) for i in range(2)], "nm": [self.sb("lnnm%d" % i, 1, F32) for i in range(2)]}


def _phase5(self, l, xsrc):
    self.wctx_u = ExitStack()
    wu = self.sb("wu", 8 * 2 * DFF, BF16, ctx=self.wctx_u)
    wu3 = wu.t.rearrange("p (k n) -> p k n", k=8)
    wub = [self.s.buf() for k in range(8)]
    self.pre_wu = (wu, wub)
    wd = self.sb("wd", 22 * 1024, BF16, ctx=self.wctx_u)
    wd3 = wd.t.rearrange("p (k n) -> p k n", k=22)
    wdb = [self.s.buf() for k in range(22)]
    self.pre_wd = (wd, wdb)
    self.phase_begin()
    wo = self.sb("wo", 8 * 1024, BF16)
    wo3 = wo.t.rearrange("p (k n) -> p k n", k=8)
    wob = [self.s.buf() for k in range(8)]
    for kc in range(8):
        self.dma("pool", wo3[:, kc, :], self.din["w_out"][l, kc * 128:(kc + 1) * 128, :], wr=[wob[kc]])
    lg = self.sb("lg", 1024, F32)
    lb = self.sb("lb", 1024, F32)
    self.dma("sp", lg.t, dap(self.din["ln1_g"], l * 1024, [(0, 128), (1, 1024)]), wr=[lg.b])
    self.dma("sp", lb.t, dap(self.din["ln1_b"], l * 1024, [(0, 128), (1, 1024)]), wr=[lb.b])
    for kc in range(8):
        self.dma("pool", wu3[:, kc, :], self.din["w_up"][l, kc * 128:(kc + 1) * 128, :], wr=[wub[kc]])
    for kc in range(22):
        self.dma("pool", wd3[:, kc, :], self.din["w_down"][l, kc * 128:(kc + 1) * 128, :], wr=[wdb[kc]])
    T = self.ln_bufs()
    cts = [self.sb("ct%d" % i, 8 * 512, BF16) for i in range(2)]
    pos = [[self.ps("po%d%d" % (i, h)) for h in range(2)] for i in range(2)]
    it = 0
    nj = 0
    for s in range(2):
        L = self.Ls[s]
        for w in range(L // 512):
            ct = cts[it % 2]
            it += 1
            ct3 = ct.t.rearrange("p (k t) -> p k t", k=8)
            self.dma("sp", ct3, dap(self.catT[s], w * 512, [(L, 128), (128 * L, 8), (1, 512)]), wr=[ct.b])
            for j in range(4):
                po = pos[nj % 2]
                nj += 1
                for h in range(2):
                    for kc in range(8):
                        self.mm(po[h].t, ct3[:, kc, j * 128:(j + 1) * 128], wo3[:, kc, h * 512:(h + 1) * 512],
                                kc == 0, kc == 7, rd=[ct.b, wob[kc]], wr=[po[h].b])
                r0 = w * 512 + j * 128
                self.ln_tail(po, 128, xsrc[s][r0:r0 + 128, :], self.grow[0 * 2 + s], lg, lb,
                             self.xres[s][r0:r0 + 128, :], T)
    self.phase_end()


def _phase6(self, l, dst):
    self.phase_begin()
    WN = 254
    wu, wub = self.pre_wu
    wu3 = wu.t.rearrange("p (k n) -> p k n", k=8)
    wd, wdb = self.pre_wd
    wd3 = wd.t.rearrange("p (k n) -> p k n", k=22)
    cw = self.sb("fcw", 44 * 4, F32)
    self.dma("sp", cw.t, self.din["ffw"][l].rearrange("p c k -> p (c k)"), wr=[cw.b])
    cw3 = cw.t.rearrange("p (c k) -> p c k", k=4)
    lg = self.sb("lg", 1024, F32)
    lb = self.sb("lb", 1024, F32)
    self.dma("sp", lg.t, dap(self.din["ln2_g"], l * 1024, [(0, 128), (1, 1024)]), wr=[lg.b])
    self.dma("sp", lb.t, dap(self.din["ln2_b"], l * 1024, [(0, 128), (1, 1024)]), wr=[lb.b])
    T = self.ln_bufs()
    xbs = [self.sb("fxb%d" % i, 2 * 1024, BF16) for i in range(2)]
    uT = self.sb("fuT", 8 * 256, BF16)
    uT3 = uT.t.rearrange("p (k t) -> p k t", k=8)
    g = self.sb("fg", 22 * 256, BF16)
    g3 = g.t.rearrange("p (k t) -> p k t", k=22)
    tgs = [self.sb("tg%d" % i, 256, F32) for i in range(2)]
    tvs = [self.sb("tv%d" % i, 256, F32) for i in range(2)]
    pts = [self.ps("fpt%d" % i, 256, BF16) for i in range(2)]
    pgs = [self.ps("fpg%d" % i, 256) for i in range(2)]
    pvs = [self.ps("fpv%d" % i, 256) for i in range(2)]
    po = [self.ps("fpo%d" % h) for h in range(2)]
    m3 = self.modT.t.rearrange("p (c s) -> p c s", s=2)
    wins = [(s, w) for s in range(2) for w in range((self.Ls[s] + WN - 1) // WN)]

    def load_win(idx):
        s, w = wins[idx]
        L = self.Ls[s]
        t0 = w * WN
        xb = xbs[idx % 2]
        xb3 = xb.t.rearrange("p (j c) -> p j c", j=2)
        if (w == 0) or (t0 + 255 > L):
            self.memset("dve", xb.t, 0.0, wr=[xb.b])
        for j in range(2):
            a = t0 - 1 + 128 * j
            lo, hi = max(a, 0), min(a + 128, L)
            if hi > lo:
                self.dma("pool", xb3[lo - a:hi - a, j, :], self.xres[s][lo:hi, :], wr=[xb.b])

    load_win(0)
    for idx, (s, w) in enumerate(wins):
        if True:
            L = self.Ls[s]
            xs_ = self.xres[s]
            t0 = w * WN
            nv = min(WN, L - t0)
            xb = xbs[idx % 2]
            xb3 = xb.t.rearrange("p (j c) -> p j c", j=2)
            for kc in range(8):
                pt = pts[kc % 2]
                for j in range(2):
                    self.tr(pt.t[:, j * 128:(j + 1) * 128], xb3[:, j, kc * 128:(kc + 1) * 128], self.ident.t,
                            rd=[xb.b, self.ident.b], wr=[pt.b])
                self.act(uT3[:, kc, :], pt.t, AF.Identity, rd=[pt.b, self.modT.b], wr=[uT.b],
                         scale=m3[:, 32 + kc, s:s + 1], bias=m3[:, 24 + kc, s:s + 1])
            if idx + 1 < len(wins):
                load_win(idx + 1)
            if w == 0:
                self.memset("dve", uT3[:, :, 0:1], 0.0, wr=[uT.b])
            if t0 + nv >= L:
                c0 = L - (t0 - 1)
                self.memset("dve", uT3[:, :, c0:256], 0.0, wr=[uT.b])
            for jc in range(22):
                pg, pv = pgs[jc % 2], pvs[jc % 2]
                tg, tv = tgs[jc % 2], tvs[jc % 2]
                for kc in range(8):
                    self.mm(pg.t, wu3[:, kc, jc * 128:(jc + 1) * 128], uT3[:, kc, :], kc == 0, kc == 7,
                            rd=[wub[kc], uT.b], wr=[pg.b])
                for kc in range(8):
                    self.mm(pv.t, wu3[:, kc, DFF + jc * 128:DFF + (jc + 1) * 128], uT3[:, kc, :], kc == 0, kc == 7,
                            rd=[wub[kc], uT.b], wr=[pv.b])
                for (pp, tt_, ch) in ((pg, tg, jc), (pv, tv, 22 + jc)):
                    self.act(tt_.t[:, 0:WN], pp.t[:, 0:WN], AF.Identity, rd=[pp.b, cw.b], wr=[tt_.b],
                             scale=cw3[:, ch, 0:1], bias=cw3[:, ch, 3:4])
                    for k in (1, 2):
                        self.stt("dve", tt_.t[:, 0:WN], pp.t[:, k:k + WN], cw3[:, ch, k:k + 1], tt_.t[:, 0:WN],
                                 ALU.mult, ALU.add, rd=[pp.b, cw.b, tt_.b], wr=[tt_.b])
                self.act(tg.t[:, 0:WN], tg.t[:, 0:WN], AF.Gelu, rd=[tg.b], wr=[tg.b])
                self.tt("dve", g3[:, jc, 0:WN], tg.t[:, 0:WN], tv.t[:, 0:WN], ALU.mult, rd=[tg.b, tv.b], wr=[g.b])
            for c0 in (0, 128):
                n = min(128, nv - c0)
                if n <= 0:
                    continue
                for h in range(2):
                    for kc in range(22):
                        self.mm(po[h].t[0:n, :], g3[:, kc, c0:c0 + n], wd3[:, kc, h * 512:(h + 1) * 512],
                                kc == 0, kc == 21, rd=[g.b, wdb[kc]], wr=[po[h].b])
                r0 = t0 + c0
                self.ln_tail(po, n, xs_[r0:r0 + n, :], self.grow[1 * 2 + s], lg, lb, dst[s][r0:r0 + n, :], T)
    self.phase_end()
    self.wctx_u.close()


K.ln_tail = _ln_tail
K.ln_bufs = _ln_bufs
K.phase5 = _phase5
K.phase6 = _phase6


def _sync(self):
    self.s.barrier()


def _phase4(self, l):
    self.phase_begin()
    TWO_PI = 2.0 * math.pi
    F128 = self.sb("F128", 3 * 128, BF16)
    self.dma("pool", F128.t, self.din["hF128"].rearrange("p r k -> p (r k)"), wr=[F128.b])
    F3 = F128.t.rearrange("p (r k) -> p r k", r=3)
    G128 = self.sb("G128", 2 * 256, BF16)
    self.dma("pool", G128.t, self.din["hG128"].rearrange("p r k -> p (r k)"), wr=[G128.b])
    G3 = G128.t.rearrange("p (r k) -> p r k", r=2)
    hyw = self.sb("hyw", 24, F32)
    self.dma("sp", hyw.t, self.din["hyw"][l].rearrange("p c k -> p (c k)"), wr=[hyw.b])
    hyw3 = hyw.t.rearrange("p (c k) -> p c k", k=4)
    drow = self.sb("drow", 512, F32)
    self.dma("sp", drow.t, dap(self.din["hy_bias"], l * 512, [(0, 128), (1, 512)]), wr=[drow.b])
    negd = self.sb("negd", 2, F32)
    self.dma("sp", negd.t, self.din["hnegd"], wr=[negd.b])
    w1 = self.sb("hw1", 64, F32)
    w2 = self.sb("hw2", 64, F32)
    w3 = self.sb("hw3", 64, F32)
    wout = self.sb("hwout", 1024, F32)
    hyb = self.sb("hyb", 4, F32)
    self.dma("sp", w1.t[0:33, :], self.din["hy_w1"][l], wr=[w1.b])
    self.dma("sp", w2.t[0:64, :], self.din["hy_w2"][l], wr=[w2.b])
    self.dma("sp", w3.t[0:64, :], self.din["hy_w3"][l], wr=[w3.b])
    self.dma("sp", wout.t[0:64, :], self.din["hy_wout"][l], wr=[wout.b])
    self.dma("sp", hyb.t[0:64, :], self.din["hyb"][l], wr=[hyb.b])
    bfq = self.sb("bfq", 3, F32)
    self.ts("dve", bfq.t[0:64, :], hyb.t[0:64, 0:3], hyb.t[0:64, 3:4], None, ALU.mult, None, rd=[hyb.b], wr=[bfq.b])
    zero1 = self.sb("zero1", 1, F32)
    self.memset("dve", zero1.t, 0.0, wr=[zero1.b])
    FN1 = self.sb("FN1", 256, BF16)
    Tt = self.sb("Tt", 256, F32)
    T2 = self.sb("T2", 256, F32)
    GN1 = self.sb("GN1", 128, BF16)
    rnrow = self.sb("rnrow", 512, F32)
    xhs = [self.sb("hxh%d" % i, 2050, F32) for i in range(2)]
    cos_ = [self.sb("hco%d" % i, 2048, F32) for i in range(2)]
    zcs = [self.sb("hzc%d" % i, 512, F32) for i in range(2)]
    tls = [self.sb("htl%d" % i, 512, F32) for i in range(2)]
    decs = [[self.sb("hdec%d%d" % (i, c), 512, F32) for c in range(2)] for i in range(2)]
    ya = self.sb("hya", 512, F32)
    kk = self.sb("hkk", 512, F32)
    hks = [self.sb("hk%d" % i, 512, F32) for i in range(3)]
    kts = [self.sb("hkt%d" % i, 512, F32) for i in range(3)]
    kab = self.sb("hkab", 512, F32)
    acc = self.sb("hacc", 8 * 16, F32)
    kb0 = self.sb("hkb0", 4, F32)
    nrm = self.sb("hnrm", 4, F32)
    Xbs = [self.sb("hXb%d" % i, 512, BF16) for i in range(2)]
    zfs = [self.sb("hzf%d" % i, 512, F32) for i in range(4)]
    gfs = [self.sb("hgf%d" % i, 512, F32) for i in range(4)]
    kfts = [self.sb("hkft%d" % i, 4 * 2 * 128, BF16) for i in range(2)]
    Apre = self.sb("hApre", 512, BF16)
    Apim = self.sb("hApim", 512, BF16)
    Yre = self.sb("hYre", 512, BF16)
    Yim = self.sb("hYim", 512, BF16)
    Bpre = self.sb("hBpre", 512, BF16)
    Bpim = self.sb("hBpim", 512, BF16)
    tq = [self.sb("htq%d" % i, 512, F32) for i in range(4)]
    outf = [self.sb("houtf%d" % i, 512, F32) for i in range(2)]
    outb = [self.sb("houtb%d" % i, 512, BF16) for i in range(2)]
    ps1 = self.ps2("hps1")
    pXre = self.ps("hpXre")
    pXim = self.ps("hpXim")
    pB = self.ps2("hpB")
    py = self.ps("hpy")
    it = 0

    def fft_s1(Xb, Kp, N1):
        NH = N1 // 2 + 1
        cw = 2 * NH
        CS = 256 if N1 == 128 else 128
        X3 = Xb.t.rearrange("p (c i) -> p c i", c=4)
        for ch in range(4):
            off = ch * CS
            self.mm(ps1.t[:, off:off + cw], X3[0:Kp, ch, :], FN1.t[0:Kp, 0:cw], True, True,
                    rd=[Xb.b, FN1.b], wr=[ps1.b])

    def fft_d1s2(N1):
        NH = N1 // 2 + 1
        cw = 2 * NH
        CS = 256 if N1 == 128 else 128
        Tt3 = Tt.t[:, 0:cw].rearrange("p (r k) -> p r k", r=2)
        Ar3 = Apre.t[:, 0:4 * NH].rearrange("p (c k) -> p c k", c=4)
        Ai3 = Apim.t[:, 0:4 * NH].rearrange("p (c k) -> p c k", c=4)
        v = ps1.t[:, 0:4 * CS].rearrange("p (c w) -> p c w", c=4)
        Are, Aim = v[:, :, 0:NH], v[:, :, NH:cw]
        Tre = Tt3[:, 0, :].unsqueeze(1).to_broadcast([128, 4, NH])
        Tim = Tt3[:, 1, :].unsqueeze(1).to_broadcast([128, 4, NH])
        t = [q.t[:, 0:4 * NH].rearrange("p (c k) -> p c k", c=4) for q in tq]
        self.tt("dve", t[0], Are, Tre, ALU.mult, rd=[ps1.b, Tt.b], wr=[tq[0].b])
        self.tt("dve", t[1], Aim, Tim, ALU.mult, rd=[ps1.b, Tt.b], wr=[tq[1].b])
        self.tt("dve", Ar3, t[0], t[1], ALU.subtract, rd=[tq[0].b, tq[1].b], wr=[Apre.b])
        self.tt("dve", t[2], Are, Tim, ALU.mult, rd=[ps1.b, Tt.b], wr=[tq[2].b])
        self.tt("dve", t[3], Aim, Tre, ALU.mult, rd=[ps1.b, Tt.b], wr=[tq[3].b])
        self.tt("dve", Ai3, t[2], t[3], ALU.add, rd=[tq[2].b, tq[3].b], wr=[Apim.b])
        W = 4 * NH
        self.mm(pXre.t[:, 0:W], F3[:, 0, :], Apre.t[:, 0:W], True, False, rd=[F128.b, Apre.b], wr=[pXre.b])
        self.mm(pXre.t[:, 0:W], F3[:, 2, :], Apim.t[:, 0:W], False, True, rd=[F128.b, Apim.b], wr=[pXre.b])
        self.mm(pXim.t[:, 0:W], F3[:, 0, :], Apim.t[:, 0:W], True, False, rd=[F128.b, Apim.b], wr=[pXim.b])
        self.mm(pXim.t[:, 0:W], F3[:, 1, :], Apre.t[:, 0:W], False, True, rd=[F128.b, Apre.b], wr=[pXim.b])

    for s in range(2):
        L = self.Ls[s]
        NB = L // 128
        N1 = 2 * NB
        NH = N1 // 2 + 1
        self.dma("pool", FN1.t[0:N1, 0:2 * NH], self.din["hFN1_%d" % s], wr=[FN1.b])
        self.dma("sp", Tt.t[:, 0:2 * NH], self.din["hT_%d" % s].rearrange("p r k -> p (r k)"), wr=[Tt.b])
        self.dma("sp", T2.t[0:NH, :], self.din["hT2_%d" % s].rearrange("p r k -> p (r k)"), wr=[T2.b])
        self.dma("pool", GN1.t[0:NH, 0:2 * NB], self.din["hGN1_%d" % s].rearrange("p r k -> p (r k)"), wr=[GN1.b])
        TCc = 2048
        for m in range(6):
            for c in range(L // TCc):
                xh, co = xhs[it % 2], cos_[it % 2]
                it += 1
                t0 = c * TCc
                lo, hi = max(t0 - 1, 0), min(t0 + TCc + 1, L)
                if c == 0:
                    self.memset("dve", xh.t[:, 0:1], 0.0, wr=[xh.b])
                if hi == L:
                    self.memset("dve", xh.t[:, TCc + 1:TCc + 2], 0.0, wr=[xh.b])
                self.dma("pool", xh.t[:, lo - (t0 - 1):hi - (t0 - 1)], self.lh[s][512 + m * 128:512 + (m + 1) * 128, lo:hi], wr=[xh.b])
                self.ts("dve", co.t, xh.t[:, 0:TCc], hyw3[:, m, 0:1], hyw3[:, m, 3:4], ALU.mult, ALU.add,
                        rd=[xh.b, hyw.b], wr=[co.b])
                for k in (1, 2):
                    self.stt("dve", co.t, xh.t[:, k:k + TCc], hyw3[:, m, k:k + 1], co.t, ALU.mult, ALU.add,
                             rd=[xh.b, hyw.b, co.b], wr=[co.b])
                self.dma("sp", self.hc[s][m * 128:(m + 1) * 128, t0:t0 + TCc], co.t, rd=[co.b])
        NCH = L // 512
        self.memset("dve", acc.t, 0.0, wr=[acc.b])
        acc3 = acc.t.rearrange("p (m c) -> p m c", m=8)
        for c in range(NCH):
            zc, tl = zcs[c % 2], tls[c % 2]
            dec = decs[c % 2]
            t0 = c * 512
            self.dma("pool", zc.t[0:33, :], self.din["hz%d" % s][:, t0:t0 + 512], wr=[zc.b])
            self.dma("pool", tl.t, dap(self.din["htl%d" % s], t0, [(0, 128), (1, 512)]), wr=[tl.b])
            for cc in range(2):
                self.act(dec[cc].t, tl.t, AF.Exp, rd=[tl.b, negd.b], wr=[dec[cc].b], scale=negd.t[:, cc:cc + 1])
            h, hK = zc, 33
            for k, wk in enumerate((w1, w2, w3)):
                ph = ps1
                pho = (k % 2) * 512
                self.mm(ph.t[0:64, pho:pho + 512], wk.t[0:hK, :], h.t[0:hK, :], True, True, rd=[wk.b, h.b], wr=[ph.b])
                self.ts("dve", ya.t[0:64, :], ph.t[0:64, pho:pho + 512], hyb.t[0:64, 3:4], bfq.t[0:64, k:k + 1], ALU.mult, ALU.add,
                        rd=[ph.b, hyb.b, bfq.b], wr=[ya.b])
                self.ts("dve", kk.t[0:64, :], ya.t[0:64, :], 1.0 / TWO_PI, MAGIC, ALU.mult, ALU.add, rd=[ya.b], wr=[kk.b])
                self.ts("dve", kk.t[0:64, :], kk.t[0:64, :], -MAGIC, -TWO_PI, ALU.add, ALU.mult, rd=[kk.b], wr=[kk.b])
                self.tt("dve", ya.t[0:64, :], kk.t[0:64, :], ya.t[0:64, :], ALU.add, rd=[kk.b, ya.b], wr=[ya.b])
                self.ts("dve", ya.t[0:64, :], ya.t[0:64, :], math.pi, -math.pi, ALU.min, ALU.max, rd=[ya.b], wr=[ya.b])
                hk = hks[k]
                self.act(hk.t[0:64, :], ya.t[0:64, :], AF.Sin, rd=[ya.b], wr=[hk.b])
                h, hK = hk, 64
            nk = 0
            for o in range(2):
                for cc in range(2):
                    for dr in (1, 0):
                        m = o * 2 + dr
                        col0 = (m * 2 + cc) * 128
                        pk = (pXre, pXim)[nk % 2]
                        kt = kts[nk % 3]
                        nk += 1
                        self.mm(pk.t, wout.t[0:64, col0:col0 + 128], h.t[0:64, :], True, True, rd=[wout.b, h.b], wr=[pk.b])
                        row0 = cc * 128
                        ai = (o * 2 + cc) * 2 + dr
                        kdst = self.kfull[s][o, row0:row0 + 128, :]
                        if dr == 1:
                            self.tt("dve", rev(kt.t), pk.t, dec[cc].t, ALU.mult, rd=[pk.b, dec[cc].b], wr=[kt.b])
                            if c == 0:
                                self.cp("dve", kb0.t[:, o * 2 + cc:o * 2 + cc + 1], kt.t[:, 511:512], rd=[kt.b], wr=[kb0.b])
                                n_ = 511
                                self.dma("sp", kdst[:, 2 * L - 511:2 * L], kt.t[:, 0:511], rd=[kt.b])
                            else:
                                n_ = 512
                                self.dma("sp", kdst[:, 2 * L - t0 - 511:2 * L - t0 + 1], kt.t, rd=[kt.b])
                        else:
                            self.tt("dve", kt.t, pk.t, dec[cc].t, ALU.mult, rd=[pk.b, dec[cc].b], wr=[kt.b])
                            if c == 0:
                                self.tt("dve", kt.t[:, 0:1], kt.t[:, 0:1], kb0.t[:, o * 2 + cc:o * 2 + cc + 1], ALU.add,
                                        rd=[kt.b, kb0.b], wr=[kt.b])
                            n_ = 512
                            self.dma("sp", kdst[:, t0:t0 + 512], kt.t, rd=[kt.b])
                        self.act(kab.t[:, 0:n_], kt.t[:, 0:n_], AF.Abs, rd=[kt.b, acc.b], wr=[kab.b, acc.b],
                                 accum=acc3[:, ai, c:c + 1])
        self.s.op("dve", (lambda e, o_=nrm.t, i_=acc.t.rearrange("p (m c) -> p m c", m=4):
                          e.tensor_reduce(out=o_, in_=i_, axis=AX.X, op=ALU.add)), rd=[acc.b], wr=[nrm.b])
        self.recip(nrm.t, nrm.t, rd=[nrm.b], wr=[nrm.b])
        for o in range(2):
            for cc in range(2):
                self.dma("sp", dap(self.rnd, o * 256 + cc * 128, [(1, 128), (1, 1)]), nrm.t[:, o * 2 + cc:o * 2 + cc + 1], rd=[nrm.b], slow=True)
                self.dma("sp", dap(self.kfull[s], (o * 256 + cc * 128) * 2 * L + L, [(2 * L, 128), (1, 1)]), zero1.t, rd=[zero1.b], slow=True)
        self.sync()
        self.dma("sp", rnrow.t, dap(self.rnd, 0, [(0, 128), (1, 512)]), wr=[rnrow.b])
        def h3_a(g):
            o, c0, i2 = g
            Xb = Xbs[i2 % 2]
            self.dma("pool", Xb.t[0:N1, :].rearrange("p (c i) -> p c i", c=4),
                     dap(self.kfull[s], (o * 256 + c0) * 2 * L, [(128, N1), (2 * L, 4), (1, 128)]), wr=[Xb.b])
            fft_s1(Xb, N1, N1)

        def h3_c(g):
            o, c0, i2 = g
            kft = kfts[i2 % 2]
            k4 = kft.t[:, 0:8 * NH].rearrange("p (c r k) -> p c r k", c=4, r=2)
            rb = rnrow.t[:, o * 256 + c0:o * 256 + c0 + 4].unsqueeze(2).to_broadcast([128, 4, NH])
            self.tt("dve", k4[:, :, 0, :], pXre.t[:, 0:4 * NH].rearrange("p (c k) -> p c k", c=4), rb, ALU.mult,
                    rd=[pXre.b, rnrow.b], wr=[kft.b])
            self.tt("dve", k4[:, :, 1, :], pXim.t[:, 0:4 * NH].rearrange("p (c k) -> p c k", c=4), rb, ALU.mult,
                    rd=[pXim.b, rnrow.b], wr=[kft.b])
            self.dma("sp", dap(self.kfs[s], (o * 128 * 256 + c0) * 2 * NH, [(256 * 2 * NH, 128), (1, 8 * NH)]),
                     kft.t[:, 0:8 * NH], rd=[kft.b])

        G = [(o, grp * 4, o * 64 + grp) for o in range(2) for grp in range(64)]
        for n in range(len(G) + 1):
            if n < len(G):
                h3_a(G[n])
            if n >= 1:
                h3_c(G[n - 1])
            if n < len(G):
                fft_d1s2(N1)
        self.sync()
        T23 = T2.t.rearrange("p (r i) -> p r i", r=2)
        GN3 = GN1.t[:, 0:2 * NB].rearrange("p (r a) -> p r a", r=2)
        W = 4 * NH
        def h4_a(g):
            o, c0, i2 = g
            zsrc = self.hc[s] if o == 0 else self.z1[s]
            Xb, kft, zf, gf = Xbs[i2 % 2], kfts[i2 % 2], zfs[i2 % 4], gfs[i2 % 4]
            zap = dap(zsrc, c0 * L, [(128, NB), (L, 4), (1, 128)])
            self.dma("pool", Xb.t[0:NB, :].rearrange("p (c i) -> p c i", c=4), zap, wr=[Xb.b])
            self.dma("sp", zf.t[0:NB, :].rearrange("p (c i) -> p c i", c=4), zap, wr=[zf.b])
            self.dma("sp", gf.t[0:NB, :].rearrange("p (c i) -> p c i", c=4),
                     dap(self.hc[s], (256 * (o + 1) + c0) * L, [(128, NB), (L, 4), (1, 128)]), wr=[gf.b])
            self.dma("sp", kft.t[:, 0:8 * NH],
                     dap(self.kfs[s], (o * 128 * 256 + c0) * 2 * NH, [(256 * 2 * NH, 128), (1, 8 * NH)]), wr=[kft.b])
            fft_s1(Xb, NB, N1)

        def h4_d2s3(g):
            o, c0, i2 = g
            kft = kfts[i2 % 2]
            k4 = kft.t[:, 0:8 * NH].rearrange("p (c r k) -> p c r k", c=4, r=2)
            Kre, Kim = k4[:, :, 0, :], k4[:, :, 1, :]
            Xr = pXre.t[:, 0:W].rearrange("p (c k) -> p c k", c=4)
            Xi = pXim.t[:, 0:W].rearrange("p (c k) -> p c k", c=4)
            t = [q.t[:, 0:W].rearrange("p (c k) -> p c k", c=4) for q in tq]
            self.tt("dve", t[0], Xr, Kre, ALU.mult, rd=[pXre.b, kft.b], wr=[tq[0].b])
            self.tt("dve", t[1], Xi, Kim, ALU.mult, rd=[pXim.b, kft.b], wr=[tq[1].b])
            self.tt("dve", Yre.t[:, 0:W].rearrange("p (c k) -> p c k", c=4), t[0], t[1], ALU.subtract,
                    rd=[tq[0].b, tq[1].b], wr=[Yre.b])
            self.tt("dve", t[2], Xr, Kim, ALU.mult, rd=[pXre.b, kft.b], wr=[tq[2].b])
            self.tt("dve", t[3], Xi, Kre, ALU.mult, rd=[pXim.b, kft.b], wr=[tq[3].b])
            self.tt("dve", Yim.t[:, 0:W].rearrange("p (c k) -> p c k", c=4), t[2], t[3], ALU.add,
                    rd=[tq[2].b, tq[3].b], wr=[Yim.b])
            Yr3 = Yre.t[:, 0:W].rearrange("p (c k) -> p c k", c=4)
            Yi3 = Yim.t[:, 0:W].rearrange("p (c k) -> p c k", c=4)
            for ch in range(4):
                off = ch * 256
                self.mm(pB.t[0:NH, off:off + 256], Yr3[:, ch, :], G3[:, 0, :], True, False,
                        rd=[Yre.b, G128.b], wr=[pB.b])
                self.mm(pB.t[0:NH, off:off + 256], Yi3[:, ch, :], G3[:, 1, :], False, True,
                        rd=[Yim.b, G128.b], wr=[pB.b])

        def h4_d3s4(g):
            Br4 = Bpre.t[0:NH, :].rearrange("p (c i) -> p c i", c=4)
            Bi4 = Bpim.t[0:NH, :].rearrange("p (c i) -> p c i", c=4)
            v = pB.t[0:NH, :].rearrange("p (c r i) -> p c r i", c=4, r=2)
            Bre, Bim = v[:, :, 0, :], v[:, :, 1, :]
            Tre = T23[0:NH, 0, :].unsqueeze(1).to_broadcast([NH, 4, 128])
            Tim = T23[0:NH, 1, :].unsqueeze(1).to_broadcast([NH, 4, 128])
            t = [q.t[0:NH, 0:512].rearrange("p (c i) -> p c i", c=4) for q in tq]
            self.tt("dve", t[0], Bre, Tre, ALU.mult, rd=[pB.b, T2.b], wr=[tq[0].b])
            self.tt("dve", t[1], Bim, Tim, ALU.mult, rd=[pB.b, T2.b], wr=[tq[1].b])
            self.tt("dve", Br4, t[0], t[1], ALU.subtract, rd=[tq[0].b, tq[1].b], wr=[Bpre.b])
            self.tt("dve", t[2], Bre, Tim, ALU.mult, rd=[pB.b, T2.b], wr=[tq[2].b])
            self.tt("dve", t[3], Bim, Tre, ALU.mult, rd=[pB.b, T2.b], wr=[tq[3].b])
            self.tt("dve", Bi4, t[2], t[3], ALU.add, rd=[tq[2].b, tq[3].b], wr=[Bpim.b])
            self.mm(py.t[0:NB, :], GN3[0:NH, 0, :], Bpre.t[0:NH, :], True, False, rd=[GN1.b, Bpre.b], wr=[py.b])
            self.mm(py.t[0:NB, :], GN3[0:NH, 1, :], Bpim.t[0:NH, :], False, True, rd=[GN1.b, Bpim.b], wr=[py.b])

        def h4_d4(g):
            o, c0, i2 = g
            zf, gf = zfs[i2 % 4], gfs[i2 % 4]
            t0_ = tq[0].t[0:NB, :]
            db = drow.t[0:NB, o * 256 + c0:o * 256 + c0 + 4].unsqueeze(2).to_broadcast([NB, 4, 128])
            self.tt("dve", t0_.rearrange("p (c i) -> p c i", c=4), zf.t[0:NB, :].rearrange("p (c i) -> p c i", c=4), db,
                    ALU.mult, rd=[zf.b, drow.b], wr=[tq[0].b])
            self.tt("dve", t0_, t0_, py.t[0:NB, :], ALU.add, rd=[tq[0].b, py.b], wr=[tq[0].b])
            if o == 0:
                ob = outf[i2 % 2]
                self.tt("dve", ob.t[0:NB, :], t0_, gf.t[0:NB, :], ALU.mult, rd=[tq[0].b, gf.b], wr=[ob.b])
                self.dma("sp", dap(self.z1[s], c0 * L, [(128, NB), (L, 4), (1, 128)]),
                         ob.t[0:NB, :].rearrange("p (c i) -> p c i", c=4), rd=[ob.b])
            else:
                ob = outb[i2 % 2]
                self.tt("dve", ob.t[0:NB, :], t0_, gf.t[0:NB, :], ALU.mult, rd=[tq[0].b, gf.b], wr=[ob.b])
                self.dma("sp", dap(self.catT[s], (768 + c0) * L, [(128, NB), (L, 4), (1, 128)]),
                         ob.t[0:NB, :].rearrange("p (c i) -> p c i", c=4), rd=[ob.b])

        for o in range(2):
            G = [(o, grp * 4, grp) for grp in range(64)]
            ng = len(G)
            for n in range(ng + 3):
                if n < ng:
                    h4_a(G[n])
                if 0 <= n - 3 < ng:
                    h4_d4(G[n - 3])
                if 0 <= n - 2 < ng:
                    h4_d3s4(G[n - 2])
                if 0 <= n - 1 < ng:
                    h4_d2s3(G[n - 1])
                if n < ng:
                    fft_d1s2(N1)
            self.sync()
    self.phase_end()


K.sync = _sync
K.phase4 = _phase4


def build_program(nlayers=DEPTH, dump=False):
    nc = bass.Bass("TRN2", target_bir_lowering=False)
    with ExitStack() as ctx:
        s = Sched(nc, ctx)
        k = K(nc, s, ctx, nlayers=nlayers, dbg={"dump": dump})
        k.setup()
        for l in range(nlayers):
            xsrc = k.xin if l == 0 else k.xa
            dst = k.yout if l == nlayers - 1 else k.xa
            k.phase0(l)
            k.phase1(l, xsrc)
            k.phase2(l)
            k.phase3(l)
            k.phase4(l)
            k.phase5(l, xsrc)
            k.phase6(l, dst)
        s.barrier()
        s.emit()
    return nc


_NC_CACHE = {}


def kernel(**inputs):
    inp = {k: np.asarray(v) for k, v in inputs.items()}
    if "nc" not in _NC_CACHE:
        _NC_CACHE["nc"] = build_program()
    nc = _NC_CACHE["nc"]
    sh = host_prep(inp)
    in_maps = []
    for i in range(8):
        m = dict(sh)
        m.update(core_prep(inp, i))
        in_maps.append(m)
    res = run_bass_kernel_spmd(nc, in_maps, core_ids=list(range(8)))
    yp = np.stack([np.asarray(r["yp"], dtype=np.float32) for r in res.results], axis=0)
    ys = np.stack([np.asarray(r["ys"], dtype=np.float32) for r in res.results], axis=0)
    return (yp, ys)
```
